# Optimizing a Trainium2 kernel written in Bass

```python
import math
import jax, jax.numpy as jnp
from jax import lax
import numpy as np

D_MODEL = 1024
BATCH = 4
SEQ = 4096
DEPTH = 2

HEAD_DIM = 64
DIFF_HEADS = D_MODEL // (4 * HEAD_DIM)
FOX_HEADS = D_MODEL // (2 * HEAD_DIM)
DIFF_QK = DIFF_HEADS * 2 * HEAD_DIM
DIFF_V = DIFF_HEADS * 2 * HEAD_DIM
FOX_QK = FOX_HEADS * HEAD_DIM
FOX_V = FOX_HEADS * HEAD_DIM
ATTN_IN_SPLIT = (DIFF_QK, DIFF_QK, DIFF_V, FOX_QK, FOX_QK, FOX_V, FOX_HEADS)
ATTN_IN_WIDTH = sum(ATTN_IN_SPLIT)
ATTN_OUT_WIDTH = DIFF_V + FOX_V
Q_BLOCK = 128
FORGET_BIAS_INIT = 3.0
RNN_BLOCK_W = 128
RNN_WIDTH = (4 * D_MODEL // 3) // RNN_BLOCK_W * RNN_BLOCK_W
RNN_BLOCKS = RNN_WIDTH // RNN_BLOCK_W
CONV_WIDTH = 4
RG_C = 8.0
D_FF = 4 * D_MODEL
PLE_DIM = 256
N_ATTN_LAYERS = (DEPTH + 1) // 2
N_REC_LAYERS = DEPTH // 2
NORM_EPS = 1e-6
SUBLN_EPS = 1e-5

kernel_name = "hybrid_diff_fox_rglru_block"


def rmsnorm(x, gain, eps=NORM_EPS):
    xf = x.astype(jnp.float32)
    y = xf * lax.rsqrt(jnp.mean(xf * xf, axis=-1, keepdims=True) + eps)
    return (y * gain.astype(jnp.float32)).astype(x.dtype)


def alibi_slopes(n_heads):
    return jnp.exp2(-8.0 * jnp.arange(1, n_heads + 1, dtype=jnp.float32) / n_heads)


def sweep_query_blocks(block_fn, batch, seq):
    out = lax.map(block_fn, jnp.arange(seq // Q_BLOCK))
    return jnp.moveaxis(out, 0, 1).reshape(batch, seq, out.shape[3], out.shape[4])


def differential_attention(q, k, v, lam):
    B, S, H, _, Dh = q.shape
    qf = q.astype(jnp.float32) * (Dh ** -0.5)
    kf = k.astype(jnp.float32)
    vf = v.astype(jnp.float32)
    slopes = alibi_slopes(H)
    kpos = jnp.arange(S)

    def block(i):
        qb = lax.dynamic_slice_in_dim(qf, i * Q_BLOCK, Q_BLOCK, axis=1)
        qpos = i * Q_BLOCK + jnp.arange(Q_BLOCK)
        s = jnp.einsum('bqhmd,bkhmd->bhmqk', qb, kf)
        dist = (qpos[:, None] - kpos[None, :]).astype(jnp.float32)
        s = s - (slopes[:, None, None] * dist)[None, :, None]
        causal = kpos[None, :] <= qpos[:, None]
        s = jnp.where(causal[None, None, None], s, -jnp.inf)
        pr = jax.nn.softmax(s, axis=-1)
        w = pr[:, :, 0] - lam * pr[:, :, 1]
        return jnp.einsum('bhqk,bkhe->bqhe', w, vf)

    return sweep_query_blocks(block, B, S)


def forgetting_attention(q, k, v, cum_log_f):
    B, S, H, Dh = q.shape
    qf = q.astype(jnp.float32) * (Dh ** -0.5)
    kf = k.astype(jnp.float32)
    vf = v.astype(jnp.float32)
    c_t = jnp.transpose(cum_log_f, (0, 2, 1))
    kpos = jnp.arange(S)

    def block(i):
        qb = lax.dynamic_slice_in_dim(qf, i * Q_BLOCK, Q_BLOCK, axis=1)
        cq = lax.dynamic_slice_in_dim(c_t, i * Q_BLOCK, Q_BLOCK, axis=2)
        qpos = i * Q_BLOCK + jnp.arange(Q_BLOCK)
        s = jnp.einsum('bqhd,bkhd->bhqk', qb, kf)
        s = s + (cq[..., :, None] - c_t[:, :, None, :])
        causal = kpos[None, :] <= qpos[:, None]
        s = jnp.where(causal[None, None], s, -jnp.inf)
        pr = jax.nn.softmax(s, axis=-1)
        return jnp.einsum('bhqk,bkhd->bqhd', pr, vf)

    return sweep_query_blocks(block, B, S)


def attention_mixer(hn, w_in, b_forget, w_out, lq1, lk1, lq2, lk2, subln, layer):
    B, S, _ = hn.shape
    z = hn @ w_in
    idx = [int(v) for v in np.cumsum(ATTN_IN_SPLIT)[:-1]]
    dq, dk, dv, fq, fk, fv, fz = jnp.split(z, idx, axis=-1)
    lam_init = 0.8 - 0.6 * math.exp(-0.3 * layer)
    lam = (jnp.exp(jnp.sum(lq1.astype(jnp.float32) * lk1.astype(jnp.float32)))
           - jnp.exp(jnp.sum(lq2.astype(jnp.float32) * lk2.astype(jnp.float32)))
           + lam_init)
    d_out = differential_attention(dq.reshape(B, S, DIFF_HEADS, 2, HEAD_DIM),
                                   dk.reshape(B, S, DIFF_HEADS, 2, HEAD_DIM),
                                   dv.reshape(B, S, DIFF_HEADS, 2 * HEAD_DIM), lam)
    d_out = rmsnorm(d_out, subln, eps=SUBLN_EPS) * (1.0 - lam_init)
    log_f = jax.nn.log_sigmoid(fz.astype(jnp.float32) + b_forget.astype(jnp.float32))
    cum_log_f = jnp.cumsum(log_f, axis=1)
    f_out = forgetting_attention(fq.reshape(B, S, FOX_HEADS, HEAD_DIM),
                                 fk.reshape(B, S, FOX_HEADS, HEAD_DIM),
                                 fv.reshape(B, S, FOX_HEADS, HEAD_DIM), cum_log_f)
    o = jnp.concatenate([d_out.reshape(B, S, DIFF_V), f_out.reshape(B, S, FOX_V)], axis=-1)
    return o.astype(hn.dtype) @ w_out


def _linear_recurrence_combine(earlier, later):
    a1, b1 = earlier
    a2, b2 = later
    return a1 * a2, a2 * b1 + b2


def recurrent_mixer(hn, w_in, conv_w, conv_b, wx, bx, wa, ba, a_param, w_out):
    B, S, _ = hn.shape
    gate_branch, xr = jnp.split(hn @ w_in, 2, axis=-1)
    y = jax.nn.gelu(gate_branch)
    xc = lax.conv_general_dilated(
        xr, conv_w.reshape(CONV_WIDTH, 1, RNN_WIDTH).astype(xr.dtype),
        window_strides=(1,), padding=[(CONV_WIDTH - 1, 0)],
        dimension_numbers=('NWC', 'WIO', 'NWC'),
        feature_group_count=RNN_WIDTH) + conv_b
    xb = xc.reshape(B, S, RNN_BLOCKS, RNN_BLOCK_W)
    gate_x = jax.nn.sigmoid(jnp.einsum('bsni,nij->bsnj', xb, wx).reshape(B, S, RNN_WIDTH) + bx)
    gate_a = jax.nn.sigmoid(jnp.einsum('bsni,nij->bsnj', xb, wa).reshape(B, S, RNN_WIDTH) + ba)
    log_a = RG_C * gate_a.astype(jnp.float32) * jax.nn.log_sigmoid(a_param.astype(jnp.float32))
    a = jnp.exp(log_a)
    mult = jnp.sqrt(-jnp.expm1(2.0 * log_a))
    mult = jnp.where((jnp.arange(S) == 0)[None, :, None], 1.0, mult)
    b = mult * gate_x.astype(jnp.float32) * xc.astype(jnp.float32)
    _, h = lax.associative_scan(_linear_recurrence_combine, (a, b), axis=1)
    return (h.astype(hn.dtype) * y) @ w_out


def setup_inputs(seed: int = 0) -> dict:
    key = jax.random.key(seed)
    ks = iter(jax.random.split(key, 40))

    def nrm(shape, scale):
        return jax.random.normal(next(ks), shape, jnp.float32) * scale

    def gain(shape):
        return 1.0 + nrm(shape, 0.02)

    NA, NR = N_ATTN_LAYERS, N_REC_LAYERS
    u = jax.random.uniform(next(ks), (NR, RNN_WIDTH), jnp.float32, 0.9, 0.999)
    s = u ** (1.0 / RG_C)
    a_param = jnp.log(s) - jnp.log1p(-s)
    return {
        "x": nrm((BATCH, SEQ, D_MODEL), 1.0),
        "p": nrm((DEPTH, BATCH, SEQ, PLE_DIM), 1.0),
        "ln_mix_pre": gain((DEPTH, D_MODEL)),
        "ln_mix_post": gain((DEPTH, D_MODEL)),
        "ln_mlp_pre": gain((DEPTH, D_MODEL)),
        "ln_mlp_post": gain((DEPTH, D_MODEL)),
        "mlp_w_up": nrm((DEPTH, D_MODEL, D_FF), D_MODEL ** -0.5),
        "mlp_w_down": nrm((DEPTH, D_FF, D_MODEL), D_FF ** -0.5),
        "ple_w_proj": nrm((DEPTH, PLE_DIM, D_MODEL), PLE_DIM ** -0.5),
        "ple_norm": gain((DEPTH, D_MODEL)),
        "ple_w_gate": nrm((DEPTH, D_MODEL, D_MODEL), D_MODEL ** -0.5),
        "attn_w_in": nrm((NA, D_MODEL, ATTN_IN_WIDTH), D_MODEL ** -0.5),
        "attn_b_forget": FORGET_BIAS_INIT + nrm((NA, FOX_HEADS), 0.1),
        "attn_w_out": nrm((NA, ATTN_OUT_WIDTH, D_MODEL), ATTN_OUT_WIDTH ** -0.5),
        "diff_lambda_q1": nrm((NA, HEAD_DIM), 0.1),
        "diff_lambda_k1": nrm((NA, HEAD_DIM), 0.1),
        "diff_lambda_q2": nrm((NA, HEAD_DIM), 0.1),
        "diff_lambda_k2": nrm((NA, HEAD_DIM), 0.1),
        "diff_subln": gain((NA, 2 * HEAD_DIM)),
        "rec_w_in": nrm((NR, D_MODEL, 2 * RNN_WIDTH), D_MODEL ** -0.5),
        "rec_conv_w": nrm((NR, CONV_WIDTH, RNN_WIDTH), CONV_WIDTH ** -0.5),
        "rec_conv_b": nrm((NR, RNN_WIDTH), 0.01),
        "rec_wx": nrm((NR, RNN_BLOCKS, RNN_BLOCK_W, RNN_BLOCK_W), RNN_BLOCK_W ** -0.5),
        "rec_bx": nrm((NR, RNN_WIDTH), 0.01),
        "rec_wa": nrm((NR, RNN_BLOCKS, RNN_BLOCK_W, RNN_BLOCK_W), RNN_BLOCK_W ** -0.5),
        "rec_ba": nrm((NR, RNN_WIDTH), 0.01),
        "rec_a_param": a_param,
        "rec_w_out": nrm((NR, RNN_WIDTH, D_MODEL), RNN_WIDTH ** -0.5),
    }


def reference(x, p, ln_mix_pre, ln_mix_post, ln_mlp_pre, ln_mlp_post, mlp_w_up, mlp_w_down,
              ple_w_proj, ple_norm, ple_w_gate, attn_w_in, attn_b_forget, attn_w_out,
              diff_lambda_q1, diff_lambda_k1, diff_lambda_q2, diff_lambda_k2, diff_subln,
              rec_w_in, rec_conv_w, rec_conv_b, rec_wx, rec_bx, rec_wa, rec_ba,
              rec_a_param, rec_w_out):
    h = x
    for layer in range(DEPTH):
        j = layer // 2
        hn = rmsnorm(h, ln_mix_pre[layer])
        if layer % 2 == 0:
            m = attention_mixer(hn, attn_w_in[j], attn_b_forget[j], attn_w_out[j],
                                diff_lambda_q1[j], diff_lambda_k1[j],
                                diff_lambda_q2[j], diff_lambda_k2[j], diff_subln[j], layer)
        else:
            m = recurrent_mixer(hn, rec_w_in[j], rec_conv_w[j], rec_conv_b[j], rec_wx[j],
                                rec_bx[j], rec_wa[j], rec_ba[j], rec_a_param[j], rec_w_out[j])
        h = h + rmsnorm(m, ln_mix_post[layer])
        u = rmsnorm(h, ln_mlp_pre[layer])
        f = jnp.square(jax.nn.relu(u @ mlp_w_up[layer])) @ mlp_w_down[layer]
        h = h + rmsnorm(f, ln_mlp_post[layer])
        e = rmsnorm(p[layer] @ ple_w_proj[layer], ple_norm[layer])
        h = h + e * jax.nn.sigmoid(h @ ple_w_gate[layer])
    return h
```

```python
import contextlib
import numpy as np
import concourse.bass as bass
import concourse.mybir as mybir
from concourse.bass_utils import run_bass_kernel_spmd

F32 = mybir.dt.float32
BF16 = mybir.dt.bfloat16
AF = mybir.ActivationFunctionType
ALU = mybir.AluOpType
AX = mybir.AxisListType

NCORES = 8
D = 1024
S = 4096
NOWN = 2048
DFF = 4096
RW = 1280
LAM_INIT0 = 0.8 - 0.6 * 1.0
SLOPES = [2.0 ** (-8.0 * (h + 1) / 4) for h in range(4)]


class Prog:
    K_DMA = 8

    def __init__(self, nc, es):
        self.nc = nc
        self.engs = {"pe": nc.tensor, "act": nc.scalar, "dve": nc.vector,
                     "pool": nc.gpsimd, "sp": nc.sync}
        self.semobj = {}
        for k in self.engs:
            self.semobj[k] = es.enter_context(nc.semaphore("sem_" + k))
        self.cnt = {k: 0 for k in self.engs}
        self.seen = {k: {} for k in self.engs}
        self.lastw = {}
        self.readers = {}
        self.dcnt = {}
        for q in ("sp", "act", "pool"):
            self.dcnt[q] = 0
            for i in range(self.K_DMA):
                self.semobj[(q, i)] = es.enter_context(nc.semaphore("d_%s_%d" % (q, i)))

    def _wait(self, eng, ev):
        sk, val = ev
        if self.seen[eng].get(sk, 0) >= val:
            return
        self.engs[eng].wait_ge(self.semobj[sk], val)
        self.seen[eng][sk] = val

    def _deps(self, eng, reads, writes):
        deps = {}
        for k in reads:
            w = self.lastw.get(k)
            if w is not None:
                deps[w[0]] = max(deps.get(w[0], 0), w[1])
        for k in writes:
            w = self.lastw.get(k)
            if w is not None:
                deps[w[0]] = max(deps.get(w[0], 0), w[1])
            for sk, v in self.readers.get(k, {}).items():
                deps[sk] = max(deps.get(sk, 0), v)
        for sk, v in deps.items():
            if eng == "pe" and sk == "pe":
                continue
            self._wait(eng, (sk, v))

    def _record(self, ev, reads, writes):
        for k in writes:
            self.lastw[k] = ev
            self.readers[k] = {}
        for k in reads:
            if k in writes:
                continue
            d = self.readers.setdefault(k, {})
            d[ev[0]] = max(d.get(ev[0], 0), ev[1])

    def op(self, eng, fn, reads=(), writes=()):
        self._deps(eng, reads, writes)
        inst = fn(self.engs[eng])
        self.cnt[eng] += 1
        inst.then_inc(self.semobj[eng], 1)
        self._record((eng, self.cnt[eng]), reads, writes)

    def dma(self, q, out, in_, reads=(), writes=()):
        self._deps(q, reads, writes)
        i = self.dcnt[q]
        s = i % self.K_DMA
        rnd = i // self.K_DMA
        if rnd > 0:
            self._wait(q, ((q, s), 16 * rnd))
        inst = self.engs[q].dma_start(out=out, in_=in_)
        inst.then_inc(self.semobj[(q, s)], 16)
        self.dcnt[q] = i + 1
        self._record(((q, s), 16 * (rnd + 1)), reads, writes)

    def barrier(self):
        evs = []
        for q in ("sp", "act", "pool"):
            n = self.dcnt[q]
            for s in range(self.K_DMA):
                m = (n - s + self.K_DMA - 1) // self.K_DMA if n > s else 0
                if m > 0:
                    evs.append(((q, s), 16 * m))
        for k in ("pe", "act", "dve", "pool", "sp"):
            if self.cnt[k] > 0:
                evs.append((k, self.cnt[k]))
        for eng in ("pe", "act", "dve", "pool", "sp"):
            for ev in evs:
                if ev[0] == eng:
                    continue
                self._wait(eng, ev)

    def barrier_keys(self, eng, keys):
        self._deps(eng, [], list(keys))

    def finish(self):
        for q in ("sp", "act", "pool"):
            n = self.dcnt[q]
            for s in range(self.K_DMA):
                m = (n - s + self.K_DMA - 1) // self.K_DMA if n > s else 0
                if m > 0:
                    self._wait("sp", ((q, s), 16 * m))
        for k in ("pe", "act", "dve", "pool"):
            if self.cnt[k] > 0:
                self._wait("sp", (k, self.cnt[k]))


class Ctx:
    pass


@contextlib.contextmanager
def scope(P):
    with contextlib.ExitStack() as es:
        yield es
        P.barrier()


def setup_common(nc, P, es, C, ident_d):
    C.nc = nc
    C.P = P
    C.identf = es.enter_context(nc.sbuf_tensor("identf", [128, 128], F32))
    C.identb = es.enter_context(nc.sbuf_tensor("identb", [128, 128], BF16))
    C.onesf = es.enter_context(nc.sbuf_tensor("onesf", [128, 512], F32))
    P.dma("sp", C.identf[:], ident_d, writes=["identf"])
    P.dma("pool", C.identb[:], ident_d, writes=["identb"])
    P.op("pool", lambda e: e.memset(C.onesf[:], 1.0), writes=["onesf"])
    C.ps = [es.enter_context(nc.psum_tensor("ps%d" % i, [128, 1024], F32)) for i in range(4)]
    C.psb = [t.bitcast(BF16) for t in C.ps]
    C.junk = es.enter_context(nc.sbuf_tensor("junk", [128, 1024], BF16))
    C.small = es.enter_context(nc.sbuf_tensor("small", [128, 64], F32))
    C.small_i = 0


def small_slot(C, n=1):
    i = C.small_i
    if i + n > 64:
        i = 0
    C.small_i = i + n
    return i


def psf(C, b):
    return C.ps[b // 2][:, (b % 2) * 512:(b % 2) * 512 + 512]


def psbf(C, b):
    return C.psb[b // 2][:, (b % 2) * 1024:(b % 2) * 1024 + 1024]


def rstd_from(C, src, src_keys, n, eps):
    P = C.P
    i = small_slot(C)
    ss = C.small[:, i:i + 1]
    key = ("small", i)
    P.op("act", lambda e: e.activation(out=C.junk[:, 0:n], in_=src, func=AF.Square, accum_out=ss),
         reads=list(src_keys), writes=["junk", key])
    P.op("dve", lambda e: e.tensor_scalar(out=ss, in0=ss, scalar1=1.0 / n, scalar2=float(eps), op0=ALU.mult, op1=ALU.add),
         reads=[key], writes=[key])
    P.op("pool", lambda e: e.tensor_tensor(out=ss, in0=ss, in1=C.negh, op=ALU.pow), reads=[key, "epsb"], writes=[key])
    return ss, key


def transpose_to(C, src_bf, src_keys, dst, dst_keys, bank, nk=8, evac="dve"):
    P = C.P
    pk = ("ps", bank)
    pv = psbf(C, bank)
    for k in range(nk):
        P.op("pe", lambda e, k=k: e.transpose(out=pv[:, k * 128:(k + 1) * 128], in_=src_bf[:, k * 128:(k + 1) * 128],
                                              identity=C.identb[:]),
             reads=list(src_keys) + ["identb"], writes=[pk])
    srcv = pv[:, 0:nk * 128].rearrange("p (k t) -> p k t", k=nk)
    if evac == "act":
        P.op("act", lambda e: e.copy(out=dst, in_=srcv), reads=[pk], writes=list(dst_keys))
    else:
        P.op("dve", lambda e: e.tensor_copy(out=dst, in_=srcv), reads=[pk], writes=list(dst_keys))


def load_w_bf16(C, dst, dram2d, nk, key, nsplit=1):
    P = C.P
    src = dram2d.rearrange("(k p) n -> p k n", p=128)
    step = nk // nsplit
    for i in range(nsplit):
        P.dma("pool", dst[:, i * step:(i + 1) * step, :], src[:, i * step:(i + 1) * step, :], writes=[(key, i)])
    return [(key, i) for i in range(nsplit)]


def emit_norm_residual(C, m_src, m_keys, gain, gain_key, res, res_keys, out, out_keys):
    P = C.P
    rs, rk = rstd_from(C, m_src, m_keys, D, 1e-6)
    P.op("dve", lambda e: e.scalar_tensor_tensor(out=out, in0=m_src, scalar=rs, in1=gain, op0=ALU.mult, op1=ALU.mult),
         reads=list(m_keys) + [rk, gain_key], writes=list(out_keys))
    P.op("pool", lambda e: e.tensor_tensor(out=out, in0=out, in1=res, op=ALU.add),
         reads=list(out_keys) + list(res_keys), writes=list(out_keys))


def emit_norm_bf16(C, src, src_keys, gain, gain_key, out_bf, out_keys):
    P = C.P
    rs, rk = rstd_from(C, src, src_keys, D, 1e-6)
    P.op("dve", lambda e: e.scalar_tensor_tensor(out=out_bf, in0=src, scalar=rs, in1=gain, op0=ALU.mult, op1=ALU.mult),
         reads=list(src_keys) + [rk, gain_key], writes=list(out_keys))


def load_gains(C, es, vec_d, rows, name):
    nc, P = C.nc, C.P
    g = es.enter_context(nc.sbuf_tensor(name, [128, len(rows), D], F32))
    for i, r in enumerate(rows):
        P.dma("sp", g[:, i, :], vec_d[r, :].partition_broadcast(128), writes=[(name, i)])
    return g


def setup_eps(C, es):
    nc, P = C.nc, C.P
    t = es.enter_context(nc.sbuf_tensor("epsb", [128, 3], F32))
    P.op("pool", lambda e: e.memset(t[:, 2:3], -0.5), writes=["epsb"])
    C.negh = t[:, 2:3]
    P.op("pool", lambda e: e.memset(t[:, 0:1], 1e-6), writes=["epsb"])
    P.op("pool", lambda e: e.memset(t[:, 1:2], 1e-5), writes=["epsb"])
    C.epsb = {1e-6: t[:, 0:1], 1e-5: t[:, 1:2]}


def emit_mlp(C, H, w_up_d, w_dn_d, vec_d, row_pre, row_post, tag, ntok=NOWN):
    nc, P = C.nc, C.P
    MT = 256
    with scope(P) as es:
        wup = es.enter_context(nc.sbuf_tensor("wup" + tag, [128, 8, DFF], BF16))
        wdn = es.enter_context(nc.sbuf_tensor("wdn" + tag, [128, 32, D], BF16))
        kup = load_w_bf16(C, wup, w_up_d, 8, "wup" + tag, nsplit=8)
        kdn = load_w_bf16(C, wdn, w_dn_d, 32, "wdn" + tag, nsplit=8)
        g = load_gains(C, es, vec_d, [row_pre, row_post], "gm" + tag)
        ha = es.enter_context(nc.sbuf_tensor("ha" + tag, [128, 2, 2, D], F32))
        ubf = es.enter_context(nc.sbuf_tensor("ubf" + tag, [128, 2, D], BF16))
        uT = es.enter_context(nc.sbuf_tensor("uT" + tag, [128, 2, 8, MT], BF16))
        hid = es.enter_context(nc.sbuf_tensor("hid" + tag, [128, 32, MT], BF16))
        rl = es.enter_context(nc.sbuf_tensor("rl" + tag, [128, 2, MT], F32))
        hb = es.enter_context(nc.sbuf_tensor("hb" + tag, [128, 2, D], F32))
        nmt = ntok // MT

        def prologue(t):
            sl = t % 2
            for e in range(2):
                lt = 2 * t + e
                P.dma("sp", ha[:, sl, e, :], H[lt * 128:(lt + 1) * 128, :], reads=[("H", lt)], writes=[("ha", sl, e)])
                emit_norm_bf16(C, ha[:, sl, e, :], [("ha", sl, e)], g[:, 0, :], ("gm" + tag, 0), ubf[:, e, :], [("ubf", e)])
                transpose_to(C, ubf[:, e, :], [("ubf", e)], uT[:, sl, :, e * 128:(e + 1) * 128], [("uT", sl, e)],
                             bank=6 + e, evac="act")

        def up(t):
            sl = t % 2
            for c in range(32):
                bank = c % 4
                pk = ("ps", bank)
                pv = psf(C, bank)[:, 0:MT]
                for k in range(8):
                    P.op("pe", lambda e: e.matmul(pv, lhsT=wup[:, k, c * 128:(c + 1) * 128], rhs=uT[:, sl, k, :],
                                                  start=(k == 0), stop=(k == 7)),
                         reads=[kup[k], ("uT", sl, 0), ("uT", sl, 1)], writes=[pk])
                rs_ = c % 2
                P.op("act", lambda e: e.activation(out=rl[:, rs_, :], in_=pv, func=AF.Relu),
                     reads=[pk], writes=[("rl", rs_)])
                P.op("pool", lambda e: e.tensor_tensor(out=hid[:, c, :], in0=rl[:, rs_, :], in1=rl[:, rs_, :], op=ALU.mult),
                     reads=[("rl", rs_)], writes=[("hid", c)])

        def down(t):
            sl = t % 2
            for e in range(2):
                lt = 2 * t + e
                for hf in range(2):
                    pk = ("ps", 4 + hf)
                    pv = psf(C, 4 + hf)
                    for c in range(32):
                        P.op("pe", lambda e_: e_.matmul(pv, lhsT=hid[:, c, e * 128:(e + 1) * 128],
                                                        rhs=wdn[:, c, hf * 512:(hf + 1) * 512],
                                                        start=(c == 0), stop=(c == 31)),
                             reads=[("hid", c), kdn[c // 4]], writes=[pk])
                fv = C.ps[2][:, :]
                emit_norm_residual(C, fv, [("ps", 4), ("ps", 5)], g[:, 1, :], ("gm" + tag, 1),
                                   ha[:, sl, e, :], [("ha", sl, e)], hb[:, e, :], [("hb", e)])
                P.dma("sp", H[lt * 128:(lt + 1) * 128, :], hb[:, e, :], reads=[("hb", e)], writes=[("H", lt)])

        prologue(0)
        for t in range(nmt):
            up(t)
            if t + 1 < nmt:
                prologue(t + 1)
            down(t)


def emit_ple(C, H, pT_d, w_pp_d, w_pg_d, vec_d, row_ple, OUT, tag, hn_row=None, HNT=None, ntiles=16):
    nc, P = C.nc, C.P
    with scope(P) as es:
        wpg = es.enter_context(nc.sbuf_tensor("wpg" + tag, [128, 8, D], BF16))
        wpp = es.enter_context(nc.sbuf_tensor("wpp" + tag, [128, 2, D], BF16))
        kpg = load_w_bf16(C, wpg, w_pg_d, 8, "wpg" + tag, nsplit=2)
        kpp = load_w_bf16(C, wpp, w_pp_d, 2, "wpp" + tag, nsplit=1)
        rows = [row_ple] + ([hn_row] if hn_row is not None else [])
        g = load_gains(C, es, vec_d, rows, "gp" + tag)
        hb = es.enter_context(nc.sbuf_tensor("phb" + tag, [128, 2, D], F32))
        hbb = es.enter_context(nc.sbuf_tensor("phbb" + tag, [128, 2, D], BF16))
        hbT = es.enter_context(nc.sbuf_tensor("phbT" + tag, [128, 2, 8, 128], BF16))
        pT = es.enter_context(nc.sbuf_tensor("ppT" + tag, [128, 2, 2, 128], BF16))
        sg = es.enter_context(nc.sbuf_tensor("psg" + tag, [128, 2, D], F32))
        ee = es.enter_context(nc.sbuf_tensor("pee" + tag, [128, 2, D], F32))
        hnb = es.enter_context(nc.sbuf_tensor("phnb" + tag, [128, 2, D], BF16))
        hnT = es.enter_context(nc.sbuf_tensor("phnT" + tag, [128, 2, 8, 128], BF16))
        pTv = pT_d.rearrange("(k f) t -> f k t", f=128)
        def front(lt):
            sl = lt % 2
            P.dma("sp", hb[:, sl, :], H[lt * 128:(lt + 1) * 128, :], reads=[("H", lt)], writes=[("phb", sl)])
            P.dma("pool", pT[:, sl, :, :], pTv[:, :, lt * 128:(lt + 1) * 128], writes=[("ppT", sl)])
            P.op("dve", lambda e: e.tensor_copy(out=hbb[:, sl, :], in_=hb[:, sl, :]), reads=[("phb", sl)], writes=[("phbb", sl)])
            transpose_to(C, hbb[:, sl, :], [("phbb", sl)], hbT[:, sl, :, :], [("phbT", sl)], bank=6, evac="act")
            for hf in range(2):
                pk = ("ps", hf)
                pv = psf(C, hf)
                for k in range(8):
                    P.op("pe", lambda e: e.matmul(pv, lhsT=hbT[:, sl, k, :], rhs=wpg[:, k, hf * 512:(hf + 1) * 512],
                                                  start=(k == 0), stop=(k == 7)),
                         reads=[("phbT", sl), kpg[k // 4]], writes=[pk])
            for hf in range(2):
                pk = ("ps", 2 + 2 * sl + hf)
                pv = psf(C, 2 + 2 * sl + hf)
                for k in range(2):
                    P.op("pe", lambda e: e.matmul(pv, lhsT=pT[:, sl, k, :], rhs=wpp[:, k, hf * 512:(hf + 1) * 512],
                                                  start=(k == 0), stop=(k == 1)),
                         reads=[("ppT", sl), kpp[0]], writes=[pk])

        def front_b(lt):
            sl = lt % 2
            P.op("act", lambda e: e.activation(out=sg[:, sl, :], in_=C.ps[0][:, :], func=AF.Sigmoid),
                 reads=[("ps", 0), ("ps", 1)], writes=[("psg", sl)])

        def back(lt):
            sl = lt % 2
            ev = C.ps[1 + sl][:, :]
            pks = [("ps", 2 + 2 * sl), ("ps", 3 + 2 * sl)]
            rs, rk = rstd_from(C, ev, pks, D, 1e-6)
            P.op("dve", lambda e: e.scalar_tensor_tensor(out=ee[:, sl, :], in0=ev, scalar=rs, in1=g[:, 0, :], op0=ALU.mult, op1=ALU.mult),
                 reads=pks + [rk, ("gp" + tag, 0)], writes=[("pee", sl)])
            P.op("pool", lambda e: e.tensor_tensor(out=ee[:, sl, :], in0=ee[:, sl, :], in1=sg[:, sl, :], op=ALU.mult),
                 reads=[("pee", sl), ("psg", sl)], writes=[("pee", sl)])
            P.op("dve", lambda e: e.tensor_tensor(out=ee[:, sl, :], in0=ee[:, sl, :], in1=hb[:, sl, :], op=ALU.add),
                 reads=[("pee", sl), ("phb", sl)], writes=[("pee", sl)])
            P.dma("sp", OUT[lt * 128:(lt + 1) * 128, :], ee[:, sl, :], reads=[("pee", sl)], writes=[("OUT", lt)])

        def hnpart(lt):
            sl = lt % 2
            emit_norm_bf16(C, ee[:, sl, :], [("pee", sl)], g[:, 1, :], ("gp" + tag, 1), hnb[:, sl, :], [("phnb", sl)])
            transpose_to(C, hnb[:, sl, :], [("phnb", sl)], hnT[:, sl, :, :], [("phnT", sl)], bank=7, evac="dve")
            P.dma("sp", HNT.rearrange("(k f) t -> f k t", f=128)[:, :, lt * 128:(lt + 1) * 128], hnT[:, sl, :, :],
                  reads=[("phnT", sl)], writes=[("HNT", lt)])

        front(0)
        front_b(0)
        for lt in range(ntiles):
            if lt + 1 < ntiles:
                front(lt + 1)
            back(lt)
            if hn_row is not None and lt >= 1:
                hnpart(lt - 1)
            if lt + 1 < ntiles:
                front_b(lt + 1)
        if hn_row is not None:
            hnpart(ntiles - 1)


def emit_attention(C, T, H, stage=9, full=False):
    nc, P = C.nc, C.P
    with scope(P) as es:
        g = load_gains(C, es, T["vec"], [0, 1], "ga")
        hnTf = es.enter_context(nc.sbuf_tensor("hnTf", [128, 8, S], BF16))
        NT = 32 if full else 16
        NCH = NT // 2
        KPC = 2 if full else 4
        NM = 2 if full else 4
        if full:
            hnTo = hnTf
        else:
            hnTo = es.enter_context(nc.sbuf_tensor("hnTo", [128, 8, NOWN], BF16))
        obuf = es.enter_context(nc.sbuf_tensor("obuf", [128, NT, D], BF16))
        masks = es.enter_context(nc.sbuf_tensor("masks_sb", [128, NM, 256], BF16))
        alibi = es.enter_context(nc.sbuf_tensor("alibi_sb", [128, 4, 32], F32))
        fb = es.enter_context(nc.sbuf_tensor("fb", [128, 2, 2, 32], F32))
        Stm = es.enter_context(nc.sbuf_tensor("Stm", [128, 32, 8], F32))
        Xb = es.enter_context(nc.sbuf_tensor("Xb", [128, NCH * 8], F32))
        neglam = es.enter_context(nc.sbuf_tensor("neglam", [128, 4], F32))
        subg = es.enter_context(nc.sbuf_tensor("subg", [128, 128], F32))
        P.dma("pool", masks[:], T["masks"].rearrange("m p q -> p m q"), writes=["masks"])
        P.dma("sp", alibi[:], T["alibi"].rearrange("p (h r) -> p h r", h=4), writes=["alibi"])
        P.dma("sp", subg[:], T["vec"][6, 0:128].partition_broadcast(128), writes=["subg"])
        P.op("dve", lambda e: e.tensor_scalar(out=subg[:], in0=subg[:], scalar1=1.0 - LAM_INIT0, scalar2=None, op0=ALU.mult),
             reads=["subg"], writes=["subg"])

        with scope(P) as es2:
            lv = es2.enter_context(nc.sbuf_tensor("lv", [1, 256], F32))
            pr = es2.enter_context(nc.sbuf_tensor("pr", [1, 128], F32))
            dots = es2.enter_context(nc.sbuf_tensor("dots", [1, 2], F32))
            P.dma("sp", lv[:], T["lamv"], writes=["lv"])
            P.op("dve", lambda e: e.tensor_tensor(out=pr[:, 0:64], in0=lv[:, 0:64], in1=lv[:, 64:128], op=ALU.mult),
                 reads=["lv"], writes=["pr"])
            P.op("dve", lambda e: e.tensor_tensor(out=pr[:, 64:128], in0=lv[:, 128:192], in1=lv[:, 192:256], op=ALU.mult),
                 reads=["lv", "pr"], writes=["pr"])
            P.op("dve", lambda e: e.tensor_reduce(out=dots[:, 0:2], in_=pr[:, :].rearrange("p (a d) -> p a d", a=2),
                                                  axis=AX.X, op=ALU.add), reads=["pr"], writes=["dots"])
            pv = psf(C, 0)[:, 0:2]
            P.op("pe", lambda e: e.matmul(pv, lhsT=C.onesf[0:1, 0:128], rhs=dots[0:1, 0:2], start=True, stop=True),
                 reads=["onesf", "dots"], writes=[("ps", 0)])
            P.op("act", lambda e: e.activation(out=neglam[:, 0:2], in_=pv, func=AF.Exp), reads=[("ps", 0)], writes=["neglam"])
            P.op("dve", lambda e: e.tensor_tensor(out=neglam[:, 2:3], in0=neglam[:, 1:2], in1=neglam[:, 0:1], op=ALU.subtract),
                 reads=["neglam"], writes=["neglam"])
            P.op("dve", lambda e: e.tensor_scalar(out=neglam[:, 2:3], in0=neglam[:, 2:3], scalar1=-LAM_INIT0, scalar2=None, op0=ALU.add),
                 reads=["neglam"], writes=["neglam"])

        with scope(P) as es2:
            xt = es2.enter_context(nc.sbuf_tensor("xt", [128, 2, D], F32))
            xb = es2.enter_context(nc.sbuf_tensor("xb", [128, 2, D], BF16))
            for i in range(32 if full else 48):
                sl = i % 2
                if i < 32:
                    src = T["xf"][i * 128:(i + 1) * 128, :]
                    dst = hnTf[:, :, i * 128:(i + 1) * 128]
                    dk = ("hnTf", i // 4)
                else:
                    j = i - 32
                    src = T["xo"][j * 128:(j + 1) * 128, :]
                    dst = hnTo[:, :, j * 128:(j + 1) * 128]
                    dk = ("hnTo", j // 4)
                if full:
                    dk = ("hnTf", i // 4)
                P.dma("sp", xt[:, sl, :], src, writes=[("xt", sl)])
                emit_norm_bf16(C, xt[:, sl, :], [("xt", sl)], g[:, 0, :], ("ga", 0), xb[:, sl, :], [("xb", sl)])
                transpose_to(C, xb[:, sl, :], [("xb", sl)], dst, [dk], bank=6 + sl, evac=("act" if sl else "dve"))

        if stage < 1:
            return
        with scope(P) as es2:
            wfz = es2.enter_context(nc.sbuf_tensor("wfz", [128, 8, 8], BF16))
            negb = es2.enter_context(nc.sbuf_tensor("negb", [8, 1], F32))
            Lf = es2.enter_context(nc.sbuf_tensor("Lf", [8, S], F32))
            Sc = es2.enter_context(nc.sbuf_tensor("Sc", [8, S], F32))
            Dm = es2.enter_context(nc.sbuf_tensor("Dm", [8, NCH, 8], F32))
            P.dma("pool", wfz[:], T["w_in"].rearrange("(k p) n -> p k n", p=128)[:, :, 3072:3080], writes=["wfz"])
            P.dma("sp", negb[:], T["bf"], writes=["negb"])
            P.op("dve", lambda e: e.tensor_scalar(out=negb[:], in0=negb[:], scalar1=-1.0, scalar2=None, op0=ALU.mult),
                 reads=["negb"], writes=["negb"])
            for n in range(8):
                bank = n % 2
                pv = psf(C, bank)[0:8, :]
                for k in range(8):
                    P.op("pe", lambda e, k=k, n=n, pv=pv: e.matmul(pv, lhsT=wfz[:, k, :], rhs=hnTf[:, k, n * 512:(n + 1) * 512],
                                                                   start=(k == 0), stop=(k == 7)),
                         reads=["wfz", ("hnTf", n)], writes=[("ps", bank)])
                P.op("act", lambda e, n=n, pv=pv: e.activation(out=Lf[:, n * 512:(n + 1) * 512], in_=pv, func=AF.Exp, scale=-1.0, bias=negb[:, 0:1]),
                     reads=[("ps", bank), "negb"], writes=[("Lf", n)])
            for n in range(8):
                P.op("act", lambda e, n=n: e.activation(out=Lf[:, n * 512:(n + 1) * 512], in_=Lf[:, n * 512:(n + 1) * 512], func=AF.Ln, bias=1.0),
                     reads=[("Lf", n)], writes=[("Lf", n)])
            for n in range(8):
                init = 0.0 if n == 0 else Sc[:, n * 512 - 1:n * 512]
                rd = [("Lf", n), "onesf"] + ([("Sc", n - 1)] if n else [])
                P.op("dve", lambda e, n=n, init=init: e.tensor_tensor_scan(out=Sc[:, n * 512:(n + 1) * 512], data0=C.onesf[0:8, :],
                                                                           data1=Lf[:, n * 512:(n + 1) * 512], initial=init,
                                                                           op0=ALU.mult, op1=ALU.add),
                     reads=rd, writes=[("Sc", n)])
            pvt = psf(C, 2)[:, 0:256]
            for kb in range(32):
                P.op("pe", lambda e, kb=kb: e.transpose(out=pvt[:, kb * 8:(kb + 1) * 8], in_=Sc[0:8, kb * 128:(kb + 1) * 128],
                                                        identity=C.identf[0:8, 0:8]),
                     reads=[("Sc", kb // 4), "identf"], writes=[("ps", 2)])
            P.op("dve", lambda e: e.tensor_copy(out=Stm[:, :, :], in_=pvt.rearrange("p (k h) -> p k h", h=8)),
                 reads=[("ps", 2)], writes=["Stm"])
            CW = 128 * KPC
            ssel = Sc[:, :].rearrange("h (p t) -> h p t", t=CW)[:, :, CW - 1:CW]
            P.op("dve", lambda e: e.tensor_tensor(out=Dm[:, :, :], in0=ssel.to_broadcast([8, NCH, 8]),
                                                  in1=C.identf[0:8, 0:8].unsqueeze(1).to_broadcast([8, NCH, 8]), op=ALU.mult),
                 reads=[("Sc", n) for n in range(8)] + ["identf"], writes=["Dm"])
            pvx = psf(C, 3)[:, 0:NCH * 8]
            P.op("pe", lambda e: e.matmul(pvx, lhsT=C.onesf[0:8, 0:128], rhs=Dm[:, :, :].rearrange("h p g -> h (p g)"), start=True, stop=True),
                 reads=["onesf", "Dm"], writes=[("ps", 3)])
            P.op("dve", lambda e: e.tensor_copy(out=Xb[:, :], in_=pvx), reads=[("ps", 3)], writes=["Xb"])
        if stage < 2:
            return
        with scope(P) as es2:
            wq = es2.enter_context(nc.sbuf_tensor("wq", [128, 2, 8, 128], BF16))
            wk = es2.enter_context(nc.sbuf_tensor("wk", [128, 2, 8, 128], BF16))
            wv = es2.enter_context(nc.sbuf_tensor("wv", [128, 2, 8, 128], BF16))
            KT = es2.enter_context(nc.sbuf_tensor("KT", [128, S], BF16))
            QT = es2.enter_context(nc.sbuf_tensor("QTz", [128, 2, NT * 128], BF16))
            Vb = es2.enter_context(nc.sbuf_tensor("Vb", [128, 32, 130], BF16))
            Vd = Vb[:, :, 0:129]
            Vf = Vb[:, :, :].rearrange("p k (h d) -> p k h d", h=2)
            PT = es2.enter_context(nc.sbuf_tensor("PT", [128, 4, 2, 256], BF16))
            t1 = es2.enter_context(nc.sbuf_tensor("t1", [128, 2, 128], F32))
            dd = es2.enter_context(nc.sbuf_tensor("dd", [128, 2, 128], F32))
            rr = es2.enter_context(nc.sbuf_tensor("rr", [128, 2, 8], F32))
            P.op("pool", lambda e: e.memset(QT[:, :, :], 0.0), writes=[("QT", n) for n in range(NT // 4)])
            w_in_v = T["w_in"].rearrange("(k p) n -> p k n", p=128)
            rr_i = [0]
            for gidx in range(8):
                ws = gidx % 2
                diff = gidx < 4
                if gidx in (0, 4):
                    vk = [("V", kq) for kq in range(8)]
                    P.op("pool", lambda e: e.memset(Vb[:, :, :], 1.0), writes=vk)
                if diff:
                    qc, kc, vc = gidx * 128, 512 + gidx * 128, 1024 + gidx * 128
                else:
                    qc, kc, vc = 1536 + (gidx - 4) * 128, 2048 + (gidx - 4) * 128, 2560 + (gidx - 4) * 128
                P.dma("pool", wq[:, ws, :, :], w_in_v[:, :, qc:qc + 128], writes=[("wq", ws)])
                P.dma("pool", wk[:, ws, :, :], w_in_v[:, :, kc:kc + 128], writes=[("wk", ws)])
                P.dma("pool", wv[:, ws, :, :], w_in_v[:, :, vc:vc + 128], writes=[("wv", ws)])
                for n in range(8):
                    bank = 2 * (n % 2)
                    pv = psf(C, bank)
                    for k in range(8):
                        P.op("pe", lambda e, k=k, n=n, pv=pv: e.matmul(pv, lhsT=wk[:, ws, k, :], rhs=hnTf[:, k, n * 512:(n + 1) * 512],
                                                                       start=(k == 0), stop=(k == 7)),
                             reads=[("wk", ws), ("hnTf", n)], writes=[("ps", bank)])
                    if n % 2 == 0:
                        P.op("act", lambda e, n=n, pv=pv: e.copy(out=KT[:, n * 512:(n + 1) * 512], in_=pv),
                             reads=[("ps", bank)], writes=[("KT", n)])
                    else:
                        P.op("dve", lambda e, n=n, pv=pv: e.tensor_copy(out=KT[:, n * 512:(n + 1) * 512], in_=pv),
                             reads=[("ps", bank)], writes=[("KT", n)])
                for n in range(NT // 4):
                    bank = 2 * (n % 2)
                    pv = psf(C, bank)
                    for k in range(8):
                        P.op("pe", lambda e, k=k, n=n, pv=pv: e.matmul(pv, lhsT=wq[:, ws, k, :], rhs=hnTo[:, k, n * 512:(n + 1) * 512],
                                                                       start=(k == 0), stop=(k == 7)),
                             reads=[("wq", ws), (("hnTf" if full else "hnTo"), n)], writes=[("ps", bank)])
                    P.op("act", lambda e: e.copy(out=QT[0:64, 0, n * 512:(n + 1) * 512], in_=pv[0:64, :]),
                         reads=[("ps", bank)], writes=[("QT", n)])
                    P.op("dve", lambda e: e.tensor_copy(out=QT[64:128, 1, n * 512:(n + 1) * 512], in_=pv[64:128, :]),
                         reads=[("ps", bank), ("QT", n)], writes=[("QT", n)])
                for kq in range(8):
                    bank = 2 * (kq % 2)
                    pv = psf(C, bank)
                    for j in range(4):
                        kb = kq * 4 + j
                        for k in range(8):
                            P.op("pe", lambda e, k=k, kb=kb, j=j, pv=pv: e.matmul(pv[:, j * 128:(j + 1) * 128], lhsT=hnTf[:, k, kb * 128:(kb + 1) * 128],
                                                                                  rhs=wv[:, ws, k, :], start=(k == 0), stop=(k == 7)),
                                 reads=[("wv", ws), ("hnTf", kb // 4)], writes=[("ps", bank)])
                    if diff:
                        dst = Vd[:, kq * 4:(kq + 1) * 4, 0:128]
                        srcv = pv.rearrange("p (j d) -> p j d", j=4)
                        vkey = ("V", kq)
                    else:
                        dst = Vf[:, kq * 4:(kq + 1) * 4, :, 0:64]
                        srcv = pv.rearrange("p (j h d) -> p j h d", j=4, h=2)
                        vkey = ("V", kq)
                    if kq % 2 == 0:
                        P.op("act", lambda e, dst=dst, srcv=srcv: e.copy(out=dst, in_=srcv), reads=[("ps", bank)], writes=[vkey])
                    else:
                        P.op("dve", lambda e, dst=dst, srcv=srcv: e.tensor_copy(out=dst, in_=srcv), reads=[("ps", bank)], writes=[vkey])

                for p in range(NCH):
                    nkb = KPC * p + KPC
                    kb0 = KPC * p
                    oset = p % 2
                    if not diff:
                        fsl = p % 2
                        for sh in range(2):
                            hh_ = 2 * (gidx - 4) + sh
                            P.op("dve", lambda e: e.tensor_scalar(out=fb[:, fsl, sh, 0:nkb], in0=Stm[:, 0:nkb, hh_],
                                                                  scalar1=Xb[:, p * 8 + hh_:p * 8 + hh_ + 1], scalar2=None, op0=ALU.subtract),
                                 reads=["Stm", "Xb"], writes=[("fb", fsl)])
                    def emit_S(kb, p=p):
                        slot = kb % 4
                        for sh in range(2):
                            pvS = psf(C, slot)[:, sh * 256:(sh + 1) * 256]
                            P.op("pe", lambda e: e.matmul(pvS, lhsT=KT[:, kb * 128:(kb + 1) * 128],
                                                          rhs=QT[:, sh, p * 256:(p + 1) * 256], start=True, stop=True),
                                 reads=[("KT", kb // 4), ("QT", p // 2)], writes=[("ps", slot)])

                    def emit_PV(kb, p=p, nkb=nkb, oset=oset):
                        slot = kb % 4
                        for sh in range(2):
                            if diff:
                                ai = kb - kb0 + (30 if full else 28)
                                bias = alibi[:, gidx, ai:ai + 1]
                                bk = "alibi"
                            else:
                                bias = fb[:, p % 2, sh, kb:kb + 1]
                                bk = ("fb", p % 2)
                            P.op("act", lambda e: e.activation(out=PT[:, slot, sh, :], in_=psf(C, slot)[:, sh * 256:(sh + 1) * 256],
                                                               func=AF.Exp, scale=0.125, bias=bias),
                                 reads=[("ps", slot), bk], writes=[("PT", slot, sh)])
                        if kb >= kb0:
                            mk = masks[:, kb - kb0, :].unsqueeze(1).to_broadcast([128, 2, 256])
                            P.op("pool", lambda e: e.tensor_tensor(out=PT[:, slot, :, :], in0=PT[:, slot, :, :], in1=mk, op=ALU.mult),
                                 reads=[("PT", slot, 0), ("PT", slot, 1), "masks"], writes=[("PT", slot, 0), ("PT", slot, 1)])
                        for sh in range(2):
                            obank = 4 + oset * 2 + sh
                            for e_ in range(2):
                                if diff:
                                    ov = psf(C, obank)[:, e_ * 129:(e_ + 1) * 129]
                                    rhs = Vd[:, kb, :]
                                else:
                                    ov = psf(C, obank)[:, e_ * 65:(e_ + 1) * 65]
                                    rhs = Vf[:, kb, sh, :]
                                first = (kb == 0 and e_ == 0)
                                P.op("pe", lambda e: e.matmul(ov, lhsT=PT[:, slot, sh, e_ * 128:(e_ + 1) * 128], rhs=rhs,
                                                              start=first, stop=(kb == nkb - 1), skip_group_check=True),
                                     reads=[("PT", slot, sh), ("V", kb // 4)], writes=[("ps", obank)])

                    LOOK = 2
                    for kb in range(min(LOOK, nkb)):
                        emit_S(kb)
                    for kb in range(nkb):
                        if kb + LOOK < nkb:
                            emit_S(kb + LOOK)
                        emit_PV(kb)

                    b0 = 4 + oset * 2
                    i0 = rr_i[0] % 2
                    rr_i[0] += 1
                    rk = ("rr", i0)
                    if diff:
                        O1 = psf(C, b0)[:, 0:258].rearrange("p (e d) -> p e d", e=2)
                        O2 = psf(C, b0 + 1)[:, 0:258].rearrange("p (e d) -> p e d", e=2)
                        P.op("dve", lambda e: e.reciprocal(out=rr[:, i0, 0:2], in_=O1[:, :, 128]), reads=[("ps", b0)], writes=[rk])
                        P.op("dve", lambda e: e.reciprocal(out=rr[:, i0, 2:4], in_=O2[:, :, 128]), reads=[("ps", b0 + 1)], writes=[rk])
                        P.op("dve", lambda e: e.tensor_scalar(out=rr[:, i0, 2:4], in0=rr[:, i0, 2:4], scalar1=neglam[:, 2:3], scalar2=None, op0=ALU.mult),
                             reads=[rk, "neglam"], writes=[rk])
                        for e_ in range(2):
                            lt = 2 * p + e_
                            P.op("act", lambda e, e_=e_: e.activation(out=t1[:, e_, :], in_=O1[:, e_, 0:128], func=AF.Copy, scale=rr[:, i0, e_:e_ + 1]),
                                 reads=[("ps", b0), rk], writes=[("t1", e_)])
                            P.op("dve", lambda e, e_=e_: e.scalar_tensor_tensor(out=dd[:, e_, :], in0=O2[:, e_, 0:128], scalar=rr[:, i0, 2 + e_:3 + e_],
                                                                                in1=t1[:, e_, :], op0=ALU.mult, op1=ALU.add),
                                 reads=[("ps", b0 + 1), rk, ("t1", e_)], writes=[("dd", e_)])
                            rs, rsk = rstd_from(C, dd[:, e_, :], [("dd", e_)], 128, 1e-5)
                            P.op("dve", lambda e, e_=e_, lt=lt, rs=rs: e.scalar_tensor_tensor(out=obuf[:, lt, gidx * 128:(gidx + 1) * 128], in0=dd[:, e_, :],
                                                                                             scalar=rs, in1=subg[:, :], op0=ALU.mult, op1=ALU.mult),
                                 reads=[("dd", e_), rsk, "subg"], writes=[("obuf", lt)])
                    else:
                        for sh in range(2):
                            Ov = psf(C, b0 + sh)[:, 0:130].rearrange("p (e d) -> p e d", e=2)
                            P.op("dve", lambda e, sh=sh, Ov=Ov: e.reciprocal(out=rr[:, i0, 4 + 2 * sh:6 + 2 * sh], in_=Ov[:, :, 64]),
                                 reads=[("ps", b0 + sh)], writes=[rk])
                            for e_ in range(2):
                                lt = 2 * p + e_
                                c0 = 512 + (2 * (gidx - 4) + sh) * 64
                                P.op("act", lambda e, sh=sh, e_=e_, lt=lt, c0=c0, Ov=Ov: e.activation(out=obuf[:, lt, c0:c0 + 64], in_=Ov[:, e_, 0:64], func=AF.Copy,
                                                                                                    scale=rr[:, i0, 4 + 2 * sh + e_:5 + 2 * sh + e_]),
                                     reads=[("ps", b0 + sh), rk], writes=[("obuf", lt)])

        if stage < 3:
            return
        with scope(P) as es2:
            wo = es2.enter_context(nc.sbuf_tensor("wo", [128, 8, D], BF16))
            kwo = load_w_bf16(C, wo, T["w_out"], 8, "wo", nsplit=2)
            oT = es2.enter_context(nc.sbuf_tensor("oT", [128, 2, 8, 128], BF16))
            xr = es2.enter_context(nc.sbuf_tensor("xres", [128, 2, D], F32))
            hh = es2.enter_context(nc.sbuf_tensor("hh", [128, 2, D], F32))
            for lt in range(NT):
                sl = lt % 2
                P.dma("sp", xr[:, sl, :], T["xf" if full else "xo"][lt * 128:(lt + 1) * 128, :], writes=[("xres", sl)])
                transpose_to(C, obuf[:, lt, :], [("obuf", lt)], oT[:, sl, :, :], [("oT", sl)], bank=6 + sl, evac="act")
                for hf in range(2):
                    pk = ("ps", 2 * sl + hf)
                    pv = psf(C, 2 * sl + hf)
                    for k in range(8):
                        P.op("pe", lambda e, k=k, hf=hf, pv=pv: e.matmul(pv, lhsT=oT[:, sl, k, :], rhs=wo[:, k, hf * 512:(hf + 1) * 512],
                                                                         start=(k == 0), stop=(k == 7)),
                             reads=[("oT", sl), kwo[k // 4]], writes=[pk])
                emit_norm_residual(C, C.ps[sl][:, :], [("ps", 2 * sl), ("ps", 2 * sl + 1)], g[:, 1, :], ("ga", 1),
                                   xr[:, sl, :], [("xres", sl)], hh[:, sl, :], [("hh", sl)])
                P.dma("sp", H[lt * 128:(lt + 1) * 128, :], hh[:, sl, :], reads=[("hh", sl)], writes=[("H", lt)])


def build_phase_A(upto=99, stage=9):
    nc = bass.Bass("TRN2", target_bir_lowering=False)
    T = {}

    def din(name, shape, dt=F32):
        T[name] = nc.dram_tensor(name, shape, dt, kind="ExternalInput").ap()

    din("xf", [S, D]); din("xo", [NOWN, D]); din("pT", [256, NOWN])
    din("w_in", [D, 3080]); din("w_out", [D, D]); din("w_up", [D, DFF]); din("w_dn", [DFF, D])
    din("w_pp", [256, D]); din("w_pg", [D, D]); din("vec", [8, D]); din("lamv", [1, 256]); din("bf", [8, 1])
    din("ident", [128, 128]); din("masks", [4, 128, 256]); din("alibi", [128, 128])
    H = nc.dram_tensor("H", [NOWN, D], F32, kind="ExternalOutput").ap()
    H1 = nc.dram_tensor("H1", [NOWN, D], F32, kind="ExternalOutput").ap()
    HNT = nc.dram_tensor("HNT", [D, NOWN], BF16, kind="ExternalOutput").ap()
    with contextlib.ExitStack() as es:
        P = Prog(nc, es)
        C = Ctx()
        setup_common(nc, P, es, C, T["ident"])
        setup_eps(C, es)
        emit_attention(C, T, H, stage)
        if upto >= 2:
            emit_mlp(C, H, T["w_up"], T["w_dn"], T["vec"], 2, 3, "0")
        if upto >= 3:
            emit_ple(C, H, T["pT"], T["w_pp"], T["w_pg"], T["vec"], 4, H1, "0", hn_row=5, HNT=HNT)
        P.finish()
    return nc


def emit_rec(C, T, G, ncb=5):
    nc, P = C.nc, C.P
    TH = 2048
    with scope(P) as es:
        hnT = es.enter_context(nc.sbuf_tensor("r_hnT", [128, 8, S], BF16))
        wg = es.enter_context(nc.sbuf_tensor("r_wg", [128, 8, ncb * 128], BF16))
        wxr = es.enter_context(nc.sbuf_tensor("r_wxr", [128, 8, ncb * 128], BF16))
        wxs = es.enter_context(nc.sbuf_tensor("r_wxs", [128, ncb, 128], BF16))
        was = es.enter_context(nc.sbuf_tensor("r_was", [128, ncb, 128], BF16))
        rv = es.enter_context(nc.sbuf_tensor("r_rv", [128, ncb, 8], F32))
        sc = es.enter_context(nc.sbuf_tensor("r_sc", [128, ncb], F32))
        hlast = es.enter_context(nc.sbuf_tensor("r_hlast", [128, 1], F32))
        ybf = es.enter_context(nc.sbuf_tensor("r_y", [128, TH], BF16))
        xr = es.enter_context(nc.sbuf_tensor("r_xr", [128, TH + 3], F32))
        xc = es.enter_context(nc.sbuf_tensor("r_xc", [128, TH], F32))
        xcb = es.enter_context(nc.sbuf_tensor("r_xcb", [128, TH], BF16))
        gx = es.enter_context(nc.sbuf_tensor("r_gx", [128, TH], F32))
        ga = es.enter_context(nc.sbuf_tensor("r_ga", [128, TH], F32))
        tt = es.enter_context(nc.sbuf_tensor("r_tt", [128, TH], F32))
        hs = es.enter_context(nc.sbuf_tensor("r_hs", [128, TH], F32))
        gout = es.enter_context(nc.sbuf_tensor("r_gout", [128, TH], BF16))
        hv = T["hnT"].rearrange("(k f) t -> f k t", f=128)
        for k in range(8):
            P.dma("sp", hnT[:, k, :], hv[:, k, :], writes=[("r_hnT", k)])
        hk = [("r_hnT", k) for k in range(8)]
        kg = load_w_bf16(C, wg, T["w_g"], 8, "r_wg", nsplit=2)
        kx = load_w_bf16(C, wxr, T["w_x"], 8, "r_wxr", nsplit=2)
        P.dma("pool", wxs[:], T["wx"].rearrange("n i j -> i n j"), writes=["r_wxs"])
        P.dma("pool", was[:], T["wa"].rearrange("n i j -> i n j"), writes=["r_was"])
        P.dma("sp", rv[:], T["rvec"], writes=["r_rv"])
        P.op("act", lambda e: e.activation(out=sc[:, :], in_=rv[:, :, 7], func=AF.Exp, scale=-1.0), reads=["r_rv"], writes=["r_sc"])
        P.op("act", lambda e: e.activation(out=sc[:, :], in_=sc[:, :], func=AF.Ln, bias=1.0), reads=["r_sc"], writes=["r_sc"])
        P.op("dve", lambda e: e.tensor_scalar(out=sc[:, :], in0=sc[:, :], scalar1=-8.0, scalar2=None, op0=ALU.mult), reads=["r_sc"], writes=["r_sc"])
        for cb in range(ncb):
            for th in range(2):
                if th == 0:
                    P.op("dve", lambda e: e.memset(xr[:, 0:3], 0.0), writes=["r_xr"])
                else:
                    P.op("dve", lambda e: e.tensor_copy(out=xr[:, 0:3], in_=xr[:, TH:TH + 3]), reads=["r_xr"], writes=["r_xr"])
                for n in range(4):
                    N = 4 * th + n
                    bg = n % 2
                    pv = psf(C, bg)
                    for k in range(8):
                        P.op("pe", lambda e: e.matmul(pv, lhsT=wg[:, k, cb * 128:(cb + 1) * 128], rhs=hnT[:, k, N * 512:(N + 1) * 512],
                                                      start=(k == 0), stop=(k == 7)),
                             reads=[kg[k // 4], ("r_hnT", k)], writes=[("ps", bg)])
                    P.op("act", lambda e: e.activation(out=ybf[:, n * 512:(n + 1) * 512], in_=pv, func=AF.Gelu_apprx_tanh),
                         reads=[("ps", bg)], writes=[("r_y", n)])
                    bx_ = 2 + n % 2
                    pv2 = psf(C, bx_)
                    for k in range(8):
                        P.op("pe", lambda e: e.matmul(pv2, lhsT=wxr[:, k, cb * 128:(cb + 1) * 128], rhs=hnT[:, k, N * 512:(N + 1) * 512],
                                                      start=(k == 0), stop=(k == 7)),
                             reads=[kx[k // 4], ("r_hnT", k)], writes=[("ps", bx_)])
                    P.op("dve", lambda e: e.tensor_copy(out=xr[:, 3 + n * 512:3 + (n + 1) * 512], in_=pv2),
                         reads=[("ps", bx_)], writes=["r_xr"])
                P.op("act", lambda e: e.activation(out=xc[:, :], in_=xr[:, 3:3 + TH], func=AF.Identity, scale=rv[:, cb, 3:4], bias=rv[:, cb, 4:5]),
                     reads=["r_xr", "r_rv"], writes=["r_xc"])
                for w in range(3):
                    P.op("dve", lambda e: e.scalar_tensor_tensor(out=xc[:, :], in0=xr[:, w:w + TH], scalar=rv[:, cb, w:w + 1], in1=xc[:, :],
                                                                 op0=ALU.mult, op1=ALU.add),
                         reads=["r_xr", "r_rv", "r_xc"], writes=["r_xc"])
                P.op("pool", lambda e: e.tensor_copy(out=xcb[:, :], in_=xc[:, :]), reads=["r_xc"], writes=["r_xcb"])
                for n in range(4):
                    b1 = 4 + n % 2
                    pv = psf(C, b1)
                    P.op("pe", lambda e: e.matmul(pv, lhsT=wxs[:, cb, :], rhs=xcb[:, n * 512:(n + 1) * 512], start=True, stop=True),
                         reads=["r_wxs", "r_xcb"], writes=[("ps", b1)])
                    P.op("act", lambda e: e.activation(out=gx[:, n * 512:(n + 1) * 512], in_=pv, func=AF.Sigmoid, bias=rv[:, cb, 5:6]),
                         reads=[("ps", b1), "r_rv"], writes=["r_gx"])
                    b2 = 6 + n % 2
                    pv2 = psf(C, b2)
                    P.op("pe", lambda e: e.matmul(pv2, lhsT=was[:, cb, :], rhs=xcb[:, n * 512:(n + 1) * 512], start=True, stop=True),
                         reads=["r_was", "r_xcb"], writes=[("ps", b2)])
                    P.op("act", lambda e: e.activation(out=ga[:, n * 512:(n + 1) * 512], in_=pv2, func=AF.Sigmoid, bias=rv[:, cb, 6:7]),
                         reads=[("ps", b2), "r_rv"], writes=["r_ga"])
                P.op("act", lambda e: e.activation(out=ga[:, :], in_=ga[:, :], func=AF.Exp, scale=sc[:, cb:cb + 1]), reads=["r_ga", "r_sc"], writes=["r_ga"])
                P.op("dve", lambda e: e.tensor_tensor(out=tt[:, :], in0=ga[:, :], in1=ga[:, :], op=ALU.mult), reads=["r_ga"], writes=["r_tt"])
                P.op("act", lambda e: e.activation(out=tt[:, :], in_=tt[:, :], func=AF.Sqrt, scale=-1.0, bias=1.0), reads=["r_tt"], writes=["r_tt"])
                P.op("pool", lambda e: e.tensor_tensor(out=gx[:, :], in0=gx[:, :], in1=xc[:, :], op=ALU.mult), reads=["r_gx", "r_xc"], writes=["r_gx"])
                P.op("dve", lambda e: e.tensor_tensor(out=tt[:, :], in0=tt[:, :], in1=gx[:, :], op=ALU.mult), reads=["r_tt", "r_gx"], writes=["r_tt"])
                if th == 0:
                    P.op("dve", lambda e: e.tensor_copy(out=tt[:, 0:1], in_=gx[:, 0:1]), reads=["r_gx", "r_tt"], writes=["r_tt"])
                init = 0.0 if th == 0 else hlast[:, 0:1]
                P.op("dve", lambda e: e.tensor_tensor_scan(out=hs[:, :], data0=ga[:, :], data1=tt[:, :], initial=init, op0=ALU.mult, op1=ALU.add),
                     reads=["r_ga", "r_tt", "r_hlast"], writes=["r_hs"])
                P.op("dve", lambda e: e.tensor_copy(out=hlast[:, 0:1], in_=hs[:, TH - 1:TH]), reads=["r_hs"], writes=["r_hlast"])
                P.op("pool", lambda e: e.tensor_tensor(out=gout[:, :], in0=hs[:, :], in1=ybf[:, :], op=ALU.mult),
                     reads=["r_hs"] + [("r_y", n) for n in range(4)], writes=["r_gout"])
                P.dma("sp", G[cb * 128:(cb + 1) * 128, th * TH:(th + 1) * TH], gout[:, :], reads=["r_gout"], writes=[("G", cb, th)])


def build_phase_B():
    nc = bass.Bass("TRN2", target_bir_lowering=False)
    T = {}

    def din(name, shape, dt=F32):
        T[name] = nc.dram_tensor(name, shape, dt, kind="ExternalInput").ap()

    din("hnT", [D, S], BF16); din("w_g", [D, 640]); din("w_x", [D, 640]); din("wx", [5, 128, 128]); din("wa", [5, 128, 128])
    din("rvec", [128, 5, 8]); din("ident", [128, 128])
    G = nc.dram_tensor("G", [640, S], BF16, kind="ExternalOutput").ap()
    with contextlib.ExitStack() as es:
        P = Prog(nc, es)
        C = Ctx()
        setup_common(nc, P, es, C, T["ident"])
        setup_eps(C, es)
        emit_rec(C, T, G)
        P.finish()
    return nc


def emit_recout(C, T, H, row=0, blend=None):
    nc, P = C.nc, C.P
    with scope(P) as es:
        g = load_gains(C, es, T["vec"], [row], "gro")
        gT = es.enter_context(nc.sbuf_tensor("gTs", [128, 10, NOWN], BF16))
        wro = es.enter_context(nc.sbuf_tensor("wro", [128, 10, D], BF16))
        if blend is None:
            gv = T["gT"].rearrange("(c p) t -> p c t", p=128)
            for c in range(10):
                P.dma("sp", gT[:, c, :], gv[:, c, :], writes=[("gTs", c)])
        else:
            Gd, H1d, sel_d = blend
            sel = es.enter_context(nc.sbuf_tensor("sel_sb", [128, 2], F32))
            P.dma("sp", sel[:], sel_d, writes=["sel"])
            gT2 = es.enter_context(nc.sbuf_tensor("gTs2", [128, 2, NOWN], BF16))
            hres2 = es.enter_context(nc.sbuf_tensor("hres2", [128, 2, D], F32))
            gv = Gd.rearrange("(c p) t -> p c t", p=128)
            for c in range(10):
                s2 = c % 2
                P.dma("sp", gT[:, c, :], gv[:, c, 0:NOWN], writes=[("gTs", c)])
                P.dma("sp", gT2[:, s2, :], gv[:, c, NOWN:2 * NOWN], writes=[("gTs2", s2)])
                P.op("act", lambda e: e.activation(out=gT[:, c, :], in_=gT[:, c, :], func=AF.Copy, scale=sel[:, 0:1]),
                     reads=[("gTs", c), "sel"], writes=[("gTs", c)])
                P.op("dve", lambda e: e.scalar_tensor_tensor(out=gT[:, c, :], in0=gT2[:, s2, :], scalar=sel[:, 1:2], in1=gT[:, c, :],
                                                             op0=ALU.mult, op1=ALU.add),
                     reads=[("gTs", c), ("gTs2", s2), "sel"], writes=[("gTs", c)])
        kw = load_w_bf16(C, wro, T["w_ro"], 10, "wro", nsplit=2)
        hres = es.enter_context(nc.sbuf_tensor("hres", [128, 2, D], F32))
        hh = es.enter_context(nc.sbuf_tensor("hh1", [128, 2, D], F32))
        for lt in range(16):
            sl = lt % 2
            if blend is None:
                P.dma("sp", hres[:, sl, :], T["h1"][lt * 128:(lt + 1) * 128, :], writes=[("hres", sl)])
            else:
                P.dma("sp", hres[:, sl, :], H1d[lt * 128:(lt + 1) * 128, :], writes=[("hres", sl)])
                P.dma("sp", hres2[:, sl, :], H1d[NOWN + lt * 128:NOWN + (lt + 1) * 128, :], writes=[("hres2", sl)])
                P.op("act", lambda e: e.activation(out=hres[:, sl, :], in_=hres[:, sl, :], func=AF.Copy, scale=sel[:, 0:1]),
                     reads=[("hres", sl), "sel"], writes=[("hres", sl)])
                P.op("dve", lambda e: e.scalar_tensor_tensor(out=hres[:, sl, :], in0=hres2[:, sl, :], scalar=sel[:, 1:2], in1=hres[:, sl, :],
                                                             op0=ALU.mult, op1=ALU.add),
                     reads=[("hres", sl), ("hres2", sl), "sel"], writes=[("hres", sl)])
            for hf in range(2):
                pk = ("ps", 2 * sl + hf)
                pv = psf(C, 2 * sl + hf)
                for c in range(10):
                    P.op("pe", lambda e: e.matmul(pv, lhsT=gT[:, c, lt * 128:(lt + 1) * 128], rhs=wro[:, c, hf * 512:(hf + 1) * 512],
                                                  start=(c == 0), stop=(c == 9)),
                         reads=[("gTs", c), kw[c // 5]], writes=[pk])
            emit_norm_residual(C, C.ps[sl][:, :], [("ps", 2 * sl), ("ps", 2 * sl + 1)], g[:, 0, :], ("gro", 0),
                               hres[:, sl, :], [("hres", sl)], hh[:, sl, :], [("hh1", sl)])
            P.dma("sp", H[lt * 128:(lt + 1) * 128, :], hh[:, sl, :], reads=[("hh1", sl)], writes=[("H", lt)])


def build_phase_C():
    nc = bass.Bass("TRN2", target_bir_lowering=False)
    T = {}

    def din(name, shape, dt=F32):
        T[name] = nc.dram_tensor(name, shape, dt, kind="ExternalInput").ap()

    din("gT", [RW, NOWN], BF16); din("h1", [NOWN, D]); din("pT", [256, NOWN]); din("w_ro", [RW, D])
    din("w_up", [D, DFF]); din("w_dn", [DFF, D]); din("w_pp", [256, D]); din("w_pg", [D, D]); din("vec", [8, D]); din("ident", [128, 128])
    H = nc.dram_tensor("H", [NOWN, D], F32, kind="ExternalOutput").ap()
    OUT = nc.dram_tensor("OUT", [NOWN, D], F32, kind="ExternalOutput").ap()
    with contextlib.ExitStack() as es:
        P = Prog(nc, es)
        C = Ctx()
        setup_common(nc, P, es, C, T["ident"])
        setup_eps(C, es)
        emit_recout(C, T, H)
        emit_mlp(C, H, T["w_up"], T["w_dn"], T["vec"], 1, 2, "1")
        emit_ple(C, H, T["pT"], T["w_pp"], T["w_pg"], T["vec"], 3, OUT, "1")
        P.finish()
    return nc


def build_fused():
    nc = bass.Bass("TRN2", target_bir_lowering=False)
    T = {}

    def din(name, shape, dt=F32):
        T[name] = nc.dram_tensor(name, shape, dt, kind="ExternalInput").ap()

    din("xf", [S, D]); din("pT", [256, S]); din("pT1", [256, NOWN])
    din("w_in", [D, 3080]); din("w_out", [D, D]); din("w_up", [D, DFF]); din("w_dn", [DFF, D])
    din("w_pp", [256, D]); din("w_pg", [D, D]); din("vec", [16, D]); din("lamv", [1, 256]); din("bf", [8, 1])
    din("ident", [128, 128]); din("masks", [2, 128, 256]); din("alibi", [128, 128]); din("sel", [128, 2])
    din("w_g", [D, RW]); din("w_x", [D, RW]); din("wx", [10, 128, 128]); din("wa", [10, 128, 128]); din("rvec", [128, 10, 8])
    din("w_ro", [RW, D]); din("w_up1", [D, DFF]); din("w_dn1", [DFF, D]); din("w_pp1", [256, D]); din("w_pg1", [D, D])
    HA = nc.dram_tensor("HA", [S, D], F32, kind="Internal").ap()
    H1 = nc.dram_tensor("H1", [S, D], F32, kind="Internal").ap()
    HNT = nc.dram_tensor("HNT", [D, S], BF16, kind="Internal").ap()
    G = nc.dram_tensor("G", [RW, S], BF16, kind="Internal").ap()
    HC = nc.dram_tensor("HC", [NOWN, D], F32, kind="Internal").ap()
    OUT = nc.dram_tensor("OUT", [NOWN, D], F32, kind="ExternalOutput").ap()
    with contextlib.ExitStack() as es:
        P = Prog(nc, es)
        C = Ctx()
        setup_common(nc, P, es, C, T["ident"])
        setup_eps(C, es)
        emit_attention(C, T, HA, full=True)
        emit_mlp(C, HA, T["w_up"], T["w_dn"], T["vec"], 2, 3, "0", ntok=S)
        emit_ple(C, HA, T["pT"], T["w_pp"], T["w_pg"], T["vec"], 4, H1, "0", hn_row=5, HNT=HNT, ntiles=32)
        T1 = {"hnT": HNT, "w_g": T["w_g"], "w_x": T["w_x"], "wx": T["wx"], "wa": T["wa"], "rvec": T["rvec"]}
        emit_rec(C, T1, G, ncb=10)
        T2 = {"vec": T["vec"], "w_ro": T["w_ro"]}
        emit_recout(C, T2, HC, row=8, blend=(G, H1, T["sel"]))
        emit_mlp(C, HC, T["w_up1"], T["w_dn1"], T["vec"], 9, 10, "1", ntok=NOWN)
        emit_ple(C, HC, T["pT1"], T["w_pp1"], T["w_pg1"], T["vec"], 11, OUT, "1", ntiles=16)
        P.finish()
    return nc


def own_tiles(r):
    return [4 * p + 2 * r + e for p in range(8) for e in range(2)]


def own_index(r):
    return np.concatenate([np.arange(g * 128, (g + 1) * 128) for g in own_tiles(r)])


def role_consts(r):
    ident = np.eye(128, dtype=np.float32)
    masks = np.zeros((4, 128, 256), np.float32)
    jj = np.arange(128)[:, None]
    ii = np.arange(128)[None, :]
    tri = (jj <= ii).astype(np.float32)
    for m in range(4):
        for e in range(2):
            qt = 2 * r + e
            if m < qt:
                masks[m, :, e * 128:(e + 1) * 128] = 1.0
            elif m == qt:
                masks[m, :, e * 128:(e + 1) * 128] = tri
    alibi = np.zeros((128, 4, 32), np.float32)
    for h in range(4):
        for idx in range(32):
            d = (idx - 28 - 2 * r - 2) * 128 + np.arange(128)
            alibi[:, h, idx] = np.minimum(SLOPES[h] * d, 0.0)
    return ident, masks, alibi.reshape(128, 128)


def f32c(a):
    return np.ascontiguousarray(a, dtype=np.float32)


def prep_A(inp, c):
    b, r = c // 2, c % 2
    oi = own_index(r)
    ident, masks, alibi = role_consts(r)
    vec = np.zeros((8, D), np.float32)
    vec[0] = inp["ln_mix_pre"][0]
    vec[1] = inp["ln_mix_post"][0]
    vec[2] = inp["ln_mlp_pre"][0]
    vec[3] = inp["ln_mlp_post"][0]
    vec[4] = inp["ple_norm"][0]
    vec[5] = inp["ln_mix_pre"][1]
    vec[6, :128] = inp["diff_subln"][0]
    lamv = np.concatenate([inp["diff_lambda_q1"][0], inp["diff_lambda_k1"][0],
                           inp["diff_lambda_q2"][0], inp["diff_lambda_k2"][0]])[None, :]
    return {
        "xf": f32c(inp["x"][b]), "xo": f32c(inp["x"][b][oi]), "pT": f32c(inp["p"][0, b][oi].T),
        "w_in": f32c(inp["attn_w_in"][0]), "w_out": f32c(inp["attn_w_out"][0]),
        "w_up": f32c(inp["mlp_w_up"][0]), "w_dn": f32c(inp["mlp_w_down"][0]),
        "w_pp": f32c(inp["ple_w_proj"][0]), "w_pg": f32c(inp["ple_w_gate"][0]),
        "vec": vec, "lamv": f32c(lamv), "bf": f32c(inp["attn_b_forget"][0][:, None]),
        "ident": ident, "masks": masks, "alibi": alibi,
    }


def prep_B(inp, c, hnT_full):
    r = c % 2
    cols_g = np.arange(5 * r * 128, (5 * r + 5) * 128)
    cols_x = RW + cols_g
    w_in = inp["rec_w_in"][0]
    rvec = np.zeros((128, 5, 8), np.float32)
    for j in range(5):
        ch = np.arange((5 * r + j) * 128, (5 * r + j + 1) * 128)
        rvec[:, j, 0:4] = inp["rec_conv_w"][0][:, ch].T
        rvec[:, j, 4] = inp["rec_conv_b"][0][ch]
        rvec[:, j, 5] = inp["rec_bx"][0][ch]
        rvec[:, j, 6] = inp["rec_ba"][0][ch]
        rvec[:, j, 7] = inp["rec_a_param"][0][ch]
    return {
        "hnT": hnT_full, "w_g": f32c(w_in[:, cols_g]), "w_x": f32c(w_in[:, cols_x]),
        "wx": f32c(inp["rec_wx"][0][5 * r:5 * r + 5]), "wa": f32c(inp["rec_wa"][0][5 * r:5 * r + 5]),
        "rvec": rvec, "ident": np.eye(128, dtype=np.float32),
    }


def prep_C(inp, c, gT_own, h1_own):
    b, r = c // 2, c % 2
    oi = own_index(r)
    vec = np.zeros((8, D), np.float32)
    vec[0] = inp["ln_mix_post"][1]
    vec[1] = inp["ln_mlp_pre"][1]
    vec[2] = inp["ln_mlp_post"][1]
    vec[3] = inp["ple_norm"][1]
    return {
        "gT": gT_own, "h1": h1_own, "pT": f32c(inp["p"][1, b][oi].T), "w_ro": f32c(inp["rec_w_out"][0]),
        "w_up": f32c(inp["mlp_w_up"][1]), "w_dn": f32c(inp["mlp_w_down"][1]),
        "w_pp": f32c(inp["ple_w_proj"][1]), "w_pg": f32c(inp["ple_w_gate"][1]),
        "vec": vec, "ident": np.eye(128, dtype=np.float32),
    }


def full_consts():
    ident = np.eye(128, dtype=np.float32)
    jj = np.arange(128)[:, None]
    ii = np.arange(128)[None, :]
    tri = (jj <= ii).astype(np.float32)
    masks = np.zeros((2, 128, 256), np.float32)
    masks[0, :, 0:128] = tri
    masks[0, :, 128:256] = 1.0
    masks[1, :, 128:256] = tri
    alibi = np.zeros((128, 4, 32), np.float32)
    for h in range(4):
        for idx in range(32):
            d = (idx - 30 - 2) * 128 + np.arange(128)
            alibi[:, h, idx] = np.minimum(SLOPES[h] * d, 0.0)
    return ident, masks, alibi.reshape(128, 128)


def prep_fused(inp, c):
    b, r = c // 2, c % 2
    ident, masks, alibi = full_consts()
    vec = np.zeros((16, D), np.float32)
    vec[0] = inp["ln_mix_pre"][0]
    vec[1] = inp["ln_mix_post"][0]
    vec[2] = inp["ln_mlp_pre"][0]
    vec[3] = inp["ln_mlp_post"][0]
    vec[4] = inp["ple_norm"][0]
    vec[5] = inp["ln_mix_pre"][1]
    vec[6, :128] = inp["diff_subln"][0]
    vec[8] = inp["ln_mix_post"][1]
    vec[9] = inp["ln_mlp_pre"][1]
    vec[10] = inp["ln_mlp_post"][1]
    vec[11] = inp["ple_norm"][1]
    lamv = np.concatenate([inp["diff_lambda_q1"][0], inp["diff_lambda_k1"][0],
                           inp["diff_lambda_q2"][0], inp["diff_lambda_k2"][0]])[None, :]
    rvec = np.zeros((128, 10, 8), np.float32)
    for j in range(10):
        ch = np.arange(j * 128, (j + 1) * 128)
        rvec[:, j, 0:4] = inp["rec_conv_w"][0][:, ch].T
        rvec[:, j, 4] = inp["rec_conv_b"][0][ch]
        rvec[:, j, 5] = inp["rec_bx"][0][ch]
        rvec[:, j, 6] = inp["rec_ba"][0][ch]
        rvec[:, j, 7] = inp["rec_a_param"][0][ch]
    sel = np.zeros((128, 2), np.float32)
    sel[:, r] = 1.0
    w_in1 = inp["rec_w_in"][0]
    return {
        "xf": f32c(inp["x"][b]), "pT": f32c(inp["p"][0, b].T), "pT1": f32c(inp["p"][1, b][r * NOWN:(r + 1) * NOWN].T),
        "w_in": f32c(inp["attn_w_in"][0]), "w_out": f32c(inp["attn_w_out"][0]),
        "w_up": f32c(inp["mlp_w_up"][0]), "w_dn": f32c(inp["mlp_w_down"][0]),
        "w_pp": f32c(inp["ple_w_proj"][0]), "w_pg": f32c(inp["ple_w_gate"][0]),
        "vec": vec, "lamv": f32c(lamv), "bf": f32c(inp["attn_b_forget"][0][:, None]),
        "ident": ident, "masks": masks, "alibi": alibi, "sel": sel,
        "w_g": f32c(w_in1[:, :RW]), "w_x": f32c(w_in1[:, RW:]), "wx": f32c(inp["rec_wx"][0]), "wa": f32c(inp["rec_wa"][0]),
        "rvec": rvec, "w_ro": f32c(inp["rec_w_out"][0]),
        "w_up1": f32c(inp["mlp_w_up"][1]), "w_dn1": f32c(inp["mlp_w_down"][1]),
        "w_pp1": f32c(inp["ple_w_proj"][1]), "w_pg1": f32c(inp["ple_w_gate"][1]),
    }


_NC_CACHE = {}


def _get(name, fn):
    if name not in _NC_CACHE:
        _NC_CACHE[name] = fn()
    return _NC_CACHE[name]


def kernel_unfused(**inputs):
    inp = {k: np.asarray(v) for k, v in inputs.items()}
    cores = list(range(NCORES))
    ncA = _get("A", build_phase_A)
    resA = run_bass_kernel_spmd(ncA, [prep_A(inp, c) for c in cores], core_ids=cores).results
    hnT_full = []
    for b in range(4):
        full = np.zeros((D, S), dtype=resA[0]["HNT"].dtype)
        for r in range(2):
            full[:, own_index(r)] = resA[2 * b + r]["HNT"]
        hnT_full.append(full)
    ncB = _get("B", build_phase_B)
    resB = run_bass_kernel_spmd(ncB, [prep_B(inp, c, hnT_full[c // 2]) for c in cores], core_ids=cores).results
    ncC = _get("C", build_phase_C)
    mapsC = []
    for c in cores:
        b, r = c // 2, c % 2
        oi = own_index(r)
        gfull = np.concatenate([resB[2 * b]["G"], resB[2 * b + 1]["G"]], axis=0)
        mapsC.append(prep_C(inp, c, np.ascontiguousarray(gfull[:, oi]), resA[c]["H1"]))
    resC = run_bass_kernel_spmd(ncC, mapsC, core_ids=cores).results
    out = np.zeros((4, S, D), np.float32)
    for c in cores:
        b, r = c // 2, c % 2
        out[b, own_index(r)] = resC[c]["OUT"]
    return out


def kernel(**inputs):
    inp = {k: np.asarray(v) for k, v in inputs.items()}
    cores = list(range(NCORES))
    nc = _get("F", build_fused)
    res = run_bass_kernel_spmd(nc, [prep_fused(inp, c) for c in cores], core_ids=cores).results
    out = np.zeros((4, S, D), np.float32)
    for c in cores:
        b, r = c // 2, c % 2
        out[b, r * NOWN:(r + 1) * NOWN] = res[c]["OUT"]
    return out
```

```python
import contextlib
import numpy as np
import concourse.bass as bass
import concourse.mybir as mybir
from concourse.bass_utils import run_bass_kernel_spmd

F32 = mybir.dt.float32
BF16 = mybir.dt.bfloat16
AF = mybir.ActivationFunctionType
ALU = mybir.AluOpType
AX = mybir.AxisListType

NCORES = 8
D = 1024
S = 4096
NOWN = 2048
DFF = 4096
RW = 1280
LAM_INIT0 = 0.8 - 0.6 * 1.0
SLOPES = [2.0 ** (-8.0 * (h + 1) / 4) for h in range(4)]


class Prog:
    K_DMA = 8

    def __init__(self, nc, es):
        self.nc = nc
        self.engs = {"pe": nc.tensor, "act": nc.scalar, "dve": nc.vector,
                     "pool": nc.gpsimd, "sp": nc.sync}
        self.semobj = {}
        for k in self.engs:
            self.semobj[k] = es.enter_context(nc.semaphore("sem_" + k))
        self.cnt = {k: 0 for k in self.engs}
        self.seen = {k: {} for k in self.engs}
        self.lastw = {}
        self.readers = {}
        self.dcnt = {}
        for q in ("sp", "act", "pool"):
            self.dcnt[q] = 0
            for i in range(self.K_DMA):
                self.semobj[(q, i)] = es.enter_context(nc.semaphore("d_%s_%d" % (q, i)))

    def _wait(self, eng, ev):
        sk, val = ev
        if self.seen[eng].get(sk, 0) >= val:
            return
        self.engs[eng].wait_ge(self.semobj[sk], val)
        self.seen[eng][sk] = val

    def _deps(self, eng, reads, writes):
        deps = {}
        for k in reads:
            w = self.lastw.get(k)
            if w is not None:
                deps[w[0]] = max(deps.get(w[0], 0), w[1])
        for k in writes:
            w = self.lastw.get(k)
            if w is not None:
                deps[w[0]] = max(deps.get(w[0], 0), w[1])
            for sk, v in self.readers.get(k, {}).items():
                deps[sk] = max(deps.get(sk, 0), v)
        for sk, v in deps.items():
            if eng == "pe" and sk == "pe":
                continue
            self._wait(eng, (sk, v))

    def _record(self, ev, reads, writes):
        for k in writes:
            self.lastw[k] = ev
            self.readers[k] = {}
        for k in reads:
            if k in writes:
                continue
            d = self.readers.setdefault(k, {})
            d[ev[0]] = max(d.get(ev[0], 0), ev[1])

    def op(self, eng, fn, reads=(), writes=()):
        self._deps(eng, reads, writes)
        inst = fn(self.engs[eng])
        self.cnt[eng] += 1
        inst.then_inc(self.semobj[eng], 1)
        self._record((eng, self.cnt[eng]), reads, writes)

    def dma(self, q, out, in_, reads=(), writes=()):
        self._deps(q, reads, writes)
        i = self.dcnt[q]
        s = i % self.K_DMA
        rnd = i // self.K_DMA
        if rnd > 0:
            self._wait(q, ((q, s), 16 * rnd))
        inst = self.engs[q].dma_start(out=out, in_=in_)
        inst.then_inc(self.semobj[(q, s)], 16)
        self.dcnt[q] = i + 1
        self._record(((q, s), 16 * (rnd + 1)), reads, writes)

    def barrier(self):
        evs = []
        for q in ("sp", "act", "pool"):
            n = self.dcnt[q]
            for s in range(self.K_DMA):
                m = (n - s + self.K_DMA - 1) // self.K_DMA if n > s else 0
                if m > 0:
                    evs.append(((q, s), 16 * m))
        for k in ("pe", "act", "dve", "pool", "sp"):
            if self.cnt[k] > 0:
                evs.append((k, self.cnt[k]))
        for eng in ("pe", "act", "dve", "pool", "sp"):
            for ev in evs:
                if ev[0] == eng:
                    continue
                self._wait(eng, ev)

    def barrier_keys(self, eng, keys):
        self._deps(eng, [], list(keys))

    def finish(self):
        for q in ("sp", "act", "pool"):
            n = self.dcnt[q]
            for s in range(self.K_DMA):
                m = (n - s + self.K_DMA - 1) // self.K_DMA if n > s else 0
                if m > 0:
                    self._wait("sp", ((q, s), 16 * m))
        for k in ("pe", "act", "dve", "pool"):
            if self.cnt[k] > 0:
                self._wait("sp", (k, self.cnt[k]))


class Ctx:
    pass


@contextlib.contextmanager
def scope(P):
    with contextlib.ExitStack() as es:
        yield es
        P.barrier()


def setup_common(nc, P, es, C, ident_d):
    C.nc = nc
    C.P = P
    C.identf = es.enter_context(nc.sbuf_tensor("identf", [128, 128], F32))
    C.identb = es.enter_context(nc.sbuf_tensor("identb", [128, 128], BF16))
    C.onesf = es.enter_context(nc.sbuf_tensor("onesf", [128, 512], F32))
    P.dma("sp", C.identf[:], ident_d, writes=["identf"])
    P.dma("pool", C.identb[:], ident_d, writes=["identb"])
    P.op("pool", lambda e: e.memset(C.onesf[:], 1.0), writes=["onesf"])
    C.ps = [es.enter_context(nc.psum_tensor("ps%d" % i, [128, 1024], F32)) for i in range(4)]
    C.psb = [t.bitcast(BF16) for t in C.ps]
    C.junk = es.enter_context(nc.sbuf_tensor("junk", [128, 1024], BF16))
    C.junk2 = es.enter_context(nc.sbuf_tensor("junk2", [128, 128], BF16))
    C.small = es.enter_context(nc.sbuf_tensor("small", [128, 64], F32))
    C.small_i = 0


def small_slot(C, n=1):
    i = C.small_i
    if i + n > 64:
        i = 0
    C.small_i = i + n
    return i


def psf(C, b):
    return C.ps[b // 2][:, (b % 2) * 512:(b % 2) * 512 + 512]


def psbf(C, b):
    return C.psb[b // 2][:, (b % 2) * 1024:(b % 2) * 1024 + 1024]


def rstd_from(C, src, src_keys, n, eps, eng="act"):
    P = C.P
    i = small_slot(C)
    ss = C.small[:, i:i + 1]
    key = ("small", i)
    if eng == "dve":
        P.op("dve", lambda e: e.scalar_tensor_tensor(out=C.junk2[:, 0:n], in0=src, scalar=1.0, in1=src,
                                                     op0=ALU.mult, op1=ALU.mult, accum_out=ss),
             reads=list(src_keys), writes=["junk2", key])
    else:
        P.op("act", lambda e: e.activation(out=C.junk[:, 0:n], in_=src, func=AF.Square, accum_out=ss),
             reads=list(src_keys), writes=["junk", key])
    P.op("dve", lambda e: e.tensor_scalar(out=ss, in0=ss, scalar1=1.0 / n, scalar2=float(eps), op0=ALU.mult, op1=ALU.add),
         reads=[key], writes=[key])
    P.op("pool", lambda e: e.tensor_tensor(out=ss, in0=ss, in1=C.negh, op=ALU.pow), reads=[key, "epsb"], writes=[key])
    return ss, key


def transpose_to(C, src_bf, src_keys, dst, dst_keys, bank, nk=8, evac="dve"):
    P = C.P
    pk = ("ps", bank)
    pv = psbf(C, bank)
    for k in range(nk):
        P.op("pe", lambda e, k=k: e.transpose(out=pv[:, k * 128:(k + 1) * 128], in_=src_bf[:, k * 128:(k + 1) * 128],
                                              identity=C.identb[:]),
             reads=list(src_keys) + ["identb"], writes=[pk])
    srcv = pv[:, 0:nk * 128].rearrange("p (k t) -> p k t", k=nk)
    if evac == "act":
        P.op("act", lambda e: e.copy(out=dst, in_=srcv), reads=[pk], writes=list(dst_keys))
    else:
        P.op("dve", lambda e: e.tensor_copy(out=dst, in_=srcv), reads=[pk], writes=list(dst_keys))


def load_w_bf16(C, dst, dram2d, nk, key, nsplit=1):
    P = C.P
    src = dram2d.rearrange("(k p) n -> p k n", p=128)
    step = nk // nsplit
    for i in range(nsplit):
        P.dma("pool", dst[:, i * step:(i + 1) * step, :], src[:, i * step:(i + 1) * step, :], writes=[(key, i)])
    return [(key, i) for i in range(nsplit)]


def emit_norm_residual(C, m_src, m_keys, gain, gain_key, res, res_keys, out, out_keys):
    P = C.P
    rs, rk = rstd_from(C, m_src, m_keys, D, 1e-6)
    P.op("dve", lambda e: e.scalar_tensor_tensor(out=out, in0=m_src, scalar=rs, in1=gain, op0=ALU.mult, op1=ALU.mult),
         reads=list(m_keys) + [rk, gain_key], writes=list(out_keys))
    P.op("pool", lambda e: e.tensor_tensor(out=out, in0=out, in1=res, op=ALU.add),
         reads=list(out_keys) + list(res_keys), writes=list(out_keys))


def emit_norm_bf16(C, src, src_keys, gain, gain_key, out_bf, out_keys):
    P = C.P
    rs, rk = rstd_from(C, src, src_keys, D, 1e-6)
    P.op("dve", lambda e: e.scalar_tensor_tensor(out=out_bf, in0=src, scalar=rs, in1=gain, op0=ALU.mult, op1=ALU.mult),
         reads=list(src_keys) + [rk, gain_key], writes=list(out_keys))


def load_gains(C, es, vec_d, rows, name):
    nc, P = C.nc, C.P
    g = es.enter_context(nc.sbuf_tensor(name, [128, len(rows), D], F32))
    for i, r in enumerate(rows):
        P.dma("sp", g[:, i, :], vec_d[r, :].partition_broadcast(128), writes=[(name, i)])
    return g


def setup_eps(C, es):
    nc, P = C.nc, C.P
    t = es.enter_context(nc.sbuf_tensor("epsb", [128, 3], F32))
    P.op("pool", lambda e: e.memset(t[:, 2:3], -0.5), writes=["epsb"])
    C.negh = t[:, 2:3]
    P.op("pool", lambda e: e.memset(t[:, 0:1], 1e-6), writes=["epsb"])
    P.op("pool", lambda e: e.memset(t[:, 1:2], 1e-5), writes=["epsb"])
    C.epsb = {1e-6: t[:, 0:1], 1e-5: t[:, 1:2]}


def emit_mlp(C, H, w_up_d, w_dn_d, vec_d, row_pre, row_post, tag, ntok=NOWN):
    nc, P = C.nc, C.P
    MT = 256
    with scope(P) as es:
        wup = es.enter_context(nc.sbuf_tensor("wup" + tag, [128, 8, DFF], BF16))
        wdn = es.enter_context(nc.sbuf_tensor("wdn" + tag, [128, 32, D], BF16))
        kup = load_w_bf16(C, wup, w_up_d, 8, "wup" + tag, nsplit=8)
        kdn = load_w_bf16(C, wdn, w_dn_d, 32, "wdn" + tag, nsplit=8)
        g = load_gains(C, es, vec_d, [row_pre, row_post], "gm" + tag)
        ha = es.enter_context(nc.sbuf_tensor("ha" + tag, [128, 2, 2, D], F32))
        ubf = es.enter_context(nc.sbuf_tensor("ubf" + tag, [128, 2, D], BF16))
        uT = es.enter_context(nc.sbuf_tensor("uT" + tag, [128, 2, 8, MT], BF16))
        hid = es.enter_context(nc.sbuf_tensor("hid" + tag, [128, 32, MT], BF16))
        rl = es.enter_context(nc.sbuf_tensor("rl" + tag, [128, 2, MT], F32))
        hb = es.enter_context(nc.sbuf_tensor("hb" + tag, [128, 2, D], F32))
        nmt = ntok // MT

        def prologue(t):
            sl = t % 2
            for e in range(2):
                lt = 2 * t + e
                P.dma("sp", ha[:, sl, e, :], H[lt * 128:(lt + 1) * 128, :], reads=[("H", lt)], writes=[("ha", sl, e)])
                emit_norm_bf16(C, ha[:, sl, e, :], [("ha", sl, e)], g[:, 0, :], ("gm" + tag, 0), ubf[:, e, :], [("ubf", e)])
                transpose_to(C, ubf[:, e, :], [("ubf", e)], uT[:, sl, :, e * 128:(e + 1) * 128], [("uT", sl, e)],
                             bank=6 + e, evac="act")

        def up(t):
            sl = t % 2
            for c in range(32):
                bank = c % 4
                pk = ("ps", bank)
                pv = psf(C, bank)[:, 0:MT]
                for k in range(8):
                    P.op("pe", lambda e: e.matmul(pv, lhsT=wup[:, k, c * 128:(c + 1) * 128], rhs=uT[:, sl, k, :],
                                                  start=(k == 0), stop=(k == 7)),
                         reads=[kup[k], ("uT", sl, 0), ("uT", sl, 1)], writes=[pk])
                rs_ = c % 2
                P.op("act", lambda e: e.activation(out=rl[:, rs_, :], in_=pv, func=AF.Relu),
                     reads=[pk], writes=[("rl", rs_)])
                P.op("pool", lambda e: e.tensor_tensor(out=hid[:, c, :], in0=rl[:, rs_, :], in1=rl[:, rs_, :], op=ALU.mult),
                     reads=[("rl", rs_)], writes=[("hid", c)])

        def down(t):
            sl = t % 2
            for e in range(2):
                lt = 2 * t + e
                for hf in range(2):
                    pk = ("ps", 4 + hf)
                    pv = psf(C, 4 + hf)
                    for c in range(32):
                        P.op("pe", lambda e_: e_.matmul(pv, lhsT=hid[:, c, e * 128:(e + 1) * 128],
                                                        rhs=wdn[:, c, hf * 512:(hf + 1) * 512],
                                                        start=(c == 0), stop=(c == 31)),
                             reads=[("hid", c), kdn[c // 4]], writes=[pk])
                fv = C.ps[2][:, :]
                emit_norm_residual(C, fv, [("ps", 4), ("ps", 5)], g[:, 1, :], ("gm" + tag, 1),
                                   ha[:, sl, e, :], [("ha", sl, e)], hb[:, e, :], [("hb", e)])
                P.dma("sp", H[lt * 128:(lt + 1) * 128, :], hb[:, e, :], reads=[("hb", e)], writes=[("H", lt)])

        prologue(0)
        for t in range(nmt):
            up(t)
            if t + 1 < nmt:
                prologue(t + 1)
            down(t)


def emit_ple(C, H, pT_d, w_pp_d, w_pg_d, vec_d, row_ple, OUT, tag, hn_row=None, HNT=None, ntiles=16):
    nc, P = C.nc, C.P
    with scope(P) as es:
        wpg = es.enter_context(nc.sbuf_tensor("wpg" + tag, [128, 8, D], BF16))
        wpp = es.enter_context(nc.sbuf_tensor("wpp" + tag, [128, 2, D], BF16))
        kpg = load_w_bf16(C, wpg, w_pg_d, 8, "wpg" + tag, nsplit=2)
        kpp = load_w_bf16(C, wpp, w_pp_d, 2, "wpp" + tag, nsplit=1)
        rows = [row_ple] + ([hn_row] if hn_row is not None else [])
        g = load_gains(C, es, vec_d, rows, "gp" + tag)
        hb = es.enter_context(nc.sbuf_tensor("phb" + tag, [128, 2, D], F32))
        hbb = es.enter_context(nc.sbuf_tensor("phbb" + tag, [128, 2, D], BF16))
        hbT = es.enter_context(nc.sbuf_tensor("phbT" + tag, [128, 2, 8, 128], BF16))
        pT = es.enter_context(nc.sbuf_tensor("ppT" + tag, [128, 2, 2, 128], BF16))
        sg = es.enter_context(nc.sbuf_tensor("psg" + tag, [128, 2, D], F32))
        ee = es.enter_context(nc.sbuf_tensor("pee" + tag, [128, 2, D], F32))
        hnb = es.enter_context(nc.sbuf_tensor("phnb" + tag, [128, 2, D], BF16))
        hnT = es.enter_context(nc.sbuf_tensor("phnT" + tag, [128, 2, 8, 128], BF16))
        pTv = pT_d.rearrange("(k f) t -> f k t", f=128)
        def front(lt):
            sl = lt % 2
            P.dma("sp", hb[:, sl, :], H[lt * 128:(lt + 1) * 128, :], reads=[("H", lt)], writes=[("phb", sl)])
            P.dma("pool", pT[:, sl, :, :], pTv[:, :, lt * 128:(lt + 1) * 128], writes=[("ppT", sl)])
            P.op("dve", lambda e: e.tensor_copy(out=hbb[:, sl, :], in_=hb[:, sl, :]), reads=[("phb", sl)], writes=[("phbb", sl)])
            transpose_to(C, hbb[:, sl, :], [("phbb", sl)], hbT[:, sl, :, :], [("phbT", sl)], bank=6, evac="act")
            for hf in range(2):
                pk = ("ps", hf)
                pv = psf(C, hf)
                for k in range(8):
                    P.op("pe", lambda e: e.matmul(pv, lhsT=hbT[:, sl, k, :], rhs=wpg[:, k, hf * 512:(hf + 1) * 512],
                                                  start=(k == 0), stop=(k == 7)),
                         reads=[("phbT", sl), kpg[k // 4]], writes=[pk])
            for hf in range(2):
                pk = ("ps", 2 + 2 * sl + hf)
                pv = psf(C, 2 + 2 * sl + hf)
                for k in range(2):
                    P.op("pe", lambda e: e.matmul(pv, lhsT=pT[:, sl, k, :], rhs=wpp[:, k, hf * 512:(hf + 1) * 512],
                                                  start=(k == 0), stop=(k == 1)),
                         reads=[("ppT", sl), kpp[0]], writes=[pk])

        def front_b(lt):
            sl = lt % 2
            P.op("act", lambda e: e.activation(out=sg[:, sl, :], in_=C.ps[0][:, :], func=AF.Sigmoid),
                 reads=[("ps", 0), ("ps", 1)], writes=[("psg", sl)])

        def back(lt):
            sl = lt % 2
            ev = C.ps[1 + sl][:, :]
            pks = [("ps", 2 + 2 * sl), ("ps", 3 + 2 * sl)]
            rs, rk = rstd_from(C, ev, pks, D, 1e-6)
            P.op("dve", lambda e: e.scalar_tensor_tensor(out=ee[:, sl, :], in0=ev, scalar=rs, in1=g[:, 0, :], op0=ALU.mult, op1=ALU.mult),
                 reads=pks + [rk, ("gp" + tag, 0)], writes=[("pee", sl)])
            P.op("pool", lambda e: e.tensor_tensor(out=ee[:, sl, :], in0=ee[:, sl, :], in1=sg[:, sl, :], op=ALU.mult),
                 reads=[("pee", sl), ("psg", sl)], writes=[("pee", sl)])
            P.op("dve", lambda e: e.tensor_tensor(out=ee[:, sl, :], in0=ee[:, sl, :], in1=hb[:, sl, :], op=ALU.add),
                 reads=[("pee", sl), ("phb", sl)], writes=[("pee", sl)])
            P.dma("sp", OUT[lt * 128:(lt + 1) * 128, :], ee[:, sl, :], reads=[("pee", sl)], writes=[("OUT", lt)])

        def hnpart(lt):
            sl = lt % 2
            emit_norm_bf16(C, ee[:, sl, :], [("pee", sl)], g[:, 1, :], ("gp" + tag, 1), hnb[:, sl, :], [("phnb", sl)])
            transpose_to(C, hnb[:, sl, :], [("phnb", sl)], hnT[:, sl, :, :], [("phnT", sl)], bank=7, evac="dve")
            P.dma("sp", HNT.rearrange("(k f) t -> f k t", f=128)[:, :, lt * 128:(lt + 1) * 128], hnT[:, sl, :, :],
                  reads=[("phnT", sl)], writes=[("HNT", lt)])

        front(0)
        front_b(0)
        for lt in range(ntiles):
            if lt + 1 < ntiles:
                front(lt + 1)
            back(lt)
            if hn_row is not None and lt >= 1:
                hnpart(lt - 1)
            if lt + 1 < ntiles:
                front_b(lt + 1)
        if hn_row is not None:
            hnpart(ntiles - 1)


def emit_attention(C, T, H, stage=9, full=False):
    nc, P = C.nc, C.P
    with scope(P) as es:
        g = load_gains(C, es, T["vec"], [0, 1], "ga")
        hnTf = es.enter_context(nc.sbuf_tensor("hnTf", [128, 8, S], BF16))
        NT = 32 if full else 16
        NCH = NT // 2
        KPC = 2 if full else 4
        NM = 2 if full else 4
        if full:
            hnTo = hnTf
        else:
            hnTo = es.enter_context(nc.sbuf_tensor("hnTo", [128, 8, NOWN], BF16))
        obuf = es.enter_context(nc.sbuf_tensor("obuf", [128, NT, D], BF16))
        masks = es.enter_context(nc.sbuf_tensor("masks_sb", [128, NM, 256], BF16))
        alibi = es.enter_context(nc.sbuf_tensor("alibi_sb", [128, 4, 32], F32))
        fb = es.enter_context(nc.sbuf_tensor("fb", [128, 2, 2, 32], F32))
        Stm = es.enter_context(nc.sbuf_tensor("Stm", [128, 32, 8], F32))
        Xb = es.enter_context(nc.sbuf_tensor("Xb", [128, NCH * 8], F32))
        neglam = es.enter_context(nc.sbuf_tensor("neglam", [128, 4], F32))
        subg = es.enter_context(nc.sbuf_tensor("subg", [128, 128], F32))
        P.dma("pool", masks[:], T["masks"].rearrange("m p q -> p m q"), writes=["masks"])
        P.dma("sp", alibi[:], T["alibi"].rearrange("p (h r) -> p h r", h=4), writes=["alibi"])
        P.dma("sp", subg[:], T["vec"][6, 0:128].partition_broadcast(128), writes=["subg"])
        P.op("dve", lambda e: e.tensor_scalar(out=subg[:], in0=subg[:], scalar1=1.0 - LAM_INIT0, scalar2=None, op0=ALU.mult),
             reads=["subg"], writes=["subg"])

        with scope(P) as es2:
            lv = es2.enter_context(nc.sbuf_tensor("lv", [1, 256], F32))
            pr = es2.enter_context(nc.sbuf_tensor("pr", [1, 128], F32))
            dots = es2.enter_context(nc.sbuf_tensor("dots", [1, 2], F32))
            P.dma("sp", lv[:], T["lamv"], writes=["lv"])
            P.op("dve", lambda e: e.tensor_tensor(out=pr[:, 0:64], in0=lv[:, 0:64], in1=lv[:, 64:128], op=ALU.mult),
                 reads=["lv"], writes=["pr"])
            P.op("dve", lambda e: e.tensor_tensor(out=pr[:, 64:128], in0=lv[:, 128:192], in1=lv[:, 192:256], op=ALU.mult),
                 reads=["lv", "pr"], writes=["pr"])
            P.op("dve", lambda e: e.tensor_reduce(out=dots[:, 0:2], in_=pr[:, :].rearrange("p (a d) -> p a d", a=2),
                                                  axis=AX.X, op=ALU.add), reads=["pr"], writes=["dots"])
            pv = psf(C, 0)[:, 0:2]
            P.op("pe", lambda e: e.matmul(pv, lhsT=C.onesf[0:1, 0:128], rhs=dots[0:1, 0:2], start=True, stop=True),
                 reads=["onesf", "dots"], writes=[("ps", 0)])
            P.op("act", lambda e: e.activation(out=neglam[:, 0:2], in_=pv, func=AF.Exp), reads=[("ps", 0)], writes=["neglam"])
            P.op("dve", lambda e: e.tensor_tensor(out=neglam[:, 2:3], in0=neglam[:, 1:2], in1=neglam[:, 0:1], op=ALU.subtract),
                 reads=["neglam"], writes=["neglam"])
            P.op("dve", lambda e: e.tensor_scalar(out=neglam[:, 2:3], in0=neglam[:, 2:3], scalar1=-LAM_INIT0, scalar2=None, op0=ALU.add),
                 reads=["neglam"], writes=["neglam"])

        with scope(P) as es2:
            xt = es2.enter_context(nc.sbuf_tensor("xt", [128, 2, D], F32))
            xb = es2.enter_context(nc.sbuf_tensor("xb", [128, 2, D], BF16))
            for i in range(32 if full else 48):
                sl = i % 2
                if i < 32:
                    src = T["xf"][i * 128:(i + 1) * 128, :]
                    dst = hnTf[:, :, i * 128:(i + 1) * 128]
                    dk = ("hnTf", i // 4)
                else:
                    j = i - 32
                    src = T["xo"][j * 128:(j + 1) * 128, :]
                    dst = hnTo[:, :, j * 128:(j + 1) * 128]
                    dk = ("hnTo", j // 4)
                if full:
                    dk = ("hnTf", i // 4)
                P.dma("sp", xt[:, sl, :], src, writes=[("xt", sl)])
                emit_norm_bf16(C, xt[:, sl, :], [("xt", sl)], g[:, 0, :], ("ga", 0), xb[:, sl, :], [("xb", sl)])
                transpose_to(C, xb[:, sl, :], [("xb", sl)], dst, [dk], bank=6 + sl, evac=("act" if sl else "dve"))

        if stage < 1:
            return
        with scope(P) as es2:
            wfz = es2.enter_context(nc.sbuf_tensor("wfz", [128, 8, 8], BF16))
            negb = es2.enter_context(nc.sbuf_tensor("negb", [8, 1], F32))
            Lf = es2.enter_context(nc.sbuf_tensor("Lf", [8, S], F32))
            Sc = es2.enter_context(nc.sbuf_tensor("Sc", [8, S], F32))
            Dm = es2.enter_context(nc.sbuf_tensor("Dm", [8, NCH, 8], F32))
            P.dma("pool", wfz[:], T["w_in"].rearrange("(k p) n -> p k n", p=128)[:, :, 3072:3080], writes=["wfz"])
            P.dma("sp", negb[:], T["bf"], writes=["negb"])
            P.op("dve", lambda e: e.tensor_scalar(out=negb[:], in0=negb[:], scalar1=-1.0, scalar2=None, op0=ALU.mult),
                 reads=["negb"], writes=["negb"])
            for n in range(8):
                bank = n % 2
                pv = psf(C, bank)[0:8, :]
                for k in range(8):
                    P.op("pe", lambda e, k=k, n=n, pv=pv: e.matmul(pv, lhsT=wfz[:, k, :], rhs=hnTf[:, k, n * 512:(n + 1) * 512],
                                                                   start=(k == 0), stop=(k == 7)),
                         reads=["wfz", ("hnTf", n)], writes=[("ps", bank)])
                P.op("act", lambda e, n=n, pv=pv: e.activation(out=Lf[:, n * 512:(n + 1) * 512], in_=pv, func=AF.Exp, scale=-1.0, bias=negb[:, 0:1]),
                     reads=[("ps", bank), "negb"], writes=[("Lf", n)])
            for n in range(8):
                P.op("act", lambda e, n=n: e.activation(out=Lf[:, n * 512:(n + 1) * 512], in_=Lf[:, n * 512:(n + 1) * 512], func=AF.Ln, bias=1.0),
                     reads=[("Lf", n)], writes=[("Lf", n)])
            for n in range(8):
                init = 0.0 if n == 0 else Sc[:, n * 512 - 1:n * 512]
                rd = [("Lf", n), "onesf"] + ([("Sc", n - 1)] if n else [])
                P.op("dve", lambda e, n=n, init=init: e.tensor_tensor_scan(out=Sc[:, n * 512:(n + 1) * 512], data0=C.onesf[0:8, :],
                                                                           data1=Lf[:, n * 512:(n + 1) * 512], initial=init,
                                                                           op0=ALU.mult, op1=ALU.add),
                     reads=rd, writes=[("Sc", n)])
            pvt = psf(C, 2)[:, 0:256]
            for kb in range(32):
                P.op("pe", lambda e, kb=kb: e.transpose(out=pvt[:, kb * 8:(kb + 1) * 8], in_=Sc[0:8, kb * 128:(kb + 1) * 128],
                                                        identity=C.identf[0:8, 0:8]),
                     reads=[("Sc", kb // 4), "identf"], writes=[("ps", 2)])
            P.op("dve", lambda e: e.tensor_copy(out=Stm[:, :, :], in_=pvt.rearrange("p (k h) -> p k h", h=8)),
                 reads=[("ps", 2)], writes=["Stm"])
            CW = 128 * KPC
            ssel = Sc[:, :].rearrange("h (p t) -> h p t", t=CW)[:, :, CW - 1:CW]
            P.op("dve", lambda e: e.tensor_tensor(out=Dm[:, :, :], in0=ssel.to_broadcast([8, NCH, 8]),
                                                  in1=C.identf[0:8, 0:8].unsqueeze(1).to_broadcast([8, NCH, 8]), op=ALU.mult),
                 reads=[("Sc", n) for n in range(8)] + ["identf"], writes=["Dm"])
            pvx = psf(C, 3)[:, 0:NCH * 8]
            P.op("pe", lambda e: e.matmul(pvx, lhsT=C.onesf[0:8, 0:128], rhs=Dm[:, :, :].rearrange("h p g -> h (p g)"), start=True, stop=True),
                 reads=["onesf", "Dm"], writes=[("ps", 3)])
            P.op("dve", lambda e: e.tensor_copy(out=Xb[:, :], in_=pvx), reads=[("ps", 3)], writes=["Xb"])
        if stage < 2:
            return
        with scope(P) as es2:
            wq = es2.enter_context(nc.sbuf_tensor("wq", [128, 2, 8, 128], BF16))
            wk = es2.enter_context(nc.sbuf_tensor("wk", [128, 2, 8, 128], BF16))
            wv = es2.enter_context(nc.sbuf_tensor("wv", [128, 2, 8, 128], BF16))
            KT = es2.enter_context(nc.sbuf_tensor("KT", [128, S], BF16))
            QT = es2.enter_context(nc.sbuf_tensor("QTz", [128, 2, NT * 128], BF16))
            Vb = es2.enter_context(nc.sbuf_tensor("Vb", [128, 32, 130], BF16))
            Vd = Vb[:, :, 0:129]
            Vf = Vb[:, :, :].rearrange("p k (h d) -> p k h d", h=2)
            PT = es2.enter_context(nc.sbuf_tensor("PT", [128, 4, 2, 256], BF16))
            t1 = es2.enter_context(nc.sbuf_tensor("t1", [128, 2, 128], F32))
            dd = es2.enter_context(nc.sbuf_tensor("dd", [128, 2, 128], F32))
            rr = es2.enter_context(nc.sbuf_tensor("rr", [128, 2, 8], F32))
            P.op("pool", lambda e: e.memset(QT[:, :, :], 0.0), writes=[("QT", n) for n in range(NT // 4)])
            w_in_v = T["w_in"].rearrange("(k p) n -> p k n", p=128)
            rr_i = [0]
            for gidx in range(8):
                ws = gidx % 2
                diff = gidx < 4
                if gidx in (0, 4):
                    vk = [("V", kq) for kq in range(8)]
                    P.op("pool", lambda e: e.memset(Vb[:, :, :], 1.0), writes=vk)
                if diff:
                    qc, kc, vc = gidx * 128, 512 + gidx * 128, 1024 + gidx * 128
                else:
                    qc, kc, vc = 1536 + (gidx - 4) * 128, 2048 + (gidx - 4) * 128, 2560 + (gidx - 4) * 128
                P.dma("pool", wq[:, ws, :, :], w_in_v[:, :, qc:qc + 128], writes=[("wq", ws)])
                P.dma("pool", wk[:, ws, :, :], w_in_v[:, :, kc:kc + 128], writes=[("wk", ws)])
                P.dma("pool", wv[:, ws, :, :], w_in_v[:, :, vc:vc + 128], writes=[("wv", ws)])
                for n in range(8):
                    bank = 2 * (n % 2)
                    pv = psf(C, bank)
                    for k in range(8):
                        P.op("pe", lambda e, k=k, n=n, pv=pv: e.matmul(pv, lhsT=wk[:, ws, k, :], rhs=hnTf[:, k, n * 512:(n + 1) * 512],
                                                                       start=(k == 0), stop=(k == 7)),
                             reads=[("wk", ws), ("hnTf", n)], writes=[("ps", bank)])
                    P.op("dve", lambda e, n=n, pv=pv: e.tensor_copy(out=KT[:, n * 512:(n + 1) * 512], in_=pv),
                         reads=[("ps", bank)], writes=[("KT", n)])
                for n in range(NT // 4):
                    bank = 2 * (n % 2)
                    pv = psf(C, bank)
                    for k in range(8):
                        P.op("pe", lambda e, k=k, n=n, pv=pv: e.matmul(pv, lhsT=wq[:, ws, k, :], rhs=hnTo[:, k, n * 512:(n + 1) * 512],
                                                                       start=(k == 0), stop=(k == 7)),
                             reads=[("wq", ws), (("hnTf" if full else "hnTo"), n)], writes=[("ps", bank)])
                    P.op("dve", lambda e: e.tensor_copy(out=QT[0:64, 0, n * 512:(n + 1) * 512], in_=pv[0:64, :]),
                         reads=[("ps", bank)], writes=[("QT", n)])
                    P.op("dve", lambda e: e.tensor_copy(out=QT[64:128, 1, n * 512:(n + 1) * 512], in_=pv[64:128, :]),
                         reads=[("ps", bank), ("QT", n)], writes=[("QT", n)])
                for kq in range(8):
                    bank = 2 * (kq % 2)
                    pv = psf(C, bank)
                    for j in range(4):
                        kb = kq * 4 + j
                        for k in range(8):
                            P.op("pe", lambda e, k=k, kb=kb, j=j, pv=pv: e.matmul(pv[:, j * 128:(j + 1) * 128], lhsT=hnTf[:, k, kb * 128:(kb + 1) * 128],
                                                                                  rhs=wv[:, ws, k, :], start=(k == 0), stop=(k == 7)),
                                 reads=[("wv", ws), ("hnTf", kb // 4)], writes=[("ps", bank)])
                    if diff:
                        dst = Vd[:, kq * 4:(kq + 1) * 4, 0:128]
                        srcv = pv.rearrange("p (j d) -> p j d", j=4)
                        vkey = ("V", kq)
                    else:
                        dst = Vf[:, kq * 4:(kq + 1) * 4, :, 0:64]
                        srcv = pv.rearrange("p (j h d) -> p j h d", j=4, h=2)
                        vkey = ("V", kq)
                    P.op("dve", lambda e, dst=dst, srcv=srcv: e.tensor_copy(out=dst, in_=srcv), reads=[("ps", bank)], writes=[vkey])

                for p in range(NCH):
                    nkb = KPC * p + KPC
                    kb0 = KPC * p
                    oset = p % 2
                    if not diff:
                        fsl = p % 2
                        for sh in range(2):
                            hh_ = 2 * (gidx - 4) + sh
                            P.op("dve", lambda e: e.tensor_scalar(out=fb[:, fsl, sh, 0:nkb], in0=Stm[:, 0:nkb, hh_],
                                                                  scalar1=Xb[:, p * 8 + hh_:p * 8 + hh_ + 1], scalar2=None, op0=ALU.subtract),
                                 reads=["Stm", "Xb"], writes=[("fb", fsl)])
                    def emit_S(kb, p=p):
                        slot = kb % 4
                        for sh in range(2):
                            pvS = psf(C, slot)[:, sh * 256:(sh + 1) * 256]
                            P.op("pe", lambda e: e.matmul(pvS, lhsT=KT[:, kb * 128:(kb + 1) * 128],
                                                          rhs=QT[:, sh, p * 256:(p + 1) * 256], start=True, stop=True),
                                 reads=[("KT", kb // 4), ("QT", p // 2)], writes=[("ps", slot)])

                    def emit_PV(kb, p=p, nkb=nkb, oset=oset):
                        slot = kb % 4
                        for sh in range(2):
                            if diff:
                                ai = kb - kb0 + (30 if full else 28)
                                bias = alibi[:, gidx, ai:ai + 1]
                                bk = "alibi"
                            else:
                                bias = fb[:, p % 2, sh, kb:kb + 1]
                                bk = ("fb", p % 2)
                            P.op("act", lambda e: e.activation(out=PT[:, slot, sh, :], in_=psf(C, slot)[:, sh * 256:(sh + 1) * 256],
                                                               func=AF.Exp, scale=0.125, bias=bias),
                                 reads=[("ps", slot), bk], writes=[("PT", slot, sh)])
                        if kb >= kb0:
                            mk = masks[:, kb - kb0, :].unsqueeze(1).to_broadcast([128, 2, 256])
                            P.op("pool", lambda e: e.tensor_tensor(out=PT[:, slot, :, :], in0=PT[:, slot, :, :], in1=mk, op=ALU.mult),
                                 reads=[("PT", slot, 0), ("PT", slot, 1), "masks"], writes=[("PT", slot, 0), ("PT", slot, 1)])
                        for sh in range(2):
                            obank = 4 + oset * 2 + sh
                            for e_ in range(2):
                                if diff:
                                    ov = psf(C, obank)[:, e_ * 129:(e_ + 1) * 129]
                                    rhs = Vd[:, kb, :]
                                else:
                                    ov = psf(C, obank)[:, e_ * 65:(e_ + 1) * 65]
                                    rhs = Vf[:, kb, sh, :]
                                first = (kb == 0 and e_ == 0)
                                P.op("pe", lambda e: e.matmul(ov, lhsT=PT[:, slot, sh, e_ * 128:(e_ + 1) * 128], rhs=rhs,
                                                              start=first, stop=(kb == nkb - 1), skip_group_check=True),
                                     reads=[("PT", slot, sh), ("V", kb // 4)], writes=[("ps", obank)])

                    LOOK = 2
                    for kb in range(min(LOOK, nkb)):
                        emit_S(kb)
                    for kb in range(nkb):
                        if kb + LOOK < nkb:
                            emit_S(kb + LOOK)
                        emit_PV(kb)

                    b0 = 4 + oset * 2
                    i0 = rr_i[0] % 2
                    rr_i[0] += 1
                    rk = ("rr", i0)
                    if diff:
                        O1 = psf(C, b0)[:, 0:258].rearrange("p (e d) -> p e d", e=2)
                        O2 = psf(C, b0 + 1)[:, 0:258].rearrange("p (e d) -> p e d", e=2)
                        P.op("dve", lambda e: e.reciprocal(out=rr[:, i0, 0:2], in_=O1[:, :, 128]), reads=[("ps", b0)], writes=[rk])
                        P.op("dve", lambda e: e.reciprocal(out=rr[:, i0, 2:4], in_=O2[:, :, 128]), reads=[("ps", b0 + 1)], writes=[rk])
                        P.op("dve", lambda e: e.tensor_scalar(out=rr[:, i0, 2:4], in0=rr[:, i0, 2:4], scalar1=neglam[:, 2:3], scalar2=None, op0=ALU.mult),
                             reads=[rk, "neglam"], writes=[rk])
                        for e_ in range(2):
                            lt = 2 * p + e_
                            P.op("dve", lambda e, e_=e_: e.tensor_scalar(out=t1[:, e_, :], in0=O1[:, e_, 0:128], scalar1=rr[:, i0, e_:e_ + 1],
                                                                         scalar2=None, op0=ALU.mult),
                                 reads=[("ps", b0), rk], writes=[("t1", e_)])
                            P.op("dve", lambda e, e_=e_: e.scalar_tensor_tensor(out=dd[:, e_, :], in0=O2[:, e_, 0:128], scalar=rr[:, i0, 2 + e_:3 + e_],
                                                                                in1=t1[:, e_, :], op0=ALU.mult, op1=ALU.add),
                                 reads=[("ps", b0 + 1), rk, ("t1", e_)], writes=[("dd", e_)])
                            rs, rsk = rstd_from(C, dd[:, e_, :], [("dd", e_)], 128, 1e-5, eng="dve")
                            P.op("dve", lambda e, e_=e_, lt=lt, rs=rs: e.scalar_tensor_tensor(out=obuf[:, lt, gidx * 128:(gidx + 1) * 128], in0=dd[:, e_, :],
                                                                                             scalar=rs, in1=subg[:, :], op0=ALU.mult, op1=ALU.mult),
                                 reads=[("dd", e_), rsk, "subg"], writes=[("obuf", lt)])
                    else:
                        for sh in range(2):
                            Ov = psf(C, b0 + sh)[:, 0:130].rearrange("p (e d) -> p e d", e=2)
                            P.op("dve", lambda e, sh=sh, Ov=Ov: e.reciprocal(out=rr[:, i0, 4 + 2 * sh:6 + 2 * sh], in_=Ov[:, :, 64]),
                                 reads=[("ps", b0 + sh)], writes=[rk])
                            for e_ in range(2):
                                lt = 2 * p + e_
                                c0 = 512 + (2 * (gidx - 4) + sh) * 64
                                P.op("dve", lambda e, sh=sh, e_=e_, lt=lt, c0=c0, Ov=Ov: e.tensor_scalar(out=obuf[:, lt, c0:c0 + 64], in0=Ov[:, e_, 0:64],
                                                                                                       scalar1=rr[:, i0, 4 + 2 * sh + e_:5 + 2 * sh + e_],
                                                                                                       scalar2=None, op0=ALU.mult),
                                     reads=[("ps", b0 + sh), rk], writes=[("obuf", lt)])

        if stage < 3:
            return
        with scope(P) as es2:
            wo = es2.enter_context(nc.sbuf_tensor("wo", [128, 8, D], BF16))
            kwo = load_w_bf16(C, wo, T["w_out"], 8, "wo", nsplit=2)
            oT = es2.enter_context(nc.sbuf_tensor("oT", [128, 2, 8, 128], BF16))
            xr = es2.enter_context(nc.sbuf_tensor("xres", [128, 2, D], F32))
            hh = es2.enter_context(nc.sbuf_tensor("hh", [128, 2, D], F32))
            for lt in range(NT):
                sl = lt % 2
                P.dma("sp", xr[:, sl, :], T["xf" if full else "xo"][lt * 128:(lt + 1) * 128, :], writes=[("xres", sl)])
                transpose_to(C, obuf[:, lt, :], [("obuf", lt)], oT[:, sl, :, :], [("oT", sl)], bank=6 + sl, evac="act")
                for hf in range(2):
                    pk = ("ps", 2 * sl + hf)
                    pv = psf(C, 2 * sl + hf)
                    for k in range(8):
                        P.op("pe", lambda e, k=k, hf=hf, pv=pv: e.matmul(pv, lhsT=oT[:, sl, k, :], rhs=wo[:, k, hf * 512:(hf + 1) * 512],
                                                                         start=(k == 0), stop=(k == 7)),
                             reads=[("oT", sl), kwo[k // 4]], writes=[pk])
                emit_norm_residual(C, C.ps[sl][:, :], [("ps", 2 * sl), ("ps", 2 * sl + 1)], g[:, 1, :], ("ga", 1),
                                   xr[:, sl, :], [("xres", sl)], hh[:, sl, :], [("hh", sl)])
                P.dma("sp", H[lt * 128:(lt + 1) * 128, :], hh[:, sl, :], reads=[("hh", sl)], writes=[("H", lt)])


def build_phase_A(upto=99, stage=9):
    nc = bass.Bass("TRN2", target_bir_lowering=False)
    T = {}

    def din(name, shape, dt=F32):
        T[name] = nc.dram_tensor(name, shape, dt, kind="ExternalInput").ap()

    din("xf", [S, D]); din("xo", [NOWN, D]); din("pT", [256, NOWN])
    din("w_in", [D, 3080]); din("w_out", [D, D]); din("w_up", [D, DFF]); din("w_dn", [DFF, D])
    din("w_pp", [256, D]); din("w_pg", [D, D]); din("vec", [8, D]); din("lamv", [1, 256]); din("bf", [8, 1])
    din("ident", [128, 128]); din("masks", [4, 128, 256]); din("alibi", [128, 128])
    H = nc.dram_tensor("H", [NOWN, D], F32, kind="ExternalOutput").ap()
    H1 = nc.dram_tensor("H1", [NOWN, D], F32, kind="ExternalOutput").ap()
    HNT = nc.dram_tensor("HNT", [D, NOWN], BF16, kind="ExternalOutput").ap()
    with contextlib.ExitStack() as es:
        P = Prog(nc, es)
        C = Ctx()
        setup_common(nc, P, es, C, T["ident"])
        setup_eps(C, es)
        emit_attention(C, T, H, stage)
        if upto >= 2:
            emit_mlp(C, H, T["w_up"], T["w_dn"], T["vec"], 2, 3, "0")
        if upto >= 3:
            emit_ple(C, H, T["pT"], T["w_pp"], T["w_pg"], T["vec"], 4, H1, "0", hn_row=5, HNT=HNT)
        P.finish()
    return nc


def emit_rec(C, T, G, ncb=5):
    nc, P = C.nc, C.P
    TH = 2048
    with scope(P) as es:
        hnT = es.enter_context(nc.sbuf_tensor("r_hnT", [128, 8, S], BF16))
        wg = es.enter_context(nc.sbuf_tensor("r_wg", [128, 8, ncb * 128], BF16))
        wxr = es.enter_context(nc.sbuf_tensor("r_wxr", [128, 8, ncb * 128], BF16))
        wxs = es.enter_context(nc.sbuf_tensor("r_wxs", [128, ncb, 128], BF16))
        was = es.enter_context(nc.sbuf_tensor("r_was", [128, ncb, 128], BF16))
        rv = es.enter_context(nc.sbuf_tensor("r_rv", [128, ncb, 8], F32))
        sc = es.enter_context(nc.sbuf_tensor("r_sc", [128, ncb], F32))
        hlast = es.enter_context(nc.sbuf_tensor("r_hlast", [128, 1], F32))
        ybf = es.enter_context(nc.sbuf_tensor("r_y", [128, 2, TH], BF16))
        xr = es.enter_context(nc.sbuf_tensor("r_xr", [128, 2, TH + 3], F32))
        xc = es.enter_context(nc.sbuf_tensor("r_xc", [128, TH], F32))
        xcb = es.enter_context(nc.sbuf_tensor("r_xcb", [128, TH], BF16))
        gx = es.enter_context(nc.sbuf_tensor("r_gx", [128, TH], F32))
        ga = es.enter_context(nc.sbuf_tensor("r_ga", [128, TH], F32))
        tt = es.enter_context(nc.sbuf_tensor("r_tt", [128, TH], F32))
        hs = es.enter_context(nc.sbuf_tensor("r_hs", [128, TH], F32))
        gout = es.enter_context(nc.sbuf_tensor("r_gout", [128, TH], BF16))
        hv = T["hnT"].rearrange("(k f) t -> f k t", f=128)
        for k in range(8):
            P.dma("sp", hnT[:, k, :], hv[:, k, :], writes=[("r_hnT", k)])
        hk = [("r_hnT", k) for k in range(8)]
        kg = load_w_bf16(C, wg, T["w_g"], 8, "r_wg", nsplit=2)
        kx = load_w_bf16(C, wxr, T["w_x"], 8, "r_wxr", nsplit=2)
        P.dma("pool", wxs[:], T["wx"].rearrange("n i j -> i n j"), writes=["r_wxs"])
        P.dma("pool", was[:], T["wa"].rearrange("n i j -> i n j"), writes=["r_was"])
        P.dma("sp", rv[:], T["rvec"], writes=["r_rv"])
        P.op("act", lambda e: e.activation(out=sc[:, :], in_=rv[:, :, 7], func=AF.Exp, scale=-1.0), reads=["r_rv"], writes=["r_sc"])
        P.op("act", lambda e: e.activation(out=sc[:, :], in_=sc[:, :], func=AF.Ln, bias=1.0), reads=["r_sc"], writes=["r_sc"])
        P.op("dve", lambda e: e.tensor_scalar(out=sc[:, :], in0=sc[:, :], scalar1=-8.0, scalar2=None, op0=ALU.mult), reads=["r_sc"], writes=["r_sc"])
        iters = [(cb, th) for cb in range(ncb) for th in range(2)]

        def stage1(i):
            cb, th = iters[i]
            bs = i % 2
            if th == 0:
                P.op("dve", lambda e: e.memset(xr[:, bs, 0:3], 0.0), writes=[("r_xr", bs)])
            else:
                P.op("dve", lambda e: e.tensor_copy(out=xr[:, bs, 0:3], in_=xr[:, 1 - bs, TH:TH + 3]),
                     reads=[("r_xr", 1 - bs)], writes=[("r_xr", bs)])
            for n in range(4):
                N = 4 * th + n
                bg = n % 2
                pv = psf(C, bg)
                for k in range(8):
                    P.op("pe", lambda e: e.matmul(pv, lhsT=wg[:, k, cb * 128:(cb + 1) * 128], rhs=hnT[:, k, N * 512:(N + 1) * 512],
                                                  start=(k == 0), stop=(k == 7)),
                         reads=[kg[k // 4], ("r_hnT", k)], writes=[("ps", bg)])
                P.op("act", lambda e: e.activation(out=ybf[:, bs, n * 512:(n + 1) * 512], in_=pv, func=AF.Gelu_apprx_tanh),
                     reads=[("ps", bg)], writes=[("r_y", bs)])
                bx_ = 2 + n % 2
                pv2 = psf(C, bx_)
                for k in range(8):
                    P.op("pe", lambda e: e.matmul(pv2, lhsT=wxr[:, k, cb * 128:(cb + 1) * 128], rhs=hnT[:, k, N * 512:(N + 1) * 512],
                                                  start=(k == 0), stop=(k == 7)),
                         reads=[kx[k // 4], ("r_hnT", k)], writes=[("ps", bx_)])
                P.op("dve", lambda e: e.tensor_copy(out=xr[:, bs, 3 + n * 512:3 + (n + 1) * 512], in_=pv2),
                     reads=[("ps", bx_)], writes=[("r_xr", bs)])

        def stage2(i):
            cb, th = iters[i]
            bs = i % 2
            xk = ("r_xr", bs)
            P.op("act", lambda e: e.activation(out=xc[:, :], in_=xr[:, bs, 3:3 + TH], func=AF.Identity, scale=rv[:, cb, 3:4], bias=rv[:, cb, 4:5]),
                 reads=[xk, "r_rv"], writes=["r_xc"])
            for w in range(3):
                P.op("dve", lambda e: e.scalar_tensor_tensor(out=xc[:, :], in0=xr[:, bs, w:w + TH], scalar=rv[:, cb, w:w + 1], in1=xc[:, :],
                                                             op0=ALU.mult, op1=ALU.add),
                     reads=[xk, "r_rv", "r_xc"], writes=["r_xc"])
            P.op("act", lambda e: e.copy(out=xcb[:, :], in_=xc[:, :]), reads=["r_xc"], writes=["r_xcb"])
            for n in range(4):
                b1 = 4 + n % 2
                pv = psf(C, b1)
                P.op("pe", lambda e: e.matmul(pv, lhsT=wxs[:, cb, :], rhs=xcb[:, n * 512:(n + 1) * 512], start=True, stop=True),
                     reads=["r_wxs", "r_xcb"], writes=[("ps", b1)])
                P.op("act", lambda e: e.activation(out=gx[:, n * 512:(n + 1) * 512], in_=pv, func=AF.Sigmoid, bias=rv[:, cb, 5:6]),
                     reads=[("ps", b1), "r_rv"], writes=["r_gx"])
                b2 = 6 + n % 2
                pv2 = psf(C, b2)
                P.op("pe", lambda e: e.matmul(pv2, lhsT=was[:, cb, :], rhs=xcb[:, n * 512:(n + 1) * 512], start=True, stop=True),
                     reads=["r_was", "r_xcb"], writes=[("ps", b2)])
                P.op("act", lambda e: e.activation(out=ga[:, n * 512:(n + 1) * 512], in_=pv2, func=AF.Sigmoid, bias=rv[:, cb, 6:7]),
                     reads=[("ps", b2), "r_rv"], writes=["r_ga"])
            P.op("act", lambda e: e.activation(out=ga[:, :], in_=ga[:, :], func=AF.Exp, scale=sc[:, cb:cb + 1]), reads=["r_ga", "r_sc"], writes=["r_ga"])
            P.op("dve", lambda e: e.tensor_tensor(out=tt[:, :], in0=ga[:, :], in1=ga[:, :], op=ALU.mult), reads=["r_ga"], writes=["r_tt"])
            P.op("act", lambda e: e.activation(out=tt[:, :], in_=tt[:, :], func=AF.Sqrt, scale=-1.0, bias=1.0), reads=["r_tt"], writes=["r_tt"])
            P.op("dve", lambda e: e.tensor_tensor(out=gx[:, :], in0=gx[:, :], in1=xc[:, :], op=ALU.mult), reads=["r_gx", "r_xc"], writes=["r_gx"])
            P.op("dve", lambda e: e.tensor_tensor(out=tt[:, :], in0=tt[:, :], in1=gx[:, :], op=ALU.mult), reads=["r_tt", "r_gx"], writes=["r_tt"])
            if th == 0:
                P.op("dve", lambda e: e.tensor_copy(out=tt[:, 0:1], in_=gx[:, 0:1]), reads=["r_gx", "r_tt"], writes=["r_tt"])
            init = 0.0 if th == 0 else hlast[:, 0:1]
            P.op("dve", lambda e: e.tensor_tensor_scan(out=hs[:, :], data0=ga[:, :], data1=tt[:, :], initial=init, op0=ALU.mult, op1=ALU.add),
                 reads=["r_ga", "r_tt", "r_hlast"], writes=["r_hs"])
            P.op("dve", lambda e: e.tensor_copy(out=hlast[:, 0:1], in_=hs[:, TH - 1:TH]), reads=["r_hs"], writes=["r_hlast"])
            P.op("dve", lambda e: e.tensor_tensor(out=gout[:, :], in0=hs[:, :], in1=ybf[:, bs, :], op=ALU.mult),
                 reads=["r_hs", ("r_y", bs)], writes=["r_gout"])
            P.dma("sp", G[cb * 128:(cb + 1) * 128, th * TH:(th + 1) * TH], gout[:, :], reads=["r_gout"], writes=[("G", cb, th)])

        stage1(0)
        for i in range(len(iters)):
            if i + 1 < len(iters):
                stage1(i + 1)
            stage2(i)


def build_phase_B():
    nc = bass.Bass("TRN2", target_bir_lowering=False)
    T = {}

    def din(name, shape, dt=F32):
        T[name] = nc.dram_tensor(name, shape, dt, kind="ExternalInput").ap()

    din("hnT", [D, S], BF16); din("w_g", [D, 640]); din("w_x", [D, 640]); din("wx", [5, 128, 128]); din("wa", [5, 128, 128])
    din("rvec", [128, 5, 8]); din("ident", [128, 128])
    G = nc.dram_tensor("G", [640, S], BF16, kind="ExternalOutput").ap()
    with contextlib.ExitStack() as es:
        P = Prog(nc, es)
        C = Ctx()
        setup_common(nc, P, es, C, T["ident"])
        setup_eps(C, es)
        emit_rec(C, T, G)
        P.finish()
    return nc


def emit_recout(C, T, H, row=0, blend=None):
    nc, P = C.nc, C.P
    with scope(P) as es:
        g = load_gains(C, es, T["vec"], [row], "gro")
        gT = es.enter_context(nc.sbuf_tensor("gTs", [128, 10, NOWN], BF16))
        wro = es.enter_context(nc.sbuf_tensor("wro", [128, 10, D], BF16))
        if blend is None:
            gv = T["gT"].rearrange("(c p) t -> p c t", p=128)
            for c in range(10):
                P.dma("sp", gT[:, c, :], gv[:, c, :], writes=[("gTs", c)])
        else:
            Gd, H1d, sel_d = blend
            sel = es.enter_context(nc.sbuf_tensor("sel_sb", [128, 2], F32))
            P.dma("sp", sel[:], sel_d, writes=["sel"])
            gT2 = es.enter_context(nc.sbuf_tensor("gTs2", [128, 2, NOWN], BF16))
            hres2 = es.enter_context(nc.sbuf_tensor("hres2", [128, 2, D], F32))
            gv = Gd.rearrange("(c p) t -> p c t", p=128)
            for c in range(10):
                s2 = c % 2
                P.dma("sp", gT[:, c, :], gv[:, c, 0:NOWN], writes=[("gTs", c)])
                P.dma("sp", gT2[:, s2, :], gv[:, c, NOWN:2 * NOWN], writes=[("gTs2", s2)])
                P.op("act", lambda e: e.activation(out=gT[:, c, :], in_=gT[:, c, :], func=AF.Copy, scale=sel[:, 0:1]),
                     reads=[("gTs", c), "sel"], writes=[("gTs", c)])
                P.op("dve", lambda e: e.scalar_tensor_tensor(out=gT[:, c, :], in0=gT2[:, s2, :], scalar=sel[:, 1:2], in1=gT[:, c, :],
                                                             op0=ALU.mult, op1=ALU.add),
                     reads=[("gTs", c), ("gTs2", s2), "sel"], writes=[("gTs", c)])
        kw = load_w_bf16(C, wro, T["w_ro"], 10, "wro", nsplit=2)
        hres = es.enter_context(nc.sbuf_tensor("hres", [128, 2, D], F32))
        hh = es.enter_context(nc.sbuf_tensor("hh1", [128, 2, D], F32))
        for lt in range(16):
            sl = lt % 2
            if blend is None:
                P.dma("sp", hres[:, sl, :], T["h1"][lt * 128:(lt + 1) * 128, :], writes=[("hres", sl)])
            else:
                P.dma("sp", hres[:, sl, :], H1d[lt * 128:(lt + 1) * 128, :], writes=[("hres", sl)])
                P.dma("sp", hres2[:, sl, :], H1d[NOWN + lt * 128:NOWN + (lt + 1) * 128, :], writes=[("hres2", sl)])
                P.op("act", lambda e: e.activation(out=hres[:, sl, :], in_=hres[:, sl, :], func=AF.Copy, scale=sel[:, 0:1]),
                     reads=[("hres", sl), "sel"], writes=[("hres", sl)])
                P.op("dve", lambda e: e.scalar_tensor_tensor(out=hres[:, sl, :], in0=hres2[:, sl, :], scalar=sel[:, 1:2], in1=hres[:, sl, :],
                                                             op0=ALU.mult, op1=ALU.add),
                     reads=[("hres", sl), ("hres2", sl), "sel"], writes=[("hres", sl)])
            for hf in range(2):
                pk = ("ps", 2 * sl + hf)
                pv = psf(C, 2 * sl + hf)
                for c in range(10):
                    P.op("pe", lambda e: e.matmul(pv, lhsT=gT[:, c, lt * 128:(lt + 1) * 128], rhs=wro[:, c, hf * 512:(hf + 1) * 512],
                                                  start=(c == 0), stop=(c == 9)),
                         reads=[("gTs", c), kw[c // 5]], writes=[pk])
            emit_norm_residual(C, C.ps[sl][:, :], [("ps", 2 * sl), ("ps", 2 * sl + 1)], g[:, 0, :], ("gro", 0),
                               hres[:, sl, :], [("hres", sl)], hh[:, sl, :], [("hh1", sl)])
            P.dma("sp", H[lt * 128:(lt + 1) * 128, :], hh[:, sl, :], reads=[("hh1", sl)], writes=[("H", lt)])


def build_phase_C():
    nc = bass.Bass("TRN2", target_bir_lowering=False)
    T = {}

    def din(name, shape, dt=F32):
        T[name] = nc.dram_tensor(name, shape, dt, kind="ExternalInput").ap()

    din("gT", [RW, NOWN], BF16); din("h1", [NOWN, D]); din("pT", [256, NOWN]); din("w_ro", [RW, D])
    din("w_up", [D, DFF]); din("w_dn", [DFF, D]); din("w_pp", [256, D]); din("w_pg", [D, D]); din("vec", [8, D]); din("ident", [128, 128])
    H = nc.dram_tensor("H", [NOWN, D], F32, kind="ExternalOutput").ap()
    OUT = nc.dram_tensor("OUT", [NOWN, D], F32, kind="ExternalOutput").ap()
    with contextlib.ExitStack() as es:
        P = Prog(nc, es)
        C = Ctx()
        setup_common(nc, P, es, C, T["ident"])
        setup_eps(C, es)
        emit_recout(C, T, H)
        emit_mlp(C, H, T["w_up"], T["w_dn"], T["vec"], 1, 2, "1")
        emit_ple(C, H, T["pT"], T["w_pp"], T["w_pg"], T["vec"], 3, OUT, "1")
        P.finish()
    return nc


def build_fused():
    nc = bass.Bass("TRN2", target_bir_lowering=False)
    T = {}

    def din(name, shape, dt=F32):
        T[name] = nc.dram_tensor(name, shape, dt, kind="ExternalInput").ap()

    din("xf", [S, D]); din("pT", [256, S]); din("pT1", [256, NOWN])
    din("w_in", [D, 3080]); din("w_out", [D, D]); din("w_up", [D, DFF]); din("w_dn", [DFF, D])
    din("w_pp", [256, D]); din("w_pg", [D, D]); din("vec", [16, D]); din("lamv", [1, 256]); din("bf", [8, 1])
    din("ident", [128, 128]); din("masks", [2, 128, 256]); din("alibi", [128, 128]); din("sel", [128, 2])
    din("w_g", [D, RW]); din("w_x", [D, RW]); din("wx", [10, 128, 128]); din("wa", [10, 128, 128]); din("rvec", [128, 10, 8])
    din("w_ro", [RW, D]); din("w_up1", [D, DFF]); din("w_dn1", [DFF, D]); din("w_pp1", [256, D]); din("w_pg1", [D, D])
    HA = nc.dram_tensor("HA", [S, D], F32, kind="Internal").ap()
    H1 = nc.dram_tensor("H1", [S, D], F32, kind="Internal").ap()
    HNT = nc.dram_tensor("HNT", [D, S], BF16, kind="Internal").ap()
    G = nc.dram_tensor("G", [RW, S], BF16, kind="Internal").ap()
    HC = nc.dram_tensor("HC", [NOWN, D], F32, kind="Internal").ap()
    OUT = nc.dram_tensor("OUT", [NOWN, D], F32, kind="ExternalOutput").ap()
    with contextlib.ExitStack() as es:
        P = Prog(nc, es)
        C = Ctx()
        setup_common(nc, P, es, C, T["ident"])
        setup_eps(C, es)
        emit_attention(C, T, HA, full=True)
        emit_mlp(C, HA, T["w_up"], T["w_dn"], T["vec"], 2, 3, "0", ntok=S)
        emit_ple(C, HA, T["pT"], T["w_pp"], T["w_pg"], T["vec"], 4, H1, "0", hn_row=5, HNT=HNT, ntiles=32)
        T1 = {"hnT": HNT, "w_g": T["w_g"], "w_x": T["w_x"], "wx": T["wx"], "wa": T["wa"], "rvec": T["rvec"]}
        emit_rec(C, T1, G, ncb=10)
        T2 = {"vec": T["vec"], "w_ro": T["w_ro"]}
        emit_recout(C, T2, HC, row=8, blend=(G, H1, T["sel"]))
        emit_mlp(C, HC, T["w_up1"], T["w_dn1"], T["vec"], 9, 10, "1", ntok=NOWN)
        emit_ple(C, HC, T["pT1"], T["w_pp1"], T["w_pg1"], T["vec"], 11, OUT, "1", ntiles=16)
        P.finish()
    return nc


def own_tiles(r):
    return [4 * p + 2 * r + e for p in range(8) for e in range(2)]


def own_index(r):
    return np.concatenate([np.arange(g * 128, (g + 1) * 128) for g in own_tiles(r)])


def role_consts(r):
    ident = np.eye(128, dtype=np.float32)
    masks = np.zeros((4, 128, 256), np.float32)
    jj = np.arange(128)[:, None]
    ii = np.arange(128)[None, :]
    tri = (jj <= ii).astype(np.float32)
    for m in range(4):
        for e in range(2):
            qt = 2 * r + e
            if m < qt:
                masks[m, :, e * 128:(e + 1) * 128] = 1.0
            elif m == qt:
                masks[m, :, e * 128:(e + 1) * 128] = tri
    alibi = np.zeros((128, 4, 32), np.float32)
    for h in range(4):
        for idx in range(32):
            d = (idx - 28 - 2 * r - 2) * 128 + np.arange(128)
            alibi[:, h, idx] = np.minimum(SLOPES[h] * d, 0.0)
    return ident, masks, alibi.reshape(128, 128)


def f32c(a):
    return np.ascontiguousarray(a, dtype=np.float32)


def prep_A(inp, c):
    b, r = c // 2, c % 2
    oi = own_index(r)
    ident, masks, alibi = role_consts(r)
    vec = np.zeros((8, D), np.float32)
    vec[0] = inp["ln_mix_pre"][0]
    vec[1] = inp["ln_mix_post"][0]
    vec[2] = inp["ln_mlp_pre"][0]
    vec[3] = inp["ln_mlp_post"][0]
    vec[4] = inp["ple_norm"][0]
    vec[5] = inp["ln_mix_pre"][1]
    vec[6, :128] = inp["diff_subln"][0]
    lamv = np.concatenate([inp["diff_lambda_q1"][0], inp["diff_lambda_k1"][0],
                           inp["diff_lambda_q2"][0], inp["diff_lambda_k2"][0]])[None, :]
    return {
        "xf": f32c(inp["x"][b]), "xo": f32c(inp["x"][b][oi]), "pT": f32c(inp["p"][0, b][oi].T),
        "w_in": f32c(inp["attn_w_in"][0]), "w_out": f32c(inp["attn_w_out"][0]),
        "w_up": f32c(inp["mlp_w_up"][0]), "w_dn": f32c(inp["mlp_w_down"][0]),
        "w_pp": f32c(inp["ple_w_proj"][0]), "w_pg": f32c(inp["ple_w_gate"][0]),
        "vec": vec, "lamv": f32c(lamv), "bf": f32c(inp["attn_b_forget"][0][:, None]),
        "ident": ident, "masks": masks, "alibi": alibi,
    }


def prep_B(inp, c, hnT_full):
    r = c % 2
    cols_g = np.arange(5 * r * 128, (5 * r + 5) * 128)
    cols_x = RW + cols_g
    w_in = inp["rec_w_in"][0]
    rvec = np.zeros((128, 5, 8), np.float32)
    for j in range(5):
        ch = np.arange((5 * r + j) * 128, (5 * r + j + 1) * 128)
        rvec[:, j, 0:4] = inp["rec_conv_w"][0][:, ch].T
        rvec[:, j, 4] = inp["rec_conv_b"][0][ch]
        rvec[:, j, 5] = inp["rec_bx"][0][ch]
        rvec[:, j, 6] = inp["rec_ba"][0][ch]
        rvec[:, j, 7] = inp["rec_a_param"][0][ch]
    return {
        "hnT": hnT_full, "w_g": f32c(w_in[:, cols_g]), "w_x": f32c(w_in[:, cols_x]),
        "wx": f32c(inp["rec_wx"][0][5 * r:5 * r + 5]), "wa": f32c(inp["rec_wa"][0][5 * r:5 * r + 5]),
        "rvec": rvec, "ident": np.eye(128, dtype=np.float32),
    }


def prep_C(inp, c, gT_own, h1_own):
    b, r = c // 2, c % 2
    oi = own_index(r)
    vec = np.zeros((8, D), np.float32)
    vec[0] = inp["ln_mix_post"][1]
    vec[1] = inp["ln_mlp_pre"][1]
    vec[2] = inp["ln_mlp_post"][1]
    vec[3] = inp["ple_norm"][1]
    return {
        "gT": gT_own, "h1": h1_own, "pT": f32c(inp["p"][1, b][oi].T), "w_ro": f32c(inp["rec_w_out"][0]),
        "w_up": f32c(inp["mlp_w_up"][1]), "w_dn": f32c(inp["mlp_w_down"][1]),
        "w_pp": f32c(inp["ple_w_proj"][1]), "w_pg": f32c(inp["ple_w_gate"][1]),
        "vec": vec, "ident": np.eye(128, dtype=np.float32),
    }


def full_consts():
    ident = np.eye(128, dtype=np.float32)
    jj = np.arange(128)[:, None]
    ii = np.arange(128)[None, :]
    tri = (jj <= ii).astype(np.float32)
    masks = np.zeros((2, 128, 256), np.float32)
    masks[0, :, 0:128] = tri
    masks[0, :, 128:256] = 1.0
    masks[1, :, 128:256] = tri
    alibi = np.zeros((128, 4, 32), np.float32)
    for h in range(4):
        for idx in range(32):
            d = (idx - 30 - 2) * 128 + np.arange(128)
            alibi[:, h, idx] = np.minimum(SLOPES[h] * d, 0.0)
    return ident, masks, alibi.reshape(128, 128)


def prep_fused(inp, c):
    b, r = c // 2, c % 2
    ident, masks, alibi = full_consts()
    vec = np.zeros((16, D), np.float32)
    vec[0] = inp["ln_mix_pre"][0]
    vec[1] = inp["ln_mix_post"][0]
    vec[2] = inp["ln_mlp_pre"][0]
    vec[3] = inp["ln_mlp_post"][0]
    vec[4] = inp["ple_norm"][0]
    vec[5] = inp["ln_mix_pre"][1]
    vec[6, :128] = inp["diff_subln"][0]
    vec[8] = inp["ln_mix_post"][1]
    vec[9] = inp["ln_mlp_pre"][1]
    vec[10] = inp["ln_mlp_post"][1]
    vec[11] = inp["ple_norm"][1]
    lamv = np.concatenate([inp["diff_lambda_q1"][0], inp["diff_lambda_k1"][0],
                           inp["diff_lambda_q2"][0], inp["diff_lambda_k2"][0]])[None, :]
    rvec = np.zeros((128, 10, 8), np.float32)
    for j in range(10):
        ch = np.arange(j * 128, (j + 1) * 128)
        rvec[:, j, 0:4] = inp["rec_conv_w"][0][:, ch].T
        rvec[:, j, 4] = inp["rec_conv_b"][0][ch]
        rvec[:, j, 5] = inp["rec_bx"][0][ch]
        rvec[:, j, 6] = inp["rec_ba"][0][ch]
        rvec[:, j, 7] = inp["rec_a_param"][0][ch]
    sel = np.zeros((128, 2), np.float32)
    sel[:, r] = 1.0
    w_in1 = inp["rec_w_in"][0]
    return {
        "xf": f32c(inp["x"][b]), "pT": f32c(inp["p"][0, b].T), "pT1": f32c(inp["p"][1, b][r * NOWN:(r + 1) * NOWN].T),
        "w_in": f32c(inp["attn_w_in"][0]), "w_out": f32c(inp["attn_w_out"][0]),
        "w_up": f32c(inp["mlp_w_up"][0]), "w_dn": f32c(inp["mlp_w_down"][0]),
        "w_pp": f32c(inp["ple_w_proj"][0]), "w_pg": f32c(inp["ple_w_gate"][0]),
        "vec": vec, "lamv": f32c(lamv), "bf": f32c(inp["attn_b_forget"][0][:, None]),
        "ident": ident, "masks": masks, "alibi": alibi, "sel": sel,
        "w_g": f32c(w_in1[:, :RW]), "w_x": f32c(w_in1[:, RW:]), "wx": f32c(inp["rec_wx"][0]), "wa": f32c(inp["rec_wa"][0]),
        "rvec": rvec, "w_ro": f32c(inp["rec_w_out"][0]),
        "w_up1": f32c(inp["mlp_w_up"][1]), "w_dn1": f32c(inp["mlp_w_down"][1]),
        "w_pp1": f32c(inp["ple_w_proj"][1]), "w_pg1": f32c(inp["ple_w_gate"][1]),
    }


_NC_CACHE = {}


def _get(name, fn):
    if name not in _NC_CACHE:
        _NC_CACHE[name] = fn()
    return _NC_CACHE[name]


def kernel_unfused(**inputs):
    inp = {k: np.asarray(v) for k, v in inputs.items()}
    cores = list(range(NCORES))
    ncA = _get("A", build_phase_A)
    resA = run_bass_kernel_spmd(ncA, [prep_A(inp, c) for c in cores], core_ids=cores).results
    hnT_full = []
    for b in range(4):
        full = np.zeros((D, S), dtype=resA[0]["HNT"].dtype)
        for r in range(2):
            full[:, own_index(r)] = resA[2 * b + r]["HNT"]
        hnT_full.append(full)
    ncB = _get("B", build_phase_B)
    resB = run_bass_kernel_spmd(ncB, [prep_B(inp, c, hnT_full[c // 2]) for c in cores], core_ids=cores).results
    ncC = _get("C", build_phase_C)
    mapsC = []
    for c in cores:
        b, r = c // 2, c % 2
        oi = own_index(r)
        gfull = np.concatenate([resB[2 * b]["G"], resB[2 * b + 1]["G"]], axis=0)
        mapsC.append(prep_C(inp, c, np.ascontiguousarray(gfull[:, oi]), resA[c]["H1"]))
    resC = run_bass_kernel_spmd(ncC, mapsC, core_ids=cores).results
    out = np.zeros((4, S, D), np.float32)
    for c in cores:
        b, r = c // 2, c % 2
        out[b, own_index(r)] = resC[c]["OUT"]
    return out


def kernel(**inputs):
    inp = {k: np.asarray(v) for k, v in inputs.items()}
    cores = list(range(NCORES))
    nc = _get("F", build_fused)
    res = run_bass_kernel_spmd(nc, [prep_fused(inp, c) for c in cores], core_ids=cores).results
    out = np.zeros((4, S, D), np.float32)
    for c in cores:
        b, r = c // 2, c % 2
        out[b, r * NOWN:(r + 1) * NOWN] = res[c]["OUT"]
    return out
```

```python
import contextlib
import numpy as np
import concourse.bass as bass
import concourse.mybir as mybir
from concourse.bass_utils import run_bass_kernel_spmd

F32 = mybir.dt.float32
BF16 = mybir.dt.bfloat16
AF = mybir.ActivationFunctionType
ALU = mybir.AluOpType
AX = mybir.AxisListType

NCORES = 8
D = 1024
S = 4096
NOWN = 2048
DFF = 4096
RW = 1280
LAM_INIT0 = 0.8 - 0.6 * 1.0
SLOPES = [2.0 ** (-8.0 * (h + 1) / 4) for h in range(4)]


class Prog:
    K_DMA = 8

    def __init__(self, nc, es):
        self.nc = nc
        self.engs = {"pe": nc.tensor, "act": nc.scalar, "dve": nc.vector,
                     "pool": nc.gpsimd, "sp": nc.sync}
        self.semobj = {}
        for k in self.engs:
            self.semobj[k] = es.enter_context(nc.semaphore("sem_" + k))
        self.cnt = {k: 0 for k in self.engs}
        self.seen = {k: {} for k in self.engs}
        self.lastw = {}
        self.readers = {}
        self.dcnt = {}
        for q in ("sp", "act", "pool"):
            self.dcnt[q] = 0
            for i in range(self.K_DMA):
                self.semobj[(q, i)] = es.enter_context(nc.semaphore("d_%s_%d" % (q, i)))

    def _wait(self, eng, ev):
        sk, val = ev
        if self.seen[eng].get(sk, 0) >= val:
            return
        self.engs[eng].wait_ge(self.semobj[sk], val)
        self.seen[eng][sk] = val

    def _deps(self, eng, reads, writes):
        deps = {}
        for k in reads:
            w = self.lastw.get(k)
            if w is not None:
                deps[w[0]] = max(deps.get(w[0], 0), w[1])
        for k in writes:
            w = self.lastw.get(k)
            if w is not None:
                deps[w[0]] = max(deps.get(w[0], 0), w[1])
            for sk, v in self.readers.get(k, {}).items():
                deps[sk] = max(deps.get(sk, 0), v)
        for sk, v in deps.items():
            if eng == "pe" and sk == "pe":
                continue
            self._wait(eng, (sk, v))

    def _record(self, ev, reads, writes):
        for k in writes:
            self.lastw[k] = ev
            self.readers[k] = {}
        for k in reads:
            if k in writes:
                continue
            d = self.readers.setdefault(k, {})
            d[ev[0]] = max(d.get(ev[0], 0), ev[1])

    def op(self, eng, fn, reads=(), writes=()):
        self._deps(eng, reads, writes)
        inst = fn(self.engs[eng])
        self.cnt[eng] += 1
        inst.then_inc(self.semobj[eng], 1)
        self._record((eng, self.cnt[eng]), reads, writes)

    def dma(self, q, out, in_, reads=(), writes=()):
        self._deps(q, reads, writes)
        i = self.dcnt[q]
        s = i % self.K_DMA
        rnd = i // self.K_DMA
        if rnd > 0:
            self._wait(q, ((q, s), 16 * rnd))
        inst = self.engs[q].dma_start(out=out, in_=in_)
        inst.then_inc(self.semobj[(q, s)], 16)
        self.dcnt[q] = i + 1
        self._record(((q, s), 16 * (rnd + 1)), reads, writes)

    def barrier(self):
        evs = []
        for q in ("sp", "act", "pool"):
            n = self.dcnt[q]
            for s in range(self.K_DMA):
                m = (n - s + self.K_DMA - 1) // self.K_DMA if n > s else 0
                if m > 0:
                    evs.append(((q, s), 16 * m))
        for k in ("pe", "act", "dve", "pool", "sp"):
            if self.cnt[k] > 0:
                evs.append((k, self.cnt[k]))
        for eng in ("pe", "act", "dve", "pool", "sp"):
            for ev in evs:
                if ev[0] == eng:
                    continue
                self._wait(eng, ev)

    def barrier_keys(self, eng, keys):
        self._deps(eng, [], list(keys))

    def finish(self):
        for q in ("sp", "act", "pool"):
            n = self.dcnt[q]
            for s in range(self.K_DMA):
                m = (n - s + self.K_DMA - 1) // self.K_DMA if n > s else 0
                if m > 0:
                    self._wait("sp", ((q, s), 16 * m))
        for k in ("pe", "act", "dve", "pool"):
            if self.cnt[k] > 0:
                self._wait("sp", (k, self.cnt[k]))


class Ctx:
    pass


@contextlib.contextmanager
def scope(P):
    with contextlib.ExitStack() as es:
        yield es
        P.barrier()


def setup_common(nc, P, es, C, ident_d):
    C.nc = nc
    C.P = P
    C.identf = es.enter_context(nc.sbuf_tensor("identf", [128, 128], F32))
    C.identb = es.enter_context(nc.sbuf_tensor("identb", [128, 128], BF16))
    C.onesf = es.enter_context(nc.sbuf_tensor("onesf", [128, 512], F32))
    P.dma("sp", C.identf[:], ident_d, writes=["identf"])
    P.dma("pool", C.identb[:], ident_d, writes=["identb"])
    P.op("pool", lambda e: e.memset(C.onesf[:], 1.0), writes=["onesf"])
    C.ps = [es.enter_context(nc.psum_tensor("ps%d" % i, [128, 1024], F32)) for i in range(4)]
    C.psb = [t.bitcast(BF16) for t in C.ps]
    C.junk = es.enter_context(nc.sbuf_tensor("junk", [128, 1024], BF16))
    C.junk2 = es.enter_context(nc.sbuf_tensor("junk2", [128, 128], BF16))
    C.small = es.enter_context(nc.sbuf_tensor("small", [128, 64], F32))
    C.small_i = 0


def small_slot(C, n=1):
    i = C.small_i
    if i + n > 64:
        i = 0
    C.small_i = i + n
    return i


def psf(C, b):
    return C.ps[b // 2][:, (b % 2) * 512:(b % 2) * 512 + 512]


def psbf(C, b):
    return C.psb[b // 2][:, (b % 2) * 1024:(b % 2) * 1024 + 1024]


def rstd_from(C, src, src_keys, n, eps, eng="act"):
    P = C.P
    i = small_slot(C)
    ss = C.small[:, i:i + 1]
    key = ("small", i)
    if eng == "dve":
        P.op("dve", lambda e: e.scalar_tensor_tensor(out=C.junk2[:, 0:n], in0=src, scalar=1.0, in1=src,
                                                     op0=ALU.mult, op1=ALU.mult, accum_out=ss),
             reads=list(src_keys), writes=["junk2", key])
    else:
        P.op("act", lambda e: e.activation(out=C.junk[:, 0:n], in_=src, func=AF.Square, accum_out=ss),
             reads=list(src_keys), writes=["junk", key])
    P.op("dve", lambda e: e.tensor_scalar(out=ss, in0=ss, scalar1=1.0 / n, scalar2=float(eps), op0=ALU.mult, op1=ALU.add),
         reads=[key], writes=[key])
    P.op("pool", lambda e: e.tensor_tensor(out=ss, in0=ss, in1=C.negh, op=ALU.pow), reads=[key, "epsb"], writes=[key])
    return ss, key


def transpose_to(C, src_bf, src_keys, dst, dst_keys, bank, nk=8, evac="dve"):
    P = C.P
    pk = ("ps", bank)
    pv = psbf(C, bank)
    for k in range(nk):
        P.op("pe", lambda e, k=k: e.transpose(out=pv[:, k * 128:(k + 1) * 128], in_=src_bf[:, k * 128:(k + 1) * 128],
                                              identity=C.identb[:]),
             reads=list(src_keys) + ["identb"], writes=[pk])
    srcv = pv[:, 0:nk * 128].rearrange("p (k t) -> p k t", k=nk)
    if evac == "act":
        P.op("act", lambda e: e.copy(out=dst, in_=srcv), reads=[pk], writes=list(dst_keys))
    else:
        P.op("dve", lambda e: e.tensor_copy(out=dst, in_=srcv), reads=[pk], writes=list(dst_keys))


def load_w_bf16(C, dst, dram2d, nk, key, nsplit=1):
    P = C.P
    src = dram2d.rearrange("(k p) n -> p k n", p=128)
    step = nk // nsplit
    for i in range(nsplit):
        P.dma("pool", dst[:, i * step:(i + 1) * step, :], src[:, i * step:(i + 1) * step, :], writes=[(key, i)])
    return [(key, i) for i in range(nsplit)]


def emit_norm_residual(C, m_src, m_keys, gain, gain_key, res, res_keys, out, out_keys):
    P = C.P
    rs, rk = rstd_from(C, m_src, m_keys, D, 1e-6)
    P.op("dve", lambda e: e.scalar_tensor_tensor(out=out, in0=m_src, scalar=rs, in1=gain, op0=ALU.mult, op1=ALU.mult),
         reads=list(m_keys) + [rk, gain_key], writes=list(out_keys))
    P.op("pool", lambda e: e.tensor_tensor(out=out, in0=out, in1=res, op=ALU.add),
         reads=list(out_keys) + list(res_keys), writes=list(out_keys))


def emit_norm_bf16(C, src, src_keys, gain, gain_key, out_bf, out_keys):
    P = C.P
    rs, rk = rstd_from(C, src, src_keys, D, 1e-6)
    P.op("dve", lambda e: e.scalar_tensor_tensor(out=out_bf, in0=src, scalar=rs, in1=gain, op0=ALU.mult, op1=ALU.mult),
         reads=list(src_keys) + [rk, gain_key], writes=list(out_keys))


def load_gains(C, es, vec_d, rows, name):
    nc, P = C.nc, C.P
    g = es.enter_context(nc.sbuf_tensor(name, [128, len(rows), D], F32))
    for i, r in enumerate(rows):
        P.dma("sp", g[:, i, :], vec_d[r, :].partition_broadcast(128), writes=[(name, i)])
    return g


def setup_eps(C, es):
    nc, P = C.nc, C.P
    t = es.enter_context(nc.sbuf_tensor("epsb", [128, 3], F32))
    P.op("pool", lambda e: e.memset(t[:, 2:3], -0.5), writes=["epsb"])
    C.negh = t[:, 2:3]
    P.op("pool", lambda e: e.memset(t[:, 0:1], 1e-6), writes=["epsb"])
    P.op("pool", lambda e: e.memset(t[:, 1:2], 1e-5), writes=["epsb"])
    C.epsb = {1e-6: t[:, 0:1], 1e-5: t[:, 1:2]}


def emit_mlp(C, H, w_up_d, w_dn_d, vec_d, row_pre, row_post, tag, ntok=NOWN):
    nc, P = C.nc, C.P
    MT = 256
    with scope(P) as es:
        wup = es.enter_context(nc.sbuf_tensor("wup" + tag, [128, 8, DFF], BF16))
        wdn = es.enter_context(nc.sbuf_tensor("wdn" + tag, [128, 32, D], BF16))
        kup = load_w_bf16(C, wup, w_up_d, 8, "wup" + tag, nsplit=8)
        kdn = load_w_bf16(C, wdn, w_dn_d, 32, "wdn" + tag, nsplit=8)
        g = load_gains(C, es, vec_d, [row_pre, row_post], "gm" + tag)
        ha = es.enter_context(nc.sbuf_tensor("ha" + tag, [128, 2, 2, D], F32))
        ubf = es.enter_context(nc.sbuf_tensor("ubf" + tag, [128, 2, D], BF16))
        uT = es.enter_context(nc.sbuf_tensor("uT" + tag, [128, 2, 8, MT], BF16))
        hid = es.enter_context(nc.sbuf_tensor("hid" + tag, [128, 32, MT], BF16))
        rl = es.enter_context(nc.sbuf_tensor("rl" + tag, [128, 2, MT], F32))
        hb = es.enter_context(nc.sbuf_tensor("hb" + tag, [128, 2, D], F32))
        nmt = ntok // MT

        def prologue(t):
            sl = t % 2
            for e in range(2):
                lt = 2 * t + e
                P.dma("sp", ha[:, sl, e, :], H[lt * 128:(lt + 1) * 128, :], reads=[("H", lt)], writes=[("ha", sl, e)])
                emit_norm_bf16(C, ha[:, sl, e, :], [("ha", sl, e)], g[:, 0, :], ("gm" + tag, 0), ubf[:, e, :], [("ubf", e)])
                transpose_to(C, ubf[:, e, :], [("ubf", e)], uT[:, sl, :, e * 128:(e + 1) * 128], [("uT", sl, e)],
                             bank=6 + e, evac="act")

        def up(t):
            sl = t % 2
            for c in range(32):
                bank = c % 4
                pk = ("ps", bank)
                pv = psf(C, bank)[:, 0:MT]
                for k in range(8):
                    P.op("pe", lambda e: e.matmul(pv, lhsT=wup[:, k, c * 128:(c + 1) * 128], rhs=uT[:, sl, k, :],
                                                  start=(k == 0), stop=(k == 7)),
                         reads=[kup[k], ("uT", sl, 0), ("uT", sl, 1)], writes=[pk])
                rs_ = c % 2
                P.op("act", lambda e: e.activation(out=rl[:, rs_, :], in_=pv, func=AF.Relu),
                     reads=[pk], writes=[("rl", rs_)])
                P.op("pool", lambda e: e.tensor_tensor(out=hid[:, c, :], in0=rl[:, rs_, :], in1=rl[:, rs_, :], op=ALU.mult),
                     reads=[("rl", rs_)], writes=[("hid", c)])

        def down(t):
            sl = t % 2
            for e in range(2):
                lt = 2 * t + e
                for hf in range(2):
                    pk = ("ps", 4 + hf)
                    pv = psf(C, 4 + hf)
                    for c in range(32):
                        P.op("pe", lambda e_: e_.matmul(pv, lhsT=hid[:, c, e * 128:(e + 1) * 128],
                                                        rhs=wdn[:, c, hf * 512:(hf + 1) * 512],
                                                        start=(c == 0), stop=(c == 31)),
                             reads=[("hid", c), kdn[c // 4]], writes=[pk])
                fv = C.ps[2][:, :]
                emit_norm_residual(C, fv, [("ps", 4), ("ps", 5)], g[:, 1, :], ("gm" + tag, 1),
                                   ha[:, sl, e, :], [("ha", sl, e)], hb[:, e, :], [("hb", e)])
                P.dma("sp", H[lt * 128:(lt + 1) * 128, :], hb[:, e, :], reads=[("hb", e)], writes=[("H", lt)])

        prologue(0)
        for t in range(nmt):
            up(t)
            if t + 1 < nmt:
                prologue(t + 1)
            down(t)


def emit_ple(C, H, pT_d, w_pp_d, w_pg_d, vec_d, row_ple, OUT, tag, hn_row=None, HNT=None, ntiles=16):
    nc, P = C.nc, C.P
    with scope(P) as es:
        wpg = es.enter_context(nc.sbuf_tensor("wpg" + tag, [128, 8, D], BF16))
        wpp = es.enter_context(nc.sbuf_tensor("wpp" + tag, [128, 2, D], BF16))
        kpg = load_w_bf16(C, wpg, w_pg_d, 8, "wpg" + tag, nsplit=2)
        kpp = load_w_bf16(C, wpp, w_pp_d, 2, "wpp" + tag, nsplit=1)
        rows = [row_ple] + ([hn_row] if hn_row is not None else [])
        g = load_gains(C, es, vec_d, rows, "gp" + tag)
        hb = es.enter_context(nc.sbuf_tensor("phb" + tag, [128, 3, D], F32))
        hbb = es.enter_context(nc.sbuf_tensor("phbb" + tag, [128, 2, D], BF16))
        hbT = es.enter_context(nc.sbuf_tensor("phbT" + tag, [128, 2, 8, 128], BF16))
        pT = es.enter_context(nc.sbuf_tensor("ppT" + tag, [128, 3, 2, 128], BF16))
        sg = es.enter_context(nc.sbuf_tensor("psg" + tag, [128, 2, D], F32))
        ee = es.enter_context(nc.sbuf_tensor("pee" + tag, [128, 2, D], F32))
        hnb = es.enter_context(nc.sbuf_tensor("phnb" + tag, [128, 2, D], BF16))
        hnT = es.enter_context(nc.sbuf_tensor("phnT" + tag, [128, 2, 8, 128], BF16))
        pTv = pT_d.rearrange("(k f) t -> f k t", f=128)
        def loads(lt):
            s3 = lt % 3
            P.dma("sp", hb[:, s3, :], H[lt * 128:(lt + 1) * 128, :], reads=[("H", lt)], writes=[("phb", s3)])
            P.dma("pool", pT[:, s3, :, :], pTv[:, :, lt * 128:(lt + 1) * 128], writes=[("ppT", s3)])

        def front(lt):
            sl = lt % 2
            s3 = lt % 3
            if lt + 1 < ntiles:
                loads(lt + 1)
            P.op("dve", lambda e: e.tensor_copy(out=hbb[:, sl, :], in_=hb[:, s3, :]), reads=[("phb", s3)], writes=[("phbb", sl)])
            transpose_to(C, hbb[:, sl, :], [("phbb", sl)], hbT[:, sl, :, :], [("phbT", sl)], bank=6, evac="act")
            for hf in range(2):
                pk = ("ps", hf)
                pv = psf(C, hf)
                for k in range(8):
                    P.op("pe", lambda e: e.matmul(pv, lhsT=hbT[:, sl, k, :], rhs=wpg[:, k, hf * 512:(hf + 1) * 512],
                                                  start=(k == 0), stop=(k == 7)),
                         reads=[("phbT", sl), kpg[k // 4]], writes=[pk])
            for hf in range(2):
                pk = ("ps", 2 + 2 * sl + hf)
                pv = psf(C, 2 + 2 * sl + hf)
                for k in range(2):
                    P.op("pe", lambda e: e.matmul(pv, lhsT=pT[:, s3, k, :], rhs=wpp[:, k, hf * 512:(hf + 1) * 512],
                                                  start=(k == 0), stop=(k == 1)),
                         reads=[("ppT", s3), kpp[0]], writes=[pk])

        def front_b(lt):
            sl = lt % 2
            P.op("act", lambda e: e.activation(out=sg[:, sl, :], in_=C.ps[0][:, :], func=AF.Sigmoid),
                 reads=[("ps", 0), ("ps", 1)], writes=[("psg", sl)])

        def back(lt):
            sl = lt % 2
            ev = C.ps[1 + sl][:, :]
            pks = [("ps", 2 + 2 * sl), ("ps", 3 + 2 * sl)]
            rs, rk = rstd_from(C, ev, pks, D, 1e-6)
            P.op("dve", lambda e: e.scalar_tensor_tensor(out=ee[:, sl, :], in0=ev, scalar=rs, in1=g[:, 0, :], op0=ALU.mult, op1=ALU.mult),
                 reads=pks + [rk, ("gp" + tag, 0)], writes=[("pee", sl)])
            P.op("pool", lambda e: e.tensor_tensor(out=ee[:, sl, :], in0=ee[:, sl, :], in1=sg[:, sl, :], op=ALU.mult),
                 reads=[("pee", sl), ("psg", sl)], writes=[("pee", sl)])
            P.op("dve", lambda e: e.tensor_tensor(out=ee[:, sl, :], in0=ee[:, sl, :], in1=hb[:, lt % 3, :], op=ALU.add),
                 reads=[("pee", sl), ("phb", lt % 3)], writes=[("pee", sl)])
            P.dma("sp", OUT[lt * 128:(lt + 1) * 128, :], ee[:, sl, :], reads=[("pee", sl)], writes=[("OUT", lt)])

        def hnpart(lt):
            sl = lt % 2
            emit_norm_bf16(C, ee[:, sl, :], [("pee", sl)], g[:, 1, :], ("gp" + tag, 1), hnb[:, sl, :], [("phnb", sl)])
            transpose_to(C, hnb[:, sl, :], [("phnb", sl)], hnT[:, sl, :, :], [("phnT", sl)], bank=7, evac="dve")
            P.dma("sp", HNT.rearrange("(k f) t -> f k t", f=128)[:, :, lt * 128:(lt + 1) * 128], hnT[:, sl, :, :],
                  reads=[("phnT", sl)], writes=[("HNT", lt)])

        loads(0)
        front(0)
        front_b(0)
        for lt in range(ntiles):
            if lt + 1 < ntiles:
                front(lt + 1)
            back(lt)
            if hn_row is not None and lt >= 1:
                hnpart(lt - 1)
            if lt + 1 < ntiles:
                front_b(lt + 1)
        if hn_row is not None:
            hnpart(ntiles - 1)


def emit_attention(C, T, H, stage=9, full=False):
    nc, P = C.nc, C.P
    with scope(P) as es:
        g = load_gains(C, es, T["vec"], [0, 1], "ga")
        hnTf = es.enter_context(nc.sbuf_tensor("hnTf", [128, 8, S], BF16))
        NT = 32 if full else 16
        NCH = NT // 2
        KPC = 2 if full else 4
        NM = 2 if full else 4
        if full:
            hnTo = hnTf
        else:
            hnTo = es.enter_context(nc.sbuf_tensor("hnTo", [128, 8, NOWN], BF16))
        obuf = es.enter_context(nc.sbuf_tensor("obuf", [128, NT, D], BF16))
        masks = es.enter_context(nc.sbuf_tensor("masks_sb", [128, NM, 256], BF16))
        alibi = es.enter_context(nc.sbuf_tensor("alibi_sb", [128, 4, 32], F32))
        fb = es.enter_context(nc.sbuf_tensor("fb", [128, 2, 2, 32], F32))
        Stm = es.enter_context(nc.sbuf_tensor("Stm", [128, 32, 8], F32))
        Xb = es.enter_context(nc.sbuf_tensor("Xb", [128, NCH * 8], F32))
        neglam = es.enter_context(nc.sbuf_tensor("neglam", [128, 4], F32))
        subg = es.enter_context(nc.sbuf_tensor("subg", [128, 128], F32))
        P.dma("pool", masks[:], T["masks"].rearrange("m p q -> p m q"), writes=["masks"])
        P.dma("sp", alibi[:], T["alibi"].rearrange("p (h r) -> p h r", h=4), writes=["alibi"])
        P.dma("sp", subg[:], T["vec"][6, 0:128].partition_broadcast(128), writes=["subg"])
        P.op("dve", lambda e: e.tensor_scalar(out=subg[:], in0=subg[:], scalar1=1.0 - LAM_INIT0, scalar2=None, op0=ALU.mult),
             reads=["subg"], writes=["subg"])

        with scope(P) as es2:
            lv = es2.enter_context(nc.sbuf_tensor("lv", [1, 256], F32))
            pr = es2.enter_context(nc.sbuf_tensor("pr", [1, 128], F32))
            dots = es2.enter_context(nc.sbuf_tensor("dots", [1, 2], F32))
            P.dma("sp", lv[:], T["lamv"], writes=["lv"])
            P.op("dve", lambda e: e.tensor_tensor(out=pr[:, 0:64], in0=lv[:, 0:64], in1=lv[:, 64:128], op=ALU.mult),
                 reads=["lv"], writes=["pr"])
            P.op("dve", lambda e: e.tensor_tensor(out=pr[:, 64:128], in0=lv[:, 128:192], in1=lv[:, 192:256], op=ALU.mult),
                 reads=["lv", "pr"], writes=["pr"])
            P.op("dve", lambda e: e.tensor_reduce(out=dots[:, 0:2], in_=pr[:, :].rearrange("p (a d) -> p a d", a=2),
                                                  axis=AX.X, op=ALU.add), reads=["pr"], writes=["dots"])
            pv = psf(C, 0)[:, 0:2]
            P.op("pe", lambda e: e.matmul(pv, lhsT=C.onesf[0:1, 0:128], rhs=dots[0:1, 0:2], start=True, stop=True),
                 reads=["onesf", "dots"], writes=[("ps", 0)])
            P.op("act", lambda e: e.activation(out=neglam[:, 0:2], in_=pv, func=AF.Exp), reads=[("ps", 0)], writes=["neglam"])
            P.op("dve", lambda e: e.tensor_tensor(out=neglam[:, 2:3], in0=neglam[:, 1:2], in1=neglam[:, 0:1], op=ALU.subtract),
                 reads=["neglam"], writes=["neglam"])
            P.op("dve", lambda e: e.tensor_scalar(out=neglam[:, 2:3], in0=neglam[:, 2:3], scalar1=-LAM_INIT0, scalar2=None, op0=ALU.add),
                 reads=["neglam"], writes=["neglam"])

        with scope(P) as es2:
            xt = es2.enter_context(nc.sbuf_tensor("xt", [128, 2, D], F32))
            xb = es2.enter_context(nc.sbuf_tensor("xb", [128, 2, D], BF16))
            for i in range(32 if full else 48):
                sl = i % 2
                if i < 32:
                    src = T["xf"][i * 128:(i + 1) * 128, :]
                    dst = hnTf[:, :, i * 128:(i + 1) * 128]
                    dk = ("hnTf", i // 4)
                else:
                    j = i - 32
                    src = T["xo"][j * 128:(j + 1) * 128, :]
                    dst = hnTo[:, :, j * 128:(j + 1) * 128]
                    dk = ("hnTo", j // 4)
                if full:
                    dk = ("hnTf", i // 4)
                P.dma("sp", xt[:, sl, :], src, writes=[("xt", sl)])
                emit_norm_bf16(C, xt[:, sl, :], [("xt", sl)], g[:, 0, :], ("ga", 0), xb[:, sl, :], [("xb", sl)])
                transpose_to(C, xb[:, sl, :], [("xb", sl)], dst, [dk], bank=6 + sl, evac=("act" if sl else "dve"))

        if stage < 1:
            return
        with scope(P) as es2:
            wfz = es2.enter_context(nc.sbuf_tensor("wfz", [128, 8, 8], BF16))
            negb = es2.enter_context(nc.sbuf_tensor("negb", [8, 1], F32))
            Lf = es2.enter_context(nc.sbuf_tensor("Lf", [8, S], F32))
            Sc = es2.enter_context(nc.sbuf_tensor("Sc", [8, S], F32))
            Dm = es2.enter_context(nc.sbuf_tensor("Dm", [8, NCH, 8], F32))
            P.dma("pool", wfz[:], T["w_in"].rearrange("(k p) n -> p k n", p=128)[:, :, 3072:3080], writes=["wfz"])
            P.dma("sp", negb[:], T["bf"], writes=["negb"])
            P.op("dve", lambda e: e.tensor_scalar(out=negb[:], in0=negb[:], scalar1=-1.0, scalar2=None, op0=ALU.mult),
                 reads=["negb"], writes=["negb"])
            for n in range(8):
                bank = n % 2
                pv = psf(C, bank)[0:8, :]
                for k in range(8):
                    P.op("pe", lambda e, k=k, n=n, pv=pv: e.matmul(pv, lhsT=wfz[:, k, :], rhs=hnTf[:, k, n * 512:(n + 1) * 512],
                                                                   start=(k == 0), stop=(k == 7)),
                         reads=["wfz", ("hnTf", n)], writes=[("ps", bank)])
                P.op("act", lambda e, n=n, pv=pv: e.activation(out=Lf[:, n * 512:(n + 1) * 512], in_=pv, func=AF.Exp, scale=-1.0, bias=negb[:, 0:1]),
                     reads=[("ps", bank), "negb"], writes=[("Lf", n)])
            for n in range(8):
                P.op("act", lambda e, n=n: e.activation(out=Lf[:, n * 512:(n + 1) * 512], in_=Lf[:, n * 512:(n + 1) * 512], func=AF.Ln, bias=1.0),
                     reads=[("Lf", n)], writes=[("Lf", n)])
            for n in range(8):
                init = 0.0 if n == 0 else Sc[:, n * 512 - 1:n * 512]
                rd = [("Lf", n), "onesf"] + ([("Sc", n - 1)] if n else [])
                P.op("dve", lambda e, n=n, init=init: e.tensor_tensor_scan(out=Sc[:, n * 512:(n + 1) * 512], data0=C.onesf[0:8, :],
                                                                           data1=Lf[:, n * 512:(n + 1) * 512], initial=init,
                                                                           op0=ALU.mult, op1=ALU.add),
                     reads=rd, writes=[("Sc", n)])
            pvt = psf(C, 2)[:, 0:256]
            for kb in range(32):
                P.op("pe", lambda e, kb=kb: e.transpose(out=pvt[:, kb * 8:(kb + 1) * 8], in_=Sc[0:8, kb * 128:(kb + 1) * 128],
                                                        identity=C.identf[0:8, 0:8]),
                     reads=[("Sc", kb // 4), "identf"], writes=[("ps", 2)])
            P.op("dve", lambda e: e.tensor_copy(out=Stm[:, :, :], in_=pvt.rearrange("p (k h) -> p k h", h=8)),
                 reads=[("ps", 2)], writes=["Stm"])
            CW = 128 * KPC
            ssel = Sc[:, :].rearrange("h (p t) -> h p t", t=CW)[:, :, CW - 1:CW]
            P.op("dve", lambda e: e.tensor_tensor(out=Dm[:, :, :], in0=ssel.to_broadcast([8, NCH, 8]),
                                                  in1=C.identf[0:8, 0:8].unsqueeze(1).to_broadcast([8, NCH, 8]), op=ALU.mult),
                 reads=[("Sc", n) for n in range(8)] + ["identf"], writes=["Dm"])
            pvx = psf(C, 3)[:, 0:NCH * 8]
            P.op("pe", lambda e: e.matmul(pvx, lhsT=C.onesf[0:8, 0:128], rhs=Dm[:, :, :].rearrange("h p g -> h (p g)"), start=True, stop=True),
                 reads=["onesf", "Dm"], writes=[("ps", 3)])
            P.op("dve", lambda e: e.tensor_copy(out=Xb[:, :], in_=pvx), reads=[("ps", 3)], writes=["Xb"])
        if stage < 2:
            return
        with scope(P) as es2:
            wq = es2.enter_context(nc.sbuf_tensor("wq", [128, 2, 8, 128], BF16))
            wk = es2.enter_context(nc.sbuf_tensor("wk", [128, 2, 8, 128], BF16))
            wv = es2.enter_context(nc.sbuf_tensor("wv", [128, 2, 8, 128], BF16))
            KT = es2.enter_context(nc.sbuf_tensor("KT", [128, S], BF16))
            QT = es2.enter_context(nc.sbuf_tensor("QTz", [128, 2, NT * 128], BF16))
            Vb = es2.enter_context(nc.sbuf_tensor("Vb", [128, 32, 130], BF16))
            Vd = Vb[:, :, 0:129]
            Vf = Vb[:, :, :].rearrange("p k (h d) -> p k h d", h=2)
            PT = es2.enter_context(nc.sbuf_tensor("PT", [128, 4, 2, 256], BF16))
            t1 = es2.enter_context(nc.sbuf_tensor("t1", [128, 2, 128], F32))
            dd = es2.enter_context(nc.sbuf_tensor("dd", [128, 2, 128], F32))
            rr = es2.enter_context(nc.sbuf_tensor("rr", [128, 2, 8], F32))
            P.op("pool", lambda e: e.memset(QT[:, :, :], 0.0), writes=[("QT", n) for n in range(NT // 4)])
            w_in_v = T["w_in"].rearrange("(k p) n -> p k n", p=128)
            rr_i = [0]
            for gidx in range(8):
                ws = gidx % 2
                diff = gidx < 4
                if gidx in (0, 4):
                    vk = [("V", kq) for kq in range(8)]
                    P.op("pool", lambda e: e.memset(Vb[:, :, :], 1.0), writes=vk)
                if diff:
                    qc, kc, vc = gidx * 128, 512 + gidx * 128, 1024 + gidx * 128
                else:
                    qc, kc, vc = 1536 + (gidx - 4) * 128, 2048 + (gidx - 4) * 128, 2560 + (gidx - 4) * 128
                P.dma("pool", wq[:, ws, :, :], w_in_v[:, :, qc:qc + 128], writes=[("wq", ws)])
                P.dma("pool", wk[:, ws, :, :], w_in_v[:, :, kc:kc + 128], writes=[("wk", ws)])
                P.dma("pool", wv[:, ws, :, :], w_in_v[:, :, vc:vc + 128], writes=[("wv", ws)])
                for n in range(8):
                    bank = 2 * (n % 2)
                    pv = psf(C, bank)
                    for k in range(8):
                        P.op("pe", lambda e, k=k, n=n, pv=pv: e.matmul(pv, lhsT=wk[:, ws, k, :], rhs=hnTf[:, k, n * 512:(n + 1) * 512],
                                                                       start=(k == 0), stop=(k == 7)),
                             reads=[("wk", ws), ("hnTf", n)], writes=[("ps", bank)])
                    P.op("dve", lambda e, n=n, pv=pv: e.tensor_copy(out=KT[:, n * 512:(n + 1) * 512], in_=pv),
                         reads=[("ps", bank)], writes=[("KT", n)])
                for n in range(NT // 4):
                    bank = 2 * (n % 2)
                    pv = psf(C, bank)
                    for k in range(8):
                        P.op("pe", lambda e, k=k, n=n, pv=pv: e.matmul(pv, lhsT=wq[:, ws, k, :], rhs=hnTo[:, k, n * 512:(n + 1) * 512],
                                                                       start=(k == 0), stop=(k == 7)),
                             reads=[("wq", ws), (("hnTf" if full else "hnTo"), n)], writes=[("ps", bank)])
                    P.op("dve", lambda e: e.tensor_copy(out=QT[0:64, 0, n * 512:(n + 1) * 512], in_=pv[0:64, :]),
                         reads=[("ps", bank)], writes=[("QT", n)])
                    P.op("dve", lambda e: e.tensor_copy(out=QT[64:128, 1, n * 512:(n + 1) * 512], in_=pv[64:128, :]),
                         reads=[("ps", bank), ("QT", n)], writes=[("QT", n)])
                for kq in range(8):
                    bank = 2 * (kq % 2)
                    pv = psf(C, bank)
                    for j in range(4):
                        kb = kq * 4 + j
                        for k in range(8):
                            P.op("pe", lambda e, k=k, kb=kb, j=j, pv=pv: e.matmul(pv[:, j * 128:(j + 1) * 128], lhsT=hnTf[:, k, kb * 128:(kb + 1) * 128],
                                                                                  rhs=wv[:, ws, k, :], start=(k == 0), stop=(k == 7)),
                                 reads=[("wv", ws), ("hnTf", kb // 4)], writes=[("ps", bank)])
                    if diff:
                        dst = Vd[:, kq * 4:(kq + 1) * 4, 0:128]
                        srcv = pv.rearrange("p (j d) -> p j d", j=4)
                        vkey = ("V", kq)
                    else:
                        dst = Vf[:, kq * 4:(kq + 1) * 4, :, 0:64]
                        srcv = pv.rearrange("p (j h d) -> p j h d", j=4, h=2)
                        vkey = ("V", kq)
                    P.op("dve", lambda e, dst=dst, srcv=srcv: e.tensor_copy(out=dst, in_=srcv), reads=[("ps", bank)], writes=[vkey])

                for p in range(NCH):
                    nkb = KPC * p + KPC
                    kb0 = KPC * p
                    oset = p % 2
                    if not diff:
                        fsl = p % 2
                        for sh in range(2):
                            hh_ = 2 * (gidx - 4) + sh
                            P.op("dve", lambda e: e.tensor_scalar(out=fb[:, fsl, sh, 0:nkb], in0=Stm[:, 0:nkb, hh_],
                                                                  scalar1=Xb[:, p * 8 + hh_:p * 8 + hh_ + 1], scalar2=None, op0=ALU.subtract),
                                 reads=["Stm", "Xb"], writes=[("fb", fsl)])
                    def emit_S(kb, p=p):
                        slot = kb % 4
                        for sh in range(2):
                            pvS = psf(C, slot)[:, sh * 256:(sh + 1) * 256]
                            P.op("pe", lambda e: e.matmul(pvS, lhsT=KT[:, kb * 128:(kb + 1) * 128],
                                                          rhs=QT[:, sh, p * 256:(p + 1) * 256], start=True, stop=True),
                                 reads=[("KT", kb // 4), ("QT", p // 2)], writes=[("ps", slot)])

                    def emit_PV(kb, p=p, nkb=nkb, oset=oset):
                        slot = kb % 4
                        for sh in range(2):
                            if diff:
                                ai = kb - kb0 + (30 if full else 28)
                                bias = alibi[:, gidx, ai:ai + 1]
                                bk = "alibi"
                            else:
                                bias = fb[:, p % 2, sh, kb:kb + 1]
                                bk = ("fb", p % 2)
                            P.op("act", lambda e: e.activation(out=PT[:, slot, sh, :], in_=psf(C, slot)[:, sh * 256:(sh + 1) * 256],
                                                               func=AF.Exp, scale=0.125, bias=bias),
                                 reads=[("ps", slot), bk], writes=[("PT", slot, sh)])
                        if kb >= kb0:
                            mk = masks[:, kb - kb0, :].unsqueeze(1).to_broadcast([128, 2, 256])
                            P.op("pool", lambda e: e.tensor_tensor(out=PT[:, slot, :, :], in0=PT[:, slot, :, :], in1=mk, op=ALU.mult),
                                 reads=[("PT", slot, 0), ("PT", slot, 1), "masks"], writes=[("PT", slot, 0), ("PT", slot, 1)])
                        for sh in range(2):
                            obank = 4 + oset * 2 + sh
                            for e_ in range(2):
                                if diff:
                                    ov = psf(C, obank)[:, e_ * 129:(e_ + 1) * 129]
                                    rhs = Vd[:, kb, :]
                                else:
                                    ov = psf(C, obank)[:, e_ * 65:(e_ + 1) * 65]
                                    rhs = Vf[:, kb, sh, :]
                                first = (kb == 0 and e_ == 0)
                                P.op("pe", lambda e: e.matmul(ov, lhsT=PT[:, slot, sh, e_ * 128:(e_ + 1) * 128], rhs=rhs,
                                                              start=first, stop=(kb == nkb - 1), skip_group_check=True),
                                     reads=[("PT", slot, sh), ("V", kb // 4)], writes=[("ps", obank)])

                    LOOK = 2
                    for kb in range(min(LOOK, nkb)):
                        emit_S(kb)
                    for kb in range(nkb):
                        if kb + LOOK < nkb:
                            emit_S(kb + LOOK)
                        emit_PV(kb)

                    b0 = 4 + oset * 2
                    i0 = rr_i[0] % 2
                    rr_i[0] += 1
                    rk = ("rr", i0)
                    if diff:
                        O1 = psf(C, b0)[:, 0:258].rearrange("p (e d) -> p e d", e=2)
                        O2 = psf(C, b0 + 1)[:, 0:258].rearrange("p (e d) -> p e d", e=2)
                        P.op("dve", lambda e: e.reciprocal(out=rr[:, i0, 0:2], in_=O1[:, :, 128]), reads=[("ps", b0)], writes=[rk])
                        P.op("dve", lambda e: e.reciprocal(out=rr[:, i0, 2:4], in_=O2[:, :, 128]), reads=[("ps", b0 + 1)], writes=[rk])
                        P.op("dve", lambda e: e.tensor_scalar(out=rr[:, i0, 2:4], in0=rr[:, i0, 2:4], scalar1=neglam[:, 2:3], scalar2=None, op0=ALU.mult),
                             reads=[rk, "neglam"], writes=[rk])
                        for e_ in range(2):
                            lt = 2 * p + e_
                            P.op("dve", lambda e, e_=e_: e.tensor_scalar(out=t1[:, e_, :], in0=O1[:, e_, 0:128], scalar1=rr[:, i0, e_:e_ + 1],
                                                                         scalar2=None, op0=ALU.mult),
                                 reads=[("ps", b0), rk], writes=[("t1", e_)])
                            P.op("dve", lambda e, e_=e_: e.scalar_tensor_tensor(out=dd[:, e_, :], in0=O2[:, e_, 0:128], scalar=rr[:, i0, 2 + e_:3 + e_],
                                                                                in1=t1[:, e_, :], op0=ALU.mult, op1=ALU.add),
                                 reads=[("ps", b0 + 1), rk, ("t1", e_)], writes=[("dd", e_)])
                            rs, rsk = rstd_from(C, dd[:, e_, :], [("dd", e_)], 128, 1e-5, eng="dve")
                            P.op("dve", lambda e, e_=e_, lt=lt, rs=rs: e.scalar_tensor_tensor(out=obuf[:, lt, gidx * 128:(gidx + 1) * 128], in0=dd[:, e_, :],
                                                                                             scalar=rs, in1=subg[:, :], op0=ALU.mult, op1=ALU.mult),
                                 reads=[("dd", e_), rsk, "subg"], writes=[("obuf", lt)])
                    else:
                        for sh in range(2):
                            Ov = psf(C, b0 + sh)[:, 0:130].rearrange("p (e d) -> p e d", e=2)
                            P.op("dve", lambda e, sh=sh, Ov=Ov: e.reciprocal(out=rr[:, i0, 4 + 2 * sh:6 + 2 * sh], in_=Ov[:, :, 64]),
                                 reads=[("ps", b0 + sh)], writes=[rk])
                            for e_ in range(2):
                                lt = 2 * p + e_
                                c0 = 512 + (2 * (gidx - 4) + sh) * 64
                                P.op("dve", lambda e, sh=sh, e_=e_, lt=lt, c0=c0, Ov=Ov: e.tensor_scalar(out=obuf[:, lt, c0:c0 + 64], in0=Ov[:, e_, 0:64],
                                                                                                       scalar1=rr[:, i0, 4 + 2 * sh + e_:5 + 2 * sh + e_],
                                                                                                       scalar2=None, op0=ALU.mult),
                                     reads=[("ps", b0 + sh), rk], writes=[("obuf", lt)])

        if stage < 3:
            return
        with scope(P) as es2:
            wo = es2.enter_context(nc.sbuf_tensor("wo", [128, 8, D], BF16))
            kwo = load_w_bf16(C, wo, T["w_out"], 8, "wo", nsplit=2)
            oT = es2.enter_context(nc.sbuf_tensor("oT", [128, 2, 8, 128], BF16))
            xr = es2.enter_context(nc.sbuf_tensor("xres", [128, 2, D], F32))
            hh = es2.enter_context(nc.sbuf_tensor("hh", [128, 2, D], F32))
            xsrc = T["xf" if full else "xo"]
            P.dma("sp", xr[:, 0, :], xsrc[0:128, :], writes=[("xres", 0)])
            for lt in range(NT):
                sl = lt % 2
                if lt + 1 < NT:
                    P.dma("sp", xr[:, 1 - sl, :], xsrc[(lt + 1) * 128:(lt + 2) * 128, :], writes=[("xres", 1 - sl)])
                transpose_to(C, obuf[:, lt, :], [("obuf", lt)], oT[:, sl, :, :], [("oT", sl)], bank=6 + sl, evac="act")
                for hf in range(2):
                    pk = ("ps", 2 * sl + hf)
                    pv = psf(C, 2 * sl + hf)
                    for k in range(8):
                        P.op("pe", lambda e, k=k, hf=hf, pv=pv: e.matmul(pv, lhsT=oT[:, sl, k, :], rhs=wo[:, k, hf * 512:(hf + 1) * 512],
                                                                         start=(k == 0), stop=(k == 7)),
                             reads=[("oT", sl), kwo[k // 4]], writes=[pk])
                emit_norm_residual(C, C.ps[sl][:, :], [("ps", 2 * sl), ("ps", 2 * sl + 1)], g[:, 1, :], ("ga", 1),
                                   xr[:, sl, :], [("xres", sl)], hh[:, sl, :], [("hh", sl)])
                P.dma("sp", H[lt * 128:(lt + 1) * 128, :], hh[:, sl, :], reads=[("hh", sl)], writes=[("H", lt)])


def build_phase_A(upto=99, stage=9):
    nc = bass.Bass("TRN2", target_bir_lowering=False)
    T = {}

    def din(name, shape, dt=F32):
        T[name] = nc.dram_tensor(name, shape, dt, kind="ExternalInput").ap()

    din("xf", [S, D]); din("xo", [NOWN, D]); din("pT", [256, NOWN])
    din("w_in", [D, 3080]); din("w_out", [D, D]); din("w_up", [D, DFF]); din("w_dn", [DFF, D])
    din("w_pp", [256, D]); din("w_pg", [D, D]); din("vec", [8, D]); din("lamv", [1, 256]); din("bf", [8, 1])
    din("ident", [128, 128]); din("masks", [4, 128, 256]); din("alibi", [128, 128])
    H = nc.dram_tensor("H", [NOWN, D], F32, kind="ExternalOutput").ap()
    H1 = nc.dram_tensor("H1", [NOWN, D], F32, kind="ExternalOutput").ap()
    HNT = nc.dram_tensor("HNT", [D, NOWN], BF16, kind="ExternalOutput").ap()
    with contextlib.ExitStack() as es:
        P = Prog(nc, es)
        C = Ctx()
        setup_common(nc, P, es, C, T["ident"])
        setup_eps(C, es)
        emit_attention(C, T, H, stage)
        if upto >= 2:
            emit_mlp(C, H, T["w_up"], T["w_dn"], T["vec"], 2, 3, "0")
        if upto >= 3:
            emit_ple(C, H, T["pT"], T["w_pp"], T["w_pg"], T["vec"], 4, H1, "0", hn_row=5, HNT=HNT)
        P.finish()
    return nc


def emit_rec(C, T, G, ncb=5):
    nc, P = C.nc, C.P
    TH = 2048
    with scope(P) as es:
        hnT = es.enter_context(nc.sbuf_tensor("r_hnT", [128, 8, S], BF16))
        wg = es.enter_context(nc.sbuf_tensor("r_wg", [128, 8, ncb * 128], BF16))
        wxr = es.enter_context(nc.sbuf_tensor("r_wxr", [128, 8, ncb * 128], BF16))
        wxs = es.enter_context(nc.sbuf_tensor("r_wxs", [128, ncb, 128], BF16))
        was = es.enter_context(nc.sbuf_tensor("r_was", [128, ncb, 128], BF16))
        rv = es.enter_context(nc.sbuf_tensor("r_rv", [128, ncb, 8], F32))
        sc = es.enter_context(nc.sbuf_tensor("r_sc", [128, ncb], F32))
        hlast = es.enter_context(nc.sbuf_tensor("r_hlast", [128, 1], F32))
        ybf = es.enter_context(nc.sbuf_tensor("r_y", [128, 2, TH], BF16))
        xr = es.enter_context(nc.sbuf_tensor("r_xr", [128, 2, TH + 3], F32))
        xc = es.enter_context(nc.sbuf_tensor("r_xc", [128, TH], F32))
        xcb = es.enter_context(nc.sbuf_tensor("r_xcb", [128, TH], BF16))
        gx = es.enter_context(nc.sbuf_tensor("r_gx", [128, TH], F32))
        ga = es.enter_context(nc.sbuf_tensor("r_ga", [128, TH], F32))
        tt = es.enter_context(nc.sbuf_tensor("r_tt", [128, TH], F32))
        hs = es.enter_context(nc.sbuf_tensor("r_hs", [128, TH], F32))
        gout = es.enter_context(nc.sbuf_tensor("r_gout", [128, TH], BF16))
        hv = T["hnT"].rearrange("(k f) t -> f k t", f=128)
        for k in range(8):
            P.dma("sp", hnT[:, k, :], hv[:, k, :], writes=[("r_hnT", k)])
        hk = [("r_hnT", k) for k in range(8)]
        kg = load_w_bf16(C, wg, T["w_g"], 8, "r_wg", nsplit=2)
        kx = load_w_bf16(C, wxr, T["w_x"], 8, "r_wxr", nsplit=2)
        P.dma("pool", wxs[:], T["wx"].rearrange("n i j -> i n j"), writes=["r_wxs"])
        P.dma("pool", was[:], T["wa"].rearrange("n i j -> i n j"), writes=["r_was"])
        P.dma("sp", rv[:], T["rvec"], writes=["r_rv"])
        P.op("act", lambda e: e.activation(out=sc[:, :], in_=rv[:, :, 7], func=AF.Exp, scale=-1.0), reads=["r_rv"], writes=["r_sc"])
        P.op("act", lambda e: e.activation(out=sc[:, :], in_=sc[:, :], func=AF.Ln, bias=1.0), reads=["r_sc"], writes=["r_sc"])
        P.op("dve", lambda e: e.tensor_scalar(out=sc[:, :], in0=sc[:, :], scalar1=-8.0, scalar2=None, op0=ALU.mult), reads=["r_sc"], writes=["r_sc"])
        iters = [(cb, th) for cb in range(ncb) for th in range(2)]

        def stage1(i):
            cb, th = iters[i]
            bs = i % 2
            if th == 0:
                P.op("dve", lambda e: e.memset(xr[:, bs, 0:3], 0.0), writes=[("r_xr", bs)])
            else:
                P.op("dve", lambda e: e.tensor_copy(out=xr[:, bs, 0:3], in_=xr[:, 1 - bs, TH:TH + 3]),
                     reads=[("r_xr", 1 - bs)], writes=[("r_xr", bs)])
            for n in range(4):
                N = 4 * th + n
                bg = n % 2
                pv = psf(C, bg)
                for k in range(8):
                    P.op("pe", lambda e: e.matmul(pv, lhsT=wg[:, k, cb * 128:(cb + 1) * 128], rhs=hnT[:, k, N * 512:(N + 1) * 512],
                                                  start=(k == 0), stop=(k == 7)),
                         reads=[kg[k // 4], ("r_hnT", k)], writes=[("ps", bg)])
                P.op("act", lambda e: e.activation(out=ybf[:, bs, n * 512:(n + 1) * 512], in_=pv, func=AF.Gelu_apprx_tanh),
                     reads=[("ps", bg)], writes=[("r_y", bs)])
                bx_ = 2 + n % 2
                pv2 = psf(C, bx_)
                for k in range(8):
                    P.op("pe", lambda e: e.matmul(pv2, lhsT=wxr[:, k, cb * 128:(cb + 1) * 128], rhs=hnT[:, k, N * 512:(N + 1) * 512],
                                                  start=(k == 0), stop=(k == 7)),
                         reads=[kx[k // 4], ("r_hnT", k)], writes=[("ps", bx_)])
                P.op("dve", lambda e: e.tensor_copy(out=xr[:, bs, 3 + n * 512:3 + (n + 1) * 512], in_=pv2),
                     reads=[("ps", bx_)], writes=[("r_xr", bs)])

        def stage2(i):
            cb, th = iters[i]
            bs = i % 2
            xk = ("r_xr", bs)
            P.op("act", lambda e: e.activation(out=xc[:, :], in_=xr[:, bs, 3:3 + TH], func=AF.Identity, scale=rv[:, cb, 3:4], bias=rv[:, cb, 4:5]),
                 reads=[xk, "r_rv"], writes=["r_xc"])
            for w in range(3):
                P.op("dve", lambda e: e.scalar_tensor_tensor(out=xc[:, :], in0=xr[:, bs, w:w + TH], scalar=rv[:, cb, w:w + 1], in1=xc[:, :],
                                                             op0=ALU.mult, op1=ALU.add),
                     reads=[xk, "r_rv", "r_xc"], writes=["r_xc"])
            P.op("act", lambda e: e.copy(out=xcb[:, :], in_=xc[:, :]), reads=["r_xc"], writes=["r_xcb"])
            for n in range(4):
                b1 = 4 + n % 2
                pv = psf(C, b1)
                P.op("pe", lambda e: e.matmul(pv, lhsT=wxs[:, cb, :], rhs=xcb[:, n * 512:(n + 1) * 512], start=True, stop=True),
                     reads=["r_wxs", "r_xcb"], writes=[("ps", b1)])
                P.op("act", lambda e: e.activation(out=gx[:, n * 512:(n + 1) * 512], in_=pv, func=AF.Sigmoid, bias=rv[:, cb, 5:6]),
                     reads=[("ps", b1), "r_rv"], writes=["r_gx"])
                b2 = 6 + n % 2
                pv2 = psf(C, b2)
                P.op("pe", lambda e: e.matmul(pv2, lhsT=was[:, cb, :], rhs=xcb[:, n * 512:(n + 1) * 512], start=True, stop=True),
                     reads=["r_was", "r_xcb"], writes=[("ps", b2)])
                P.op("act", lambda e: e.activation(out=ga[:, n * 512:(n + 1) * 512], in_=pv2, func=AF.Sigmoid, bias=rv[:, cb, 6:7]),
                     reads=[("ps", b2), "r_rv"], writes=["r_ga"])
            P.op("act", lambda e: e.activation(out=ga[:, :], in_=ga[:, :], func=AF.Exp, scale=sc[:, cb:cb + 1]), reads=["r_ga", "r_sc"], writes=["r_ga"])
            P.op("dve", lambda e: e.tensor_tensor(out=tt[:, :], in0=ga[:, :], in1=ga[:, :], op=ALU.mult), reads=["r_ga"], writes=["r_tt"])
            P.op("act", lambda e: e.activation(out=tt[:, :], in_=tt[:, :], func=AF.Sqrt, scale=-1.0, bias=1.0), reads=["r_tt"], writes=["r_tt"])
            P.op("dve", lambda e: e.tensor_tensor(out=gx[:, :], in0=gx[:, :], in1=xc[:, :], op=ALU.mult), reads=["r_gx", "r_xc"], writes=["r_gx"])
            P.op("dve", lambda e: e.tensor_tensor(out=tt[:, :], in0=tt[:, :], in1=gx[:, :], op=ALU.mult), reads=["r_tt", "r_gx"], writes=["r_tt"])
            if th == 0:
                P.op("dve", lambda e: e.tensor_copy(out=tt[:, 0:1], in_=gx[:, 0:1]), reads=["r_gx", "r_tt"], writes=["r_tt"])
            init = 0.0 if th == 0 else hlast[:, 0:1]
            P.op("dve", lambda e: e.tensor_tensor_scan(out=hs[:, :], data0=ga[:, :], data1=tt[:, :], initial=init, op0=ALU.mult, op1=ALU.add),
                 reads=["r_ga", "r_tt", "r_hlast"], writes=["r_hs"])
            P.op("dve", lambda e: e.tensor_copy(out=hlast[:, 0:1], in_=hs[:, TH - 1:TH]), reads=["r_hs"], writes=["r_hlast"])
            P.op("dve", lambda e: e.tensor_tensor(out=gout[:, :], in0=hs[:, :], in1=ybf[:, bs, :], op=ALU.mult),
                 reads=["r_hs", ("r_y", bs)], writes=["r_gout"])
            P.dma("sp", G[cb * 128:(cb + 1) * 128, th * TH:(th + 1) * TH], gout[:, :], reads=["r_gout"], writes=[("G", cb, th)])

        stage1(0)
        for i in range(len(iters)):
            if i + 1 < len(iters):
                stage1(i + 1)
            stage2(i)


def build_phase_B():
    nc = bass.Bass("TRN2", target_bir_lowering=False)
    T = {}

    def din(name, shape, dt=F32):
        T[name] = nc.dram_tensor(name, shape, dt, kind="ExternalInput").ap()

    din("hnT", [D, S], BF16); din("w_g", [D, 640]); din("w_x", [D, 640]); din("wx", [5, 128, 128]); din("wa", [5, 128, 128])
    din("rvec", [128, 5, 8]); din("ident", [128, 128])
    G = nc.dram_tensor("G", [640, S], BF16, kind="ExternalOutput").ap()
    with contextlib.ExitStack() as es:
        P = Prog(nc, es)
        C = Ctx()
        setup_common(nc, P, es, C, T["ident"])
        setup_eps(C, es)
        emit_rec(C, T, G)
        P.finish()
    return nc


def emit_recout(C, T, H, row=0, blend=None):
    nc, P = C.nc, C.P
    with scope(P) as es:
        g = load_gains(C, es, T["vec"], [row], "gro")
        gT = es.enter_context(nc.sbuf_tensor("gTs", [128, 10, NOWN], BF16))
        wro = es.enter_context(nc.sbuf_tensor("wro", [128, 10, D], BF16))
        if blend is None:
            gv = T["gT"].rearrange("(c p) t -> p c t", p=128)
            for c in range(10):
                P.dma("sp", gT[:, c, :], gv[:, c, :], writes=[("gTs", c)])
        else:
            Gd, H1d, sel_d = blend
            sel = es.enter_context(nc.sbuf_tensor("sel_sb", [128, 2], F32))
            P.dma("sp", sel[:], sel_d, writes=["sel"])
            gT2 = es.enter_context(nc.sbuf_tensor("gTs2", [128, 2, NOWN], BF16))
            hres2 = es.enter_context(nc.sbuf_tensor("hres2", [128, 2, D], F32))
            gv = Gd.rearrange("(c p) t -> p c t", p=128)
            for c in range(10):
                s2 = c % 2
                P.dma("sp", gT[:, c, :], gv[:, c, 0:NOWN], writes=[("gTs", c)])
                P.dma("sp", gT2[:, s2, :], gv[:, c, NOWN:2 * NOWN], writes=[("gTs2", s2)])
                P.op("act", lambda e: e.activation(out=gT[:, c, :], in_=gT[:, c, :], func=AF.Copy, scale=sel[:, 0:1]),
                     reads=[("gTs", c), "sel"], writes=[("gTs", c)])
                P.op("dve", lambda e: e.scalar_tensor_tensor(out=gT[:, c, :], in0=gT2[:, s2, :], scalar=sel[:, 1:2], in1=gT[:, c, :],
                                                             op0=ALU.mult, op1=ALU.add),
                     reads=[("gTs", c), ("gTs2", s2), "sel"], writes=[("gTs", c)])
        kw = load_w_bf16(C, wro, T["w_ro"], 10, "wro", nsplit=2)
        hres = es.enter_context(nc.sbuf_tensor("hres", [128, 2, D], F32))
        hh = es.enter_context(nc.sbuf_tensor("hh1", [128, 2, D], F32))
        for lt in range(16):
            sl = lt % 2
            if blend is None:
                P.dma("sp", hres[:, sl, :], T["h1"][lt * 128:(lt + 1) * 128, :], writes=[("hres", sl)])
            else:
                P.dma("sp", hres[:, sl, :], H1d[lt * 128:(lt + 1) * 128, :], writes=[("hres", sl)])
                P.dma("sp", hres2[:, sl, :], H1d[NOWN + lt * 128:NOWN + (lt + 1) * 128, :], writes=[("hres2", sl)])
                P.op("act", lambda e: e.activation(out=hres[:, sl, :], in_=hres[:, sl, :], func=AF.Copy, scale=sel[:, 0:1]),
                     reads=[("hres", sl), "sel"], writes=[("hres", sl)])
                P.op("dve", lambda e: e.scalar_tensor_tensor(out=hres[:, sl, :], in0=hres2[:, sl, :], scalar=sel[:, 1:2], in1=hres[:, sl, :],
                                                             op0=ALU.mult, op1=ALU.add),
                     reads=[("hres", sl), ("hres2", sl), "sel"], writes=[("hres", sl)])
            for hf in range(2):
                pk = ("ps", 2 * sl + hf)
                pv = psf(C, 2 * sl + hf)
                for c in range(10):
                    P.op("pe", lambda e: e.matmul(pv, lhsT=gT[:, c, lt * 128:(lt + 1) * 128], rhs=wro[:, c, hf * 512:(hf + 1) * 512],
                                                  start=(c == 0), stop=(c == 9)),
                         reads=[("gTs", c), kw[c // 5]], writes=[pk])
            emit_norm_residual(C, C.ps[sl][:, :], [("ps", 2 * sl), ("ps", 2 * sl + 1)], g[:, 0, :], ("gro", 0),
                               hres[:, sl, :], [("hres", sl)], hh[:, sl, :], [("hh1", sl)])
            P.dma("sp", H[lt * 128:(lt + 1) * 128, :], hh[:, sl, :], reads=[("hh1", sl)], writes=[("H", lt)])


def build_phase_C():
    nc = bass.Bass("TRN2", target_bir_lowering=False)
    T = {}

    def din(name, shape, dt=F32):
        T[name] = nc.dram_tensor(name, shape, dt, kind="ExternalInput").ap()

    din("gT", [RW, NOWN], BF16); din("h1", [NOWN, D]); din("pT", [256, NOWN]); din("w_ro", [RW, D])
    din("w_up", [D, DFF]); din("w_dn", [DFF, D]); din("w_pp", [256, D]); din("w_pg", [D, D]); din("vec", [8, D]); din("ident", [128, 128])
    H = nc.dram_tensor("H", [NOWN, D], F32, kind="ExternalOutput").ap()
    OUT = nc.dram_tensor("OUT", [NOWN, D], F32, kind="ExternalOutput").ap()
    with contextlib.ExitStack() as es:
        P = Prog(nc, es)
        C = Ctx()
        setup_common(nc, P, es, C, T["ident"])
        setup_eps(C, es)
        emit_recout(C, T, H)
        emit_mlp(C, H, T["w_up"], T["w_dn"], T["vec"], 1, 2, "1")
        emit_ple(C, H, T["pT"], T["w_pp"], T["w_pg"], T["vec"], 3, OUT, "1")
        P.finish()
    return nc


def build_fused():
    nc = bass.Bass("TRN2", target_bir_lowering=False)
    T = {}

    def din(name, shape, dt=F32):
        T[name] = nc.dram_tensor(name, shape, dt, kind="ExternalInput").ap()

    din("xf", [S, D]); din("pT", [256, S]); din("pT1", [256, NOWN])
    din("w_in", [D, 3080]); din("w_out", [D, D]); din("w_up", [D, DFF]); din("w_dn", [DFF, D])
    din("w_pp", [256, D]); din("w_pg", [D, D]); din("vec", [16, D]); din("lamv", [1, 256]); din("bf", [8, 1])
    din("ident", [128, 128]); din("masks", [2, 128, 256]); din("alibi", [128, 128]); din("sel", [128, 2])
    din("w_g", [D, RW]); din("w_x", [D, RW]); din("wx", [10, 128, 128]); din("wa", [10, 128, 128]); din("rvec", [128, 10, 8])
    din("w_ro", [RW, D]); din("w_up1", [D, DFF]); din("w_dn1", [DFF, D]); din("w_pp1", [256, D]); din("w_pg1", [D, D])
    HA = nc.dram_tensor("HA", [S, D], F32, kind="Internal").ap()
    H1 = nc.dram_tensor("H1", [S, D], F32, kind="Internal").ap()
    HNT = nc.dram_tensor("HNT", [D, S], BF16, kind="Internal").ap()
    G = nc.dram_tensor("G", [RW, S], BF16, kind="Internal").ap()
    HC = nc.dram_tensor("HC", [NOWN, D], F32, kind="Internal").ap()
    OUT = nc.dram_tensor("OUT", [NOWN, D], F32, kind="ExternalOutput").ap()
    with contextlib.ExitStack() as es:
        P = Prog(nc, es)
        C = Ctx()
        setup_common(nc, P, es, C, T["ident"])
        setup_eps(C, es)
        emit_attention(C, T, HA, full=True)
        emit_mlp(C, HA, T["w_up"], T["w_dn"], T["vec"], 2, 3, "0", ntok=S)
        emit_ple(C, HA, T["pT"], T["w_pp"], T["w_pg"], T["vec"], 4, H1, "0", hn_row=5, HNT=HNT, ntiles=32)
        T1 = {"hnT": HNT, "w_g": T["w_g"], "w_x": T["w_x"], "wx": T["wx"], "wa": T["wa"], "rvec": T["rvec"]}
        emit_rec(C, T1, G, ncb=10)
        T2 = {"vec": T["vec"], "w_ro": T["w_ro"]}
        emit_recout(C, T2, HC, row=8, blend=(G, H1, T["sel"]))
        emit_mlp(C, HC, T["w_up1"], T["w_dn1"], T["vec"], 9, 10, "1", ntok=NOWN)
        emit_ple(C, HC, T["pT1"], T["w_pp1"], T["w_pg1"], T["vec"], 11, OUT, "1", ntiles=16)
        P.finish()
    return nc


def own_tiles(r):
    return [4 * p + 2 * r + e for p in range(8) for e in range(2)]


def own_index(r):
    return np.concatenate([np.arange(g * 128, (g + 1) * 128) for g in own_tiles(r)])


def role_consts(r):
    ident = np.eye(128, dtype=np.float32)
    masks = np.zeros((4, 128, 256), np.float32)
    jj = np.arange(128)[:, None]
    ii = np.arange(128)[None, :]
    tri = (jj <= ii).astype(np.float32)
    for m in range(4):
        for e in range(2):
            qt = 2 * r + e
            if m < qt:
                masks[m, :, e * 128:(e + 1) * 128] = 1.0
            elif m == qt:
                masks[m, :, e * 128:(e + 1) * 128] = tri
    alibi = np.zeros((128, 4, 32), np.float32)
    for h in range(4):
        for idx in range(32):
            d = (idx - 28 - 2 * r - 2) * 128 + np.arange(128)
            alibi[:, h, idx] = np.minimum(SLOPES[h] * d, 0.0)
    return ident, masks, alibi.reshape(128, 128)


def f32c(a):
    return np.ascontiguousarray(a, dtype=np.float32)


def prep_A(inp, c):
    b, r = c // 2, c % 2
    oi = own_index(r)
    ident, masks, alibi = role_consts(r)
    vec = np.zeros((8, D), np.float32)
    vec[0] = inp["ln_mix_pre"][0]
    vec[1] = inp["ln_mix_post"][0]
    vec[2] = inp["ln_mlp_pre"][0]
    vec[3] = inp["ln_mlp_post"][0]
    vec[4] = inp["ple_norm"][0]
    vec[5] = inp["ln_mix_pre"][1]
    vec[6, :128] = inp["diff_subln"][0]
    lamv = np.concatenate([inp["diff_lambda_q1"][0], inp["diff_lambda_k1"][0],
                           inp["diff_lambda_q2"][0], inp["diff_lambda_k2"][0]])[None, :]
    return {
        "xf": f32c(inp["x"][b]), "xo": f32c(inp["x"][b][oi]), "pT": f32c(inp["p"][0, b][oi].T),
        "w_in": f32c(inp["attn_w_in"][0]), "w_out": f32c(inp["attn_w_out"][0]),
        "w_up": f32c(inp["mlp_w_up"][0]), "w_dn": f32c(inp["mlp_w_down"][0]),
        "w_pp": f32c(inp["ple_w_proj"][0]), "w_pg": f32c(inp["ple_w_gate"][0]),
        "vec": vec, "lamv": f32c(lamv), "bf": f32c(inp["attn_b_forget"][0][:, None]),
        "ident": ident, "masks": masks, "alibi": alibi,
    }


def prep_B(inp, c, hnT_full):
    r = c % 2
    cols_g = np.arange(5 * r * 128, (5 * r + 5) * 128)
    cols_x = RW + cols_g
    w_in = inp["rec_w_in"][0]
    rvec = np.zeros((128, 5, 8), np.float32)
    for j in range(5):
        ch = np.arange((5 * r + j) * 128, (5 * r + j + 1) * 128)
        rvec[:, j, 0:4] = inp["rec_conv_w"][0][:, ch].T
        rvec[:, j, 4] = inp["rec_conv_b"][0][ch]
        rvec[:, j, 5] = inp["rec_bx"][0][ch]
        rvec[:, j, 6] = inp["rec_ba"][0][ch]
        rvec[:, j, 7] = inp["rec_a_param"][0][ch]
    return {
        "hnT": hnT_full, "w_g": f32c(w_in[:, cols_g]), "w_x": f32c(w_in[:, cols_x]),
        "wx": f32c(inp["rec_wx"][0][5 * r:5 * r + 5]), "wa": f32c(inp["rec_wa"][0][5 * r:5 * r + 5]),
        "rvec": rvec, "ident": np.eye(128, dtype=np.float32),
    }


def prep_C(inp, c, gT_own, h1_own):
    b, r = c // 2, c % 2
    oi = own_index(r)
    vec = np.zeros((8, D), np.float32)
    vec[0] = inp["ln_mix_post"][1]
    vec[1] = inp["ln_mlp_pre"][1]
    vec[2] = inp["ln_mlp_post"][1]
    vec[3] = inp["ple_norm"][1]
    return {
        "gT": gT_own, "h1": h1_own, "pT": f32c(inp["p"][1, b][oi].T), "w_ro": f32c(inp["rec_w_out"][0]),
        "w_up": f32c(inp["mlp_w_up"][1]), "w_dn": f32c(inp["mlp_w_down"][1]),
        "w_pp": f32c(inp["ple_w_proj"][1]), "w_pg": f32c(inp["ple_w_gate"][1]),
        "vec": vec, "ident": np.eye(128, dtype=np.float32),
    }


def full_consts():
    ident = np.eye(128, dtype=np.float32)
    jj = np.arange(128)[:, None]
    ii = np.arange(128)[None, :]
    tri = (jj <= ii).astype(np.float32)
    masks = np.zeros((2, 128, 256), np.float32)
    masks[0, :, 0:128] = tri
    masks[0, :, 128:256] = 1.0
    masks[1, :, 128:256] = tri
    alibi = np.zeros((128, 4, 32), np.float32)
    for h in range(4):
        for idx in range(32):
            d = (idx - 30 - 2) * 128 + np.arange(128)
            alibi[:, h, idx] = np.minimum(SLOPES[h] * d, 0.0)
    return ident, masks, alibi.reshape(128, 128)


def prep_fused(inp, c):
    b, r = c // 2, c % 2
    ident, masks, alibi = full_consts()
    vec = np.zeros((16, D), np.float32)
    vec[0] = inp["ln_mix_pre"][0]
    vec[1] = inp["ln_mix_post"][0]
    vec[2] = inp["ln_mlp_pre"][0]
    vec[3] = inp["ln_mlp_post"][0]
    vec[4] = inp["ple_norm"][0]
    vec[5] = inp["ln_mix_pre"][1]
    vec[6, :128] = inp["diff_subln"][0]
    vec[8] = inp["ln_mix_post"][1]
    vec[9] = inp["ln_mlp_pre"][1]
    vec[10] = inp["ln_mlp_post"][1]
    vec[11] = inp["ple_norm"][1]
    lamv = np.concatenate([inp["diff_lambda_q1"][0], inp["diff_lambda_k1"][0],
                           inp["diff_lambda_q2"][0], inp["diff_lambda_k2"][0]])[None, :]
    rvec = np.zeros((128, 10, 8), np.float32)
    for j in range(10):
        ch = np.arange(j * 128, (j + 1) * 128)
        rvec[:, j, 0:4] = inp["rec_conv_w"][0][:, ch].T
        rvec[:, j, 4] = inp["rec_conv_b"][0][ch]
        rvec[:, j, 5] = inp["rec_bx"][0][ch]
        rvec[:, j, 6] = inp["rec_ba"][0][ch]
        rvec[:, j, 7] = inp["rec_a_param"][0][ch]
    sel = np.zeros((128, 2), np.float32)
    sel[:, r] = 1.0
    w_in1 = inp["rec_w_in"][0]
    return {
        "xf": f32c(inp["x"][b]), "pT": f32c(inp["p"][0, b].T), "pT1": f32c(inp["p"][1, b][r * NOWN:(r + 1) * NOWN].T),
        "w_in": f32c(inp["attn_w_in"][0]), "w_out": f32c(inp["attn_w_out"][0]),
        "w_up": f32c(inp["mlp_w_up"][0]), "w_dn": f32c(inp["mlp_w_down"][0]),
        "w_pp": f32c(inp["ple_w_proj"][0]), "w_pg": f32c(inp["ple_w_gate"][0]),
        "vec": vec, "lamv": f32c(lamv), "bf": f32c(inp["attn_b_forget"][0][:, None]),
        "ident": ident, "masks": masks, "alibi": alibi, "sel": sel,
        "w_g": f32c(w_in1[:, :RW]), "w_x": f32c(w_in1[:, RW:]), "wx": f32c(inp["rec_wx"][0]), "wa": f32c(inp["rec_wa"][0]),
        "rvec": rvec, "w_ro": f32c(inp["rec_w_out"][0]),
        "w_up1": f32c(inp["mlp_w_up"][1]), "w_dn1": f32c(inp["mlp_w_down"][1]),
        "w_pp1": f32c(inp["ple_w_proj"][1]), "w_pg1": f32c(inp["ple_w_gate"][1]),
    }


_NC_CACHE = {}


def _get(name, fn):
    if name not in _NC_CACHE:
        _NC_CACHE[name] = fn()
    return _NC_CACHE[name]


def kernel_unfused(**inputs):
    inp = {k: np.asarray(v) for k, v in inputs.items()}
    cores = list(range(NCORES))
    ncA = _get("A", build_phase_A)
    resA = run_bass_kernel_spmd(ncA, [prep_A(inp, c) for c in cores], core_ids=cores).results
    hnT_full = []
    for b in range(4):
        full = np.zeros((D, S), dtype=resA[0]["HNT"].dtype)
        for r in range(2):
            full[:, own_index(r)] = resA[2 * b + r]["HNT"]
        hnT_full.append(full)
    ncB = _get("B", build_phase_B)
    resB = run_bass_kernel_spmd(ncB, [prep_B(inp, c, hnT_full[c // 2]) for c in cores], core_ids=cores).results
    ncC = _get("C", build_phase_C)
    mapsC = []
    for c in cores:
        b, r = c // 2, c % 2
        oi = own_index(r)
        gfull = np.concatenate([resB[2 * b]["G"], resB[2 * b + 1]["G"]], axis=0)
        mapsC.append(prep_C(inp, c, np.ascontiguousarray(gfull[:, oi]), resA[c]["H1"]))
    resC = run_bass_kernel_spmd(ncC, mapsC, core_ids=cores).results
    out = np.zeros((4, S, D), np.float32)
    for c in cores:
        b, r = c // 2, c % 2
        out[b, own_index(r)] = resC[c]["OUT"]
    return out


def kernel(**inputs):
    inp = {k: np.asarray(v) for k, v in inputs.items()}
    cores = list(range(NCORES))
    nc = _get("F", build_fused)
    res = run_bass_kernel_spmd(nc, [prep_fused(inp, c) for c in cores], core_ids=cores).results
    out = np.zeros((4, S, D), np.float32)
    for c in cores:
        b, r = c // 2, c % 2
        out[b, r * NOWN:(r + 1) * NOWN] = res[c]["OUT"]
    return out
```

```python
import contextlib
import numpy as np
import concourse.bass as bass
import concourse.mybir as mybir
from concourse.bass_utils import run_bass_kernel_spmd

F32 = mybir.dt.float32
BF16 = mybir.dt.bfloat16
AF = mybir.ActivationFunctionType
ALU = mybir.AluOpType
AX = mybir.AxisListType

NCORES = 8
D = 1024
S = 4096
NOWN = 2048
DFF = 4096
RW = 1280
LAM_INIT0 = 0.8 - 0.6 * 1.0
SLOPES = [2.0 ** (-8.0 * (h + 1) / 4) for h in range(4)]


class Prog:
    K_DMA = 8

    def __init__(self, nc, es):
        self.nc = nc
        self.engs = {"pe": nc.tensor, "act": nc.scalar, "dve": nc.vector,
                     "pool": nc.gpsimd, "sp": nc.sync}
        self.semobj = {}
        for k in self.engs:
            self.semobj[k] = es.enter_context(nc.semaphore("sem_" + k))
        self.cnt = {k: 0 for k in self.engs}
        self.seen = {k: {} for k in self.engs}
        self.lastw = {}
        self.readers = {}
        self.dcnt = {}
        for q in ("sp", "act", "pool"):
            self.dcnt[q] = 0
            for i in range(self.K_DMA):
                self.semobj[(q, i)] = es.enter_context(nc.semaphore("d_%s_%d" % (q, i)))

    def _wait(self, eng, ev):
        sk, val = ev
        if self.seen[eng].get(sk, 0) >= val:
            return
        self.engs[eng].wait_ge(self.semobj[sk], val)
        self.seen[eng][sk] = val

    def _deps(self, eng, reads, writes):
        deps = {}
        for k in reads:
            w = self.lastw.get(k)
            if w is not None:
                deps[w[0]] = max(deps.get(w[0], 0), w[1])
        for k in writes:
            w = self.lastw.get(k)
            if w is not None:
                deps[w[0]] = max(deps.get(w[0], 0), w[1])
            for sk, v in self.readers.get(k, {}).items():
                deps[sk] = max(deps.get(sk, 0), v)
        for sk, v in deps.items():
            if eng == "pe" and sk == "pe":
                continue
            self._wait(eng, (sk, v))

    def _record(self, ev, reads, writes):
        for k in writes:
            self.lastw[k] = ev
            self.readers[k] = {}
        for k in reads:
            if k in writes:
                continue
            d = self.readers.setdefault(k, {})
            d[ev[0]] = max(d.get(ev[0], 0), ev[1])

    def op(self, eng, fn, reads=(), writes=()):
        self._deps(eng, reads, writes)
        inst = fn(self.engs[eng])
        self.cnt[eng] += 1
        inst.then_inc(self.semobj[eng], 1)
        self._record((eng, self.cnt[eng]), reads, writes)

    def dma(self, q, out, in_, reads=(), writes=()):
        self._deps(q, reads, writes)
        i = self.dcnt[q]
        s = i % self.K_DMA
        rnd = i // self.K_DMA
        if rnd > 0:
            self._wait(q, ((q, s), 16 * rnd))
        inst = self.engs[q].dma_start(out=out, in_=in_)
        inst.then_inc(self.semobj[(q, s)], 16)
        self.dcnt[q] = i + 1
        self._record(((q, s), 16 * (rnd + 1)), reads, writes)

    def barrier(self):
        evs = []
        for q in ("sp", "act", "pool"):
            n = self.dcnt[q]
            for s in range(self.K_DMA):
                m = (n - s + self.K_DMA - 1) // self.K_DMA if n > s else 0
                if m > 0:
                    evs.append(((q, s), 16 * m))
        for k in ("pe", "act", "dve", "pool", "sp"):
            if self.cnt[k] > 0:
                evs.append((k, self.cnt[k]))
        for eng in ("pe", "act", "dve", "pool", "sp"):
            for ev in evs:
                if ev[0] == eng:
                    continue
                self._wait(eng, ev)

    def barrier_keys(self, eng, keys):
        self._deps(eng, [], list(keys))

    def finish(self):
        for q in ("sp", "act", "pool"):
            n = self.dcnt[q]
            for s in range(self.K_DMA):
                m = (n - s + self.K_DMA - 1) // self.K_DMA if n > s else 0
                if m > 0:
                    self._wait("sp", ((q, s), 16 * m))
        for k in ("pe", "act", "dve", "pool"):
            if self.cnt[k] > 0:
                self._wait("sp", (k, self.cnt[k]))


class Ctx:
    pass


@contextlib.contextmanager
def scope(P):
    with contextlib.ExitStack() as es:
        yield es
        P.barrier()


def setup_common(nc, P, es, C, ident_d):
    C.nc = nc
    C.P = P
    C.identf = es.enter_context(nc.sbuf_tensor("identf", [128, 128], F32))
    C.identb = es.enter_context(nc.sbuf_tensor("identb", [128, 128], BF16))
    C.onesf = es.enter_context(nc.sbuf_tensor("onesf", [128, 512], F32))
    P.dma("sp", C.identf[:], ident_d, writes=["identf"])
    P.dma("pool", C.identb[:], ident_d, writes=["identb"])
    P.op("pool", lambda e: e.memset(C.onesf[:], 1.0), writes=["onesf"])
    C.ps = [es.enter_context(nc.psum_tensor("ps%d" % i, [128, 1024], F32)) for i in range(4)]
    C.psb = [t.bitcast(BF16) for t in C.ps]
    C.junk = es.enter_context(nc.sbuf_tensor("junk", [128, 1024], BF16))
    C.junk2 = es.enter_context(nc.sbuf_tensor("junk2", [128, 128], BF16))
    C.small = es.enter_context(nc.sbuf_tensor("small", [128, 64], F32))
    C.small_i = 0


def small_slot(C, n=1):
    i = C.small_i
    if i + n > 64:
        i = 0
    C.small_i = i + n
    return i


def psf(C, b):
    return C.ps[b // 2][:, (b % 2) * 512:(b % 2) * 512 + 512]


def psbf(C, b):
    return C.psb[b // 2][:, (b % 2) * 1024:(b % 2) * 1024 + 1024]


def rstd_from(C, src, src_keys, n, eps, eng="act"):
    P = C.P
    i = small_slot(C)
    ss = C.small[:, i:i + 1]
    key = ("small", i)
    if eng == "dve":
        P.op("dve", lambda e: e.scalar_tensor_tensor(out=C.junk2[:, 0:n], in0=src, scalar=1.0, in1=src,
                                                     op0=ALU.mult, op1=ALU.mult, accum_out=ss),
             reads=list(src_keys), writes=["junk2", key])
    else:
        P.op("act", lambda e: e.activation(out=C.junk[:, 0:n], in_=src, func=AF.Square, accum_out=ss),
             reads=list(src_keys), writes=["junk", key])
    P.op("dve", lambda e: e.tensor_scalar(out=ss, in0=ss, scalar1=1.0 / n, scalar2=float(eps), op0=ALU.mult, op1=ALU.add),
         reads=[key], writes=[key])
    P.op("pool", lambda e: e.tensor_tensor(out=ss, in0=ss, in1=C.negh, op=ALU.pow), reads=[key, "epsb"], writes=[key])
    return ss, key


def transpose_to(C, src_bf, src_keys, dst, dst_keys, bank, nk=8, evac="dve"):
    P = C.P
    pk = ("ps", bank)
    pv = psbf(C, bank)
    for k in range(nk):
        P.op("pe", lambda e, k=k: e.transpose(out=pv[:, k * 128:(k + 1) * 128], in_=src_bf[:, k * 128:(k + 1) * 128],
                                              identity=C.identb[:]),
             reads=list(src_keys) + ["identb"], writes=[pk])
    srcv = pv[:, 0:nk * 128].rearrange("p (k t) -> p k t", k=nk)
    if evac == "act":
        P.op("act", lambda e: e.copy(out=dst, in_=srcv), reads=[pk], writes=list(dst_keys))
    else:
        P.op("dve", lambda e: e.tensor_copy(out=dst, in_=srcv), reads=[pk], writes=list(dst_keys))


def load_w_bf16(C, dst, dram2d, nk, key, nsplit=1):
    P = C.P
    src = dram2d.rearrange("(k p) n -> p k n", p=128)
    step = nk // nsplit
    for i in range(nsplit):
        P.dma("pool", dst[:, i * step:(i + 1) * step, :], src[:, i * step:(i + 1) * step, :], writes=[(key, i)])
    return [(key, i) for i in range(nsplit)]


def emit_norm_residual(C, m_src, m_keys, gain, gain_key, res, res_keys, out, out_keys):
    P = C.P
    rs, rk = rstd_from(C, m_src, m_keys, D, 1e-6)
    P.op("dve", lambda e: e.scalar_tensor_tensor(out=out, in0=m_src, scalar=rs, in1=gain, op0=ALU.mult, op1=ALU.mult),
         reads=list(m_keys) + [rk, gain_key], writes=list(out_keys))
    P.op("pool", lambda e: e.tensor_tensor(out=out, in0=out, in1=res, op=ALU.add),
         reads=list(out_keys) + list(res_keys), writes=list(out_keys))


def emit_norm_bf16(C, src, src_keys, gain, gain_key, out_bf, out_keys):
    P = C.P
    rs, rk = rstd_from(C, src, src_keys, D, 1e-6)
    P.op("dve", lambda e: e.scalar_tensor_tensor(out=out_bf, in0=src, scalar=rs, in1=gain, op0=ALU.mult, op1=ALU.mult),
         reads=list(src_keys) + [rk, gain_key], writes=list(out_keys))


def load_gains(C, es, vec_d, rows, name):
    nc, P = C.nc, C.P
    g = es.enter_context(nc.sbuf_tensor(name, [128, len(rows), D], F32))
    for i, r in enumerate(rows):
        P.dma("sp", g[:, i, :], vec_d[r, :].partition_broadcast(128), writes=[(name, i)])
    return g


def setup_eps(C, es):
    nc, P = C.nc, C.P
    t = es.enter_context(nc.sbuf_tensor("epsb", [128, 3], F32))
    P.op("pool", lambda e: e.memset(t[:, 2:3], -0.5), writes=["epsb"])
    C.negh = t[:, 2:3]
    P.op("pool", lambda e: e.memset(t[:, 0:1], 1e-6), writes=["epsb"])
    P.op("pool", lambda e: e.memset(t[:, 1:2], 1e-5), writes=["epsb"])
    C.epsb = {1e-6: t[:, 0:1], 1e-5: t[:, 1:2]}


def emit_mlp(C, H, w_up_d, w_dn_d, vec_d, row_pre, row_post, tag, ntok=NOWN):
    nc, P = C.nc, C.P
    MT = 256
    with scope(P) as es:
        wup = es.enter_context(nc.sbuf_tensor("wup" + tag, [128, 8, DFF], BF16))
        wdn = es.enter_context(nc.sbuf_tensor("wdn" + tag, [128, 32, D], BF16))
        kup = load_w_bf16(C, wup, w_up_d, 8, "wup" + tag, nsplit=8)
        kdn = load_w_bf16(C, wdn, w_dn_d, 32, "wdn" + tag, nsplit=8)
        g = load_gains(C, es, vec_d, [row_pre, row_post], "gm" + tag)
        ha = es.enter_context(nc.sbuf_tensor("ha" + tag, [128, 2, 2, D], F32))
        ubf = es.enter_context(nc.sbuf_tensor("ubf" + tag, [128, 2, D], BF16))
        uT = es.enter_context(nc.sbuf_tensor("uT" + tag, [128, 2, 8, MT], BF16))
        hid = es.enter_context(nc.sbuf_tensor("hid" + tag, [128, 32, MT], BF16))
        rl = es.enter_context(nc.sbuf_tensor("rl" + tag, [128, 2, MT], F32))
        hb = es.enter_context(nc.sbuf_tensor("hb" + tag, [128, 2, D], F32))
        nmt = ntok // MT

        def prologue(t):
            sl = t % 2
            for e in range(2):
                lt = 2 * t + e
                P.dma("sp", ha[:, sl, e, :], H[lt * 128:(lt + 1) * 128, :], reads=[("H", lt)], writes=[("ha", sl, e)])
                emit_norm_bf16(C, ha[:, sl, e, :], [("ha", sl, e)], g[:, 0, :], ("gm" + tag, 0), ubf[:, e, :], [("ubf", e)])
                transpose_to(C, ubf[:, e, :], [("ubf", e)], uT[:, sl, :, e * 128:(e + 1) * 128], [("uT", sl, e)],
                             bank=e, evac="act")

        def up(t):
            sl = t % 2
            for c in range(32):
                bank = c % 4
                pk = ("ps", bank)
                pv = psf(C, bank)[:, 0:MT]
                for k in range(8):
                    P.op("pe", lambda e: e.matmul(pv, lhsT=wup[:, k, c * 128:(c + 1) * 128], rhs=uT[:, sl, k, :],
                                                  start=(k == 0), stop=(k == 7)),
                         reads=[kup[k], ("uT", sl, 0), ("uT", sl, 1)], writes=[pk])
                rs_ = c % 2
                P.op("act", lambda e: e.activation(out=rl[:, rs_, :], in_=pv, func=AF.Relu),
                     reads=[pk], writes=[("rl", rs_)])
                P.op("pool", lambda e: e.tensor_tensor(out=hid[:, c, :], in0=rl[:, rs_, :], in1=rl[:, rs_, :], op=ALU.mult),
                     reads=[("rl", rs_)], writes=[("hid", c)])

        def down(t):
            sl = t % 2
            for e in range(2):
                lt = 2 * t + e
                for hf in range(2):
                    pk = ("ps", 4 + 2 * e + hf)
                    pv = psf(C, 4 + 2 * e + hf)
                    for c in range(32):
                        P.op("pe", lambda e_: e_.matmul(pv, lhsT=hid[:, c, e * 128:(e + 1) * 128],
                                                        rhs=wdn[:, c, hf * 512:(hf + 1) * 512],
                                                        start=(c == 0), stop=(c == 31)),
                             reads=[("hid", c), kdn[c // 4]], writes=[pk])
                fv = C.ps[2 + e][:, :]
                emit_norm_residual(C, fv, [("ps", 4 + 2 * e), ("ps", 5 + 2 * e)], g[:, 1, :], ("gm" + tag, 1),
                                   ha[:, sl, e, :], [("ha", sl, e)], hb[:, e, :], [("hb", e)])
                P.dma("sp", H[lt * 128:(lt + 1) * 128, :], hb[:, e, :], reads=[("hb", e)], writes=[("H", lt)])

        prologue(0)
        for t in range(nmt):
            up(t)
            if t + 1 < nmt:
                prologue(t + 1)
            down(t)


def emit_ple(C, H, pT_d, w_pp_d, w_pg_d, vec_d, row_ple, OUT, tag, hn_row=None, HNT=None, ntiles=16):
    nc, P = C.nc, C.P
    with scope(P) as es:
        wpg = es.enter_context(nc.sbuf_tensor("wpg" + tag, [128, 8, D], BF16))
        wpp = es.enter_context(nc.sbuf_tensor("wpp" + tag, [128, 2, D], BF16))
        kpg = load_w_bf16(C, wpg, w_pg_d, 8, "wpg" + tag, nsplit=2)
        kpp = load_w_bf16(C, wpp, w_pp_d, 2, "wpp" + tag, nsplit=1)
        rows = [row_ple] + ([hn_row] if hn_row is not None else [])
        g = load_gains(C, es, vec_d, rows, "gp" + tag)
        hb = es.enter_context(nc.sbuf_tensor("phb" + tag, [128, 3, D], F32))
        hbb = es.enter_context(nc.sbuf_tensor("phbb" + tag, [128, 2, D], BF16))
        hbT = es.enter_context(nc.sbuf_tensor("phbT" + tag, [128, 2, 8, 128], BF16))
        pT = es.enter_context(nc.sbuf_tensor("ppT" + tag, [128, 3, 2, 128], BF16))
        sg = es.enter_context(nc.sbuf_tensor("psg" + tag, [128, 2, D], F32))
        ee = es.enter_context(nc.sbuf_tensor("pee" + tag, [128, 2, D], F32))
        hnb = es.enter_context(nc.sbuf_tensor("phnb" + tag, [128, 2, D], BF16))
        hnT = es.enter_context(nc.sbuf_tensor("phnT" + tag, [128, 2, 8, 128], BF16))
        pTv = pT_d.rearrange("(k f) t -> f k t", f=128)
        def loads(lt):
            s3 = lt % 3
            P.dma("sp", hb[:, s3, :], H[lt * 128:(lt + 1) * 128, :], reads=[("H", lt)], writes=[("phb", s3)])
            P.dma("pool", pT[:, s3, :, :], pTv[:, :, lt * 128:(lt + 1) * 128], writes=[("ppT", s3)])

        def front(lt):
            sl = lt % 2
            s3 = lt % 3
            if lt + 1 < ntiles:
                loads(lt + 1)
            P.op("dve", lambda e: e.tensor_copy(out=hbb[:, sl, :], in_=hb[:, s3, :]), reads=[("phb", s3)], writes=[("phbb", sl)])
            transpose_to(C, hbb[:, sl, :], [("phbb", sl)], hbT[:, sl, :, :], [("phbT", sl)], bank=6, evac="act")
            for hf in range(2):
                pk = ("ps", hf)
                pv = psf(C, hf)
                for k in range(8):
                    P.op("pe", lambda e: e.matmul(pv, lhsT=hbT[:, sl, k, :], rhs=wpg[:, k, hf * 512:(hf + 1) * 512],
                                                  start=(k == 0), stop=(k == 7)),
                         reads=[("phbT", sl), kpg[k // 4]], writes=[pk])
            for hf in range(2):
                pk = ("ps", 2 + 2 * sl + hf)
                pv = psf(C, 2 + 2 * sl + hf)
                for k in range(2):
                    P.op("pe", lambda e: e.matmul(pv, lhsT=pT[:, s3, k, :], rhs=wpp[:, k, hf * 512:(hf + 1) * 512],
                                                  start=(k == 0), stop=(k == 1)),
                         reads=[("ppT", s3), kpp[0]], writes=[pk])

        def front_b(lt):
            sl = lt % 2
            P.op("act", lambda e: e.activation(out=sg[:, sl, :], in_=C.ps[0][:, :], func=AF.Sigmoid),
                 reads=[("ps", 0), ("ps", 1)], writes=[("psg", sl)])

        def back(lt):
            sl = lt % 2
            ev = C.ps[1 + sl][:, :]
            pks = [("ps", 2 + 2 * sl), ("ps", 3 + 2 * sl)]
            rs, rk = rstd_from(C, ev, pks, D, 1e-6)
            P.op("dve", lambda e: e.scalar_tensor_tensor(out=ee[:, sl, :], in0=ev, scalar=rs, in1=g[:, 0, :], op0=ALU.mult, op1=ALU.mult),
                 reads=pks + [rk, ("gp" + tag, 0)], writes=[("pee", sl)])
            P.op("pool", lambda e: e.tensor_tensor(out=ee[:, sl, :], in0=ee[:, sl, :], in1=sg[:, sl, :], op=ALU.mult),
                 reads=[("pee", sl), ("psg", sl)], writes=[("pee", sl)])
            P.op("dve", lambda e: e.tensor_tensor(out=ee[:, sl, :], in0=ee[:, sl, :], in1=hb[:, lt % 3, :], op=ALU.add),
                 reads=[("pee", sl), ("phb", lt % 3)], writes=[("pee", sl)])
            P.dma("sp", OUT[lt * 128:(lt + 1) * 128, :], ee[:, sl, :], reads=[("pee", sl)], writes=[("OUT", lt)])

        def hnpart(lt):
            sl = lt % 2
            emit_norm_bf16(C, ee[:, sl, :], [("pee", sl)], g[:, 1, :], ("gp" + tag, 1), hnb[:, sl, :], [("phnb", sl)])
            transpose_to(C, hnb[:, sl, :], [("phnb", sl)], hnT[:, sl, :, :], [("phnT", sl)], bank=7, evac="dve")
            P.dma("sp", HNT.rearrange("(k f) t -> f k t", f=128)[:, :, lt * 128:(lt + 1) * 128], hnT[:, sl, :, :],
                  reads=[("phnT", sl)], writes=[("HNT", lt)])

        loads(0)
        front(0)
        front_b(0)
        for lt in range(ntiles):
            if lt + 1 < ntiles:
                front(lt + 1)
            back(lt)
            if hn_row is not None and lt >= 1:
                hnpart(lt - 1)
            if lt + 1 < ntiles:
                front_b(lt + 1)
        if hn_row is not None:
            hnpart(ntiles - 1)


def emit_attention(C, T, H, stage=9, full=False):
    nc, P = C.nc, C.P
    with scope(P) as es:
        g = load_gains(C, es, T["vec"], [0, 1], "ga")
        hnTf = es.enter_context(nc.sbuf_tensor("hnTf", [128, 8, S], BF16))
        NT = 32 if full else 16
        NCH = NT // 2
        KPC = 2 if full else 4
        NM = 2 if full else 4
        if full:
            hnTo = hnTf
        else:
            hnTo = es.enter_context(nc.sbuf_tensor("hnTo", [128, 8, NOWN], BF16))
        obuf = es.enter_context(nc.sbuf_tensor("obuf", [128, NT, D], BF16))
        masks = es.enter_context(nc.sbuf_tensor("masks_sb", [128, NM, 256], BF16))
        alibi = es.enter_context(nc.sbuf_tensor("alibi_sb", [128, 4, 32], F32))
        fb = es.enter_context(nc.sbuf_tensor("fb", [128, 2, 2, 32], F32))
        Stm = es.enter_context(nc.sbuf_tensor("Stm", [128, 32, 8], F32))
        Xb = es.enter_context(nc.sbuf_tensor("Xb", [128, NCH * 8], F32))
        neglam = es.enter_context(nc.sbuf_tensor("neglam", [128, 4], F32))
        subg = es.enter_context(nc.sbuf_tensor("subg", [128, 128], F32))
        P.dma("pool", masks[:], T["masks"].rearrange("m p q -> p m q"), writes=["masks"])
        P.dma("sp", alibi[:], T["alibi"].rearrange("p (h r) -> p h r", h=4), writes=["alibi"])
        P.dma("sp", subg[:], T["vec"][6, 0:128].partition_broadcast(128), writes=["subg"])
        P.op("dve", lambda e: e.tensor_scalar(out=subg[:], in0=subg[:], scalar1=1.0 - LAM_INIT0, scalar2=None, op0=ALU.mult),
             reads=["subg"], writes=["subg"])

        with scope(P) as es2:
            lv = es2.enter_context(nc.sbuf_tensor("lv", [1, 256], F32))
            pr = es2.enter_context(nc.sbuf_tensor("pr", [1, 128], F32))
            dots = es2.enter_context(nc.sbuf_tensor("dots", [1, 2], F32))
            P.dma("sp", lv[:], T["lamv"], writes=["lv"])
            P.op("dve", lambda e: e.tensor_tensor(out=pr[:, 0:64], in0=lv[:, 0:64], in1=lv[:, 64:128], op=ALU.mult),
                 reads=["lv"], writes=["pr"])
            P.op("dve", lambda e: e.tensor_tensor(out=pr[:, 64:128], in0=lv[:, 128:192], in1=lv[:, 192:256], op=ALU.mult),
                 reads=["lv", "pr"], writes=["pr"])
            P.op("dve", lambda e: e.tensor_reduce(out=dots[:, 0:2], in_=pr[:, :].rearrange("p (a d) -> p a d", a=2),
                                                  axis=AX.X, op=ALU.add), reads=["pr"], writes=["dots"])
            pv = psf(C, 0)[:, 0:2]
            P.op("pe", lambda e: e.matmul(pv, lhsT=C.onesf[0:1, 0:128], rhs=dots[0:1, 0:2], start=True, stop=True),
                 reads=["onesf", "dots"], writes=[("ps", 0)])
            P.op("act", lambda e: e.activation(out=neglam[:, 0:2], in_=pv, func=AF.Exp), reads=[("ps", 0)], writes=["neglam"])
            P.op("dve", lambda e: e.tensor_tensor(out=neglam[:, 2:3], in0=neglam[:, 1:2], in1=neglam[:, 0:1], op=ALU.subtract),
                 reads=["neglam"], writes=["neglam"])
            P.op("dve", lambda e: e.tensor_scalar(out=neglam[:, 2:3], in0=neglam[:, 2:3], scalar1=-LAM_INIT0, scalar2=None, op0=ALU.add),
                 reads=["neglam"], writes=["neglam"])

        with scope(P) as es2:
            xt = es2.enter_context(nc.sbuf_tensor("xt", [128, 2, D], F32))
            xb = es2.enter_context(nc.sbuf_tensor("xb", [128, 2, D], BF16))
            for i in range(32 if full else 48):
                sl = i % 2
                if i < 32:
                    src = T["xf"][i * 128:(i + 1) * 128, :]
                    dst = hnTf[:, :, i * 128:(i + 1) * 128]
                    dk = ("hnTf", i // 4)
                else:
                    j = i - 32
                    src = T["xo"][j * 128:(j + 1) * 128, :]
                    dst = hnTo[:, :, j * 128:(j + 1) * 128]
                    dk = ("hnTo", j // 4)
                if full:
                    dk = ("hnTf", i // 4)
                P.dma("sp", xt[:, sl, :], src, writes=[("xt", sl)])
                emit_norm_bf16(C, xt[:, sl, :], [("xt", sl)], g[:, 0, :], ("ga", 0), xb[:, sl, :], [("xb", sl)])
                transpose_to(C, xb[:, sl, :], [("xb", sl)], dst, [dk], bank=6 + sl, evac=("act" if sl else "dve"))

        if stage < 1:
            return
        with scope(P) as es2:
            wfz = es2.enter_context(nc.sbuf_tensor("wfz", [128, 8, 8], BF16))
            negb = es2.enter_context(nc.sbuf_tensor("negb", [8, 1], F32))
            Lf = es2.enter_context(nc.sbuf_tensor("Lf", [8, S], F32))
            Sc = es2.enter_context(nc.sbuf_tensor("Sc", [8, S], F32))
            Dm = es2.enter_context(nc.sbuf_tensor("Dm", [8, NCH, 8], F32))
            P.dma("pool", wfz[:], T["w_in"].rearrange("(k p) n -> p k n", p=128)[:, :, 3072:3080], writes=["wfz"])
            P.dma("sp", negb[:], T["bf"], writes=["negb"])
            P.op("dve", lambda e: e.tensor_scalar(out=negb[:], in0=negb[:], scalar1=-1.0, scalar2=None, op0=ALU.mult),
                 reads=["negb"], writes=["negb"])
            for n in range(8):
                bank = n % 2
                pv = psf(C, bank)[0:8, :]
                for k in range(8):
                    P.op("pe", lambda e, k=k, n=n, pv=pv: e.matmul(pv, lhsT=wfz[:, k, :], rhs=hnTf[:, k, n * 512:(n + 1) * 512],
                                                                   start=(k == 0), stop=(k == 7)),
                         reads=["wfz", ("hnTf", n)], writes=[("ps", bank)])
                P.op("act", lambda e, n=n, pv=pv: e.activation(out=Lf[:, n * 512:(n + 1) * 512], in_=pv, func=AF.Exp, scale=-1.0, bias=negb[:, 0:1]),
                     reads=[("ps", bank), "negb"], writes=[("Lf", n)])
            for n in range(8):
                P.op("act", lambda e, n=n: e.activation(out=Lf[:, n * 512:(n + 1) * 512], in_=Lf[:, n * 512:(n + 1) * 512], func=AF.Ln, bias=1.0),
                     reads=[("Lf", n)], writes=[("Lf", n)])
            for n in range(8):
                init = 0.0 if n == 0 else Sc[:, n * 512 - 1:n * 512]
                rd = [("Lf", n), "onesf"] + ([("Sc", n - 1)] if n else [])
                P.op("dve", lambda e, n=n, init=init: e.tensor_tensor_scan(out=Sc[:, n * 512:(n + 1) * 512], data0=C.onesf[0:8, :],
                                                                           data1=Lf[:, n * 512:(n + 1) * 512], initial=init,
                                                                           op0=ALU.mult, op1=ALU.add),
                     reads=rd, writes=[("Sc", n)])
            pvt = psf(C, 2)[:, 0:256]
            for kb in range(32):
                P.op("pe", lambda e, kb=kb: e.transpose(out=pvt[:, kb * 8:(kb + 1) * 8], in_=Sc[0:8, kb * 128:(kb + 1) * 128],
                                                        identity=C.identf[0:8, 0:8]),
                     reads=[("Sc", kb // 4), "identf"], writes=[("ps", 2)])
            P.op("dve", lambda e: e.tensor_copy(out=Stm[:, :, :], in_=pvt.rearrange("p (k h) -> p k h", h=8)),
                 reads=[("ps", 2)], writes=["Stm"])
            CW = 128 * KPC
            ssel = Sc[:, :].rearrange("h (p t) -> h p t", t=CW)[:, :, CW - 1:CW]
            P.op("dve", lambda e: e.tensor_tensor(out=Dm[:, :, :], in0=ssel.to_broadcast([8, NCH, 8]),
                                                  in1=C.identf[0:8, 0:8].unsqueeze(1).to_broadcast([8, NCH, 8]), op=ALU.mult),
                 reads=[("Sc", n) for n in range(8)] + ["identf"], writes=["Dm"])
            pvx = psf(C, 3)[:, 0:NCH * 8]
            P.op("pe", lambda e: e.matmul(pvx, lhsT=C.onesf[0:8, 0:128], rhs=Dm[:, :, :].rearrange("h p g -> h (p g)"), start=True, stop=True),
                 reads=["onesf", "Dm"], writes=[("ps", 3)])
            P.op("dve", lambda e: e.tensor_copy(out=Xb[:, :], in_=pvx), reads=[("ps", 3)], writes=["Xb"])
        if stage < 2:
            return
        with scope(P) as es2:
            wq = es2.enter_context(nc.sbuf_tensor("wq", [128, 2, 8, 128], BF16))
            wk = es2.enter_context(nc.sbuf_tensor("wk", [128, 2, 8, 128], BF16))
            wv = es2.enter_context(nc.sbuf_tensor("wv", [128, 2, 8, 128], BF16))
            KT = es2.enter_context(nc.sbuf_tensor("KT", [128, S], BF16))
            QT = es2.enter_context(nc.sbuf_tensor("QTz", [128, 2, NT * 128], BF16))
            Vb = es2.enter_context(nc.sbuf_tensor("Vb", [128, 32, 130], BF16))
            Vd = Vb[:, :, 0:129]
            Vf = Vb[:, :, :].rearrange("p k (h d) -> p k h d", h=2)
            PT = es2.enter_context(nc.sbuf_tensor("PT", [128, 4, 2, 256], BF16))
            t1 = es2.enter_context(nc.sbuf_tensor("t1", [128, 2, 128], F32))
            dd = es2.enter_context(nc.sbuf_tensor("dd", [128, 2, 128], F32))
            rr = es2.enter_context(nc.sbuf_tensor("rr", [128, 2, 8], F32))
            P.op("pool", lambda e: e.memset(QT[:, :, :], 0.0), writes=[("QT", n) for n in range(NT // 4)])
            w_in_v = T["w_in"].rearrange("(k p) n -> p k n", p=128)
            rr_i = [0]
            for gidx in range(8):
                ws = gidx % 2
                diff = gidx < 4
                if gidx in (0, 4):
                    vk = [("V", kq) for kq in range(8)]
                    P.op("pool", lambda e: e.memset(Vb[:, :, :], 1.0), writes=vk)
                if diff:
                    qc, kc, vc = gidx * 128, 512 + gidx * 128, 1024 + gidx * 128
                else:
                    qc, kc, vc = 1536 + (gidx - 4) * 128, 2048 + (gidx - 4) * 128, 2560 + (gidx - 4) * 128
                P.dma("pool", wq[:, ws, :, :], w_in_v[:, :, qc:qc + 128], writes=[("wq", ws)])
                P.dma("pool", wk[:, ws, :, :], w_in_v[:, :, kc:kc + 128], writes=[("wk", ws)])
                P.dma("pool", wv[:, ws, :, :], w_in_v[:, :, vc:vc + 128], writes=[("wv", ws)])
                for n in range(8):
                    bank = 2 * (n % 2)
                    pv = psf(C, bank)
                    for k in range(8):
                        P.op("pe", lambda e, k=k, n=n, pv=pv: e.matmul(pv, lhsT=wk[:, ws, k, :], rhs=hnTf[:, k, n * 512:(n + 1) * 512],
                                                                       start=(k == 0), stop=(k == 7)),
                             reads=[("wk", ws), ("hnTf", n)], writes=[("ps", bank)])
                    P.op("dve", lambda e, n=n, pv=pv: e.tensor_copy(out=KT[:, n * 512:(n + 1) * 512], in_=pv),
                         reads=[("ps", bank)], writes=[("KT", n)])
                for n in range(NT // 4):
                    bank = 2 * (n % 2)
                    pv = psf(C, bank)
                    for k in range(8):
                        P.op("pe", lambda e, k=k, n=n, pv=pv: e.matmul(pv, lhsT=wq[:, ws, k, :], rhs=hnTo[:, k, n * 512:(n + 1) * 512],
                                                                       start=(k == 0), stop=(k == 7)),
                             reads=[("wq", ws), (("hnTf" if full else "hnTo"), n)], writes=[("ps", bank)])
                    P.op("dve", lambda e: e.tensor_copy(out=QT[0:64, 0, n * 512:(n + 1) * 512], in_=pv[0:64, :]),
                         reads=[("ps", bank)], writes=[("QT", n)])
                    P.op("dve", lambda e: e.tensor_copy(out=QT[64:128, 1, n * 512:(n + 1) * 512], in_=pv[64:128, :]),
                         reads=[("ps", bank), ("QT", n)], writes=[("QT", n)])
                for kq in range(8):
                    bank = 2 * (kq % 2)
                    pv = psf(C, bank)
                    for j in range(4):
                        kb = kq * 4 + j
                        for k in range(8):
                            P.op("pe", lambda e, k=k, kb=kb, j=j, pv=pv: e.matmul(pv[:, j * 128:(j + 1) * 128], lhsT=hnTf[:, k, kb * 128:(kb + 1) * 128],
                                                                                  rhs=wv[:, ws, k, :], start=(k == 0), stop=(k == 7)),
                                 reads=[("wv", ws), ("hnTf", kb // 4)], writes=[("ps", bank)])
                    if diff:
                        dst = Vd[:, kq * 4:(kq + 1) * 4, 0:128]
                        srcv = pv.rearrange("p (j d) -> p j d", j=4)
                        vkey = ("V", kq)
                    else:
                        dst = Vf[:, kq * 4:(kq + 1) * 4, :, 0:64]
                        srcv = pv.rearrange("p (j h d) -> p j h d", j=4, h=2)
                        vkey = ("V", kq)
                    P.op("dve", lambda e, dst=dst, srcv=srcv: e.tensor_copy(out=dst, in_=srcv), reads=[("ps", bank)], writes=[vkey])

                for p in range(NCH):
                    nkb = KPC * p + KPC
                    kb0 = KPC * p
                    oset = p % 2
                    if not diff:
                        fsl = p % 2
                        for sh in range(2):
                            hh_ = 2 * (gidx - 4) + sh
                            P.op("dve", lambda e: e.tensor_scalar(out=fb[:, fsl, sh, 0:nkb], in0=Stm[:, 0:nkb, hh_],
                                                                  scalar1=Xb[:, p * 8 + hh_:p * 8 + hh_ + 1], scalar2=None, op0=ALU.subtract),
                                 reads=["Stm", "Xb"], writes=[("fb", fsl)])
                    def emit_S(kb, p=p):
                        slot = kb % 4
                        for sh in range(2):
                            pvS = psf(C, slot)[:, sh * 256:(sh + 1) * 256]
                            P.op("pe", lambda e: e.matmul(pvS, lhsT=KT[:, kb * 128:(kb + 1) * 128],
                                                          rhs=QT[:, sh, p * 256:(p + 1) * 256], start=True, stop=True),
                                 reads=[("KT", kb // 4), ("QT", p // 2)], writes=[("ps", slot)])

                    def emit_PV(kb, p=p, nkb=nkb, oset=oset):
                        slot = kb % 4
                        for sh in range(2):
                            if diff:
                                ai = kb - kb0 + (30 if full else 28)
                                bias = alibi[:, gidx, ai:ai + 1]
                                bk = "alibi"
                            else:
                                bias = fb[:, p % 2, sh, kb:kb + 1]
                                bk = ("fb", p % 2)
                            P.op("act", lambda e: e.activation(out=PT[:, slot, sh, :], in_=psf(C, slot)[:, sh * 256:(sh + 1) * 256],
                                                               func=AF.Exp, scale=0.125, bias=bias),
                                 reads=[("ps", slot), bk], writes=[("PT", slot, sh)])
                        if kb >= kb0:
                            mk = masks[:, kb - kb0, :].unsqueeze(1).to_broadcast([128, 2, 256])
                            P.op("pool", lambda e: e.tensor_tensor(out=PT[:, slot, :, :], in0=PT[:, slot, :, :], in1=mk, op=ALU.mult),
                                 reads=[("PT", slot, 0), ("PT", slot, 1), "masks"], writes=[("PT", slot, 0), ("PT", slot, 1)])
                        for sh in range(2):
                            obank = 4 + oset * 2 + sh
                            for e_ in range(2):
                                if diff:
                                    ov = psf(C, obank)[:, e_ * 129:(e_ + 1) * 129]
                                    rhs = Vd[:, kb, :]
                                else:
                                    ov = psf(C, obank)[:, e_ * 65:(e_ + 1) * 65]
                                    rhs = Vf[:, kb, sh, :]
                                first = (kb == 0 and e_ == 0)
                                P.op("pe", lambda e: e.matmul(ov, lhsT=PT[:, slot, sh, e_ * 128:(e_ + 1) * 128], rhs=rhs,
                                                              start=first, stop=(kb == nkb - 1), skip_group_check=True),
                                     reads=[("PT", slot, sh), ("V", kb // 4)], writes=[("ps", obank)])

                    LOOK = 2
                    for kb in range(min(LOOK, nkb)):
                        emit_S(kb)
                    for kb in range(nkb):
                        if kb + LOOK < nkb:
                            emit_S(kb + LOOK)
                        emit_PV(kb)

                    b0 = 4 + oset * 2
                    i0 = rr_i[0] % 2
                    rr_i[0] += 1
                    rk = ("rr", i0)
                    if diff:
                        O1 = psf(C, b0)[:, 0:258].rearrange("p (e d) -> p e d", e=2)
                        O2 = psf(C, b0 + 1)[:, 0:258].rearrange("p (e d) -> p e d", e=2)
                        P.op("dve", lambda e: e.reciprocal(out=rr[:, i0, 0:2], in_=O1[:, :, 128]), reads=[("ps", b0)], writes=[rk])
                        P.op("dve", lambda e: e.reciprocal(out=rr[:, i0, 2:4], in_=O2[:, :, 128]), reads=[("ps", b0 + 1)], writes=[rk])
                        P.op("dve", lambda e: e.tensor_scalar(out=rr[:, i0, 2:4], in0=rr[:, i0, 2:4], scalar1=neglam[:, 2:3], scalar2=None, op0=ALU.mult),
                             reads=[rk, "neglam"], writes=[rk])
                        for e_ in range(2):
                            lt = 2 * p + e_
                            P.op("dve", lambda e, e_=e_: e.tensor_scalar(out=t1[:, e_, :], in0=O1[:, e_, 0:128], scalar1=rr[:, i0, e_:e_ + 1],
                                                                         scalar2=None, op0=ALU.mult),
                                 reads=[("ps", b0), rk], writes=[("t1", e_)])
                            P.op("dve", lambda e, e_=e_: e.scalar_tensor_tensor(out=dd[:, e_, :], in0=O2[:, e_, 0:128], scalar=rr[:, i0, 2 + e_:3 + e_],
                                                                                in1=t1[:, e_, :], op0=ALU.mult, op1=ALU.add),
                                 reads=[("ps", b0 + 1), rk, ("t1", e_)], writes=[("dd", e_)])
                            rs, rsk = rstd_from(C, dd[:, e_, :], [("dd", e_)], 128, 1e-5, eng="dve")
                            P.op("dve", lambda e, e_=e_, lt=lt, rs=rs: e.scalar_tensor_tensor(out=obuf[:, lt, gidx * 128:(gidx + 1) * 128], in0=dd[:, e_, :],
                                                                                             scalar=rs, in1=subg[:, :], op0=ALU.mult, op1=ALU.mult),
                                 reads=[("dd", e_), rsk, "subg"], writes=[("obuf", lt)])
                    else:
                        for sh in range(2):
                            Ov = psf(C, b0 + sh)[:, 0:130].rearrange("p (e d) -> p e d", e=2)
                            P.op("dve", lambda e, sh=sh, Ov=Ov: e.reciprocal(out=rr[:, i0, 4 + 2 * sh:6 + 2 * sh], in_=Ov[:, :, 64]),
                                 reads=[("ps", b0 + sh)], writes=[rk])
                            for e_ in range(2):
                                lt = 2 * p + e_
                                c0 = 512 + (2 * (gidx - 4) + sh) * 64
                                P.op("dve", lambda e, sh=sh, e_=e_, lt=lt, c0=c0, Ov=Ov: e.tensor_scalar(out=obuf[:, lt, c0:c0 + 64], in0=Ov[:, e_, 0:64],
                                                                                                       scalar1=rr[:, i0, 4 + 2 * sh + e_:5 + 2 * sh + e_],
                                                                                                       scalar2=None, op0=ALU.mult),
                                     reads=[("ps", b0 + sh), rk], writes=[("obuf", lt)])

        if stage < 3:
            return
        with scope(P) as es2:
            wo = es2.enter_context(nc.sbuf_tensor("wo", [128, 8, D], BF16))
            kwo = load_w_bf16(C, wo, T["w_out"], 8, "wo", nsplit=2)
            oT = es2.enter_context(nc.sbuf_tensor("oT", [128, 2, 8, 128], BF16))
            xr = es2.enter_context(nc.sbuf_tensor("xres", [128, 2, D], F32))
            hh = es2.enter_context(nc.sbuf_tensor("hh", [128, 2, D], F32))
            xsrc = T["xf" if full else "xo"]
            P.dma("sp", xr[:, 0, :], xsrc[0:128, :], writes=[("xres", 0)])
            for lt in range(NT):
                sl = lt % 2
                if lt + 1 < NT:
                    P.dma("sp", xr[:, 1 - sl, :], xsrc[(lt + 1) * 128:(lt + 2) * 128, :], writes=[("xres", 1 - sl)])
                transpose_to(C, obuf[:, lt, :], [("obuf", lt)], oT[:, sl, :, :], [("oT", sl)], bank=6 + sl, evac="act")
                for hf in range(2):
                    pk = ("ps", 2 * sl + hf)
                    pv = psf(C, 2 * sl + hf)
                    for k in range(8):
                        P.op("pe", lambda e, k=k, hf=hf, pv=pv: e.matmul(pv, lhsT=oT[:, sl, k, :], rhs=wo[:, k, hf * 512:(hf + 1) * 512],
                                                                         start=(k == 0), stop=(k == 7)),
                             reads=[("oT", sl), kwo[k // 4]], writes=[pk])
                emit_norm_residual(C, C.ps[sl][:, :], [("ps", 2 * sl), ("ps", 2 * sl + 1)], g[:, 1, :], ("ga", 1),
                                   xr[:, sl, :], [("xres", sl)], hh[:, sl, :], [("hh", sl)])
                P.dma("sp", H[lt * 128:(lt + 1) * 128, :], hh[:, sl, :], reads=[("hh", sl)], writes=[("H", lt)])


def build_phase_A(upto=99, stage=9):
    nc = bass.Bass("TRN2", target_bir_lowering=False)
    T = {}

    def din(name, shape, dt=F32):
        T[name] = nc.dram_tensor(name, shape, dt, kind="ExternalInput").ap()

    din("xf", [S, D]); din("xo", [NOWN, D]); din("pT", [256, NOWN])
    din("w_in", [D, 3080]); din("w_out", [D, D]); din("w_up", [D, DFF]); din("w_dn", [DFF, D])
    din("w_pp", [256, D]); din("w_pg", [D, D]); din("vec", [8, D]); din("lamv", [1, 256]); din("bf", [8, 1])
    din("ident", [128, 128]); din("masks", [4, 128, 256]); din("alibi", [128, 128])
    H = nc.dram_tensor("H", [NOWN, D], F32, kind="ExternalOutput").ap()
    H1 = nc.dram_tensor("H1", [NOWN, D], F32, kind="ExternalOutput").ap()
    HNT = nc.dram_tensor("HNT", [D, NOWN], BF16, kind="ExternalOutput").ap()
    with contextlib.ExitStack() as es:
        P = Prog(nc, es)
        C = Ctx()
        setup_common(nc, P, es, C, T["ident"])
        setup_eps(C, es)
        emit_attention(C, T, H, stage)
        if upto >= 2:
            emit_mlp(C, H, T["w_up"], T["w_dn"], T["vec"], 2, 3, "0")
        if upto >= 3:
            emit_ple(C, H, T["pT"], T["w_pp"], T["w_pg"], T["vec"], 4, H1, "0", hn_row=5, HNT=HNT)
        P.finish()
    return nc


def emit_rec(C, T, G, ncb=5):
    import itertools
    nc, P = C.nc, C.P
    TH = 1024
    NTH = S // TH
    assert ncb % 2 == 0 or ncb == 5
    with scope(P) as es:
        hnT = es.enter_context(nc.sbuf_tensor("r_hnT", [128, 8, S], BF16))
        wg = es.enter_context(nc.sbuf_tensor("r_wg", [128, 8, ncb * 128], BF16))
        wxr = es.enter_context(nc.sbuf_tensor("r_wxr", [128, 8, ncb * 128], BF16))
        wxs = es.enter_context(nc.sbuf_tensor("r_wxs", [128, ncb, 128], BF16))
        was = es.enter_context(nc.sbuf_tensor("r_was", [128, ncb, 128], BF16))
        rv = es.enter_context(nc.sbuf_tensor("r_rv", [128, ncb, 8], F32))
        sc = es.enter_context(nc.sbuf_tensor("r_sc", [128, ncb], F32))
        hlast = es.enter_context(nc.sbuf_tensor("r_hlast", [128, 2], F32))
        ybf = es.enter_context(nc.sbuf_tensor("r_y", [128, 2, 2, TH], BF16))
        xr = es.enter_context(nc.sbuf_tensor("r_xr", [128, 2, 2, TH + 3], F32))
        xc = es.enter_context(nc.sbuf_tensor("r_xc", [128, 2, TH], F32))
        xcb = es.enter_context(nc.sbuf_tensor("r_xcb", [128, 2, TH], BF16))
        gx = es.enter_context(nc.sbuf_tensor("r_gx", [128, 2, TH], F32))
        ga = es.enter_context(nc.sbuf_tensor("r_ga", [128, 2, TH], F32))
        tt = es.enter_context(nc.sbuf_tensor("r_tt", [128, 2, TH], F32))
        hs = es.enter_context(nc.sbuf_tensor("r_hs", [128, 2, TH], F32))
        gout = es.enter_context(nc.sbuf_tensor("r_gout", [128, 2, TH], BF16))
        hv = T["hnT"].rearrange("(k f) t -> f k t", f=128)
        for k in range(8):
            P.dma("sp", hnT[:, k, :], hv[:, k, :], writes=[("r_hnT", k)])
        kg = load_w_bf16(C, wg, T["w_g"], 8, "r_wg", nsplit=2)
        kx = load_w_bf16(C, wxr, T["w_x"], 8, "r_wxr", nsplit=2)
        P.dma("pool", wxs[:], T["wx"].rearrange("n i j -> i n j"), writes=["r_wxs"])
        P.dma("pool", was[:], T["wa"].rearrange("n i j -> i n j"), writes=["r_was"])
        P.dma("sp", rv[:], T["rvec"], writes=["r_rv"])
        P.op("act", lambda e: e.activation(out=sc[:, :], in_=rv[:, :, 7], func=AF.Exp, scale=-1.0), reads=["r_rv"], writes=["r_sc"])
        P.op("act", lambda e: e.activation(out=sc[:, :], in_=sc[:, :], func=AF.Ln, bias=1.0), reads=["r_sc"], writes=["r_sc"])
        P.op("dve", lambda e: e.tensor_scalar(out=sc[:, :], in0=sc[:, :], scalar1=-8.0, scalar2=None, op0=ALU.mult), reads=["r_sc"], writes=["r_sc"])

        def stage1(cb, th, st, bs):
            if th == 0:
                P.op("dve", lambda e: e.memset(xr[:, st, bs, 0:3], 0.0), writes=[("r_xr", st, bs)])
            else:
                P.op("dve", lambda e: e.tensor_copy(out=xr[:, st, bs, 0:3], in_=xr[:, st, 1 - bs, TH:TH + 3]),
                     reads=[("r_xr", st, 1 - bs)], writes=[("r_xr", st, bs)])
            yield
            for n in range(TH // 512):
                N = (TH // 512) * th + n
                bg = 2 * st
                pv = psf(C, bg)
                for k in range(8):
                    P.op("pe", lambda e: e.matmul(pv, lhsT=wg[:, k, cb * 128:(cb + 1) * 128], rhs=hnT[:, k, N * 512:(N + 1) * 512],
                                                  start=(k == 0), stop=(k == 7)),
                         reads=[kg[k // 4], ("r_hnT", k)], writes=[("ps", bg)])
                P.op("act", lambda e: e.activation(out=ybf[:, st, bs, n * 512:(n + 1) * 512], in_=pv, func=AF.Gelu_apprx_tanh),
                     reads=[("ps", bg)], writes=[("r_y", st, bs)])
                yield
                bx_ = 2 * st + 1
                pv2 = psf(C, bx_)
                for k in range(8):
                    P.op("pe", lambda e: e.matmul(pv2, lhsT=wxr[:, k, cb * 128:(cb + 1) * 128], rhs=hnT[:, k, N * 512:(N + 1) * 512],
                                                  start=(k == 0), stop=(k == 7)),
                         reads=[kx[k // 4], ("r_hnT", k)], writes=[("ps", bx_)])
                P.op("dve", lambda e: e.tensor_copy(out=xr[:, st, bs, 3 + n * 512:3 + (n + 1) * 512], in_=pv2),
                     reads=[("ps", bx_)], writes=[("r_xr", st, bs)])
                yield

        def stage2(cb, th, st, bs):
            xk = ("r_xr", st, bs)
            K = lambda name: (name, st)
            P.op("act", lambda e: e.activation(out=xc[:, st, :], in_=xr[:, st, bs, 3:3 + TH], func=AF.Identity, scale=rv[:, cb, 3:4], bias=rv[:, cb, 4:5]),
                 reads=[xk, "r_rv"], writes=[K("r_xc")])
            yield
            for w in range(3):
                P.op("dve", lambda e: e.scalar_tensor_tensor(out=xc[:, st, :], in0=xr[:, st, bs, w:w + TH], scalar=rv[:, cb, w:w + 1], in1=xc[:, st, :],
                                                             op0=ALU.mult, op1=ALU.add),
                     reads=[xk, "r_rv", K("r_xc")], writes=[K("r_xc")])
                yield
            P.op("act", lambda e: e.copy(out=xcb[:, st, :], in_=xc[:, st, :]), reads=[K("r_xc")], writes=[K("r_xcb")])
            yield
            for n in range(TH // 512):
                b1 = 4 + 2 * st
                pv = psf(C, b1)
                P.op("pe", lambda e: e.matmul(pv, lhsT=wxs[:, cb, :], rhs=xcb[:, st, n * 512:(n + 1) * 512], start=True, stop=True),
                     reads=["r_wxs", K("r_xcb")], writes=[("ps", b1)])
                P.op("act", lambda e: e.activation(out=gx[:, st, n * 512:(n + 1) * 512], in_=pv, func=AF.Sigmoid, bias=rv[:, cb, 5:6]),
                     reads=[("ps", b1), "r_rv"], writes=[K("r_gx")])
                yield
                b2 = 5 + 2 * st
                pv2 = psf(C, b2)
                P.op("pe", lambda e: e.matmul(pv2, lhsT=was[:, cb, :], rhs=xcb[:, st, n * 512:(n + 1) * 512], start=True, stop=True),
                     reads=["r_was", K("r_xcb")], writes=[("ps", b2)])
                P.op("act", lambda e: e.activation(out=ga[:, st, n * 512:(n + 1) * 512], in_=pv2, func=AF.Sigmoid, bias=rv[:, cb, 6:7]),
                     reads=[("ps", b2), "r_rv"], writes=[K("r_ga")])
                yield
            P.op("act", lambda e: e.activation(out=ga[:, st, :], in_=ga[:, st, :], func=AF.Exp, scale=sc[:, cb:cb + 1]),
                 reads=[K("r_ga"), "r_sc"], writes=[K("r_ga")])
            yield
            P.op("dve", lambda e: e.tensor_tensor(out=tt[:, st, :], in0=ga[:, st, :], in1=ga[:, st, :], op=ALU.mult), reads=[K("r_ga")], writes=[K("r_tt")])
            yield
            P.op("dve", lambda e: e.tensor_tensor(out=gx[:, st, :], in0=gx[:, st, :], in1=xc[:, st, :], op=ALU.mult),
                 reads=[K("r_gx"), K("r_xc")], writes=[K("r_gx")])
            yield
            P.op("act", lambda e: e.activation(out=tt[:, st, :], in_=tt[:, st, :], func=AF.Sqrt, scale=-1.0, bias=1.0), reads=[K("r_tt")], writes=[K("r_tt")])
            yield
            P.op("dve", lambda e: e.tensor_tensor(out=tt[:, st, :], in0=tt[:, st, :], in1=gx[:, st, :], op=ALU.mult),
                 reads=[K("r_tt"), K("r_gx")], writes=[K("r_tt")])
            if th == 0:
                P.op("dve", lambda e: e.tensor_copy(out=tt[:, st, 0:1], in_=gx[:, st, 0:1]), reads=[K("r_gx"), K("r_tt")], writes=[K("r_tt")])
            yield
            init = 0.0 if th == 0 else hlast[:, st:st + 1]
            P.op("dve", lambda e: e.tensor_tensor_scan(out=hs[:, st, :], data0=ga[:, st, :], data1=tt[:, st, :], initial=init, op0=ALU.mult, op1=ALU.add),
                 reads=[K("r_ga"), K("r_tt"), K("r_hlast")], writes=[K("r_hs")])
            P.op("dve", lambda e: e.tensor_copy(out=hlast[:, st:st + 1], in_=hs[:, st, TH - 1:TH]), reads=[K("r_hs")], writes=[K("r_hlast")])
            yield
            P.op("dve", lambda e: e.tensor_tensor(out=gout[:, st, :], in0=hs[:, st, :], in1=ybf[:, st, bs, :], op=ALU.mult),
                 reads=[K("r_hs"), ("r_y", st, bs)], writes=[K("r_gout")])
            P.dma("sp", G[cb * 128:(cb + 1) * 128, th * TH:(th + 1) * TH], gout[:, st, :], reads=[K("r_gout")], writes=[("G", cb, th)])
            yield

        def interleave(gens):
            for _ in itertools.zip_longest(*gens):
                pass

        supers = []
        for j in range((ncb + 1) // 2):
            cbs = [c for c in (2 * j, 2 * j + 1) if c < ncb]
            for th in range(NTH):
                supers.append((cbs, th))
        interleave([stage1(cb, supers[0][1], st, 0) for st, cb in enumerate(supers[0][0])])
        for i, (cbs, th) in enumerate(supers):
            bs = i % 2
            if i + 1 < len(supers):
                ncbs, nth = supers[i + 1]
                interleave([stage1(cb, nth, st, 1 - bs) for st, cb in enumerate(ncbs)])
            interleave([stage2(cb, th, st, bs) for st, cb in enumerate(cbs)])


def build_phase_B():
    nc = bass.Bass("TRN2", target_bir_lowering=False)
    T = {}

    def din(name, shape, dt=F32):
        T[name] = nc.dram_tensor(name, shape, dt, kind="ExternalInput").ap()

    din("hnT", [D, S], BF16); din("w_g", [D, 640]); din("w_x", [D, 640]); din("wx", [5, 128, 128]); din("wa", [5, 128, 128])
    din("rvec", [128, 5, 8]); din("ident", [128, 128])
    G = nc.dram_tensor("G", [640, S], BF16, kind="ExternalOutput").ap()
    with contextlib.ExitStack() as es:
        P = Prog(nc, es)
        C = Ctx()
        setup_common(nc, P, es, C, T["ident"])
        setup_eps(C, es)
        emit_rec(C, T, G)
        P.finish()
    return nc


def emit_recout(C, T, H, row=0, blend=None):
    nc, P = C.nc, C.P
    with scope(P) as es:
        g = load_gains(C, es, T["vec"], [row], "gro")
        gT = es.enter_context(nc.sbuf_tensor("gTs", [128, 10, NOWN], BF16))
        wro = es.enter_context(nc.sbuf_tensor("wro", [128, 10, D], BF16))
        if blend is None:
            gv = T["gT"].rearrange("(c p) t -> p c t", p=128)
            for c in range(10):
                P.dma("sp", gT[:, c, :], gv[:, c, :], writes=[("gTs", c)])
        else:
            Gd, H1d, sel_d = blend
            sel = es.enter_context(nc.sbuf_tensor("sel_sb", [128, 2], F32))
            P.dma("sp", sel[:], sel_d, writes=["sel"])
            gT2 = es.enter_context(nc.sbuf_tensor("gTs2", [128, 2, NOWN], BF16))
            hres2 = es.enter_context(nc.sbuf_tensor("hres2", [128, 2, D], F32))
            gv = Gd.rearrange("(c p) t -> p c t", p=128)
            for c in range(10):
                s2 = c % 2
                P.dma("sp", gT[:, c, :], gv[:, c, 0:NOWN], writes=[("gTs", c)])
                P.dma("sp", gT2[:, s2, :], gv[:, c, NOWN:2 * NOWN], writes=[("gTs2", s2)])
                P.op("act", lambda e: e.activation(out=gT[:, c, :], in_=gT[:, c, :], func=AF.Copy, scale=sel[:, 0:1]),
                     reads=[("gTs", c), "sel"], writes=[("gTs", c)])
                P.op("dve", lambda e: e.scalar_tensor_tensor(out=gT[:, c, :], in0=gT2[:, s2, :], scalar=sel[:, 1:2], in1=gT[:, c, :],
                                                             op0=ALU.mult, op1=ALU.add),
                     reads=[("gTs", c), ("gTs2", s2), "sel"], writes=[("gTs", c)])
        kw = load_w_bf16(C, wro, T["w_ro"], 10, "wro", nsplit=2)
        hres = es.enter_context(nc.sbuf_tensor("hres", [128, 2, D], F32))
        hh = es.enter_context(nc.sbuf_tensor("hh1", [128, 2, D], F32))
        for lt in range(16):
            sl = lt % 2
            if blend is None:
                P.dma("sp", hres[:, sl, :], T["h1"][lt * 128:(lt + 1) * 128, :], writes=[("hres", sl)])
            else:
                P.dma("sp", hres[:, sl, :], H1d[lt * 128:(lt + 1) * 128, :], writes=[("hres", sl)])
                P.dma("sp", hres2[:, sl, :], H1d[NOWN + lt * 128:NOWN + (lt + 1) * 128, :], writes=[("hres2", sl)])
                P.op("act", lambda e: e.activation(out=hres[:, sl, :], in_=hres[:, sl, :], func=AF.Copy, scale=sel[:, 0:1]),
                     reads=[("hres", sl), "sel"], writes=[("hres", sl)])
                P.op("dve", lambda e: e.scalar_tensor_tensor(out=hres[:, sl, :], in0=hres2[:, sl, :], scalar=sel[:, 1:2], in1=hres[:, sl, :],
                                                             op0=ALU.mult, op1=ALU.add),
                     reads=[("hres", sl), ("hres2", sl), "sel"], writes=[("hres", sl)])
            for hf in range(2):
                pk = ("ps", 2 * sl + hf)
                pv = psf(C, 2 * sl + hf)
                for c in range(10):
                    P.op("pe", lambda e: e.matmul(pv, lhsT=gT[:, c, lt * 128:(lt + 1) * 128], rhs=wro[:, c, hf * 512:(hf + 1) * 512],
                                                  start=(c == 0), stop=(c == 9)),
                         reads=[("gTs", c), kw[c // 5]], writes=[pk])
            emit_norm_residual(C, C.ps[sl][:, :], [("ps", 2 * sl), ("ps", 2 * sl + 1)], g[:, 0, :], ("gro", 0),
                               hres[:, sl, :], [("hres", sl)], hh[:, sl, :], [("hh1", sl)])
            P.dma("sp", H[lt * 128:(lt + 1) * 128, :], hh[:, sl, :], reads=[("hh1", sl)], writes=[("H", lt)])


def build_phase_C():
    nc = bass.Bass("TRN2", target_bir_lowering=False)
    T = {}

    def din(name, shape, dt=F32):
        T[name] = nc.dram_tensor(name, shape, dt, kind="ExternalInput").ap()

    din("gT", [RW, NOWN], BF16); din("h1", [NOWN, D]); din("pT", [256, NOWN]); din("w_ro", [RW, D])
    din("w_up", [D, DFF]); din("w_dn", [DFF, D]); din("w_pp", [256, D]); din("w_pg", [D, D]); din("vec", [8, D]); din("ident", [128, 128])
    H = nc.dram_tensor("H", [NOWN, D], F32, kind="ExternalOutput").ap()
    OUT = nc.dram_tensor("OUT", [NOWN, D], F32, kind="ExternalOutput").ap()
    with contextlib.ExitStack() as es:
        P = Prog(nc, es)
        C = Ctx()
        setup_common(nc, P, es, C, T["ident"])
        setup_eps(C, es)
        emit_recout(C, T, H)
        emit_mlp(C, H, T["w_up"], T["w_dn"], T["vec"], 1, 2, "1")
        emit_ple(C, H, T["pT"], T["w_pp"], T["w_pg"], T["vec"], 3, OUT, "1")
        P.finish()
    return nc


def build_fused():
    nc = bass.Bass("TRN2", target_bir_lowering=False)
    T = {}

    def din(name, shape, dt=F32):
        T[name] = nc.dram_tensor(name, shape, dt, kind="ExternalInput").ap()

    din("xf", [S, D]); din("pT", [256, S]); din("pT1", [256, NOWN])
    din("w_in", [D, 3080]); din("w_out", [D, D]); din("w_up", [D, DFF]); din("w_dn", [DFF, D])
    din("w_pp", [256, D]); din("w_pg", [D, D]); din("vec", [16, D]); din("lamv", [1, 256]); din("bf", [8, 1])
    din("ident", [128, 128]); din("masks", [2, 128, 256]); din("alibi", [128, 128]); din("sel", [128, 2])
    din("w_g", [D, RW]); din("w_x", [D, RW]); din("wx", [10, 128, 128]); din("wa", [10, 128, 128]); din("rvec", [128, 10, 8])
    din("w_ro", [RW, D]); din("w_up1", [D, DFF]); din("w_dn1", [DFF, D]); din("w_pp1", [256, D]); din("w_pg1", [D, D])
    HA = nc.dram_tensor("HA", [S, D], F32, kind="Internal").ap()
    H1 = nc.dram_tensor("H1", [S, D], F32, kind="Internal").ap()
    HNT = nc.dram_tensor("HNT", [D, S], BF16, kind="Internal").ap()
    G = nc.dram_tensor("G", [RW, S], BF16, kind="Internal").ap()
    HC = nc.dram_tensor("HC", [NOWN, D], F32, kind="Internal").ap()
    OUT = nc.dram_tensor("OUT", [NOWN, D], F32, kind="ExternalOutput").ap()
    with contextlib.ExitStack() as es:
        P = Prog(nc, es)
        C = Ctx()
        setup_common(nc, P, es, C, T["ident"])
        setup_eps(C, es)
        emit_attention(C, T, HA, full=True)
        emit_mlp(C, HA, T["w_up"], T["w_dn"], T["vec"], 2, 3, "0", ntok=S)
        emit_ple(C, HA, T["pT"], T["w_pp"], T["w_pg"], T["vec"], 4, H1, "0", hn_row=5, HNT=HNT, ntiles=32)
        T1 = {"hnT": HNT, "w_g": T["w_g"], "w_x": T["w_x"], "wx": T["wx"], "wa": T["wa"], "rvec": T["rvec"]}
        emit_rec(C, T1, G, ncb=10)
        T2 = {"vec": T["vec"], "w_ro": T["w_ro"]}
        emit_recout(C, T2, HC, row=8, blend=(G, H1, T["sel"]))
        emit_mlp(C, HC, T["w_up1"], T["w_dn1"], T["vec"], 9, 10, "1", ntok=NOWN)
        emit_ple(C, HC, T["pT1"], T["w_pp1"], T["w_pg1"], T["vec"], 11, OUT, "1", ntiles=16)
        P.finish()
    return nc


def own_tiles(r):
    return [4 * p + 2 * r + e for p in range(8) for e in range(2)]


def own_index(r):
    return np.concatenate([np.arange(g * 128, (g + 1) * 128) for g in own_tiles(r)])


def role_consts(r):
    ident = np.eye(128, dtype=np.float32)
    masks = np.zeros((4, 128, 256), np.float32)
    jj = np.arange(128)[:, None]
    ii = np.arange(128)[None, :]
    tri = (jj <= ii).astype(np.float32)
    for m in range(4):
        for e in range(2):
            qt = 2 * r + e
            if m < qt:
                masks[m, :, e * 128:(e + 1) * 128] = 1.0
            elif m == qt:
                masks[m, :, e * 128:(e + 1) * 128] = tri
    alibi = np.zeros((128, 4, 32), np.float32)
    for h in range(4):
        for idx in range(32):
            d = (idx - 28 - 2 * r - 2) * 128 + np.arange(128)
            alibi[:, h, idx] = np.minimum(SLOPES[h] * d, 0.0)
    return ident, masks, alibi.reshape(128, 128)


def f32c(a):
    return np.ascontiguousarray(a, dtype=np.float32)


def prep_A(inp, c):
    b, r = c // 2, c % 2
    oi = own_index(r)
    ident, masks, alibi = role_consts(r)
    vec = np.zeros((8, D), np.float32)
    vec[0] = inp["ln_mix_pre"][0]
    vec[1] = inp["ln_mix_post"][0]
    vec[2] = inp["ln_mlp_pre"][0]
    vec[3] = inp["ln_mlp_post"][0]
    vec[4] = inp["ple_norm"][0]
    vec[5] = inp["ln_mix_pre"][1]
    vec[6, :128] = inp["diff_subln"][0]
    lamv = np.concatenate([inp["diff_lambda_q1"][0], inp["diff_lambda_k1"][0],
                           inp["diff_lambda_q2"][0], inp["diff_lambda_k2"][0]])[None, :]
    return {
        "xf": f32c(inp["x"][b]), "xo": f32c(inp["x"][b][oi]), "pT": f32c(inp["p"][0, b][oi].T),
        "w_in": f32c(inp["attn_w_in"][0]), "w_out": f32c(inp["attn_w_out"][0]),
        "w_up": f32c(inp["mlp_w_up"][0]), "w_dn": f32c(inp["mlp_w_down"][0]),
        "w_pp": f32c(inp["ple_w_proj"][0]), "w_pg": f32c(inp["ple_w_gate"][0]),
        "vec": vec, "lamv": f32c(lamv), "bf": f32c(inp["attn_b_forget"][0][:, None]),
        "ident": ident, "masks": masks, "alibi": alibi,
    }


def prep_B(inp, c, hnT_full):
    r = c % 2
    cols_g = np.arange(5 * r * 128, (5 * r + 5) * 128)
    cols_x = RW + cols_g
    w_in = inp["rec_w_in"][0]
    rvec = np.zeros((128, 5, 8), np.float32)
    for j in range(5):
        ch = np.arange((5 * r + j) * 128, (5 * r + j + 1) * 128)
        rvec[:, j, 0:4] = inp["rec_conv_w"][0][:, ch].T
        rvec[:, j, 4] = inp["rec_conv_b"][0][ch]
        rvec[:, j, 5] = inp["rec_bx"][0][ch]
        rvec[:, j, 6] = inp["rec_ba"][0][ch]
        rvec[:, j, 7] = inp["rec_a_param"][0][ch]
    return {
        "hnT": hnT_full, "w_g": f32c(w_in[:, cols_g]), "w_x": f32c(w_in[:, cols_x]),
        "wx": f32c(inp["rec_wx"][0][5 * r:5 * r + 5]), "wa": f32c(inp["rec_wa"][0][5 * r:5 * r + 5]),
        "rvec": rvec, "ident": np.eye(128, dtype=np.float32),
    }


def prep_C(inp, c, gT_own, h1_own):
    b, r = c // 2, c % 2
    oi = own_index(r)
    vec = np.zeros((8, D), np.float32)
    vec[0] = inp["ln_mix_post"][1]
    vec[1] = inp["ln_mlp_pre"][1]
    vec[2] = inp["ln_mlp_post"][1]
    vec[3] = inp["ple_norm"][1]
    return {
        "gT": gT_own, "h1": h1_own, "pT": f32c(inp["p"][1, b][oi].T), "w_ro": f32c(inp["rec_w_out"][0]),
        "w_up": f32c(inp["mlp_w_up"][1]), "w_dn": f32c(inp["mlp_w_down"][1]),
        "w_pp": f32c(inp["ple_w_proj"][1]), "w_pg": f32c(inp["ple_w_gate"][1]),
        "vec": vec, "ident": np.eye(128, dtype=np.float32),
    }


def full_consts():
    ident = np.eye(128, dtype=np.float32)
    jj = np.arange(128)[:, None]
    ii = np.arange(128)[None, :]
    tri = (jj <= ii).astype(np.float32)
    masks = np.zeros((2, 128, 256), np.float32)
    masks[0, :, 0:128] = tri
    masks[0, :, 128:256] = 1.0
    masks[1, :, 128:256] = tri
    alibi = np.zeros((128, 4, 32), np.float32)
    for h in range(4):
        for idx in range(32):
            d = (idx - 30 - 2) * 128 + np.arange(128)
            alibi[:, h, idx] = np.minimum(SLOPES[h] * d, 0.0)
    return ident, masks, alibi.reshape(128, 128)


def prep_fused(inp, c):
    b, r = c // 2, c % 2
    ident, masks, alibi = full_consts()
    vec = np.zeros((16, D), np.float32)
    vec[0] = inp["ln_mix_pre"][0]
    vec[1] = inp["ln_mix_post"][0]
    vec[2] = inp["ln_mlp_pre"][0]
    vec[3] = inp["ln_mlp_post"][0]
    vec[4] = inp["ple_norm"][0]
    vec[5] = inp["ln_mix_pre"][1]
    vec[6, :128] = inp["diff_subln"][0]
    vec[8] = inp["ln_mix_post"][1]
    vec[9] = inp["ln_mlp_pre"][1]
    vec[10] = inp["ln_mlp_post"][1]
    vec[11] = inp["ple_norm"][1]
    lamv = np.concatenate([inp["diff_lambda_q1"][0], inp["diff_lambda_k1"][0],
                           inp["diff_lambda_q2"][0], inp["diff_lambda_k2"][0]])[None, :]
    rvec = np.zeros((128, 10, 8), np.float32)
    for j in range(10):
        ch = np.arange(j * 128, (j + 1) * 128)
        rvec[:, j, 0:4] = inp["rec_conv_w"][0][:, ch].T
        rvec[:, j, 4] = inp["rec_conv_b"][0][ch]
        rvec[:, j, 5] = inp["rec_bx"][0][ch]
        rvec[:, j, 6] = inp["rec_ba"][0][ch]
        rvec[:, j, 7] = inp["rec_a_param"][0][ch]
    sel = np.zeros((128, 2), np.float32)
    sel[:, r] = 1.0
    w_in1 = inp["rec_w_in"][0]
    return {
        "xf": f32c(inp["x"][b]), "pT": f32c(inp["p"][0, b].T), "pT1": f32c(inp["p"][1, b][r * NOWN:(r + 1) * NOWN].T),
        "w_in": f32c(inp["attn_w_in"][0]), "w_out": f32c(inp["attn_w_out"][0]),
        "w_up": f32c(inp["mlp_w_up"][0]), "w_dn": f32c(inp["mlp_w_down"][0]),
        "w_pp": f32c(inp["ple_w_proj"][0]), "w_pg": f32c(inp["ple_w_gate"][0]),
        "vec": vec, "lamv": f32c(lamv), "bf": f32c(inp["attn_b_forget"][0][:, None]),
        "ident": ident, "masks": masks, "alibi": alibi, "sel": sel,
        "w_g": f32c(w_in1[:, :RW]), "w_x": f32c(w_in1[:, RW:]), "wx": f32c(inp["rec_wx"][0]), "wa": f32c(inp["rec_wa"][0]),
        "rvec": rvec, "w_ro": f32c(inp["rec_w_out"][0]),
        "w_up1": f32c(inp["mlp_w_up"][1]), "w_dn1": f32c(inp["mlp_w_down"][1]),
        "w_pp1": f32c(inp["ple_w_proj"][1]), "w_pg1": f32c(inp["ple_w_gate"][1]),
    }


_NC_CACHE = {}


def _get(name, fn):
    if name not in _NC_CACHE:
        _NC_CACHE[name] = fn()
    return _NC_CACHE[name]


def kernel_unfused(**inputs):
    inp = {k: np.asarray(v) for k, v in inputs.items()}
    cores = list(range(NCORES))
    ncA = _get("A", build_phase_A)
    resA = run_bass_kernel_spmd(ncA, [prep_A(inp, c) for c in cores], core_ids=cores).results
    hnT_full = []
    for b in range(4):
        full = np.zeros((D, S), dtype=resA[0]["HNT"].dtype)
        for r in range(2):
            full[:, own_index(r)] = resA[2 * b + r]["HNT"]
        hnT_full.append(full)
    ncB = _get("B", build_phase_B)
    resB = run_bass_kernel_spmd(ncB, [prep_B(inp, c, hnT_full[c // 2]) for c in cores], core_ids=cores).results
    ncC = _get("C", build_phase_C)
    mapsC = []
    for c in cores:
        b, r = c // 2, c % 2
        oi = own_index(r)
        gfull = np.concatenate([resB[2 * b]["G"], resB[2 * b + 1]["G"]], axis=0)
        mapsC.append(prep_C(inp, c, np.ascontiguousarray(gfull[:, oi]), resA[c]["H1"]))
    resC = run_bass_kernel_spmd(ncC, mapsC, core_ids=cores).results
    out = np.zeros((4, S, D), np.float32)
    for c in cores:
        b, r = c // 2, c % 2
        out[b, own_index(r)] = resC[c]["OUT"]
    return out


def kernel(**inputs):
    inp = {k: np.asarray(v) for k, v in inputs.items()}
    cores = list(range(NCORES))
    nc = _get("F", build_fused)
    res = run_bass_kernel_spmd(nc, [prep_fused(inp, c) for c in cores], core_ids=cores).results
    out = np.zeros((4, S, D), np.float32)
    for c in cores:
        b, r = c // 2, c % 2
        out[b, r * NOWN:(r + 1) * NOWN] = res[c]["OUT"]
    return out
```

```python
import contextlib
import numpy as np
import concourse.bass as bass
import concourse.mybir as mybir
from concourse.bass_utils import run_bass_kernel_spmd

F32 = mybir.dt.float32
BF16 = mybir.dt.bfloat16
AF = mybir.ActivationFunctionType
ALU = mybir.AluOpType
AX = mybir.AxisListType

NCORES = 8
D = 1024
S = 4096
NOWN = 2048
DFF = 4096
RW = 1280
LAM_INIT0 = 0.8 - 0.6 * 1.0
SLOPES = [2.0 ** (-8.0 * (h + 1) / 4) for h in range(4)]


class Prog:
    K_DMA = 8

    def __init__(self, nc, es):
        self.nc = nc
        self.engs = {"pe": nc.tensor, "act": nc.scalar, "dve": nc.vector,
                     "pool": nc.gpsimd, "sp": nc.sync}
        self.semobj = {}
        for k in self.engs:
            self.semobj[k] = es.enter_context(nc.semaphore("sem_" + k))
        self.cnt = {k: 0 for k in self.engs}
        self.seen = {k: {} for k in self.engs}
        self.lastw = {}
        self.readers = {}
        self.dcnt = {}
        for q in ("sp", "act", "pool"):
            self.dcnt[q] = 0
            for i in range(self.K_DMA):
                self.semobj[(q, i)] = es.enter_context(nc.semaphore("d_%s_%d" % (q, i)))

    def _wait(self, eng, ev):
        sk, val = ev
        if self.seen[eng].get(sk, 0) >= val:
            return
        self.engs[eng].wait_ge(self.semobj[sk], val)
        self.seen[eng][sk] = val

    def _deps(self, eng, reads, writes):
        deps = {}
        for k in reads:
            w = self.lastw.get(k)
            if w is not None:
                deps[w[0]] = max(deps.get(w[0], 0), w[1])
        for k in writes:
            w = self.lastw.get(k)
            if w is not None:
                deps[w[0]] = max(deps.get(w[0], 0), w[1])
            for sk, v in self.readers.get(k, {}).items():
                deps[sk] = max(deps.get(sk, 0), v)
        for sk, v in deps.items():
            if eng == "pe" and sk == "pe":
                continue
            self._wait(eng, (sk, v))

    def _record(self, ev, reads, writes):
        for k in writes:
            self.lastw[k] = ev
            self.readers[k] = {}
        for k in reads:
            if k in writes:
                continue
            d = self.readers.setdefault(k, {})
            d[ev[0]] = max(d.get(ev[0], 0), ev[1])

    def op(self, eng, fn, reads=(), writes=()):
        self._deps(eng, reads, writes)
        inst = fn(self.engs[eng])
        self.cnt[eng] += 1
        inst.then_inc(self.semobj[eng], 1)
        self._record((eng, self.cnt[eng]), reads, writes)

    def dma(self, q, out, in_, reads=(), writes=()):
        self._deps(q, reads, writes)
        i = self.dcnt[q]
        s = i % self.K_DMA
        rnd = i // self.K_DMA
        if rnd > 0:
            self._wait(q, ((q, s), 16 * rnd))
        inst = self.engs[q].dma_start(out=out, in_=in_)
        inst.then_inc(self.semobj[(q, s)], 16)
        self.dcnt[q] = i + 1
        self._record(((q, s), 16 * (rnd + 1)), reads, writes)

    def barrier(self):
        evs = []
        for q in ("sp", "act", "pool"):
            n = self.dcnt[q]
            for s in range(self.K_DMA):
                m = (n - s + self.K_DMA - 1) // self.K_DMA if n > s else 0
                if m > 0:
                    evs.append(((q, s), 16 * m))
        for k in ("pe", "act", "dve", "pool", "sp"):
            if self.cnt[k] > 0:
                evs.append((k, self.cnt[k]))
        for eng in ("pe", "act", "dve", "pool", "sp"):
            for ev in evs:
                if ev[0] == eng:
                    continue
                self._wait(eng, ev)

    def barrier_keys(self, eng, keys):
        self._deps(eng, [], list(keys))

    def finish(self):
        for q in ("sp", "act", "pool"):
            n = self.dcnt[q]
            for s in range(self.K_DMA):
                m = (n - s + self.K_DMA - 1) // self.K_DMA if n > s else 0
                if m > 0:
                    self._wait("sp", ((q, s), 16 * m))
        for k in ("pe", "act", "dve", "pool"):
            if self.cnt[k] > 0:
                self._wait("sp", (k, self.cnt[k]))


class Ctx:
    pass


@contextlib.contextmanager
def scope(P):
    with contextlib.ExitStack() as es:
        yield es
        P.barrier()


def setup_common(nc, P, es, C, ident_d):
    C.nc = nc
    C.P = P
    C.identf = es.enter_context(nc.sbuf_tensor("identf", [128, 128], F32))
    C.identb = es.enter_context(nc.sbuf_tensor("identb", [128, 128], BF16))
    C.onesf = es.enter_context(nc.sbuf_tensor("onesf", [128, 512], F32))
    P.dma("sp", C.identf[:], ident_d, writes=["identf"])
    P.dma("pool", C.identb[:], ident_d, writes=["identb"])
    P.op("pool", lambda e: e.memset(C.onesf[:], 1.0), writes=["onesf"])
    C.ps = [es.enter_context(nc.psum_tensor("ps%d" % i, [128, 1024], F32)) for i in range(4)]
    C.psb = [t.bitcast(BF16) for t in C.ps]
    C.junk = es.enter_context(nc.sbuf_tensor("junk", [128, 1024], BF16))
    C.junk2 = es.enter_context(nc.sbuf_tensor("junk2", [128, 128], BF16))
    C.small = es.enter_context(nc.sbuf_tensor("small", [128, 64], F32))
    C.small_i = 0


def small_slot(C, n=1):
    i = C.small_i
    if i + n > 64:
        i = 0
    C.small_i = i + n
    return i


def psf(C, b):
    return C.ps[b // 2][:, (b % 2) * 512:(b % 2) * 512 + 512]


def psbf(C, b):
    return C.psb[b // 2][:, (b % 2) * 1024:(b % 2) * 1024 + 1024]


def rstd_from(C, src, src_keys, n, eps, eng="act"):
    P = C.P
    i = small_slot(C)
    ss = C.small[:, i:i + 1]
    key = ("small", i)
    if eng == "dve":
        P.op("dve", lambda e: e.scalar_tensor_tensor(out=C.junk2[:, 0:n], in0=src, scalar=1.0, in1=src,
                                                     op0=ALU.mult, op1=ALU.mult, accum_out=ss),
             reads=list(src_keys), writes=["junk2", key])
    else:
        P.op("act", lambda e: e.activation(out=C.junk[:, 0:n], in_=src, func=AF.Square, accum_out=ss),
             reads=list(src_keys), writes=["junk", key])
    P.op("dve", lambda e: e.tensor_scalar(out=ss, in0=ss, scalar1=1.0 / n, scalar2=float(eps), op0=ALU.mult, op1=ALU.add),
         reads=[key], writes=[key])
    P.op("pool", lambda e: e.tensor_tensor(out=ss, in0=ss, in1=C.negh, op=ALU.pow), reads=[key, "epsb"], writes=[key])
    return ss, key


def transpose_to(C, src_bf, src_keys, dst, dst_keys, bank, nk=8, evac="dve"):
    P = C.P
    pk = ("ps", bank)
    pv = psbf(C, bank)
    for k in range(nk):
        P.op("pe", lambda e, k=k: e.transpose(out=pv[:, k * 128:(k + 1) * 128], in_=src_bf[:, k * 128:(k + 1) * 128],
                                              identity=C.identb[:]),
             reads=list(src_keys) + ["identb"], writes=[pk])
    srcv = pv[:, 0:nk * 128].rearrange("p (k t) -> p k t", k=nk)
    if evac == "act":
        P.op("act", lambda e: e.copy(out=dst, in_=srcv), reads=[pk], writes=list(dst_keys))
    else:
        P.op("dve", lambda e: e.tensor_copy(out=dst, in_=srcv), reads=[pk], writes=list(dst_keys))


def load_w_bf16(C, dst, dram2d, nk, key, nsplit=1):
    P = C.P
    src = dram2d.rearrange("(k p) n -> p k n", p=128)
    step = nk // nsplit
    for i in range(nsplit):
        P.dma("pool", dst[:, i * step:(i + 1) * step, :], src[:, i * step:(i + 1) * step, :], writes=[(key, i)])
    return [(key, i) for i in range(nsplit)]


def emit_norm_residual(C, m_src, m_keys, gain, gain_key, res, res_keys, out, out_keys):
    P = C.P
    rs, rk = rstd_from(C, m_src, m_keys, D, 1e-6)
    P.op("dve", lambda e: e.scalar_tensor_tensor(out=out, in0=m_src, scalar=rs, in1=gain, op0=ALU.mult, op1=ALU.mult),
         reads=list(m_keys) + [rk, gain_key], writes=list(out_keys))
    P.op("pool", lambda e: e.tensor_tensor(out=out, in0=out, in1=res, op=ALU.add),
         reads=list(out_keys) + list(res_keys), writes=list(out_keys))


def emit_norm_bf16(C, src, src_keys, gain, gain_key, out_bf, out_keys):
    P = C.P
    rs, rk = rstd_from(C, src, src_keys, D, 1e-6)
    P.op("dve", lambda e: e.scalar_tensor_tensor(out=out_bf, in0=src, scalar=rs, in1=gain, op0=ALU.mult, op1=ALU.mult),
         reads=list(src_keys) + [rk, gain_key], writes=list(out_keys))


def load_gains(C, es, vec_d, rows, name):
    nc, P = C.nc, C.P
    g = es.enter_context(nc.sbuf_tensor(name, [128, len(rows), D], F32))
    for i, r in enumerate(rows):
        P.dma("sp", g[:, i, :], vec_d[r, :].partition_broadcast(128), writes=[(name, i)])
    return g


def setup_eps(C, es):
    nc, P = C.nc, C.P
    t = es.enter_context(nc.sbuf_tensor("epsb", [128, 3], F32))
    P.op("pool", lambda e: e.memset(t[:, 2:3], -0.5), writes=["epsb"])
    C.negh = t[:, 2:3]
    P.op("pool", lambda e: e.memset(t[:, 0:1], 1e-6), writes=["epsb"])
    P.op("pool", lambda e: e.memset(t[:, 1:2], 1e-5), writes=["epsb"])
    C.epsb = {1e-6: t[:, 0:1], 1e-5: t[:, 1:2]}


def emit_mlp(C, H, w_up_d, w_dn_d, vec_d, row_pre, row_post, tag, ntok=NOWN):
    nc, P = C.nc, C.P
    MT = 256
    with scope(P) as es:
        wup = es.enter_context(nc.sbuf_tensor("wup" + tag, [128, 8, DFF], BF16))
        wdn = es.enter_context(nc.sbuf_tensor("wdn" + tag, [128, 32, D], BF16))
        wup_src = w_up_d.rearrange("(k p) n -> p k n", p=128)
        kup = []
        for i in range(8):
            P.dma("pool", wup[:, :, i * 512:(i + 1) * 512], wup_src[:, :, i * 512:(i + 1) * 512], writes=[("wup" + tag, i)])
            kup.append(("wup" + tag, i))
        kdn = load_w_bf16(C, wdn, w_dn_d, 32, "wdn" + tag, nsplit=8)
        g = load_gains(C, es, vec_d, [row_pre, row_post], "gm" + tag)
        ha = es.enter_context(nc.sbuf_tensor("ha" + tag, [128, 2, 2, D], F32))
        ubf = es.enter_context(nc.sbuf_tensor("ubf" + tag, [128, 2, D], BF16))
        uT = es.enter_context(nc.sbuf_tensor("uT" + tag, [128, 2, 8, MT], BF16))
        hid = es.enter_context(nc.sbuf_tensor("hid" + tag, [128, 32, MT], BF16))
        rl = es.enter_context(nc.sbuf_tensor("rl" + tag, [128, 2, MT], F32))
        hb = es.enter_context(nc.sbuf_tensor("hb" + tag, [128, 2, D], F32))
        nmt = ntok // MT

        def prologue(t):
            sl = t % 2
            for e in range(2):
                lt = 2 * t + e
                P.dma("sp", ha[:, sl, e, :], H[lt * 128:(lt + 1) * 128, :], reads=[("H", lt)], writes=[("ha", sl, e)])
                emit_norm_bf16(C, ha[:, sl, e, :], [("ha", sl, e)], g[:, 0, :], ("gm" + tag, 0), ubf[:, e, :], [("ubf", e)])
                transpose_to(C, ubf[:, e, :], [("ubf", e)], uT[:, sl, :, e * 128:(e + 1) * 128], [("uT", sl, e)],
                             bank=e, evac="act")

        def up(t):
            sl = t % 2
            for c in range(32):
                bank = c % 4
                pk = ("ps", bank)
                pv = psf(C, bank)[:, 0:MT]
                for k in range(8):
                    P.op("pe", lambda e: e.matmul(pv, lhsT=wup[:, k, c * 128:(c + 1) * 128], rhs=uT[:, sl, k, :],
                                                  start=(k == 0), stop=(k == 7)),
                         reads=[kup[c // 4], ("uT", sl, 0), ("uT", sl, 1)], writes=[pk])
                rs_ = c % 2
                P.op("act", lambda e: e.activation(out=rl[:, rs_, :], in_=pv, func=AF.Relu),
                     reads=[pk], writes=[("rl", rs_)])
                P.op("pool", lambda e: e.tensor_tensor(out=hid[:, c, :], in0=rl[:, rs_, :], in1=rl[:, rs_, :], op=ALU.mult),
                     reads=[("rl", rs_)], writes=[("hid", c)])

        def down(t):
            sl = t % 2
            for e in range(2):
                lt = 2 * t + e
                for hf in range(2):
                    pk = ("ps", 4 + 2 * e + hf)
                    pv = psf(C, 4 + 2 * e + hf)
                    for c in range(32):
                        P.op("pe", lambda e_: e_.matmul(pv, lhsT=hid[:, c, e * 128:(e + 1) * 128],
                                                        rhs=wdn[:, c, hf * 512:(hf + 1) * 512],
                                                        start=(c == 0), stop=(c == 31)),
                             reads=[("hid", c), kdn[c // 4]], writes=[pk])
                fv = C.ps[2 + e][:, :]
                emit_norm_residual(C, fv, [("ps", 4 + 2 * e), ("ps", 5 + 2 * e)], g[:, 1, :], ("gm" + tag, 1),
                                   ha[:, sl, e, :], [("ha", sl, e)], hb[:, e, :], [("hb", e)])
                P.dma("sp", H[lt * 128:(lt + 1) * 128, :], hb[:, e, :], reads=[("hb", e)], writes=[("H", lt)])

        prologue(0)
        for t in range(nmt):
            up(t)
            if t + 1 < nmt:
                prologue(t + 1)
            down(t)


def emit_ple(C, H, pT_d, w_pp_d, w_pg_d, vec_d, row_ple, OUT, tag, hn_row=None, HNT=None, ntiles=16):
    nc, P = C.nc, C.P
    with scope(P) as es:
        wpg = es.enter_context(nc.sbuf_tensor("wpg" + tag, [128, 8, D], BF16))
        wpp = es.enter_context(nc.sbuf_tensor("wpp" + tag, [128, 2, D], BF16))
        kpg = load_w_bf16(C, wpg, w_pg_d, 8, "wpg" + tag, nsplit=2)
        kpp = load_w_bf16(C, wpp, w_pp_d, 2, "wpp" + tag, nsplit=1)
        rows = [row_ple] + ([hn_row] if hn_row is not None else [])
        g = load_gains(C, es, vec_d, rows, "gp" + tag)
        hb = es.enter_context(nc.sbuf_tensor("phb" + tag, [128, 3, D], F32))
        hbb = es.enter_context(nc.sbuf_tensor("phbb" + tag, [128, 2, D], BF16))
        hbT = es.enter_context(nc.sbuf_tensor("phbT" + tag, [128, 2, 8, 128], BF16))
        pT = es.enter_context(nc.sbuf_tensor("ppT" + tag, [128, 3, 2, 128], BF16))
        sg = es.enter_context(nc.sbuf_tensor("psg" + tag, [128, 2, D], F32))
        ee = es.enter_context(nc.sbuf_tensor("pee" + tag, [128, 2, D], F32))
        hnb = es.enter_context(nc.sbuf_tensor("phnb" + tag, [128, 2, D], BF16))
        hnT = es.enter_context(nc.sbuf_tensor("phnT" + tag, [128, 2, 8, 128], BF16))
        pTv = pT_d.rearrange("(k f) t -> f k t", f=128)
        def loads(lt):
            s3 = lt % 3
            P.dma("sp", hb[:, s3, :], H[lt * 128:(lt + 1) * 128, :], reads=[("H", lt)], writes=[("phb", s3)])
            P.dma("pool", pT[:, s3, :, :], pTv[:, :, lt * 128:(lt + 1) * 128], writes=[("ppT", s3)])

        def front(lt):
            sl = lt % 2
            s3 = lt % 3
            if lt + 1 < ntiles:
                loads(lt + 1)
            P.op("dve", lambda e: e.tensor_copy(out=hbb[:, sl, :], in_=hb[:, s3, :]), reads=[("phb", s3)], writes=[("phbb", sl)])
            transpose_to(C, hbb[:, sl, :], [("phbb", sl)], hbT[:, sl, :, :], [("phbT", sl)], bank=6, evac="act")
            for hf in range(2):
                pk = ("ps", hf)
                pv = psf(C, hf)
                for k in range(8):
                    P.op("pe", lambda e: e.matmul(pv, lhsT=hbT[:, sl, k, :], rhs=wpg[:, k, hf * 512:(hf + 1) * 512],
                                                  start=(k == 0), stop=(k == 7)),
                         reads=[("phbT", sl), kpg[k // 4]], writes=[pk])
            for hf in range(2):
                pk = ("ps", 2 + 2 * sl + hf)
                pv = psf(C, 2 + 2 * sl + hf)
                for k in range(2):
                    P.op("pe", lambda e: e.matmul(pv, lhsT=pT[:, s3, k, :], rhs=wpp[:, k, hf * 512:(hf + 1) * 512],
                                                  start=(k == 0), stop=(k == 1)),
                         reads=[("ppT", s3), kpp[0]], writes=[pk])

        def front_b(lt):
            sl = lt % 2
            P.op("act", lambda e: e.activation(out=sg[:, sl, :], in_=C.ps[0][:, :], func=AF.Sigmoid),
                 reads=[("ps", 0), ("ps", 1)], writes=[("psg", sl)])

        def back(lt):
            sl = lt % 2
            ev = C.ps[1 + sl][:, :]
            pks = [("ps", 2 + 2 * sl), ("ps", 3 + 2 * sl)]
            rs, rk = rstd_from(C, ev, pks, D, 1e-6)
            P.op("dve", lambda e: e.scalar_tensor_tensor(out=ee[:, sl, :], in0=ev, scalar=rs, in1=g[:, 0, :], op0=ALU.mult, op1=ALU.mult),
                 reads=pks + [rk, ("gp" + tag, 0)], writes=[("pee", sl)])
            P.op("pool", lambda e: e.tensor_tensor(out=ee[:, sl, :], in0=ee[:, sl, :], in1=sg[:, sl, :], op=ALU.mult),
                 reads=[("pee", sl), ("psg", sl)], writes=[("pee", sl)])
            P.op("dve", lambda e: e.tensor_tensor(out=ee[:, sl, :], in0=ee[:, sl, :], in1=hb[:, lt % 3, :], op=ALU.add),
                 reads=[("pee", sl), ("phb", lt % 3)], writes=[("pee", sl)])
            P.dma("sp", OUT[lt * 128:(lt + 1) * 128, :], ee[:, sl, :], reads=[("pee", sl)], writes=[("OUT", lt)])

        def hnpart(lt):
            sl = lt % 2
            emit_norm_bf16(C, ee[:, sl, :], [("pee", sl)], g[:, 1, :], ("gp" + tag, 1), hnb[:, sl, :], [("phnb", sl)])
            transpose_to(C, hnb[:, sl, :], [("phnb", sl)], hnT[:, sl, :, :], [("phnT", sl)], bank=7, evac="dve")
            P.dma("sp", HNT.rearrange("(k f) t -> f k t", f=128)[:, :, lt * 128:(lt + 1) * 128], hnT[:, sl, :, :],
                  reads=[("phnT", sl)], writes=[("HNT", lt)])

        loads(0)
        front(0)
        front_b(0)
        for lt in range(ntiles):
            if lt + 1 < ntiles:
                front(lt + 1)
            back(lt)
            if hn_row is not None and lt >= 1:
                hnpart(lt - 1)
            if lt + 1 < ntiles:
                front_b(lt + 1)
        if hn_row is not None:
            hnpart(ntiles - 1)


def emit_attention(C, T, H, stage=9, full=False):
    nc, P = C.nc, C.P
    with scope(P) as es:
        g = load_gains(C, es, T["vec"], [0, 1], "ga")
        hnTf = es.enter_context(nc.sbuf_tensor("hnTf", [128, 8, S], BF16))
        NT = 32 if full else 16
        NCH = NT // 2
        KPC = 2 if full else 4
        NM = 2 if full else 4
        if full:
            hnTo = hnTf
        else:
            hnTo = es.enter_context(nc.sbuf_tensor("hnTo", [128, 8, NOWN], BF16))
        obuf = es.enter_context(nc.sbuf_tensor("obuf", [128, NT, D], BF16))
        masks = es.enter_context(nc.sbuf_tensor("masks_sb", [128, NM, 256], BF16))
        alibi = es.enter_context(nc.sbuf_tensor("alibi_sb", [128, 4, 32], F32))
        fb = es.enter_context(nc.sbuf_tensor("fb", [128, 2, 2, 32], F32))
        Stm = es.enter_context(nc.sbuf_tensor("Stm", [128, 32, 8], F32))
        Xb = es.enter_context(nc.sbuf_tensor("Xb", [128, NCH * 8], F32))
        neglam = es.enter_context(nc.sbuf_tensor("neglam", [128, 4], F32))
        subg = es.enter_context(nc.sbuf_tensor("subg", [128, 128], F32))
        P.dma("pool", masks[:], T["masks"].rearrange("m p q -> p m q"), writes=["masks"])
        P.dma("sp", alibi[:], T["alibi"].rearrange("p (h r) -> p h r", h=4), writes=["alibi"])
        P.dma("sp", subg[:], T["vec"][6, 0:128].partition_broadcast(128), writes=["subg"])
        P.op("dve", lambda e: e.tensor_scalar(out=subg[:], in0=subg[:], scalar1=1.0 - LAM_INIT0, scalar2=None, op0=ALU.mult),
             reads=["subg"], writes=["subg"])

        with scope(P) as es2:
            lv = es2.enter_context(nc.sbuf_tensor("lv", [1, 256], F32))
            pr = es2.enter_context(nc.sbuf_tensor("pr", [1, 128], F32))
            dots = es2.enter_context(nc.sbuf_tensor("dots", [1, 2], F32))
            P.dma("sp", lv[:], T["lamv"], writes=["lv"])
            P.op("dve", lambda e: e.tensor_tensor(out=pr[:, 0:64], in0=lv[:, 0:64], in1=lv[:, 64:128], op=ALU.mult),
                 reads=["lv"], writes=["pr"])
            P.op("dve", lambda e: e.tensor_tensor(out=pr[:, 64:128], in0=lv[:, 128:192], in1=lv[:, 192:256], op=ALU.mult),
                 reads=["lv", "pr"], writes=["pr"])
            P.op("dve", lambda e: e.tensor_reduce(out=dots[:, 0:2], in_=pr[:, :].rearrange("p (a d) -> p a d", a=2),
                                                  axis=AX.X, op=ALU.add), reads=["pr"], writes=["dots"])
            pv = psf(C, 0)[:, 0:2]
            P.op("pe", lambda e: e.matmul(pv, lhsT=C.onesf[0:1, 0:128], rhs=dots[0:1, 0:2], start=True, stop=True),
                 reads=["onesf", "dots"], writes=[("ps", 0)])
            P.op("act", lambda e: e.activation(out=neglam[:, 0:2], in_=pv, func=AF.Exp), reads=[("ps", 0)], writes=["neglam"])
            P.op("dve", lambda e: e.tensor_tensor(out=neglam[:, 2:3], in0=neglam[:, 1:2], in1=neglam[:, 0:1], op=ALU.subtract),
                 reads=["neglam"], writes=["neglam"])
            P.op("dve", lambda e: e.tensor_scalar(out=neglam[:, 2:3], in0=neglam[:, 2:3], scalar1=-LAM_INIT0, scalar2=None, op0=ALU.add),
                 reads=["neglam"], writes=["neglam"])

        with scope(P) as es2:
            xt = es2.enter_context(nc.sbuf_tensor("xt", [128, 2, D], F32))
            xb = es2.enter_context(nc.sbuf_tensor("xb", [128, 2, D], BF16))
            for i in range(32 if full else 48):
                sl = i % 2
                if i < 32:
                    src = T["xf"][i * 128:(i + 1) * 128, :]
                    dst = hnTf[:, :, i * 128:(i + 1) * 128]
                    dk = ("hnTf", i // 4)
                else:
                    j = i - 32
                    src = T["xo"][j * 128:(j + 1) * 128, :]
                    dst = hnTo[:, :, j * 128:(j + 1) * 128]
                    dk = ("hnTo", j // 4)
                if full:
                    dk = ("hnTf", i // 4)
                P.dma("sp", xt[:, sl, :], src, writes=[("xt", sl)])
                emit_norm_bf16(C, xt[:, sl, :], [("xt", sl)], g[:, 0, :], ("ga", 0), xb[:, sl, :], [("xb", sl)])
                transpose_to(C, xb[:, sl, :], [("xb", sl)], dst, [dk], bank=6 + sl, evac=("act" if sl else "dve"))

        if stage < 1:
            return
        with scope(P) as es2:
            wfz = es2.enter_context(nc.sbuf_tensor("wfz", [128, 8, 8], BF16))
            negb = es2.enter_context(nc.sbuf_tensor("negb", [8, 1], F32))
            Lf = es2.enter_context(nc.sbuf_tensor("Lf", [8, S], F32))
            Sc = es2.enter_context(nc.sbuf_tensor("Sc", [8, S], F32))
            Dm = es2.enter_context(nc.sbuf_tensor("Dm", [8, NCH, 8], F32))
            P.dma("pool", wfz[:], T["w_in"].rearrange("(k p) n -> p k n", p=128)[:, :, 3072:3080], writes=["wfz"])
            P.dma("sp", negb[:], T["bf"], writes=["negb"])
            P.op("dve", lambda e: e.tensor_scalar(out=negb[:], in0=negb[:], scalar1=-1.0, scalar2=None, op0=ALU.mult),
                 reads=["negb"], writes=["negb"])
            for n in range(8):
                bank = n % 2
                pv = psf(C, bank)[0:8, :]
                for k in range(8):
                    P.op("pe", lambda e, k=k, n=n, pv=pv: e.matmul(pv, lhsT=wfz[:, k, :], rhs=hnTf[:, k, n * 512:(n + 1) * 512],
                                                                   start=(k == 0), stop=(k == 7)),
                         reads=["wfz", ("hnTf", n)], writes=[("ps", bank)])
                P.op("act", lambda e, n=n, pv=pv: e.activation(out=Lf[:, n * 512:(n + 1) * 512], in_=pv, func=AF.Exp, scale=-1.0, bias=negb[:, 0:1]),
                     reads=[("ps", bank), "negb"], writes=[("Lf", n)])
            for n in range(8):
                P.op("act", lambda e, n=n: e.activation(out=Lf[:, n * 512:(n + 1) * 512], in_=Lf[:, n * 512:(n + 1) * 512], func=AF.Ln, bias=1.0),
                     reads=[("Lf", n)], writes=[("Lf", n)])
            for n in range(8):
                init = 0.0 if n == 0 else Sc[:, n * 512 - 1:n * 512]
                rd = [("Lf", n), "onesf"] + ([("Sc", n - 1)] if n else [])
                P.op("dve", lambda e, n=n, init=init: e.tensor_tensor_scan(out=Sc[:, n * 512:(n + 1) * 512], data0=C.onesf[0:8, :],
                                                                           data1=Lf[:, n * 512:(n + 1) * 512], initial=init,
                                                                           op0=ALU.mult, op1=ALU.add),
                     reads=rd, writes=[("Sc", n)])
            pvt = psf(C, 2)[:, 0:256]
            for kb in range(32):
                P.op("pe", lambda e, kb=kb: e.transpose(out=pvt[:, kb * 8:(kb + 1) * 8], in_=Sc[0:8, kb * 128:(kb + 1) * 128],
                                                        identity=C.identf[0:8, 0:8]),
                     reads=[("Sc", kb // 4), "identf"], writes=[("ps", 2)])
            P.op("dve", lambda e: e.tensor_copy(out=Stm[:, :, :], in_=pvt.rearrange("p (k h) -> p k h", h=8)),
                 reads=[("ps", 2)], writes=["Stm"])
            CW = 128 * KPC
            ssel = Sc[:, :].rearrange("h (p t) -> h p t", t=CW)[:, :, CW - 1:CW]
            P.op("dve", lambda e: e.tensor_tensor(out=Dm[:, :, :], in0=ssel.to_broadcast([8, NCH, 8]),
                                                  in1=C.identf[0:8, 0:8].unsqueeze(1).to_broadcast([8, NCH, 8]), op=ALU.mult),
                 reads=[("Sc", n) for n in range(8)] + ["identf"], writes=["Dm"])
            pvx = psf(C, 3)[:, 0:NCH * 8]
            P.op("pe", lambda e: e.matmul(pvx, lhsT=C.onesf[0:8, 0:128], rhs=Dm[:, :, :].rearrange("h p g -> h (p g)"), start=True, stop=True),
                 reads=["onesf", "Dm"], writes=[("ps", 3)])
            P.op("dve", lambda e: e.tensor_copy(out=Xb[:, :], in_=pvx), reads=[("ps", 3)], writes=["Xb"])
        if stage < 2:
            return
        with scope(P) as es2:
            wq = es2.enter_context(nc.sbuf_tensor("wq", [128, 2, 8, 128], BF16))
            wk = es2.enter_context(nc.sbuf_tensor("wk", [128, 2, 8, 128], BF16))
            wv = es2.enter_context(nc.sbuf_tensor("wv", [128, 2, 8, 128], BF16))
            KT = es2.enter_context(nc.sbuf_tensor("KT", [128, S], BF16))
            QT = es2.enter_context(nc.sbuf_tensor("QTz", [128, 2, NT * 128], BF16))
            Vb = es2.enter_context(nc.sbuf_tensor("Vb", [128, 32, 130], BF16))
            Vd = Vb[:, :, 0:129]
            Vf = Vb[:, :, :].rearrange("p k (h d) -> p k h d", h=2)
            PT = es2.enter_context(nc.sbuf_tensor("PT", [128, 4, 2, 256], BF16))
            t1 = es2.enter_context(nc.sbuf_tensor("t1", [128, 2, 128], F32))
            dd = es2.enter_context(nc.sbuf_tensor("dd", [128, 2, 128], F32))
            rr = es2.enter_context(nc.sbuf_tensor("rr", [128, 2, 8], F32))
            P.op("pool", lambda e: e.memset(QT[:, :, :], 0.0), writes=[("QT", n) for n in range(NT // 4)])
            w_in_v = T["w_in"].rearrange("(k p) n -> p k n", p=128)
            rr_i = [0]
            for gidx in range(8):
                ws = gidx % 2
                diff = gidx < 4
                if gidx in (0, 4):
                    vk = [("V", kq) for kq in range(8)]
                    P.op("pool", lambda e: e.memset(Vb[:, :, :], 1.0), writes=vk)
                if diff:
                    qc, kc, vc = gidx * 128, 512 + gidx * 128, 1024 + gidx * 128
                else:
                    qc, kc, vc = 1536 + (gidx - 4) * 128, 2048 + (gidx - 4) * 128, 2560 + (gidx - 4) * 128
                P.dma("pool", wq[:, ws, :, :], w_in_v[:, :, qc:qc + 128], writes=[("wq", ws)])
                P.dma("pool", wk[:, ws, :, :], w_in_v[:, :, kc:kc + 128], writes=[("wk", ws)])
                P.dma("pool", wv[:, ws, :, :], w_in_v[:, :, vc:vc + 128], writes=[("wv", ws)])
                for n in range(8):
                    bank = 2 * (n % 2)
                    pv = psf(C, bank)
                    for k in range(8):
                        P.op("pe", lambda e, k=k, n=n, pv=pv: e.matmul(pv, lhsT=wk[:, ws, k, :], rhs=hnTf[:, k, n * 512:(n + 1) * 512],
                                                                       start=(k == 0), stop=(k == 7)),
                             reads=[("wk", ws), ("hnTf", n)], writes=[("ps", bank)])
                    P.op("dve", lambda e, n=n, pv=pv: e.tensor_copy(out=KT[:, n * 512:(n + 1) * 512], in_=pv),
                         reads=[("ps", bank)], writes=[("KT", n)])
                for n in range(NT // 4):
                    bank = 2 * (n % 2)
                    pv = psf(C, bank)
                    for k in range(8):
                        P.op("pe", lambda e, k=k, n=n, pv=pv: e.matmul(pv, lhsT=wq[:, ws, k, :], rhs=hnTo[:, k, n * 512:(n + 1) * 512],
                                                                       start=(k == 0), stop=(k == 7)),
                             reads=[("wq", ws), (("hnTf" if full else "hnTo"), n)], writes=[("ps", bank)])
                    P.op("dve", lambda e: e.tensor_copy(out=QT[0:64, 0, n * 512:(n + 1) * 512], in_=pv[0:64, :]),
                         reads=[("ps", bank)], writes=[("QT", n)])
                    P.op("dve", lambda e: e.tensor_copy(out=QT[64:128, 1, n * 512:(n + 1) * 512], in_=pv[64:128, :]),
                         reads=[("ps", bank), ("QT", n)], writes=[("QT", n)])
                for kq in range(8):
                    bank = 2 * (kq % 2)
                    pv = psf(C, bank)
                    for j in range(4):
                        kb = kq * 4 + j
                        for k in range(8):
                            P.op("pe", lambda e, k=k, kb=kb, j=j, pv=pv: e.matmul(pv[:, j * 128:(j + 1) * 128], lhsT=hnTf[:, k, kb * 128:(kb + 1) * 128],
                                                                                  rhs=wv[:, ws, k, :], start=(k == 0), stop=(k == 7)),
                                 reads=[("wv", ws), ("hnTf", kb // 4)], writes=[("ps", bank)])
                    if diff:
                        dst = Vd[:, kq * 4:(kq + 1) * 4, 0:128]
                        srcv = pv.rearrange("p (j d) -> p j d", j=4)
                        vkey = ("V", kq)
                    else:
                        dst = Vf[:, kq * 4:(kq + 1) * 4, :, 0:64]
                        srcv = pv.rearrange("p (j h d) -> p j h d", j=4, h=2)
                        vkey = ("V", kq)
                    P.op("dve", lambda e, dst=dst, srcv=srcv: e.tensor_copy(out=dst, in_=srcv), reads=[("ps", bank)], writes=[vkey])

                for p in range(NCH):
                    nkb = KPC * p + KPC
                    kb0 = KPC * p
                    oset = p % 2
                    if not diff:
                        fsl = p % 2
                        for sh in range(2):
                            hh_ = 2 * (gidx - 4) + sh
                            P.op("dve", lambda e: e.tensor_scalar(out=fb[:, fsl, sh, 0:nkb], in0=Stm[:, 0:nkb, hh_],
                                                                  scalar1=Xb[:, p * 8 + hh_:p * 8 + hh_ + 1], scalar2=None, op0=ALU.subtract),
                                 reads=["Stm", "Xb"], writes=[("fb", fsl)])
                    def emit_S(kb, p=p):
                        slot = kb % 4
                        for sh in range(2):
                            pvS = psf(C, slot)[:, sh * 256:(sh + 1) * 256]
                            P.op("pe", lambda e: e.matmul(pvS, lhsT=KT[:, kb * 128:(kb + 1) * 128],
                                                          rhs=QT[:, sh, p * 256:(p + 1) * 256], start=True, stop=True),
                                 reads=[("KT", kb // 4), ("QT", p // 2)], writes=[("ps", slot)])

                    def emit_PV(kb, p=p, nkb=nkb, oset=oset):
                        slot = kb % 4
                        for sh in range(2):
                            if diff:
                                ai = kb - kb0 + (30 if full else 28)
                                bias = alibi[:, gidx, ai:ai + 1]
                                bk = "alibi"
                            else:
                                bias = fb[:, p % 2, sh, kb:kb + 1]
                                bk = ("fb", p % 2)
                            P.op("act", lambda e: e.activation(out=PT[:, slot, sh, :], in_=psf(C, slot)[:, sh * 256:(sh + 1) * 256],
                                                               func=AF.Exp, scale=0.125, bias=bias),
                                 reads=[("ps", slot), bk], writes=[("PT", slot, sh)])
                        if kb >= kb0:
                            mk = masks[:, kb - kb0, :].unsqueeze(1).to_broadcast([128, 2, 256])
                            P.op("pool", lambda e: e.tensor_tensor(out=PT[:, slot, :, :], in0=PT[:, slot, :, :], in1=mk, op=ALU.mult),
                                 reads=[("PT", slot, 0), ("PT", slot, 1), "masks"], writes=[("PT", slot, 0), ("PT", slot, 1)])
                        for sh in range(2):
                            obank = 4 + oset * 2 + sh
                            for e_ in range(2):
                                if diff:
                                    ov = psf(C, obank)[:, e_ * 129:(e_ + 1) * 129]
                                    rhs = Vd[:, kb, :]
                                else:
                                    ov = psf(C, obank)[:, e_ * 65:(e_ + 1) * 65]
                                    rhs = Vf[:, kb, sh, :]
                                first = (kb == 0 and e_ == 0)
                                P.op("pe", lambda e: e.matmul(ov, lhsT=PT[:, slot, sh, e_ * 128:(e_ + 1) * 128], rhs=rhs,
                                                              start=first, stop=(kb == nkb - 1), skip_group_check=True),
                                     reads=[("PT", slot, sh), ("V", kb // 4)], writes=[("ps", obank)])

                    LOOK = 2
                    for kb in range(min(LOOK, nkb)):
                        emit_S(kb)
                    for kb in range(nkb):
                        if kb + LOOK < nkb:
                            emit_S(kb + LOOK)
                        emit_PV(kb)

                    b0 = 4 + oset * 2
                    i0 = rr_i[0] % 2
                    rr_i[0] += 1
                    rk = ("rr", i0)
                    if diff:
                        O1 = psf(C, b0)[:, 0:258].rearrange("p (e d) -> p e d", e=2)
                        O2 = psf(C, b0 + 1)[:, 0:258].rearrange("p (e d) -> p e d", e=2)
                        P.op("dve", lambda e: e.reciprocal(out=rr[:, i0, 0:2], in_=O1[:, :, 128]), reads=[("ps", b0)], writes=[rk])
                        P.op("dve", lambda e: e.reciprocal(out=rr[:, i0, 2:4], in_=O2[:, :, 128]), reads=[("ps", b0 + 1)], writes=[rk])
                        P.op("dve", lambda e: e.tensor_scalar(out=rr[:, i0, 2:4], in0=rr[:, i0, 2:4], scalar1=neglam[:, 2:3], scalar2=None, op0=ALU.mult),
                             reads=[rk, "neglam"], writes=[rk])
                        for e_ in range(2):
                            lt = 2 * p + e_
                            P.op("dve", lambda e, e_=e_: e.tensor_scalar(out=t1[:, e_, :], in0=O1[:, e_, 0:128], scalar1=rr[:, i0, e_:e_ + 1],
                                                                         scalar2=None, op0=ALU.mult),
                                 reads=[("ps", b0), rk], writes=[("t1", e_)])
                            P.op("dve", lambda e, e_=e_: e.scalar_tensor_tensor(out=dd[:, e_, :], in0=O2[:, e_, 0:128], scalar=rr[:, i0, 2 + e_:3 + e_],
                                                                                in1=t1[:, e_, :], op0=ALU.mult, op1=ALU.add),
                                 reads=[("ps", b0 + 1), rk, ("t1", e_)], writes=[("dd", e_)])
                            rs, rsk = rstd_from(C, dd[:, e_, :], [("dd", e_)], 128, 1e-5, eng="dve")
                            P.op("dve", lambda e, e_=e_, lt=lt, rs=rs: e.scalar_tensor_tensor(out=obuf[:, lt, gidx * 128:(gidx + 1) * 128], in0=dd[:, e_, :],
                                                                                             scalar=rs, in1=subg[:, :], op0=ALU.mult, op1=ALU.mult),
                                 reads=[("dd", e_), rsk, "subg"], writes=[("obuf", lt)])
                    else:
                        for sh in range(2):
                            Ov = psf(C, b0 + sh)[:, 0:130].rearrange("p (e d) -> p e d", e=2)
                            P.op("dve", lambda e, sh=sh, Ov=Ov: e.reciprocal(out=rr[:, i0, 4 + 2 * sh:6 + 2 * sh], in_=Ov[:, :, 64]),
                                 reads=[("ps", b0 + sh)], writes=[rk])
                            for e_ in range(2):
                                lt = 2 * p + e_
                                c0 = 512 + (2 * (gidx - 4) + sh) * 64
                                P.op("dve", lambda e, sh=sh, e_=e_, lt=lt, c0=c0, Ov=Ov: e.tensor_scalar(out=obuf[:, lt, c0:c0 + 64], in0=Ov[:, e_, 0:64],
                                                                                                       scalar1=rr[:, i0, 4 + 2 * sh + e_:5 + 2 * sh + e_],
                                                                                                       scalar2=None, op0=ALU.mult),
                                     reads=[("ps", b0 + sh), rk], writes=[("obuf", lt)])

        if stage < 3:
            return
        with scope(P) as es2:
            wo = es2.enter_context(nc.sbuf_tensor("wo", [128, 8, D], BF16))
            kwo = load_w_bf16(C, wo, T["w_out"], 8, "wo", nsplit=2)
            oT = es2.enter_context(nc.sbuf_tensor("oT", [128, 2, 8, 128], BF16))
            xr = es2.enter_context(nc.sbuf_tensor("xres", [128, 2, D], F32))
            hh = es2.enter_context(nc.sbuf_tensor("hh", [128, 2, D], F32))
            xsrc = T["xf" if full else "xo"]
            P.dma("sp", xr[:, 0, :], xsrc[0:128, :], writes=[("xres", 0)])
            for lt in range(NT):
                sl = lt % 2
                if lt + 1 < NT:
                    P.dma("sp", xr[:, 1 - sl, :], xsrc[(lt + 1) * 128:(lt + 2) * 128, :], writes=[("xres", 1 - sl)])
                transpose_to(C, obuf[:, lt, :], [("obuf", lt)], oT[:, sl, :, :], [("oT", sl)], bank=6 + sl, evac="act")
                for hf in range(2):
                    pk = ("ps", 2 * sl + hf)
                    pv = psf(C, 2 * sl + hf)
                    for k in range(8):
                        P.op("pe", lambda e, k=k, hf=hf, pv=pv: e.matmul(pv, lhsT=oT[:, sl, k, :], rhs=wo[:, k, hf * 512:(hf + 1) * 512],
                                                                         start=(k == 0), stop=(k == 7)),
                             reads=[("oT", sl), kwo[k // 4]], writes=[pk])
                emit_norm_residual(C, C.ps[sl][:, :], [("ps", 2 * sl), ("ps", 2 * sl + 1)], g[:, 1, :], ("ga", 1),
                                   xr[:, sl, :], [("xres", sl)], hh[:, sl, :], [("hh", sl)])
                P.dma("sp", H[lt * 128:(lt + 1) * 128, :], hh[:, sl, :], reads=[("hh", sl)], writes=[("H", lt)])


def build_phase_A(upto=99, stage=9):
    nc = bass.Bass("TRN2", target_bir_lowering=False)
    T = {}

    def din(name, shape, dt=F32):
        T[name] = nc.dram_tensor(name, shape, dt, kind="ExternalInput").ap()

    din("xf", [S, D]); din("xo", [NOWN, D]); din("pT", [256, NOWN])
    din("w_in", [D, 3080]); din("w_out", [D, D]); din("w_up", [D, DFF]); din("w_dn", [DFF, D])
    din("w_pp", [256, D]); din("w_pg", [D, D]); din("vec", [8, D]); din("lamv", [1, 256]); din("bf", [8, 1])
    din("ident", [128, 128]); din("masks", [4, 128, 256]); din("alibi", [128, 128])
    H = nc.dram_tensor("H", [NOWN, D], F32, kind="ExternalOutput").ap()
    H1 = nc.dram_tensor("H1", [NOWN, D], F32, kind="ExternalOutput").ap()
    HNT = nc.dram_tensor("HNT", [D, NOWN], BF16, kind="ExternalOutput").ap()
    with contextlib.ExitStack() as es:
        P = Prog(nc, es)
        C = Ctx()
        setup_common(nc, P, es, C, T["ident"])
        setup_eps(C, es)
        emit_attention(C, T, H, stage)
        if upto >= 2:
            emit_mlp(C, H, T["w_up"], T["w_dn"], T["vec"], 2, 3, "0")
        if upto >= 3:
            emit_ple(C, H, T["pT"], T["w_pp"], T["w_pg"], T["vec"], 4, H1, "0", hn_row=5, HNT=HNT)
        P.finish()
    return nc


def emit_rec(C, T, G, ncb=5):
    import itertools
    nc, P = C.nc, C.P
    TH = 1024
    NTH = S // TH
    assert ncb % 2 == 0 or ncb == 5
    with scope(P) as es:
        hnT = es.enter_context(nc.sbuf_tensor("r_hnT", [128, 8, S], BF16))
        wg = es.enter_context(nc.sbuf_tensor("r_wg", [128, 8, ncb * 128], BF16))
        wxr = es.enter_context(nc.sbuf_tensor("r_wxr", [128, 8, ncb * 128], BF16))
        wxs = es.enter_context(nc.sbuf_tensor("r_wxs", [128, ncb, 128], BF16))
        was = es.enter_context(nc.sbuf_tensor("r_was", [128, ncb, 128], BF16))
        rv = es.enter_context(nc.sbuf_tensor("r_rv", [128, ncb, 8], F32))
        sc = es.enter_context(nc.sbuf_tensor("r_sc", [128, ncb], F32))
        hlast = es.enter_context(nc.sbuf_tensor("r_hlast", [128, 2], F32))
        ybf = es.enter_context(nc.sbuf_tensor("r_y", [128, 2, 2, TH], BF16))
        xr = es.enter_context(nc.sbuf_tensor("r_xr", [128, 2, 2, TH + 3], F32))
        xc = es.enter_context(nc.sbuf_tensor("r_xc", [128, 2, TH], F32))
        xcb = es.enter_context(nc.sbuf_tensor("r_xcb", [128, 2, TH], BF16))
        gx = es.enter_context(nc.sbuf_tensor("r_gx", [128, 2, TH], F32))
        ga = es.enter_context(nc.sbuf_tensor("r_ga", [128, 2, TH], F32))
        tt = es.enter_context(nc.sbuf_tensor("r_tt", [128, 2, TH], F32))
        hs = es.enter_context(nc.sbuf_tensor("r_hs", [128, 2, TH], F32))
        gout = es.enter_context(nc.sbuf_tensor("r_gout", [128, 2, TH], BF16))
        hv = T["hnT"].rearrange("(k f) t -> f k t", f=128)
        for k in range(8):
            P.dma("sp", hnT[:, k, :], hv[:, k, :], writes=[("r_hnT", k)])
        kg = load_w_bf16(C, wg, T["w_g"], 8, "r_wg", nsplit=2)
        kx = load_w_bf16(C, wxr, T["w_x"], 8, "r_wxr", nsplit=2)
        P.dma("pool", wxs[:], T["wx"].rearrange("n i j -> i n j"), writes=["r_wxs"])
        P.dma("pool", was[:], T["wa"].rearrange("n i j -> i n j"), writes=["r_was"])
        P.dma("sp", rv[:], T["rvec"], writes=["r_rv"])
        P.op("act", lambda e: e.activation(out=sc[:, :], in_=rv[:, :, 7], func=AF.Exp, scale=-1.0), reads=["r_rv"], writes=["r_sc"])
        P.op("act", lambda e: e.activation(out=sc[:, :], in_=sc[:, :], func=AF.Ln, bias=1.0), reads=["r_sc"], writes=["r_sc"])
        P.op("dve", lambda e: e.tensor_scalar(out=sc[:, :], in0=sc[:, :], scalar1=-8.0, scalar2=None, op0=ALU.mult), reads=["r_sc"], writes=["r_sc"])

        def stage1(cb, th, st, bs):
            if th == 0:
                P.op("dve", lambda e: e.memset(xr[:, st, bs, 0:3], 0.0), writes=[("r_xr", st, bs)])
            else:
                P.op("dve", lambda e: e.tensor_copy(out=xr[:, st, bs, 0:3], in_=xr[:, st, 1 - bs, TH:TH + 3]),
                     reads=[("r_xr", st, 1 - bs)], writes=[("r_xr", st, bs)])
            yield
            for n in range(TH // 512):
                N = (TH // 512) * th + n
                bg = 2 * st
                pv = psf(C, bg)
                for k in range(8):
                    P.op("pe", lambda e: e.matmul(pv, lhsT=wg[:, k, cb * 128:(cb + 1) * 128], rhs=hnT[:, k, N * 512:(N + 1) * 512],
                                                  start=(k == 0), stop=(k == 7)),
                         reads=[kg[k // 4], ("r_hnT", k)], writes=[("ps", bg)])
                P.op("act", lambda e: e.activation(out=ybf[:, st, bs, n * 512:(n + 1) * 512], in_=pv, func=AF.Gelu_apprx_tanh),
                     reads=[("ps", bg)], writes=[("r_y", st, bs)])
                yield
                bx_ = 2 * st + 1
                pv2 = psf(C, bx_)
                for k in range(8):
                    P.op("pe", lambda e: e.matmul(pv2, lhsT=wxr[:, k, cb * 128:(cb + 1) * 128], rhs=hnT[:, k, N * 512:(N + 1) * 512],
                                                  start=(k == 0), stop=(k == 7)),
                         reads=[kx[k // 4], ("r_hnT", k)], writes=[("ps", bx_)])
                P.op("dve", lambda e: e.tensor_copy(out=xr[:, st, bs, 3 + n * 512:3 + (n + 1) * 512], in_=pv2),
                     reads=[("ps", bx_)], writes=[("r_xr", st, bs)])
                yield

        def stage2(cb, th, st, bs):
            xk = ("r_xr", st, bs)
            K = lambda name: (name, st)
            P.op("act", lambda e: e.activation(out=xc[:, st, :], in_=xr[:, st, bs, 3:3 + TH], func=AF.Identity, scale=rv[:, cb, 3:4], bias=rv[:, cb, 4:5]),
                 reads=[xk, "r_rv"], writes=[K("r_xc")])
            yield
            for w in range(3):
                P.op("dve", lambda e: e.scalar_tensor_tensor(out=xc[:, st, :], in0=xr[:, st, bs, w:w + TH], scalar=rv[:, cb, w:w + 1], in1=xc[:, st, :],
                                                             op0=ALU.mult, op1=ALU.add),
                     reads=[xk, "r_rv", K("r_xc")], writes=[K("r_xc")])
                yield
            P.op("act", lambda e: e.copy(out=xcb[:, st, :], in_=xc[:, st, :]), reads=[K("r_xc")], writes=[K("r_xcb")])
            yield
            for n in range(TH // 512):
                b1 = 4 + 2 * st
                pv = psf(C, b1)
                P.op("pe", lambda e: e.matmul(pv, lhsT=wxs[:, cb, :], rhs=xcb[:, st, n * 512:(n + 1) * 512], start=True, stop=True),
                     reads=["r_wxs", K("r_xcb")], writes=[("ps", b1)])
                P.op("act", lambda e: e.activation(out=gx[:, st, n * 512:(n + 1) * 512], in_=pv, func=AF.Sigmoid, bias=rv[:, cb, 5:6]),
                     reads=[("ps", b1), "r_rv"], writes=[K("r_gx")])
                yield
                b2 = 5 + 2 * st
                pv2 = psf(C, b2)
                P.op("pe", lambda e: e.matmul(pv2, lhsT=was[:, cb, :], rhs=xcb[:, st, n * 512:(n + 1) * 512], start=True, stop=True),
                     reads=["r_was", K("r_xcb")], writes=[("ps", b2)])
                P.op("act", lambda e: e.activation(out=ga[:, st, n * 512:(n + 1) * 512], in_=pv2, func=AF.Sigmoid, bias=rv[:, cb, 6:7]),
                     reads=[("ps", b2), "r_rv"], writes=[K("r_ga")])
                yield
            P.op("act", lambda e: e.activation(out=ga[:, st, :], in_=ga[:, st, :], func=AF.Exp, scale=sc[:, cb:cb + 1]),
                 reads=[K("r_ga"), "r_sc"], writes=[K("r_ga")])
            yield
            P.op("dve", lambda e: e.tensor_tensor(out=tt[:, st, :], in0=ga[:, st, :], in1=ga[:, st, :], op=ALU.mult), reads=[K("r_ga")], writes=[K("r_tt")])
            yield
            P.op("dve", lambda e: e.tensor_tensor(out=gx[:, st, :], in0=gx[:, st, :], in1=xc[:, st, :], op=ALU.mult),
                 reads=[K("r_gx"), K("r_xc")], writes=[K("r_gx")])
            yield
            P.op("act", lambda e: e.activation(out=tt[:, st, :], in_=tt[:, st, :], func=AF.Sqrt, scale=-1.0, bias=1.0), reads=[K("r_tt")], writes=[K("r_tt")])
            yield
            P.op("dve", lambda e: e.tensor_tensor(out=tt[:, st, :], in0=tt[:, st, :], in1=gx[:, st, :], op=ALU.mult),
                 reads=[K("r_tt"), K("r_gx")], writes=[K("r_tt")])
            if th == 0:
                P.op("dve", lambda e: e.tensor_copy(out=tt[:, st, 0:1], in_=gx[:, st, 0:1]), reads=[K("r_gx"), K("r_tt")], writes=[K("r_tt")])
            yield
            init = 0.0 if th == 0 else hlast[:, st:st + 1]
            P.op("dve", lambda e: e.tensor_tensor_scan(out=hs[:, st, :], data0=ga[:, st, :], data1=tt[:, st, :], initial=init, op0=ALU.mult, op1=ALU.add),
                 reads=[K("r_ga"), K("r_tt"), K("r_hlast")], writes=[K("r_hs")])
            P.op("dve", lambda e: e.tensor_copy(out=hlast[:, st:st + 1], in_=hs[:, st, TH - 1:TH]), reads=[K("r_hs")], writes=[K("r_hlast")])
            yield
            P.op("dve", lambda e: e.tensor_tensor(out=gout[:, st, :], in0=hs[:, st, :], in1=ybf[:, st, bs, :], op=ALU.mult),
                 reads=[K("r_hs"), ("r_y", st, bs)], writes=[K("r_gout")])
            P.dma("sp", G[cb * 128:(cb + 1) * 128, th * TH:(th + 1) * TH], gout[:, st, :], reads=[K("r_gout")], writes=[("G", cb, th)])
            yield

        def interleave(gens):
            for _ in itertools.zip_longest(*gens):
                pass

        supers = []
        for j in range((ncb + 1) // 2):
            cbs = [c for c in (2 * j, 2 * j + 1) if c < ncb]
            for th in range(NTH):
                supers.append((cbs, th))
        interleave([stage1(cb, supers[0][1], st, 0) for st, cb in enumerate(supers[0][0])])
        for i, (cbs, th) in enumerate(supers):
            bs = i % 2
            if i + 1 < len(supers):
                ncbs, nth = supers[i + 1]
                interleave([stage1(cb, nth, st, 1 - bs) for st, cb in enumerate(ncbs)])
            interleave([stage2(cb, th, st, bs) for st, cb in enumerate(cbs)])


def build_phase_B():
    nc = bass.Bass("TRN2", target_bir_lowering=False)
    T = {}

    def din(name, shape, dt=F32):
        T[name] = nc.dram_tensor(name, shape, dt, kind="ExternalInput").ap()

    din("hnT", [D, S], BF16); din("w_g", [D, 640]); din("w_x", [D, 640]); din("wx", [5, 128, 128]); din("wa", [5, 128, 128])
    din("rvec", [128, 5, 8]); din("ident", [128, 128])
    G = nc.dram_tensor("G", [640, S], BF16, kind="ExternalOutput").ap()
    with contextlib.ExitStack() as es:
        P = Prog(nc, es)
        C = Ctx()
        setup_common(nc, P, es, C, T["ident"])
        setup_eps(C, es)
        emit_rec(C, T, G)
        P.finish()
    return nc


def emit_recout(C, T, H, row=0, blend=None):
    nc, P = C.nc, C.P
    with scope(P) as es:
        g = load_gains(C, es, T["vec"], [row], "gro")
        gT = es.enter_context(nc.sbuf_tensor("gTs", [128, 10, NOWN], BF16))
        wro = es.enter_context(nc.sbuf_tensor("wro", [128, 10, D], BF16))
        if blend is None:
            gv = T["gT"].rearrange("(c p) t -> p c t", p=128)
            for c in range(10):
                P.dma("sp", gT[:, c, :], gv[:, c, :], writes=[("gTs", c)])
        else:
            Gd, H1d, sel_d = blend
            sel = es.enter_context(nc.sbuf_tensor("sel_sb", [128, 2], F32))
            P.dma("sp", sel[:], sel_d, writes=["sel"])
            gT2 = es.enter_context(nc.sbuf_tensor("gTs2", [128, 2, NOWN], BF16))
            hres2 = es.enter_context(nc.sbuf_tensor("hres2", [128, 2, D], F32))
            gv = Gd.rearrange("(c p) t -> p c t", p=128)
            for c in range(10):
                s2 = c % 2
                P.dma("sp", gT[:, c, :], gv[:, c, 0:NOWN], writes=[("gTs", c)])
                P.dma("sp", gT2[:, s2, :], gv[:, c, NOWN:2 * NOWN], writes=[("gTs2", s2)])
                P.op("act", lambda e: e.activation(out=gT[:, c, :], in_=gT[:, c, :], func=AF.Copy, scale=sel[:, 0:1]),
                     reads=[("gTs", c), "sel"], writes=[("gTs", c)])
                P.op("dve", lambda e: e.scalar_tensor_tensor(out=gT[:, c, :], in0=gT2[:, s2, :], scalar=sel[:, 1:2], in1=gT[:, c, :],
                                                             op0=ALU.mult, op1=ALU.add),
                     reads=[("gTs", c), ("gTs2", s2), "sel"], writes=[("gTs", c)])
        kw = load_w_bf16(C, wro, T["w_ro"], 10, "wro", nsplit=2)
        hres = es.enter_context(nc.sbuf_tensor("hres", [128, 2, D], F32))
        hh = es.enter_context(nc.sbuf_tensor("hh1", [128, 2, D], F32))
        for lt in range(16):
            sl = lt % 2
            if blend is None:
                P.dma("sp", hres[:, sl, :], T["h1"][lt * 128:(lt + 1) * 128, :], writes=[("hres", sl)])
            else:
                P.dma("sp", hres[:, sl, :], H1d[lt * 128:(lt + 1) * 128, :], writes=[("hres", sl)])
                P.dma("sp", hres2[:, sl, :], H1d[NOWN + lt * 128:NOWN + (lt + 1) * 128, :], writes=[("hres2", sl)])
                P.op("act", lambda e: e.activation(out=hres[:, sl, :], in_=hres[:, sl, :], func=AF.Copy, scale=sel[:, 0:1]),
                     reads=[("hres", sl), "sel"], writes=[("hres", sl)])
                P.op("dve", lambda e: e.scalar_tensor_tensor(out=hres[:, sl, :], in0=hres2[:, sl, :], scalar=sel[:, 1:2], in1=hres[:, sl, :],
                                                             op0=ALU.mult, op1=ALU.add),
                     reads=[("hres", sl), ("hres2", sl), "sel"], writes=[("hres", sl)])
            for hf in range(2):
                pk = ("ps", 2 * sl + hf)
                pv = psf(C, 2 * sl + hf)
                for c in range(10):
                    P.op("pe", lambda e: e.matmul(pv, lhsT=gT[:, c, lt * 128:(lt + 1) * 128], rhs=wro[:, c, hf * 512:(hf + 1) * 512],
                                                  start=(c == 0), stop=(c == 9)),
                         reads=[("gTs", c), kw[c // 5]], writes=[pk])
            emit_norm_residual(C, C.ps[sl][:, :], [("ps", 2 * sl), ("ps", 2 * sl + 1)], g[:, 0, :], ("gro", 0),
                               hres[:, sl, :], [("hres", sl)], hh[:, sl, :], [("hh1", sl)])
            P.dma("sp", H[lt * 128:(lt + 1) * 128, :], hh[:, sl, :], reads=[("hh1", sl)], writes=[("H", lt)])


def build_phase_C():
    nc = bass.Bass("TRN2", target_bir_lowering=False)
    T = {}

    def din(name, shape, dt=F32):
        T[name] = nc.dram_tensor(name, shape, dt, kind="ExternalInput").ap()

    din("gT", [RW, NOWN], BF16); din("h1", [NOWN, D]); din("pT", [256, NOWN]); din("w_ro", [RW, D])
    din("w_up", [D, DFF]); din("w_dn", [DFF, D]); din("w_pp", [256, D]); din("w_pg", [D, D]); din("vec", [8, D]); din("ident", [128, 128])
    H = nc.dram_tensor("H", [NOWN, D], F32, kind="ExternalOutput").ap()
    OUT = nc.dram_tensor("OUT", [NOWN, D], F32, kind="ExternalOutput").ap()
    with contextlib.ExitStack() as es:
        P = Prog(nc, es)
        C = Ctx()
        setup_common(nc, P, es, C, T["ident"])
        setup_eps(C, es)
        emit_recout(C, T, H)
        emit_mlp(C, H, T["w_up"], T["w_dn"], T["vec"], 1, 2, "1")
        emit_ple(C, H, T["pT"], T["w_pp"], T["w_pg"], T["vec"], 3, OUT, "1")
        P.finish()
    return nc


def build_fused():
    nc = bass.Bass("TRN2", target_bir_lowering=False)
    T = {}

    def din(name, shape, dt=F32):
        T[name] = nc.dram_tensor(name, shape, dt, kind="ExternalInput").ap()

    din("xf", [S, D]); din("pT", [256, S]); din("pT1", [256, NOWN])
    din("w_in", [D, 3080]); din("w_out", [D, D]); din("w_up", [D, DFF]); din("w_dn", [DFF, D])
    din("w_pp", [256, D]); din("w_pg", [D, D]); din("vec", [16, D]); din("lamv", [1, 256]); din("bf", [8, 1])
    din("ident", [128, 128]); din("masks", [2, 128, 256]); din("alibi", [128, 128]); din("sel", [128, 2])
    din("w_g", [D, RW]); din("w_x", [D, RW]); din("wx", [10, 128, 128]); din("wa", [10, 128, 128]); din("rvec", [128, 10, 8])
    din("w_ro", [RW, D]); din("w_up1", [D, DFF]); din("w_dn1", [DFF, D]); din("w_pp1", [256, D]); din("w_pg1", [D, D])
    HA = nc.dram_tensor("HA", [S, D], F32, kind="Internal").ap()
    H1 = nc.dram_tensor("H1", [S, D], F32, kind="Internal").ap()
    HNT = nc.dram_tensor("HNT", [D, S], BF16, kind="Internal").ap()
    G = nc.dram_tensor("G", [RW, S], BF16, kind="Internal").ap()
    HC = nc.dram_tensor("HC", [NOWN, D], F32, kind="Internal").ap()
    OUT = nc.dram_tensor("OUT", [NOWN, D], F32, kind="ExternalOutput").ap()
    with contextlib.ExitStack() as es:
        P = Prog(nc, es)
        C = Ctx()
        setup_common(nc, P, es, C, T["ident"])
        setup_eps(C, es)
        emit_attention(C, T, HA, full=True)
        emit_mlp(C, HA, T["w_up"], T["w_dn"], T["vec"], 2, 3, "0", ntok=S)
        emit_ple(C, HA, T["pT"], T["w_pp"], T["w_pg"], T["vec"], 4, H1, "0", hn_row=5, HNT=HNT, ntiles=32)
        T1 = {"hnT": HNT, "w_g": T["w_g"], "w_x": T["w_x"], "wx": T["wx"], "wa": T["wa"], "rvec": T["rvec"]}
        emit_rec(C, T1, G, ncb=10)
        T2 = {"vec": T["vec"], "w_ro": T["w_ro"]}
        emit_recout(C, T2, HC, row=8, blend=(G, H1, T["sel"]))
        emit_mlp(C, HC, T["w_up1"], T["w_dn1"], T["vec"], 9, 10, "1", ntok=NOWN)
        emit_ple(C, HC, T["pT1"], T["w_pp1"], T["w_pg1"], T["vec"], 11, OUT, "1", ntiles=16)
        P.finish()
    return nc


def own_tiles(r):
    return [4 * p + 2 * r + e for p in range(8) for e in range(2)]


def own_index(r):
    return np.concatenate([np.arange(g * 128, (g + 1) * 128) for g in own_tiles(r)])


def role_consts(r):
    ident = np.eye(128, dtype=np.float32)
    masks = np.zeros((4, 128, 256), np.float32)
    jj = np.arange(128)[:, None]
    ii = np.arange(128)[None, :]
    tri = (jj <= ii).astype(np.float32)
    for m in range(4):
        for e in range(2):
            qt = 2 * r + e
            if m < qt:
                masks[m, :, e * 128:(e + 1) * 128] = 1.0
            elif m == qt:
                masks[m, :, e * 128:(e + 1) * 128] = tri
    alibi = np.zeros((128, 4, 32), np.float32)
    for h in range(4):
        for idx in range(32):
            d = (idx - 28 - 2 * r - 2) * 128 + np.arange(128)
            alibi[:, h, idx] = np.minimum(SLOPES[h] * d, 0.0)
    return ident, masks, alibi.reshape(128, 128)


def f32c(a):
    return np.ascontiguousarray(a, dtype=np.float32)


def prep_A(inp, c):
    b, r = c // 2, c % 2
    oi = own_index(r)
    ident, masks, alibi = role_consts(r)
    vec = np.zeros((8, D), np.float32)
    vec[0] = inp["ln_mix_pre"][0]
    vec[1] = inp["ln_mix_post"][0]
    vec[2] = inp["ln_mlp_pre"][0]
    vec[3] = inp["ln_mlp_post"][0]
    vec[4] = inp["ple_norm"][0]
    vec[5] = inp["ln_mix_pre"][1]
    vec[6, :128] = inp["diff_subln"][0]
    lamv = np.concatenate([inp["diff_lambda_q1"][0], inp["diff_lambda_k1"][0],
                           inp["diff_lambda_q2"][0], inp["diff_lambda_k2"][0]])[None, :]
    return {
        "xf": f32c(inp["x"][b]), "xo": f32c(inp["x"][b][oi]), "pT": f32c(inp["p"][0, b][oi].T),
        "w_in": f32c(inp["attn_w_in"][0]), "w_out": f32c(inp["attn_w_out"][0]),
        "w_up": f32c(inp["mlp_w_up"][0]), "w_dn": f32c(inp["mlp_w_down"][0]),
        "w_pp": f32c(inp["ple_w_proj"][0]), "w_pg": f32c(inp["ple_w_gate"][0]),
        "vec": vec, "lamv": f32c(lamv), "bf": f32c(inp["attn_b_forget"][0][:, None]),
        "ident": ident, "masks": masks, "alibi": alibi,
    }


def prep_B(inp, c, hnT_full):
    r = c % 2
    cols_g = np.arange(5 * r * 128, (5 * r + 5) * 128)
    cols_x = RW + cols_g
    w_in = inp["rec_w_in"][0]
    rvec = np.zeros((128, 5, 8), np.float32)
    for j in range(5):
        ch = np.arange((5 * r + j) * 128, (5 * r + j + 1) * 128)
        rvec[:, j, 0:4] = inp["rec_conv_w"][0][:, ch].T
        rvec[:, j, 4] = inp["rec_conv_b"][0][ch]
        rvec[:, j, 5] = inp["rec_bx"][0][ch]
        rvec[:, j, 6] = inp["rec_ba"][0][ch]
        rvec[:, j, 7] = inp["rec_a_param"][0][ch]
    return {
        "hnT": hnT_full, "w_g": f32c(w_in[:, cols_g]), "w_x": f32c(w_in[:, cols_x]),
        "wx": f32c(inp["rec_wx"][0][5 * r:5 * r + 5]), "wa": f32c(inp["rec_wa"][0][5 * r:5 * r + 5]),
        "rvec": rvec, "ident": np.eye(128, dtype=np.float32),
    }


def prep_C(inp, c, gT_own, h1_own):
    b, r = c // 2, c % 2
    oi = own_index(r)
    vec = np.zeros((8, D), np.float32)
    vec[0] = inp["ln_mix_post"][1]
    vec[1] = inp["ln_mlp_pre"][1]
    vec[2] = inp["ln_mlp_post"][1]
    vec[3] = inp["ple_norm"][1]
    return {
        "gT": gT_own, "h1": h1_own, "pT": f32c(inp["p"][1, b][oi].T), "w_ro": f32c(inp["rec_w_out"][0]),
        "w_up": f32c(inp["mlp_w_up"][1]), "w_dn": f32c(inp["mlp_w_down"][1]),
        "w_pp": f32c(inp["ple_w_proj"][1]), "w_pg": f32c(inp["ple_w_gate"][1]),
        "vec": vec, "ident": np.eye(128, dtype=np.float32),
    }


def full_consts():
    ident = np.eye(128, dtype=np.float32)
    jj = np.arange(128)[:, None]
    ii = np.arange(128)[None, :]
    tri = (jj <= ii).astype(np.float32)
    masks = np.zeros((2, 128, 256), np.float32)
    masks[0, :, 0:128] = tri
    masks[0, :, 128:256] = 1.0
    masks[1, :, 128:256] = tri
    alibi = np.zeros((128, 4, 32), np.float32)
    for h in range(4):
        for idx in range(32):
            d = (idx - 30 - 2) * 128 + np.arange(128)
            alibi[:, h, idx] = np.minimum(SLOPES[h] * d, 0.0)
    return ident, masks, alibi.reshape(128, 128)


def prep_fused(inp, c):
    b, r = c // 2, c % 2
    ident, masks, alibi = full_consts()
    vec = np.zeros((16, D), np.float32)
    vec[0] = inp["ln_mix_pre"][0]
    vec[1] = inp["ln_mix_post"][0]
    vec[2] = inp["ln_mlp_pre"][0]
    vec[3] = inp["ln_mlp_post"][0]
    vec[4] = inp["ple_norm"][0]
    vec[5] = inp["ln_mix_pre"][1]
    vec[6, :128] = inp["diff_subln"][0]
    vec[8] = inp["ln_mix_post"][1]
    vec[9] = inp["ln_mlp_pre"][1]
    vec[10] = inp["ln_mlp_post"][1]
    vec[11] = inp["ple_norm"][1]
    lamv = np.concatenate([inp["diff_lambda_q1"][0], inp["diff_lambda_k1"][0],
                           inp["diff_lambda_q2"][0], inp["diff_lambda_k2"][0]])[None, :]
    rvec = np.zeros((128, 10, 8), np.float32)
    for j in range(10):
        ch = np.arange(j * 128, (j + 1) * 128)
        rvec[:, j, 0:4] = inp["rec_conv_w"][0][:, ch].T
        rvec[:, j, 4] = inp["rec_conv_b"][0][ch]
        rvec[:, j, 5] = inp["rec_bx"][0][ch]
        rvec[:, j, 6] = inp["rec_ba"][0][ch]
        rvec[:, j, 7] = inp["rec_a_param"][0][ch]
    sel = np.zeros((128, 2), np.float32)
    sel[:, r] = 1.0
    w_in1 = inp["rec_w_in"][0]
    return {
        "xf": f32c(inp["x"][b]), "pT": f32c(inp["p"][0, b].T), "pT1": f32c(inp["p"][1, b][r * NOWN:(r + 1) * NOWN].T),
        "w_in": f32c(inp["attn_w_in"][0]), "w_out": f32c(inp["attn_w_out"][0]),
        "w_up": f32c(inp["mlp_w_up"][0]), "w_dn": f32c(inp["mlp_w_down"][0]),
        "w_pp": f32c(inp["ple_w_proj"][0]), "w_pg": f32c(inp["ple_w_gate"][0]),
        "vec": vec, "lamv": f32c(lamv), "bf": f32c(inp["attn_b_forget"][0][:, None]),
        "ident": ident, "masks": masks, "alibi": alibi, "sel": sel,
        "w_g": f32c(w_in1[:, :RW]), "w_x": f32c(w_in1[:, RW:]), "wx": f32c(inp["rec_wx"][0]), "wa": f32c(inp["rec_wa"][0]),
        "rvec": rvec, "w_ro": f32c(inp["rec_w_out"][0]),
        "w_up1": f32c(inp["mlp_w_up"][1]), "w_dn1": f32c(inp["mlp_w_down"][1]),
        "w_pp1": f32c(inp["ple_w_proj"][1]), "w_pg1": f32c(inp["ple_w_gate"][1]),
    }


_NC_CACHE = {}


def _get(name, fn):
    if name not in _NC_CACHE:
        _NC_CACHE[name] = fn()
    return _NC_CACHE[name]


def kernel_unfused(**inputs):
    inp = {k: np.asarray(v) for k, v in inputs.items()}
    cores = list(range(NCORES))
    ncA = _get("A", build_phase_A)
    resA = run_bass_kernel_spmd(ncA, [prep_A(inp, c) for c in cores], core_ids=cores).results
    hnT_full = []
    for b in range(4):
        full = np.zeros((D, S), dtype=resA[0]["HNT"].dtype)
        for r in range(2):
            full[:, own_index(r)] = resA[2 * b + r]["HNT"]
        hnT_full.append(full)
    ncB = _get("B", build_phase_B)
    resB = run_bass_kernel_spmd(ncB, [prep_B(inp, c, hnT_full[c // 2]) for c in cores], core_ids=cores).results
    ncC = _get("C", build_phase_C)
    mapsC = []
    for c in cores:
        b, r = c // 2, c % 2
        oi = own_index(r)
        gfull = np.concatenate([resB[2 * b]["G"], resB[2 * b + 1]["G"]], axis=0)
        mapsC.append(prep_C(inp, c, np.ascontiguousarray(gfull[:, oi]), resA[c]["H1"]))
    resC = run_bass_kernel_spmd(ncC, mapsC, core_ids=cores).results
    out = np.zeros((4, S, D), np.float32)
    for c in cores:
        b, r = c // 2, c % 2
        out[b, own_index(r)] = resC[c]["OUT"]
    return out


def kernel(**inputs):
    inp = {k: np.asarray(v) for k, v in inputs.items()}
    cores = list(range(NCORES))
    nc = _get("F", build_fused)
    res = run_bass_kernel_spmd(nc, [prep_fused(inp, c) for c in cores], core_ids=cores).results
    out = np.zeros((4, S, D), np.float32)
    for c in cores:
        b, r = c // 2, c % 2
        out[b, r * NOWN:(r + 1) * NOWN] = res[c]["OUT"]
    return out
```

```python
import contextlib
import numpy as np
import concourse.bass as bass
import concourse.mybir as mybir
from concourse.bass_utils import run_bass_kernel_spmd

F32 = mybir.dt.float32
BF16 = mybir.dt.bfloat16
AF = mybir.ActivationFunctionType
ALU = mybir.AluOpType
AX = mybir.AxisListType

NCORES = 8
D = 1024
S = 4096
NOWN = 2048
DFF = 4096
RW = 1280
LAM_INIT0 = 0.8 - 0.6 * 1.0
SLOPES = [2.0 ** (-8.0 * (h + 1) / 4) for h in range(4)]


class Prog:
    K_DMA = 8

    def __init__(self, nc, es):
        self.nc = nc
        self.engs = {"pe": nc.tensor, "act": nc.scalar, "dve": nc.vector,
                     "pool": nc.gpsimd, "sp": nc.sync}
        self.semobj = {}
        for k in self.engs:
            self.semobj[k] = es.enter_context(nc.semaphore("sem_" + k))
        self.cnt = {k: 0 for k in self.engs}
        self.seen = {k: {} for k in self.engs}
        self.lastw = {}
        self.readers = {}
        self.dcnt = {}
        for q in ("sp", "act", "pool"):
            self.dcnt[q] = 0
            for i in range(self.K_DMA):
                self.semobj[(q, i)] = es.enter_context(nc.semaphore("d_%s_%d" % (q, i)))

    def _wait(self, eng, ev):
        sk, val = ev
        if self.seen[eng].get(sk, 0) >= val:
            return
        self.engs[eng].wait_ge(self.semobj[sk], val)
        self.seen[eng][sk] = val

    def _deps(self, eng, reads, writes):
        deps = {}
        for k in reads:
            w = self.lastw.get(k)
            if w is not None:
                deps[w[0]] = max(deps.get(w[0], 0), w[1])
        for k in writes:
            w = self.lastw.get(k)
            if w is not None:
                deps[w[0]] = max(deps.get(w[0], 0), w[1])
            for sk, v in self.readers.get(k, {}).items():
                deps[sk] = max(deps.get(sk, 0), v)
        for sk, v in deps.items():
            if eng == "pe" and sk == "pe":
                continue
            self._wait(eng, (sk, v))

    def _record(self, ev, reads, writes):
        for k in writes:
            self.lastw[k] = ev
            self.readers[k] = {}
        for k in reads:
            if k in writes:
                continue
            d = self.readers.setdefault(k, {})
            d[ev[0]] = max(d.get(ev[0], 0), ev[1])

    def op(self, eng, fn, reads=(), writes=()):
        self._deps(eng, reads, writes)
        inst = fn(self.engs[eng])
        self.cnt[eng] += 1
        inst.then_inc(self.semobj[eng], 1)
        self._record((eng, self.cnt[eng]), reads, writes)

    def dma(self, q, out, in_, reads=(), writes=()):
        self._deps(q, reads, writes)
        i = self.dcnt[q]
        s = i % self.K_DMA
        rnd = i // self.K_DMA
        if rnd > 0:
            self._wait(q, ((q, s), 16 * rnd))
        inst = self.engs[q].dma_start(out=out, in_=in_)
        inst.then_inc(self.semobj[(q, s)], 16)
        self.dcnt[q] = i + 1
        self._record(((q, s), 16 * (rnd + 1)), reads, writes)

    def barrier(self):
        evs = []
        for q in ("sp", "act", "pool"):
            n = self.dcnt[q]
            for s in range(self.K_DMA):
                m = (n - s + self.K_DMA - 1) // self.K_DMA if n > s else 0
                if m > 0:
                    evs.append(((q, s), 16 * m))
        for k in ("pe", "act", "dve", "pool", "sp"):
            if self.cnt[k] > 0:
                evs.append((k, self.cnt[k]))
        for eng in ("pe", "act", "dve", "pool", "sp"):
            for ev in evs:
                if ev[0] == eng:
                    continue
                self._wait(eng, ev)

    def barrier_keys(self, eng, keys):
        self._deps(eng, [], list(keys))

    def finish(self):
        for q in ("sp", "act", "pool"):
            n = self.dcnt[q]
            for s in range(self.K_DMA):
                m = (n - s + self.K_DMA - 1) // self.K_DMA if n > s else 0
                if m > 0:
                    self._wait("sp", ((q, s), 16 * m))
        for k in ("pe", "act", "dve", "pool"):
            if self.cnt[k] > 0:
                self._wait("sp", (k, self.cnt[k]))


class Ctx:
    pass


@contextlib.contextmanager
def scope(P):
    with contextlib.ExitStack() as es:
        yield es
        P.barrier()


def setup_common(nc, P, es, C, ident_d):
    C.nc = nc
    C.P = P
    C.identf = es.enter_context(nc.sbuf_tensor("identf", [128, 128], F32))
    C.identb = es.enter_context(nc.sbuf_tensor("identb", [128, 128], BF16))
    C.onesf = es.enter_context(nc.sbuf_tensor("onesf", [128, 512], F32))
    P.dma("sp", C.identf[:], ident_d, writes=["identf"])
    P.dma("pool", C.identb[:], ident_d, writes=["identb"])
    P.op("pool", lambda e: e.memset(C.onesf[:], 1.0), writes=["onesf"])
    C.ps = [es.enter_context(nc.psum_tensor("ps%d" % i, [128, 1024], F32)) for i in range(4)]
    C.psb = [t.bitcast(BF16) for t in C.ps]
    C.junk = es.enter_context(nc.sbuf_tensor("junk", [128, 1024], BF16))
    C.junk2 = es.enter_context(nc.sbuf_tensor("junk2", [128, 128], BF16))
    C.small = es.enter_context(nc.sbuf_tensor("small", [128, 64], F32))
    C.small_i = 0


def small_slot(C, n=1):
    i = C.small_i
    if i + n > 64:
        i = 0
    C.small_i = i + n
    return i


def psf(C, b):
    return C.ps[b // 2][:, (b % 2) * 512:(b % 2) * 512 + 512]


def psbf(C, b):
    return C.psb[b // 2][:, (b % 2) * 1024:(b % 2) * 1024 + 1024]


def rstd_from(C, src, src_keys, n, eps, eng="act"):
    P = C.P
    i = small_slot(C)
    ss = C.small[:, i:i + 1]
    key = ("small", i)
    if eng == "dve":
        P.op("dve", lambda e: e.scalar_tensor_tensor(out=C.junk2[:, 0:n], in0=src, scalar=1.0, in1=src,
                                                     op0=ALU.mult, op1=ALU.mult, accum_out=ss),
             reads=list(src_keys), writes=["junk2", key])
    else:
        P.op("act", lambda e: e.activation(out=C.junk[:, 0:n], in_=src, func=AF.Square, accum_out=ss),
             reads=list(src_keys), writes=["junk", key])
    P.op("dve", lambda e: e.tensor_scalar(out=ss, in0=ss, scalar1=1.0 / n, scalar2=float(eps), op0=ALU.mult, op1=ALU.add),
         reads=[key], writes=[key])
    P.op("pool", lambda e: e.tensor_tensor(out=ss, in0=ss, in1=C.negh, op=ALU.pow), reads=[key, "epsb"], writes=[key])
    return ss, key


def transpose_to(C, src_bf, src_keys, dst, dst_keys, bank, nk=8, evac="dve"):
    P = C.P
    pk = ("ps", bank)
    pv = psbf(C, bank)
    for k in range(nk):
        P.op("pe", lambda e, k=k: e.transpose(out=pv[:, k * 128:(k + 1) * 128], in_=src_bf[:, k * 128:(k + 1) * 128],
                                              identity=C.identb[:]),
             reads=list(src_keys) + ["identb"], writes=[pk])
    srcv = pv[:, 0:nk * 128].rearrange("p (k t) -> p k t", k=nk)
    if evac == "act":
        P.op("act", lambda e: e.copy(out=dst, in_=srcv), reads=[pk], writes=list(dst_keys))
    else:
        P.op("dve", lambda e: e.tensor_copy(out=dst, in_=srcv), reads=[pk], writes=list(dst_keys))


def load_w_bf16(C, dst, dram2d, nk, key, nsplit=1):
    P = C.P
    src = dram2d.rearrange("(k p) n -> p k n", p=128)
    step = nk // nsplit
    for i in range(nsplit):
        P.dma("pool", dst[:, i * step:(i + 1) * step, :], src[:, i * step:(i + 1) * step, :], writes=[(key, i)])
    return [(key, i) for i in range(nsplit)]


def emit_norm_residual(C, m_src, m_keys, gain, gain_key, res, res_keys, out, out_keys):
    P = C.P
    rs, rk = rstd_from(C, m_src, m_keys, D, 1e-6)
    P.op("dve", lambda e: e.scalar_tensor_tensor(out=out, in0=m_src, scalar=rs, in1=gain, op0=ALU.mult, op1=ALU.mult),
         reads=list(m_keys) + [rk, gain_key], writes=list(out_keys))
    P.op("pool", lambda e: e.tensor_tensor(out=out, in0=out, in1=res, op=ALU.add),
         reads=list(out_keys) + list(res_keys), writes=list(out_keys))


def emit_norm_bf16(C, src, src_keys, gain, gain_key, out_bf, out_keys):
    P = C.P
    rs, rk = rstd_from(C, src, src_keys, D, 1e-6)
    P.op("dve", lambda e: e.scalar_tensor_tensor(out=out_bf, in0=src, scalar=rs, in1=gain, op0=ALU.mult, op1=ALU.mult),
         reads=list(src_keys) + [rk, gain_key], writes=list(out_keys))


def load_gains(C, es, vec_d, rows, name):
    nc, P = C.nc, C.P
    g = es.enter_context(nc.sbuf_tensor(name, [128, len(rows), D], F32))
    for i, r in enumerate(rows):
        P.dma("sp", g[:, i, :], vec_d[r, :].partition_broadcast(128), writes=[(name, i)])
    return g


def setup_eps(C, es):
    nc, P = C.nc, C.P
    t = es.enter_context(nc.sbuf_tensor("epsb", [128, 3], F32))
    P.op("pool", lambda e: e.memset(t[:, 2:3], -0.5), writes=["epsb"])
    C.negh = t[:, 2:3]
    P.op("pool", lambda e: e.memset(t[:, 0:1], 1e-6), writes=["epsb"])
    P.op("pool", lambda e: e.memset(t[:, 1:2], 1e-5), writes=["epsb"])
    C.epsb = {1e-6: t[:, 0:1], 1e-5: t[:, 1:2]}


def emit_mlp(C, H, w_up_d, w_dn_d, vec_d, row_pre, row_post, tag, ntok=NOWN):
    nc, P = C.nc, C.P
    MT = 256
    with scope(P) as es:
        wup = es.enter_context(nc.sbuf_tensor("wup" + tag, [128, 8, DFF], BF16))
        wdn = es.enter_context(nc.sbuf_tensor("wdn" + tag, [128, 32, D], BF16))
        wup_src = w_up_d.rearrange("(k p) n -> p k n", p=128)
        kup = []
        for i in range(8):
            P.dma("pool", wup[:, :, i * 512:(i + 1) * 512], wup_src[:, :, i * 512:(i + 1) * 512], writes=[("wup" + tag, i)])
            kup.append(("wup" + tag, i))
        kdn = load_w_bf16(C, wdn, w_dn_d, 32, "wdn" + tag, nsplit=8)
        g = load_gains(C, es, vec_d, [row_pre, row_post], "gm" + tag)
        ha = es.enter_context(nc.sbuf_tensor("ha" + tag, [128, 2, 2, D], F32))
        ubf = es.enter_context(nc.sbuf_tensor("ubf" + tag, [128, 2, D], BF16))
        uT = es.enter_context(nc.sbuf_tensor("uT" + tag, [128, 2, 8, MT], BF16))
        hid = es.enter_context(nc.sbuf_tensor("hid" + tag, [128, 32, MT], BF16))
        rl = es.enter_context(nc.sbuf_tensor("rl" + tag, [128, 2, MT], F32))
        hb = es.enter_context(nc.sbuf_tensor("hb" + tag, [128, 2, D], F32))
        nmt = ntok // MT

        def prologue(t):
            sl = t % 2
            for e in range(2):
                lt = 2 * t + e
                P.dma("sp", ha[:, sl, e, :], H[lt * 128:(lt + 1) * 128, :], reads=[("H", lt)], writes=[("ha", sl, e)])
                emit_norm_bf16(C, ha[:, sl, e, :], [("ha", sl, e)], g[:, 0, :], ("gm" + tag, 0), ubf[:, e, :], [("ubf", e)])
                transpose_to(C, ubf[:, e, :], [("ubf", e)], uT[:, sl, :, e * 128:(e + 1) * 128], [("uT", sl, e)],
                             bank=e, evac="act")

        def up(t):
            sl = t % 2
            for c in range(32):
                bank = c % 4
                pk = ("ps", bank)
                pv = psf(C, bank)[:, 0:MT]
                for k in range(8):
                    P.op("pe", lambda e: e.matmul(pv, lhsT=wup[:, k, c * 128:(c + 1) * 128], rhs=uT[:, sl, k, :],
                                                  start=(k == 0), stop=(k == 7)),
                         reads=[kup[c // 4], ("uT", sl, 0), ("uT", sl, 1)], writes=[pk])
                rs_ = c % 2
                P.op("act", lambda e: e.activation(out=rl[:, rs_, :], in_=pv, func=AF.Relu),
                     reads=[pk], writes=[("rl", rs_)])
                P.op("pool", lambda e: e.tensor_tensor(out=hid[:, c, :], in0=rl[:, rs_, :], in1=rl[:, rs_, :], op=ALU.mult),
                     reads=[("rl", rs_)], writes=[("hid", c)])

        def down(t):
            sl = t % 2
            for e in range(2):
                lt = 2 * t + e
                for hf in range(2):
                    pk = ("ps", 4 + 2 * e + hf)
                    pv = psf(C, 4 + 2 * e + hf)
                    for c in range(32):
                        P.op("pe", lambda e_: e_.matmul(pv, lhsT=hid[:, c, e * 128:(e + 1) * 128],
                                                        rhs=wdn[:, c, hf * 512:(hf + 1) * 512],
                                                        start=(c == 0), stop=(c == 31)),
                             reads=[("hid", c), kdn[c // 4]], writes=[pk])
                fv = C.ps[2 + e][:, :]
                emit_norm_residual(C, fv, [("ps", 4 + 2 * e), ("ps", 5 + 2 * e)], g[:, 1, :], ("gm" + tag, 1),
                                   ha[:, sl, e, :], [("ha", sl, e)], hb[:, e, :], [("hb", e)])
                P.dma("sp", H[lt * 128:(lt + 1) * 128, :], hb[:, e, :], reads=[("hb", e)], writes=[("H", lt)])

        prologue(0)
        for t in range(nmt):
            up(t)
            if t + 1 < nmt:
                prologue(t + 1)
            down(t)


def emit_ple(C, H, pT_d, w_pp_d, w_pg_d, vec_d, row_ple, OUT, tag, hn_row=None, HNT=None, ntiles=16):
    nc, P = C.nc, C.P
    with scope(P) as es:
        wpg = es.enter_context(nc.sbuf_tensor("wpg" + tag, [128, 8, D], BF16))
        wpp = es.enter_context(nc.sbuf_tensor("wpp" + tag, [128, 2, D], BF16))
        kpg = load_w_bf16(C, wpg, w_pg_d, 8, "wpg" + tag, nsplit=2)
        kpp = load_w_bf16(C, wpp, w_pp_d, 2, "wpp" + tag, nsplit=1)
        rows = [row_ple] + ([hn_row] if hn_row is not None else [])
        g = load_gains(C, es, vec_d, rows, "gp" + tag)
        hb = es.enter_context(nc.sbuf_tensor("phb" + tag, [128, 3, D], F32))
        hbb = es.enter_context(nc.sbuf_tensor("phbb" + tag, [128, 2, D], BF16))
        hbT = es.enter_context(nc.sbuf_tensor("phbT" + tag, [128, 2, 8, 128], BF16))
        pT = es.enter_context(nc.sbuf_tensor("ppT" + tag, [128, 3, 2, 128], BF16))
        sg = es.enter_context(nc.sbuf_tensor("psg" + tag, [128, 2, D], F32))
        ee = es.enter_context(nc.sbuf_tensor("pee" + tag, [128, 2, D], F32))
        hnb = es.enter_context(nc.sbuf_tensor("phnb" + tag, [128, 2, D], BF16))
        hnT = es.enter_context(nc.sbuf_tensor("phnT" + tag, [128, 2, 8, 128], BF16))
        pTv = pT_d.rearrange("(k f) t -> f k t", f=128)
        def loads(lt):
            s3 = lt % 3
            P.dma("sp", hb[:, s3, :], H[lt * 128:(lt + 1) * 128, :], reads=[("H", lt)], writes=[("phb", s3)])
            P.dma("pool", pT[:, s3, :, :], pTv[:, :, lt * 128:(lt + 1) * 128], writes=[("ppT", s3)])

        def front(lt):
            sl = lt % 2
            s3 = lt % 3
            if lt + 1 < ntiles:
                loads(lt + 1)
            P.op("dve", lambda e: e.tensor_copy(out=hbb[:, sl, :], in_=hb[:, s3, :]), reads=[("phb", s3)], writes=[("phbb", sl)])
            transpose_to(C, hbb[:, sl, :], [("phbb", sl)], hbT[:, sl, :, :], [("phbT", sl)], bank=6, evac="act")
            for hf in range(2):
                pk = ("ps", hf)
                pv = psf(C, hf)
                for k in range(8):
                    P.op("pe", lambda e: e.matmul(pv, lhsT=hbT[:, sl, k, :], rhs=wpg[:, k, hf * 512:(hf + 1) * 512],
                                                  start=(k == 0), stop=(k == 7)),
                         reads=[("phbT", sl), kpg[k // 4]], writes=[pk])
            for hf in range(2):
                pk = ("ps", 2 + 2 * sl + hf)
                pv = psf(C, 2 + 2 * sl + hf)
                for k in range(2):
                    P.op("pe", lambda e: e.matmul(pv, lhsT=pT[:, s3, k, :], rhs=wpp[:, k, hf * 512:(hf + 1) * 512],
                                                  start=(k == 0), stop=(k == 1)),
                         reads=[("ppT", s3), kpp[0]], writes=[pk])

        def front_b(lt):
            sl = lt % 2
            P.op("act", lambda e: e.activation(out=sg[:, sl, :], in_=C.ps[0][:, :], func=AF.Sigmoid),
                 reads=[("ps", 0), ("ps", 1)], writes=[("psg", sl)])

        def back(lt):
            sl = lt % 2
            ev = C.ps[1 + sl][:, :]
            pks = [("ps", 2 + 2 * sl), ("ps", 3 + 2 * sl)]
            rs, rk = rstd_from(C, ev, pks, D, 1e-6)
            P.op("dve", lambda e: e.scalar_tensor_tensor(out=ee[:, sl, :], in0=ev, scalar=rs, in1=g[:, 0, :], op0=ALU.mult, op1=ALU.mult),
                 reads=pks + [rk, ("gp" + tag, 0)], writes=[("pee", sl)])
            P.op("pool", lambda e: e.tensor_tensor(out=ee[:, sl, :], in0=ee[:, sl, :], in1=sg[:, sl, :], op=ALU.mult),
                 reads=[("pee", sl), ("psg", sl)], writes=[("pee", sl)])
            P.op("dve", lambda e: e.tensor_tensor(out=ee[:, sl, :], in0=ee[:, sl, :], in1=hb[:, lt % 3, :], op=ALU.add),
                 reads=[("pee", sl), ("phb", lt % 3)], writes=[("pee", sl)])
            P.dma("sp", OUT[lt * 128:(lt + 1) * 128, :], ee[:, sl, :], reads=[("pee", sl)], writes=[("OUT", lt)])

        def hnpart(lt):
            sl = lt % 2
            emit_norm_bf16(C, ee[:, sl, :], [("pee", sl)], g[:, 1, :], ("gp" + tag, 1), hnb[:, sl, :], [("phnb", sl)])
            transpose_to(C, hnb[:, sl, :], [("phnb", sl)], hnT[:, sl, :, :], [("phnT", sl)], bank=7, evac="dve")
            P.dma("sp", HNT.rearrange("(k f) t -> f k t", f=128)[:, :, lt * 128:(lt + 1) * 128], hnT[:, sl, :, :],
                  reads=[("phnT", sl)], writes=[("HNT", lt)])

        loads(0)
        front(0)
        front_b(0)
        for lt in range(ntiles):
            if lt + 1 < ntiles:
                front(lt + 1)
            back(lt)
            if hn_row is not None and lt >= 1:
                hnpart(lt - 1)
            if lt + 1 < ntiles:
                front_b(lt + 1)
        if hn_row is not None:
            hnpart(ntiles - 1)


def emit_attention(C, T, H, stage=9, full=False):
    nc, P = C.nc, C.P
    with scope(P) as es:
        g = load_gains(C, es, T["vec"], [0, 1], "ga")
        hnTf = es.enter_context(nc.sbuf_tensor("hnTf", [128, 8, S], BF16))
        NT = 32 if full else 16
        NCH = NT // 2
        KPC = 2 if full else 4
        NM = 2 if full else 4
        if full:
            hnTo = hnTf
        else:
            hnTo = es.enter_context(nc.sbuf_tensor("hnTo", [128, 8, NOWN], BF16))
        obuf = es.enter_context(nc.sbuf_tensor("obuf", [128, NT, D], BF16))
        masks = es.enter_context(nc.sbuf_tensor("masks_sb", [128, NM, 256], BF16))
        alibi = es.enter_context(nc.sbuf_tensor("alibi_sb", [128, 4, 32], F32))
        fb = es.enter_context(nc.sbuf_tensor("fb", [128, 2, 2, 32], F32))
        Stm = es.enter_context(nc.sbuf_tensor("Stm", [128, 32, 8], F32))
        Xb = es.enter_context(nc.sbuf_tensor("Xb", [128, NCH * 8], F32))
        neglam = es.enter_context(nc.sbuf_tensor("neglam", [128, 4], F32))
        subg = es.enter_context(nc.sbuf_tensor("subg", [128, 128], F32))
        P.dma("pool", masks[:], T["masks"].rearrange("m p q -> p m q"), writes=["masks"])
        P.dma("sp", alibi[:], T["alibi"].rearrange("p (h r) -> p h r", h=4), writes=["alibi"])
        P.dma("sp", subg[:], T["vec"][6, 0:128].partition_broadcast(128), writes=["subg"])
        P.op("dve", lambda e: e.tensor_scalar(out=subg[:], in0=subg[:], scalar1=1.0 - LAM_INIT0, scalar2=None, op0=ALU.mult),
             reads=["subg"], writes=["subg"])

        with scope(P) as es2:
            lv = es2.enter_context(nc.sbuf_tensor("lv", [1, 256], F32))
            pr = es2.enter_context(nc.sbuf_tensor("pr", [1, 128], F32))
            dots = es2.enter_context(nc.sbuf_tensor("dots", [1, 2], F32))
            P.dma("sp", lv[:], T["lamv"], writes=["lv"])
            P.op("dve", lambda e: e.tensor_tensor(out=pr[:, 0:64], in0=lv[:, 0:64], in1=lv[:, 64:128], op=ALU.mult),
                 reads=["lv"], writes=["pr"])
            P.op("dve", lambda e: e.tensor_tensor(out=pr[:, 64:128], in0=lv[:, 128:192], in1=lv[:, 192:256], op=ALU.mult),
                 reads=["lv", "pr"], writes=["pr"])
            P.op("dve", lambda e: e.tensor_reduce(out=dots[:, 0:2], in_=pr[:, :].rearrange("p (a d) -> p a d", a=2),
                                                  axis=AX.X, op=ALU.add), reads=["pr"], writes=["dots"])
            pv = psf(C, 0)[:, 0:2]
            P.op("pe", lambda e: e.matmul(pv, lhsT=C.onesf[0:1, 0:128], rhs=dots[0:1, 0:2], start=True, stop=True),
                 reads=["onesf", "dots"], writes=[("ps", 0)])
            P.op("act", lambda e: e.activation(out=neglam[:, 0:2], in_=pv, func=AF.Exp), reads=[("ps", 0)], writes=["neglam"])
            P.op("dve", lambda e: e.tensor_tensor(out=neglam[:, 2:3], in0=neglam[:, 1:2], in1=neglam[:, 0:1], op=ALU.subtract),
                 reads=["neglam"], writes=["neglam"])
            P.op("dve", lambda e: e.tensor_scalar(out=neglam[:, 2:3], in0=neglam[:, 2:3], scalar1=-LAM_INIT0, scalar2=None, op0=ALU.add),
                 reads=["neglam"], writes=["neglam"])

        with scope(P) as es2:
            xt = es2.enter_context(nc.sbuf_tensor("xt", [128, 2, D], F32))
            xb = es2.enter_context(nc.sbuf_tensor("xb", [128, 2, D], BF16))
            def a1_info(i):
                if i < 32:
                    src = T["xf"][i * 128:(i + 1) * 128, :]
                    dst = hnTf[:, :, i * 128:(i + 1) * 128]
                    dk = ("hnTf", i // 4)
                else:
                    j = i - 32
                    src = T["xo"][j * 128:(j + 1) * 128, :]
                    dst = hnTo[:, :, j * 128:(j + 1) * 128]
                    dk = ("hnTo", j // 4)
                if full:
                    dk = ("hnTf", i // 4)
                return src, dst, dk

            def a1_front(i):
                sl = i % 2
                src, dst, dk = a1_info(i)
                P.dma("sp", xt[:, sl, :], src, writes=[("xt", sl)])
                emit_norm_bf16(C, xt[:, sl, :], [("xt", sl)], g[:, 0, :], ("ga", 0), xb[:, sl, :], [("xb", sl)])

            def a1_back(i):
                sl = i % 2
                src, dst, dk = a1_info(i)
                transpose_to(C, xb[:, sl, :], [("xb", sl)], dst, [dk], bank=6 + sl, evac=("act" if sl else "dve"))

            n_a1 = 32 if full else 48
            a1_front(0)
            for i in range(n_a1):
                if i + 1 < n_a1:
                    a1_front(i + 1)
                a1_back(i)

        if stage < 1:
            return
        with scope(P) as es2:
            wfz = es2.enter_context(nc.sbuf_tensor("wfz", [128, 8, 8], BF16))
            negb = es2.enter_context(nc.sbuf_tensor("negb", [8, 1], F32))
            Lf = es2.enter_context(nc.sbuf_tensor("Lf", [8, S], F32))
            Sc = es2.enter_context(nc.sbuf_tensor("Sc", [8, S], F32))
            Dm = es2.enter_context(nc.sbuf_tensor("Dm", [8, NCH, 8], F32))
            P.dma("pool", wfz[:], T["w_in"].rearrange("(k p) n -> p k n", p=128)[:, :, 3072:3080], writes=["wfz"])
            P.dma("sp", negb[:], T["bf"], writes=["negb"])
            P.op("dve", lambda e: e.tensor_scalar(out=negb[:], in0=negb[:], scalar1=-1.0, scalar2=None, op0=ALU.mult),
                 reads=["negb"], writes=["negb"])
            for n in range(8):
                bank = n % 2
                pv = psf(C, bank)[0:8, :]
                for k in range(8):
                    P.op("pe", lambda e, k=k, n=n, pv=pv: e.matmul(pv, lhsT=wfz[:, k, :], rhs=hnTf[:, k, n * 512:(n + 1) * 512],
                                                                   start=(k == 0), stop=(k == 7)),
                         reads=["wfz", ("hnTf", n)], writes=[("ps", bank)])
                P.op("act", lambda e, n=n, pv=pv: e.activation(out=Lf[:, n * 512:(n + 1) * 512], in_=pv, func=AF.Exp, scale=-1.0, bias=negb[:, 0:1]),
                     reads=[("ps", bank), "negb"], writes=[("Lf", n)])
            for n in range(8):
                P.op("act", lambda e, n=n: e.activation(out=Lf[:, n * 512:(n + 1) * 512], in_=Lf[:, n * 512:(n + 1) * 512], func=AF.Ln, bias=1.0),
                     reads=[("Lf", n)], writes=[("Lf", n)])
            for n in range(8):
                init = 0.0 if n == 0 else Sc[:, n * 512 - 1:n * 512]
                rd = [("Lf", n), "onesf"] + ([("Sc", n - 1)] if n else [])
                P.op("dve", lambda e, n=n, init=init: e.tensor_tensor_scan(out=Sc[:, n * 512:(n + 1) * 512], data0=C.onesf[0:8, :],
                                                                           data1=Lf[:, n * 512:(n + 1) * 512], initial=init,
                                                                           op0=ALU.mult, op1=ALU.add),
                     reads=rd, writes=[("Sc", n)])
            pvt = psf(C, 2)[:, 0:256]
            for kb in range(32):
                P.op("pe", lambda e, kb=kb: e.transpose(out=pvt[:, kb * 8:(kb + 1) * 8], in_=Sc[0:8, kb * 128:(kb + 1) * 128],
                                                        identity=C.identf[0:8, 0:8]),
                     reads=[("Sc", kb // 4), "identf"], writes=[("ps", 2)])
            P.op("dve", lambda e: e.tensor_copy(out=Stm[:, :, :], in_=pvt.rearrange("p (k h) -> p k h", h=8)),
                 reads=[("ps", 2)], writes=["Stm"])
            CW = 128 * KPC
            ssel = Sc[:, :].rearrange("h (p t) -> h p t", t=CW)[:, :, CW - 1:CW]
            P.op("dve", lambda e: e.tensor_tensor(out=Dm[:, :, :], in0=ssel.to_broadcast([8, NCH, 8]),
                                                  in1=C.identf[0:8, 0:8].unsqueeze(1).to_broadcast([8, NCH, 8]), op=ALU.mult),
                 reads=[("Sc", n) for n in range(8)] + ["identf"], writes=["Dm"])
            pvx = psf(C, 3)[:, 0:NCH * 8]
            P.op("pe", lambda e: e.matmul(pvx, lhsT=C.onesf[0:8, 0:128], rhs=Dm[:, :, :].rearrange("h p g -> h (p g)"), start=True, stop=True),
                 reads=["onesf", "Dm"], writes=[("ps", 3)])
            P.op("dve", lambda e: e.tensor_copy(out=Xb[:, :], in_=pvx), reads=[("ps", 3)], writes=["Xb"])
        if stage < 2:
            return
        with scope(P) as es2:
            wq = es2.enter_context(nc.sbuf_tensor("wq", [128, 2, 8, 128], BF16))
            wk = es2.enter_context(nc.sbuf_tensor("wk", [128, 2, 8, 128], BF16))
            wv = es2.enter_context(nc.sbuf_tensor("wv", [128, 2, 8, 128], BF16))
            KT = es2.enter_context(nc.sbuf_tensor("KT", [128, S], BF16))
            QT = es2.enter_context(nc.sbuf_tensor("QTz", [128, 2, NT * 128], BF16))
            Vb = es2.enter_context(nc.sbuf_tensor("Vb", [128, 32, 130], BF16))
            Vd = Vb[:, :, 0:129]
            Vf = Vb[:, :, :].rearrange("p k (h d) -> p k h d", h=2)
            PT = es2.enter_context(nc.sbuf_tensor("PT", [128, 4, 2, 256], BF16))
            t1 = es2.enter_context(nc.sbuf_tensor("t1", [128, 2, 128], F32))
            dd = es2.enter_context(nc.sbuf_tensor("dd", [128, 2, 128], F32))
            rr = es2.enter_context(nc.sbuf_tensor("rr", [128, 2, 8], F32))
            P.op("pool", lambda e: e.memset(QT[:, :, :], 0.0), writes=[("QT", n) for n in range(NT // 4)])
            w_in_v = T["w_in"].rearrange("(k p) n -> p k n", p=128)
            rr_i = [0]
            for gidx in range(8):
                ws = gidx % 2
                diff = gidx < 4
                if gidx in (0, 4):
                    vk = [("V", kq) for kq in range(8)]
                    P.op("pool", lambda e: e.memset(Vb[:, :, :], 1.0), writes=vk)
                if diff:
                    qc, kc, vc = gidx * 128, 512 + gidx * 128, 1024 + gidx * 128
                else:
                    qc, kc, vc = 1536 + (gidx - 4) * 128, 2048 + (gidx - 4) * 128, 2560 + (gidx - 4) * 128
                P.dma("pool", wq[:, ws, :, :], w_in_v[:, :, qc:qc + 128], writes=[("wq", ws)])
                P.dma("pool", wk[:, ws, :, :], w_in_v[:, :, kc:kc + 128], writes=[("wk", ws)])
                P.dma("pool", wv[:, ws, :, :], w_in_v[:, :, vc:vc + 128], writes=[("wv", ws)])
                for n in range(8):
                    bank = 2 * (n % 2)
                    pv = psf(C, bank)
                    for k in range(8):
                        P.op("pe", lambda e, k=k, n=n, pv=pv: e.matmul(pv, lhsT=wk[:, ws, k, :], rhs=hnTf[:, k, n * 512:(n + 1) * 512],
                                                                       start=(k == 0), stop=(k == 7)),
                             reads=[("wk", ws), ("hnTf", n)], writes=[("ps", bank)])
                    P.op("dve", lambda e, n=n, pv=pv: e.tensor_copy(out=KT[:, n * 512:(n + 1) * 512], in_=pv),
                         reads=[("ps", bank)], writes=[("KT", n)])
                for n in range(NT // 4):
                    bank = 2 * (n % 2)
                    pv = psf(C, bank)
                    for k in range(8):
                        P.op("pe", lambda e, k=k, n=n, pv=pv: e.matmul(pv, lhsT=wq[:, ws, k, :], rhs=hnTo[:, k, n * 512:(n + 1) * 512],
                                                                       start=(k == 0), stop=(k == 7)),
                             reads=[("wq", ws), (("hnTf" if full else "hnTo"), n)], writes=[("ps", bank)])
                    P.op("dve", lambda e: e.tensor_copy(out=QT[0:64, 0, n * 512:(n + 1) * 512], in_=pv[0:64, :]),
                         reads=[("ps", bank)], writes=[("QT", n)])
                    P.op("dve", lambda e: e.tensor_copy(out=QT[64:128, 1, n * 512:(n + 1) * 512], in_=pv[64:128, :]),
                         reads=[("ps", bank), ("QT", n)], writes=[("QT", n)])
                for kq in range(8):
                    bank = 2 * (kq % 2)
                    pv = psf(C, bank)
                    for j in range(4):
                        kb = kq * 4 + j
                        for k in range(8):
                            P.op("pe", lambda e, k=k, kb=kb, j=j, pv=pv: e.matmul(pv[:, j * 128:(j + 1) * 128], lhsT=hnTf[:, k, kb * 128:(kb + 1) * 128],
                                                                                  rhs=wv[:, ws, k, :], start=(k == 0), stop=(k == 7)),
                                 reads=[("wv", ws), ("hnTf", kb // 4)], writes=[("ps", bank)])
                    if diff:
                        dst = Vd[:, kq * 4:(kq + 1) * 4, 0:128]
                        srcv = pv.rearrange("p (j d) -> p j d", j=4)
                        vkey = ("V", kq)
                    else:
                        dst = Vf[:, kq * 4:(kq + 1) * 4, :, 0:64]
                        srcv = pv.rearrange("p (j h d) -> p j h d", j=4, h=2)
                        vkey = ("V", kq)
                    P.op("dve", lambda e, dst=dst, srcv=srcv: e.tensor_copy(out=dst, in_=srcv), reads=[("ps", bank)], writes=[vkey])

                for p in range(NCH):
                    nkb = KPC * p + KPC
                    kb0 = KPC * p
                    oset = p % 2
                    if not diff:
                        fsl = p % 2
                        for sh in range(2):
                            hh_ = 2 * (gidx - 4) + sh
                            P.op("dve", lambda e: e.tensor_scalar(out=fb[:, fsl, sh, 0:nkb], in0=Stm[:, 0:nkb, hh_],
                                                                  scalar1=Xb[:, p * 8 + hh_:p * 8 + hh_ + 1], scalar2=None, op0=ALU.subtract),
                                 reads=["Stm", "Xb"], writes=[("fb", fsl)])
                    def emit_S(kb, p=p):
                        slot = kb % 4
                        for sh in range(2):
                            pvS = psf(C, slot)[:, sh * 256:(sh + 1) * 256]
                            P.op("pe", lambda e: e.matmul(pvS, lhsT=KT[:, kb * 128:(kb + 1) * 128],
                                                          rhs=QT[:, sh, p * 256:(p + 1) * 256], start=True, stop=True),
                                 reads=[("KT", kb // 4), ("QT", p // 2)], writes=[("ps", slot)])

                    def emit_PV(kb, p=p, nkb=nkb, oset=oset):
                        slot = kb % 4
                        for sh in range(2):
                            if diff:
                                ai = kb - kb0 + (30 if full else 28)
                                bias = alibi[:, gidx, ai:ai + 1]
                                bk = "alibi"
                            else:
                                bias = fb[:, p % 2, sh, kb:kb + 1]
                                bk = ("fb", p % 2)
                            P.op("act", lambda e: e.activation(out=PT[:, slot, sh, :], in_=psf(C, slot)[:, sh * 256:(sh + 1) * 256],
                                                               func=AF.Exp, scale=0.125, bias=bias),
                                 reads=[("ps", slot), bk], writes=[("PT", slot, sh)])
                        if kb >= kb0:
                            mk = masks[:, kb - kb0, :].unsqueeze(1).to_broadcast([128, 2, 256])
                            P.op("pool", lambda e: e.tensor_tensor(out=PT[:, slot, :, :], in0=PT[:, slot, :, :], in1=mk, op=ALU.mult),
                                 reads=[("PT", slot, 0), ("PT", slot, 1), "masks"], writes=[("PT", slot, 0), ("PT", slot, 1)])
                        for sh in range(2):
                            obank = 4 + oset * 2 + sh
                            for e_ in range(2):
                                if diff:
                                    ov = psf(C, obank)[:, e_ * 129:(e_ + 1) * 129]
                                    rhs = Vd[:, kb, :]
                                else:
                                    ov = psf(C, obank)[:, e_ * 65:(e_ + 1) * 65]
                                    rhs = Vf[:, kb, sh, :]
                                first = (kb == 0 and e_ == 0)
                                P.op("pe", lambda e: e.matmul(ov, lhsT=PT[:, slot, sh, e_ * 128:(e_ + 1) * 128], rhs=rhs,
                                                              start=first, stop=(kb == nkb - 1), skip_group_check=True),
                                     reads=[("PT", slot, sh), ("V", kb // 4)], writes=[("ps", obank)])

                    LOOK = 2
                    for kb in range(min(LOOK, nkb)):
                        emit_S(kb)
                    for kb in range(nkb):
                        if kb + LOOK < nkb:
                            emit_S(kb + LOOK)
                        emit_PV(kb)

                    b0 = 4 + oset * 2
                    i0 = rr_i[0] % 2
                    rr_i[0] += 1
                    rk = ("rr", i0)
                    if diff:
                        O1 = psf(C, b0)[:, 0:258].rearrange("p (e d) -> p e d", e=2)
                        O2 = psf(C, b0 + 1)[:, 0:258].rearrange("p (e d) -> p e d", e=2)
                        P.op("dve", lambda e: e.reciprocal(out=rr[:, i0, 0:2], in_=O1[:, :, 128]), reads=[("ps", b0)], writes=[rk])
                        P.op("dve", lambda e: e.reciprocal(out=rr[:, i0, 2:4], in_=O2[:, :, 128]), reads=[("ps", b0 + 1)], writes=[rk])
                        P.op("dve", lambda e: e.tensor_scalar(out=rr[:, i0, 2:4], in0=rr[:, i0, 2:4], scalar1=neglam[:, 2:3], scalar2=None, op0=ALU.mult),
                             reads=[rk, "neglam"], writes=[rk])
                        for e_ in range(2):
                            lt = 2 * p + e_
                            P.op("dve", lambda e, e_=e_: e.tensor_scalar(out=t1[:, e_, :], in0=O1[:, e_, 0:128], scalar1=rr[:, i0, e_:e_ + 1],
                                                                         scalar2=None, op0=ALU.mult),
                                 reads=[("ps", b0), rk], writes=[("t1", e_)])
                            P.op("dve", lambda e, e_=e_: e.scalar_tensor_tensor(out=dd[:, e_, :], in0=O2[:, e_, 0:128], scalar=rr[:, i0, 2 + e_:3 + e_],
                                                                                in1=t1[:, e_, :], op0=ALU.mult, op1=ALU.add),
                                 reads=[("ps", b0 + 1), rk, ("t1", e_)], writes=[("dd", e_)])
                            rs, rsk = rstd_from(C, dd[:, e_, :], [("dd", e_)], 128, 1e-5, eng="dve")
                            P.op("dve", lambda e, e_=e_, lt=lt, rs=rs: e.scalar_tensor_tensor(out=obuf[:, lt, gidx * 128:(gidx + 1) * 128], in0=dd[:, e_, :],
                                                                                             scalar=rs, in1=subg[:, :], op0=ALU.mult, op1=ALU.mult),
                                 reads=[("dd", e_), rsk, "subg"], writes=[("obuf", lt)])
                    else:
                        for sh in range(2):
                            Ov = psf(C, b0 + sh)[:, 0:130].rearrange("p (e d) -> p e d", e=2)
                            P.op("dve", lambda e, sh=sh, Ov=Ov: e.reciprocal(out=rr[:, i0, 4 + 2 * sh:6 + 2 * sh], in_=Ov[:, :, 64]),
                                 reads=[("ps", b0 + sh)], writes=[rk])
                            for e_ in range(2):
                                lt = 2 * p + e_
                                c0 = 512 + (2 * (gidx - 4) + sh) * 64
                                P.op("dve", lambda e, sh=sh, e_=e_, lt=lt, c0=c0, Ov=Ov: e.tensor_scalar(out=obuf[:, lt, c0:c0 + 64], in0=Ov[:, e_, 0:64],
                                                                                                       scalar1=rr[:, i0, 4 + 2 * sh + e_:5 + 2 * sh + e_],
                                                                                                       scalar2=None, op0=ALU.mult),
                                     reads=[("ps", b0 + sh), rk], writes=[("obuf", lt)])

        if stage < 3:
            return
        with scope(P) as es2:
            wo = es2.enter_context(nc.sbuf_tensor("wo", [128, 8, D], BF16))
            kwo = load_w_bf16(C, wo, T["w_out"], 8, "wo", nsplit=2)
            oT = es2.enter_context(nc.sbuf_tensor("oT", [128, 2, 8, 128], BF16))
            xr = es2.enter_context(nc.sbuf_tensor("xres", [128, 2, D], F32))
            hh = es2.enter_context(nc.sbuf_tensor("hh", [128, 2, D], F32))
            xsrc = T["xf" if full else "xo"]

            def a4_front(lt):
                sl = lt % 2
                P.dma("sp", xr[:, sl, :], xsrc[lt * 128:(lt + 1) * 128, :], writes=[("xres", sl)])
                transpose_to(C, obuf[:, lt, :], [("obuf", lt)], oT[:, sl, :, :], [("oT", sl)], bank=6 + sl, evac="act")
                for hf in range(2):
                    pk = ("ps", 2 * sl + hf)
                    pv = psf(C, 2 * sl + hf)
                    for k in range(8):
                        P.op("pe", lambda e: e.matmul(pv, lhsT=oT[:, sl, k, :], rhs=wo[:, k, hf * 512:(hf + 1) * 512],
                                                      start=(k == 0), stop=(k == 7)),
                             reads=[("oT", sl), kwo[k // 4]], writes=[pk])

            def a4_back(lt):
                sl = lt % 2
                emit_norm_residual(C, C.ps[sl][:, :], [("ps", 2 * sl), ("ps", 2 * sl + 1)], g[:, 1, :], ("ga", 1),
                                   xr[:, sl, :], [("xres", sl)], hh[:, sl, :], [("hh", sl)])
                P.dma("sp", H[lt * 128:(lt + 1) * 128, :], hh[:, sl, :], reads=[("hh", sl)], writes=[("H", lt)])

            a4_front(0)
            for lt in range(NT):
                if lt + 1 < NT:
                    a4_front(lt + 1)
                a4_back(lt)


def build_phase_A(upto=99, stage=9):
    nc = bass.Bass("TRN2", target_bir_lowering=False)
    T = {}

    def din(name, shape, dt=F32):
        T[name] = nc.dram_tensor(name, shape, dt, kind="ExternalInput").ap()

    din("xf", [S, D]); din("xo", [NOWN, D]); din("pT", [256, NOWN])
    din("w_in", [D, 3080]); din("w_out", [D, D]); din("w_up", [D, DFF]); din("w_dn", [DFF, D])
    din("w_pp", [256, D]); din("w_pg", [D, D]); din("vec", [8, D]); din("lamv", [1, 256]); din("bf", [8, 1])
    din("ident", [128, 128]); din("masks", [4, 128, 256]); din("alibi", [128, 128])
    H = nc.dram_tensor("H", [NOWN, D], F32, kind="ExternalOutput").ap()
    H1 = nc.dram_tensor("H1", [NOWN, D], F32, kind="ExternalOutput").ap()
    HNT = nc.dram_tensor("HNT", [D, NOWN], BF16, kind="ExternalOutput").ap()
    with contextlib.ExitStack() as es:
        P = Prog(nc, es)
        C = Ctx()
        setup_common(nc, P, es, C, T["ident"])
        setup_eps(C, es)
        emit_attention(C, T, H, stage)
        if upto >= 2:
            emit_mlp(C, H, T["w_up"], T["w_dn"], T["vec"], 2, 3, "0")
        if upto >= 3:
            emit_ple(C, H, T["pT"], T["w_pp"], T["w_pg"], T["vec"], 4, H1, "0", hn_row=5, HNT=HNT)
        P.finish()
    return nc


def emit_rec(C, T, G, ncb=5):
    import itertools
    nc, P = C.nc, C.P
    TH = 1024
    NTH = S // TH
    assert ncb % 2 == 0 or ncb == 5
    with scope(P) as es:
        hnT = es.enter_context(nc.sbuf_tensor("r_hnT", [128, 8, S], BF16))
        wg = es.enter_context(nc.sbuf_tensor("r_wg", [128, 8, ncb * 128], BF16))
        wxr = es.enter_context(nc.sbuf_tensor("r_wxr", [128, 8, ncb * 128], BF16))
        wxs = es.enter_context(nc.sbuf_tensor("r_wxs", [128, ncb, 128], BF16))
        was = es.enter_context(nc.sbuf_tensor("r_was", [128, ncb, 128], BF16))
        rv = es.enter_context(nc.sbuf_tensor("r_rv", [128, ncb, 8], F32))
        sc = es.enter_context(nc.sbuf_tensor("r_sc", [128, ncb], F32))
        hlast = es.enter_context(nc.sbuf_tensor("r_hlast", [128, 2], F32))
        ybf = es.enter_context(nc.sbuf_tensor("r_y", [128, 2, 2, TH], BF16))
        xr = es.enter_context(nc.sbuf_tensor("r_xr", [128, 2, 2, TH + 3], F32))
        xc = es.enter_context(nc.sbuf_tensor("r_xc", [128, 2, TH], F32))
        xcb = es.enter_context(nc.sbuf_tensor("r_xcb", [128, 2, TH], BF16))
        gx = es.enter_context(nc.sbuf_tensor("r_gx", [128, 2, TH], F32))
        ga = es.enter_context(nc.sbuf_tensor("r_ga", [128, 2, TH], F32))
        tt = es.enter_context(nc.sbuf_tensor("r_tt", [128, 2, TH], F32))
        hs = es.enter_context(nc.sbuf_tensor("r_hs", [128, 2, TH], F32))
        gout = es.enter_context(nc.sbuf_tensor("r_gout", [128, 2, TH], BF16))
        hv = T["hnT"].rearrange("(k f) t -> f k t", f=128)
        for k in range(8):
            P.dma("sp", hnT[:, k, :], hv[:, k, :], writes=[("r_hnT", k)])
        kg = load_w_bf16(C, wg, T["w_g"], 8, "r_wg", nsplit=2)
        kx = load_w_bf16(C, wxr, T["w_x"], 8, "r_wxr", nsplit=2)
        P.dma("pool", wxs[:], T["wx"].rearrange("n i j -> i n j"), writes=["r_wxs"])
        P.dma("pool", was[:], T["wa"].rearrange("n i j -> i n j"), writes=["r_was"])
        P.dma("sp", rv[:], T["rvec"], writes=["r_rv"])
        P.op("act", lambda e: e.activation(out=sc[:, :], in_=rv[:, :, 7], func=AF.Exp, scale=-1.0), reads=["r_rv"], writes=["r_sc"])
        P.op("act", lambda e: e.activation(out=sc[:, :], in_=sc[:, :], func=AF.Ln, bias=1.0), reads=["r_sc"], writes=["r_sc"])
        P.op("dve", lambda e: e.tensor_scalar(out=sc[:, :], in0=sc[:, :], scalar1=-8.0, scalar2=None, op0=ALU.mult), reads=["r_sc"], writes=["r_sc"])

        def stage1(cb, th, st, bs):
            if th == 0:
                P.op("dve", lambda e: e.memset(xr[:, st, bs, 0:3], 0.0), writes=[("r_xr", st, bs)])
            else:
                P.op("dve", lambda e: e.tensor_copy(out=xr[:, st, bs, 0:3], in_=xr[:, st, 1 - bs, TH:TH + 3]),
                     reads=[("r_xr", st, 1 - bs)], writes=[("r_xr", st, bs)])
            yield
            for n in range(TH // 512):
                N = (TH // 512) * th + n
                bg = 2 * st
                pv = psf(C, bg)
                for k in range(8):
                    P.op("pe", lambda e: e.matmul(pv, lhsT=wg[:, k, cb * 128:(cb + 1) * 128], rhs=hnT[:, k, N * 512:(N + 1) * 512],
                                                  start=(k == 0), stop=(k == 7)),
                         reads=[kg[k // 4], ("r_hnT", k)], writes=[("ps", bg)])
                P.op("act", lambda e: e.activation(out=ybf[:, st, bs, n * 512:(n + 1) * 512], in_=pv, func=AF.Gelu_apprx_tanh),
                     reads=[("ps", bg)], writes=[("r_y", st, bs)])
                yield
                bx_ = 2 * st + 1
                pv2 = psf(C, bx_)
                for k in range(8):
                    P.op("pe", lambda e: e.matmul(pv2, lhsT=wxr[:, k, cb * 128:(cb + 1) * 128], rhs=hnT[:, k, N * 512:(N + 1) * 512],
                                                  start=(k == 0), stop=(k == 7)),
                         reads=[kx[k // 4], ("r_hnT", k)], writes=[("ps", bx_)])
                P.op("dve", lambda e: e.tensor_copy(out=xr[:, st, bs, 3 + n * 512:3 + (n + 1) * 512], in_=pv2),
                     reads=[("ps", bx_)], writes=[("r_xr", st, bs)])
                yield

        def stage2(cb, th, st, bs):
            xk = ("r_xr", st, bs)
            K = lambda name: (name, st)
            P.op("act", lambda e: e.activation(out=xc[:, st, :], in_=xr[:, st, bs, 3:3 + TH], func=AF.Identity, scale=rv[:, cb, 3:4], bias=rv[:, cb, 4:5]),
                 reads=[xk, "r_rv"], writes=[K("r_xc")])
            yield
            for w in range(3):
                P.op("dve", lambda e: e.scalar_tensor_tensor(out=xc[:, st, :], in0=xr[:, st, bs, w:w + TH], scalar=rv[:, cb, w:w + 1], in1=xc[:, st, :],
                                                             op0=ALU.mult, op1=ALU.add),
                     reads=[xk, "r_rv", K("r_xc")], writes=[K("r_xc")])
                yield
            P.op("act", lambda e: e.copy(out=xcb[:, st, :], in_=xc[:, st, :]), reads=[K("r_xc")], writes=[K("r_xcb")])
            yield
            for n in range(TH // 512):
                b1 = 4 + 2 * st
                pv = psf(C, b1)
                P.op("pe", lambda e: e.matmul(pv, lhsT=wxs[:, cb, :], rhs=xcb[:, st, n * 512:(n + 1) * 512], start=True, stop=True),
                     reads=["r_wxs", K("r_xcb")], writes=[("ps", b1)])
                P.op("act", lambda e: e.activation(out=gx[:, st, n * 512:(n + 1) * 512], in_=pv, func=AF.Sigmoid, bias=rv[:, cb, 5:6]),
                     reads=[("ps", b1), "r_rv"], writes=[K("r_gx")])
                yield
                b2 = 5 + 2 * st
                pv2 = psf(C, b2)
                P.op("pe", lambda e: e.matmul(pv2, lhsT=was[:, cb, :], rhs=xcb[:, st, n * 512:(n + 1) * 512], start=True, stop=True),
                     reads=["r_was", K("r_xcb")], writes=[("ps", b2)])
                P.op("act", lambda e: e.activation(out=ga[:, st, n * 512:(n + 1) * 512], in_=pv2, func=AF.Sigmoid, bias=rv[:, cb, 6:7]),
                     reads=[("ps", b2), "r_rv"], writes=[K("r_ga")])
                yield
            P.op("act", lambda e: e.activation(out=ga[:, st, :], in_=ga[:, st, :], func=AF.Exp, scale=sc[:, cb:cb + 1]),
                 reads=[K("r_ga"), "r_sc"], writes=[K("r_ga")])
            yield
            P.op("dve", lambda e: e.tensor_tensor(out=tt[:, st, :], in0=ga[:, st, :], in1=ga[:, st, :], op=ALU.mult), reads=[K("r_ga")], writes=[K("r_tt")])
            yield
            P.op("dve", lambda e: e.tensor_tensor(out=gx[:, st, :], in0=gx[:, st, :], in1=xc[:, st, :], op=ALU.mult),
                 reads=[K("r_gx"), K("r_xc")], writes=[K("r_gx")])
            yield
            P.op("act", lambda e: e.activation(out=tt[:, st, :], in_=tt[:, st, :], func=AF.Sqrt, scale=-1.0, bias=1.0), reads=[K("r_tt")], writes=[K("r_tt")])
            yield
            P.op("dve", lambda e: e.tensor_tensor(out=tt[:, st, :], in0=tt[:, st, :], in1=gx[:, st, :], op=ALU.mult),
                 reads=[K("r_tt"), K("r_gx")], writes=[K("r_tt")])
            if th == 0:
                P.op("dve", lambda e: e.tensor_copy(out=tt[:, st, 0:1], in_=gx[:, st, 0:1]), reads=[K("r_gx"), K("r_tt")], writes=[K("r_tt")])
            yield
            init = 0.0 if th == 0 else hlast[:, st:st + 1]
            P.op("dve", lambda e: e.tensor_tensor_scan(out=hs[:, st, :], data0=ga[:, st, :], data1=tt[:, st, :], initial=init, op0=ALU.mult, op1=ALU.add),
                 reads=[K("r_ga"), K("r_tt"), K("r_hlast")], writes=[K("r_hs")])
            P.op("dve", lambda e: e.tensor_copy(out=hlast[:, st:st + 1], in_=hs[:, st, TH - 1:TH]), reads=[K("r_hs")], writes=[K("r_hlast")])
            yield
            P.op("dve", lambda e: e.tensor_tensor(out=gout[:, st, :], in0=hs[:, st, :], in1=ybf[:, st, bs, :], op=ALU.mult),
                 reads=[K("r_hs"), ("r_y", st, bs)], writes=[K("r_gout")])
            P.dma("sp", G[cb * 128:(cb + 1) * 128, th * TH:(th + 1) * TH], gout[:, st, :], reads=[K("r_gout")], writes=[("G", cb, th)])
            yield

        def interleave(gens):
            for _ in itertools.zip_longest(*gens):
                pass

        supers = []
        for j in range((ncb + 1) // 2):
            cbs = [c for c in (2 * j, 2 * j + 1) if c < ncb]
            for th in range(NTH):
                supers.append((cbs, th))
        interleave([stage1(cb, supers[0][1], st, 0) for st, cb in enumerate(supers[0][0])])
        for i, (cbs, th) in enumerate(supers):
            bs = i % 2
            if i + 1 < len(supers):
                ncbs, nth = supers[i + 1]
                interleave([stage1(cb, nth, st, 1 - bs) for st, cb in enumerate(ncbs)])
            interleave([stage2(cb, th, st, bs) for st, cb in enumerate(cbs)])


def build_phase_B():
    nc = bass.Bass("TRN2", target_bir_lowering=False)
    T = {}

    def din(name, shape, dt=F32):
        T[name] = nc.dram_tensor(name, shape, dt, kind="ExternalInput").ap()

    din("hnT", [D, S], BF16); din("w_g", [D, 640]); din("w_x", [D, 640]); din("wx", [5, 128, 128]); din("wa", [5, 128, 128])
    din("rvec", [128, 5, 8]); din("ident", [128, 128])
    G = nc.dram_tensor("G", [640, S], BF16, kind="ExternalOutput").ap()
    with contextlib.ExitStack() as es:
        P = Prog(nc, es)
        C = Ctx()
        setup_common(nc, P, es, C, T["ident"])
        setup_eps(C, es)
        emit_rec(C, T, G)
        P.finish()
    return nc


def emit_recout(C, T, H, row=0, blend=None):
    nc, P = C.nc, C.P
    with scope(P) as es:
        g = load_gains(C, es, T["vec"], [row], "gro")
        gT = es.enter_context(nc.sbuf_tensor("gTs", [128, 10, NOWN], BF16))
        wro = es.enter_context(nc.sbuf_tensor("wro", [128, 10, D], BF16))
        if blend is None:
            gv = T["gT"].rearrange("(c p) t -> p c t", p=128)
            for c in range(10):
                P.dma("sp", gT[:, c, :], gv[:, c, :], writes=[("gTs", c)])
        else:
            Gd, H1d, sel_d = blend
            sel = es.enter_context(nc.sbuf_tensor("sel_sb", [128, 2], F32))
            P.dma("sp", sel[:], sel_d, writes=["sel"])
            gT2 = es.enter_context(nc.sbuf_tensor("gTs2", [128, 2, NOWN], BF16))
            hres2 = es.enter_context(nc.sbuf_tensor("hres2", [128, 2, D], F32))
            gv = Gd.rearrange("(c p) t -> p c t", p=128)
            for c in range(10):
                s2 = c % 2
                P.dma("sp", gT[:, c, :], gv[:, c, 0:NOWN], writes=[("gTs", c)])
                P.dma("sp", gT2[:, s2, :], gv[:, c, NOWN:2 * NOWN], writes=[("gTs2", s2)])
                P.op("act", lambda e: e.activation(out=gT[:, c, :], in_=gT[:, c, :], func=AF.Copy, scale=sel[:, 0:1]),
                     reads=[("gTs", c), "sel"], writes=[("gTs", c)])
                P.op("dve", lambda e: e.scalar_tensor_tensor(out=gT[:, c, :], in0=gT2[:, s2, :], scalar=sel[:, 1:2], in1=gT[:, c, :],
                                                             op0=ALU.mult, op1=ALU.add),
                     reads=[("gTs", c), ("gTs2", s2), "sel"], writes=[("gTs", c)])
        kw = load_w_bf16(C, wro, T["w_ro"], 10, "wro", nsplit=2)
        hres = es.enter_context(nc.sbuf_tensor("hres", [128, 2, D], F32))
        hh = es.enter_context(nc.sbuf_tensor("hh1", [128, 2, D], F32))
        for lt in range(16):
            sl = lt % 2
            if blend is None:
                P.dma("sp", hres[:, sl, :], T["h1"][lt * 128:(lt + 1) * 128, :], writes=[("hres", sl)])
            else:
                P.dma("sp", hres[:, sl, :], H1d[lt * 128:(lt + 1) * 128, :], writes=[("hres", sl)])
                P.dma("sp", hres2[:, sl, :], H1d[NOWN + lt * 128:NOWN + (lt + 1) * 128, :], writes=[("hres2", sl)])
                P.op("act", lambda e: e.activation(out=hres[:, sl, :], in_=hres[:, sl, :], func=AF.Copy, scale=sel[:, 0:1]),
                     reads=[("hres", sl), "sel"], writes=[("hres", sl)])
                P.op("dve", lambda e: e.scalar_tensor_tensor(out=hres[:, sl, :], in0=hres2[:, sl, :], scalar=sel[:, 1:2], in1=hres[:, sl, :],
                                                             op0=ALU.mult, op1=ALU.add),
                     reads=[("hres", sl), ("hres2", sl), "sel"], writes=[("hres", sl)])
            for hf in range(2):
                pk = ("ps", 2 * sl + hf)
                pv = psf(C, 2 * sl + hf)
                for c in range(10):
                    P.op("pe", lambda e: e.matmul(pv, lhsT=gT[:, c, lt * 128:(lt + 1) * 128], rhs=wro[:, c, hf * 512:(hf + 1) * 512],
                                                  start=(c == 0), stop=(c == 9)),
                         reads=[("gTs", c), kw[c // 5]], writes=[pk])
            emit_norm_residual(C, C.ps[sl][:, :], [("ps", 2 * sl), ("ps", 2 * sl + 1)], g[:, 0, :], ("gro", 0),
                               hres[:, sl, :], [("hres", sl)], hh[:, sl, :], [("hh1", sl)])
            P.dma("sp", H[lt * 128:(lt + 1) * 128, :], hh[:, sl, :], reads=[("hh1", sl)], writes=[("H", lt)])


def build_phase_C():
    nc = bass.Bass("TRN2", target_bir_lowering=False)
    T = {}

    def din(name, shape, dt=F32):
        T[name] = nc.dram_tensor(name, shape, dt, kind="ExternalInput").ap()

    din("gT", [RW, NOWN], BF16); din("h1", [NOWN, D]); din("pT", [256, NOWN]); din("w_ro", [RW, D])
    din("w_up", [D, DFF]); din("w_dn", [DFF, D]); din("w_pp", [256, D]); din("w_pg", [D, D]); din("vec", [8, D]); din("ident", [128, 128])
    H = nc.dram_tensor("H", [NOWN, D], F32, kind="ExternalOutput").ap()
    OUT = nc.dram_tensor("OUT", [NOWN, D], F32, kind="ExternalOutput").ap()
    with contextlib.ExitStack() as es:
        P = Prog(nc, es)
        C = Ctx()
        setup_common(nc, P, es, C, T["ident"])
        setup_eps(C, es)
        emit_recout(C, T, H)
        emit_mlp(C, H, T["w_up"], T["w_dn"], T["vec"], 1, 2, "1")
        emit_ple(C, H, T["pT"], T["w_pp"], T["w_pg"], T["vec"], 3, OUT, "1")
        P.finish()
    return nc


def build_fused():
    nc = bass.Bass("TRN2", target_bir_lowering=False)
    T = {}

    def din(name, shape, dt=F32):
        T[name] = nc.dram_tensor(name, shape, dt, kind="ExternalInput").ap()

    din("xf", [S, D]); din("pT", [256, S]); din("pT1", [256, NOWN])
    din("w_in", [D, 3080]); din("w_out", [D, D]); din("w_up", [D, DFF]); din("w_dn", [DFF, D])
    din("w_pp", [256, D]); din("w_pg", [D, D]); din("vec", [16, D]); din("lamv", [1, 256]); din("bf", [8, 1])
    din("ident", [128, 128]); din("masks", [2, 128, 256]); din("alibi", [128, 128]); din("sel", [128, 2])
    din("w_g", [D, RW]); din("w_x", [D, RW]); din("wx", [10, 128, 128]); din("wa", [10, 128, 128]); din("rvec", [128, 10, 8])
    din("w_ro", [RW, D]); din("w_up1", [D, DFF]); din("w_dn1", [DFF, D]); din("w_pp1", [256, D]); din("w_pg1", [D, D])
    HA = nc.dram_tensor("HA", [S, D], F32, kind="Internal").ap()
    H1 = nc.dram_tensor("H1", [S, D], F32, kind="Internal").ap()
    HNT = nc.dram_tensor("HNT", [D, S], BF16, kind="Internal").ap()
    G = nc.dram_tensor("G", [RW, S], BF16, kind="Internal").ap()
    HC = nc.dram_tensor("HC", [NOWN, D], F32, kind="Internal").ap()
    OUT = nc.dram_tensor("OUT", [NOWN, D], F32, kind="ExternalOutput").ap()
    with contextlib.ExitStack() as es:
        P = Prog(nc, es)
        C = Ctx()
        setup_common(nc, P, es, C, T["ident"])
        setup_eps(C, es)
        emit_attention(C, T, HA, full=True)
        emit_mlp(C, HA, T["w_up"], T["w_dn"], T["vec"], 2, 3, "0", ntok=S)
        emit_ple(C, HA, T["pT"], T["w_pp"], T["w_pg"], T["vec"], 4, H1, "0", hn_row=5, HNT=HNT, ntiles=32)
        T1 = {"hnT": HNT, "w_g": T["w_g"], "w_x": T["w_x"], "wx": T["wx"], "wa": T["wa"], "rvec": T["rvec"]}
        emit_rec(C, T1, G, ncb=10)
        T2 = {"vec": T["vec"], "w_ro": T["w_ro"]}
        emit_recout(C, T2, HC, row=8, blend=(G, H1, T["sel"]))
        emit_mlp(C, HC, T["w_up1"], T["w_dn1"], T["vec"], 9, 10, "1", ntok=NOWN)
        emit_ple(C, HC, T["pT1"], T["w_pp1"], T["w_pg1"], T["vec"], 11, OUT, "1", ntiles=16)
        P.finish()
    return nc


def own_tiles(r):
    return [4 * p + 2 * r + e for p in range(8) for e in range(2)]


def own_index(r):
    return np.concatenate([np.arange(g * 128, (g + 1) * 128) for g in own_tiles(r)])


def role_consts(r):
    ident = np.eye(128, dtype=np.float32)
    masks = np.zeros((4, 128, 256), np.float32)
    jj = np.arange(128)[:, None]
    ii = np.arange(128)[None, :]
    tri = (jj <= ii).astype(np.float32)
    for m in range(4):
        for e in range(2):
            qt = 2 * r + e
            if m < qt:
                masks[m, :, e * 128:(e + 1) * 128] = 1.0
            elif m == qt:
                masks[m, :, e * 128:(e + 1) * 128] = tri
    alibi = np.zeros((128, 4, 32), np.float32)
    for h in range(4):
        for idx in range(32):
            d = (idx - 28 - 2 * r - 2) * 128 + np.arange(128)
            alibi[:, h, idx] = np.minimum(SLOPES[h] * d, 0.0)
    return ident, masks, alibi.reshape(128, 128)


def f32c(a):
    return np.ascontiguousarray(a, dtype=np.float32)


def prep_A(inp, c):
    b, r = c // 2, c % 2
    oi = own_index(r)
    ident, masks, alibi = role_consts(r)
    vec = np.zeros((8, D), np.float32)
    vec[0] = inp["ln_mix_pre"][0]
    vec[1] = inp["ln_mix_post"][0]
    vec[2] = inp["ln_mlp_pre"][0]
    vec[3] = inp["ln_mlp_post"][0]
    vec[4] = inp["ple_norm"][0]
    vec[5] = inp["ln_mix_pre"][1]
    vec[6, :128] = inp["diff_subln"][0]
    lamv = np.concatenate([inp["diff_lambda_q1"][0], inp["diff_lambda_k1"][0],
                           inp["diff_lambda_q2"][0], inp["diff_lambda_k2"][0]])[None, :]
    return {
        "xf": f32c(inp["x"][b]), "xo": f32c(inp["x"][b][oi]), "pT": f32c(inp["p"][0, b][oi].T),
        "w_in": f32c(inp["attn_w_in"][0]), "w_out": f32c(inp["attn_w_out"][0]),
        "w_up": f32c(inp["mlp_w_up"][0]), "w_dn": f32c(inp["mlp_w_down"][0]),
        "w_pp": f32c(inp["ple_w_proj"][0]), "w_pg": f32c(inp["ple_w_gate"][0]),
        "vec": vec, "lamv": f32c(lamv), "bf": f32c(inp["attn_b_forget"][0][:, None]),
        "ident": ident, "masks": masks, "alibi": alibi,
    }


def prep_B(inp, c, hnT_full):
    r = c % 2
    cols_g = np.arange(5 * r * 128, (5 * r + 5) * 128)
    cols_x = RW + cols_g
    w_in = inp["rec_w_in"][0]
    rvec = np.zeros((128, 5, 8), np.float32)
    for j in range(5):
        ch = np.arange((5 * r + j) * 128, (5 * r + j + 1) * 128)
        rvec[:, j, 0:4] = inp["rec_conv_w"][0][:, ch].T
        rvec[:, j, 4] = inp["rec_conv_b"][0][ch]
        rvec[:, j, 5] = inp["rec_bx"][0][ch]
        rvec[:, j, 6] = inp["rec_ba"][0][ch]
        rvec[:, j, 7] = inp["rec_a_param"][0][ch]
    return {
        "hnT": hnT_full, "w_g": f32c(w_in[:, cols_g]), "w_x": f32c(w_in[:, cols_x]),
        "wx": f32c(inp["rec_wx"][0][5 * r:5 * r + 5]), "wa": f32c(inp["rec_wa"][0][5 * r:5 * r + 5]),
        "rvec": rvec, "ident": np.eye(128, dtype=np.float32),
    }


def prep_C(inp, c, gT_own, h1_own):
    b, r = c // 2, c % 2
    oi = own_index(r)
    vec = np.zeros((8, D), np.float32)
    vec[0] = inp["ln_mix_post"][1]
    vec[1] = inp["ln_mlp_pre"][1]
    vec[2] = inp["ln_mlp_post"][1]
    vec[3] = inp["ple_norm"][1]
    return {
        "gT": gT_own, "h1": h1_own, "pT": f32c(inp["p"][1, b][oi].T), "w_ro": f32c(inp["rec_w_out"][0]),
        "w_up": f32c(inp["mlp_w_up"][1]), "w_dn": f32c(inp["mlp_w_down"][1]),
        "w_pp": f32c(inp["ple_w_proj"][1]), "w_pg": f32c(inp["ple_w_gate"][1]),
        "vec": vec, "ident": np.eye(128, dtype=np.float32),
    }


def full_consts():
    ident = np.eye(128, dtype=np.float32)
    jj = np.arange(128)[:, None]
    ii = np.arange(128)[None, :]
    tri = (jj <= ii).astype(np.float32)
    masks = np.zeros((2, 128, 256), np.float32)
    masks[0, :, 0:128] = tri
    masks[0, :, 128:256] = 1.0
    masks[1, :, 128:256] = tri
    alibi = np.zeros((128, 4, 32), np.float32)
    for h in range(4):
        for idx in range(32):
            d = (idx - 30 - 2) * 128 + np.arange(128)
            alibi[:, h, idx] = np.minimum(SLOPES[h] * d, 0.0)
    return ident, masks, alibi.reshape(128, 128)


def prep_fused(inp, c):
    b, r = c // 2, c % 2
    ident, masks, alibi = full_consts()
    vec = np.zeros((16, D), np.float32)
    vec[0] = inp["ln_mix_pre"][0]
    vec[1] = inp["ln_mix_post"][0]
    vec[2] = inp["ln_mlp_pre"][0]
    vec[3] = inp["ln_mlp_post"][0]
    vec[4] = inp["ple_norm"][0]
    vec[5] = inp["ln_mix_pre"][1]
    vec[6, :128] = inp["diff_subln"][0]
    vec[8] = inp["ln_mix_post"][1]
    vec[9] = inp["ln_mlp_pre"][1]
    vec[10] = inp["ln_mlp_post"][1]
    vec[11] = inp["ple_norm"][1]
    lamv = np.concatenate([inp["diff_lambda_q1"][0], inp["diff_lambda_k1"][0],
                           inp["diff_lambda_q2"][0], inp["diff_lambda_k2"][0]])[None, :]
    rvec = np.zeros((128, 10, 8), np.float32)
    for j in range(10):
        ch = np.arange(j * 128, (j + 1) * 128)
        rvec[:, j, 0:4] = inp["rec_conv_w"][0][:, ch].T
        rvec[:, j, 4] = inp["rec_conv_b"][0][ch]
        rvec[:, j, 5] = inp["rec_bx"][0][ch]
        rvec[:, j, 6] = inp["rec_ba"][0][ch]
        rvec[:, j, 7] = inp["rec_a_param"][0][ch]
    sel = np.zeros((128, 2), np.float32)
    sel[:, r] = 1.0
    w_in1 = inp["rec_w_in"][0]
    return {
        "xf": f32c(inp["x"][b]), "pT": f32c(inp["p"][0, b].T), "pT1": f32c(inp["p"][1, b][r * NOWN:(r + 1) * NOWN].T),
        "w_in": f32c(inp["attn_w_in"][0]), "w_out": f32c(inp["attn_w_out"][0]),
        "w_up": f32c(inp["mlp_w_up"][0]), "w_dn": f32c(inp["mlp_w_down"][0]),
        "w_pp": f32c(inp["ple_w_proj"][0]), "w_pg": f32c(inp["ple_w_gate"][0]),
        "vec": vec, "lamv": f32c(lamv), "bf": f32c(inp["attn_b_forget"][0][:, None]),
        "ident": ident, "masks": masks, "alibi": alibi, "sel": sel,
        "w_g": f32c(w_in1[:, :RW]), "w_x": f32c(w_in1[:, RW:]), "wx": f32c(inp["rec_wx"][0]), "wa": f32c(inp["rec_wa"][0]),
        "rvec": rvec, "w_ro": f32c(inp["rec_w_out"][0]),
        "w_up1": f32c(inp["mlp_w_up"][1]), "w_dn1": f32c(inp["mlp_w_down"][1]),
        "w_pp1": f32c(inp["ple_w_proj"][1]), "w_pg1": f32c(inp["ple_w_gate"][1]),
    }


_NC_CACHE = {}


def _get(name, fn):
    if name not in _NC_CACHE:
        _NC_CACHE[name] = fn()
    return _NC_CACHE[name]


def kernel_unfused(**inputs):
    inp = {k: np.asarray(v) for k, v in inputs.items()}
    cores = list(range(NCORES))
    ncA = _get("A", build_phase_A)
    resA = run_bass_kernel_spmd(ncA, [prep_A(inp, c) for c in cores], core_ids=cores).results
    hnT_full = []
    for b in range(4):
        full = np.zeros((D, S), dtype=resA[0]["HNT"].dtype)
        for r in range(2):
            full[:, own_index(r)] = resA[2 * b + r]["HNT"]
        hnT_full.append(full)
    ncB = _get("B", build_phase_B)
    resB = run_bass_kernel_spmd(ncB, [prep_B(inp, c, hnT_full[c // 2]) for c in cores], core_ids=cores).results
    ncC = _get("C", build_phase_C)
    mapsC = []
    for c in cores:
        b, r = c // 2, c % 2
        oi = own_index(r)
        gfull = np.concatenate([resB[2 * b]["G"], resB[2 * b + 1]["G"]], axis=0)
        mapsC.append(prep_C(inp, c, np.ascontiguousarray(gfull[:, oi]), resA[c]["H1"]))
    resC = run_bass_kernel_spmd(ncC, mapsC, core_ids=cores).results
    out = np.zeros((4, S, D), np.float32)
    for c in cores:
        b, r = c // 2, c % 2
        out[b, own_index(r)] = resC[c]["OUT"]
    return out


def kernel(**inputs):
    inp = {k: np.asarray(v) for k, v in inputs.items()}
    cores = list(range(NCORES))
    nc = _get("F", build_fused)
    res = run_bass_kernel_spmd(nc, [prep_fused(inp, c) for c in cores], core_ids=cores).results
    out = np.zeros((4, S, D), np.float32)
    for c in cores:
        b, r = c // 2, c % 2
        out[b, r * NOWN:(r + 1) * NOWN] = res[c]["OUT"]
    return out
```

```python
import contextlib
import numpy as np
import concourse.bass as bass
import concourse.mybir as mybir
from concourse.bass_utils import run_bass_kernel_spmd

F32 = mybir.dt.float32
BF16 = mybir.dt.bfloat16
AF = mybir.ActivationFunctionType
ALU = mybir.AluOpType
AX = mybir.AxisListType

NCORES = 8
D = 1024
S = 4096
NOWN = 2048
DFF = 4096
RW = 1280
LAM_INIT0 = 0.8 - 0.6 * 1.0
SLOPES = [2.0 ** (-8.0 * (h + 1) / 4) for h in range(4)]


class Prog:
    K_DMA = 8

    def __init__(self, nc, es):
        self.nc = nc
        self.engs = {"pe": nc.tensor, "act": nc.scalar, "dve": nc.vector,
                     "pool": nc.gpsimd, "sp": nc.sync}
        self.semobj = {}
        for k in self.engs:
            self.semobj[k] = es.enter_context(nc.semaphore("sem_" + k))
        self.cnt = {k: 0 for k in self.engs}
        self.seen = {k: {} for k in self.engs}
        self.lastw = {}
        self.readers = {}
        self.dcnt = {}
        for q in ("sp", "act", "pool"):
            self.dcnt[q] = 0
            for i in range(self.K_DMA):
                self.semobj[(q, i)] = es.enter_context(nc.semaphore("d_%s_%d" % (q, i)))

    def _wait(self, eng, ev):
        sk, val = ev
        if self.seen[eng].get(sk, 0) >= val:
            return
        self.engs[eng].wait_ge(self.semobj[sk], val)
        self.seen[eng][sk] = val

    def _deps(self, eng, reads, writes):
        deps = {}
        for k in reads:
            w = self.lastw.get(k)
            if w is not None:
                deps[w[0]] = max(deps.get(w[0], 0), w[1])
        for k in writes:
            w = self.lastw.get(k)
            if w is not None:
                deps[w[0]] = max(deps.get(w[0], 0), w[1])
            for sk, v in self.readers.get(k, {}).items():
                deps[sk] = max(deps.get(sk, 0), v)
        for sk, v in deps.items():
            if eng == "pe" and sk == "pe":
                continue
            self._wait(eng, (sk, v))

    def _record(self, ev, reads, writes):
        for k in writes:
            self.lastw[k] = ev
            self.readers[k] = {}
        for k in reads:
            if k in writes:
                continue
            d = self.readers.setdefault(k, {})
            d[ev[0]] = max(d.get(ev[0], 0), ev[1])

    def op(self, eng, fn, reads=(), writes=()):
        self._deps(eng, reads, writes)
        inst = fn(self.engs[eng])
        self.cnt[eng] += 1
        inst.then_inc(self.semobj[eng], 1)
        self._record((eng, self.cnt[eng]), reads, writes)

    def dma(self, q, out, in_, reads=(), writes=()):
        self._deps(q, reads, writes)
        i = self.dcnt[q]
        s = i % self.K_DMA
        rnd = i // self.K_DMA
        if rnd > 0:
            self._wait(q, ((q, s), 16 * rnd))
        inst = self.engs[q].dma_start(out=out, in_=in_)
        inst.then_inc(self.semobj[(q, s)], 16)
        self.dcnt[q] = i + 1
        self._record(((q, s), 16 * (rnd + 1)), reads, writes)

    def barrier(self):
        evs = []
        for q in ("sp", "act", "pool"):
            n = self.dcnt[q]
            for s in range(self.K_DMA):
                m = (n - s + self.K_DMA - 1) // self.K_DMA if n > s else 0
                if m > 0:
                    evs.append(((q, s), 16 * m))
        for k in ("pe", "act", "dve", "pool", "sp"):
            if self.cnt[k] > 0:
                evs.append((k, self.cnt[k]))
        for eng in ("pe", "act", "dve", "pool", "sp"):
            for ev in evs:
                if ev[0] == eng:
                    continue
                self._wait(eng, ev)

    def barrier_keys(self, eng, keys):
        self._deps(eng, [], list(keys))

    def finish(self):
        for q in ("sp", "act", "pool"):
            n = self.dcnt[q]
            for s in range(self.K_DMA):
                m = (n - s + self.K_DMA - 1) // self.K_DMA if n > s else 0
                if m > 0:
                    self._wait("sp", ((q, s), 16 * m))
        for k in ("pe", "act", "dve", "pool"):
            if self.cnt[k] > 0:
                self._wait("sp", (k, self.cnt[k]))


class Ctx:
    pass


@contextlib.contextmanager
def scope(P):
    with contextlib.ExitStack() as es:
        yield es
        P.barrier()


def setup_common(nc, P, es, C, ident_d):
    C.nc = nc
    C.P = P
    C.identf = es.enter_context(nc.sbuf_tensor("identf", [128, 128], F32))
    C.identb = es.enter_context(nc.sbuf_tensor("identb", [128, 128], BF16))
    C.onesf = es.enter_context(nc.sbuf_tensor("onesf", [128, 512], F32))
    P.dma("sp", C.identf[:], ident_d, writes=["identf"])
    P.dma("pool", C.identb[:], ident_d, writes=["identb"])
    P.op("pool", lambda e: e.memset(C.onesf[:], 1.0), writes=["onesf"])
    C.ps = [es.enter_context(nc.psum_tensor("ps%d" % i, [128, 1024], F32)) for i in range(4)]
    C.psb = [t.bitcast(BF16) for t in C.ps]
    C.junk = es.enter_context(nc.sbuf_tensor("junk", [128, 1024], BF16))
    C.junk2 = es.enter_context(nc.sbuf_tensor("junk2", [128, 128], BF16))
    C.small = es.enter_context(nc.sbuf_tensor("small", [128, 64], F32))
    C.small_i = 0


def small_slot(C, n=1):
    i = C.small_i
    if i + n > 64:
        i = 0
    C.small_i = i + n
    return i


def psf(C, b):
    return C.ps[b // 2][:, (b % 2) * 512:(b % 2) * 512 + 512]


def psbf(C, b):
    return C.psb[b // 2][:, (b % 2) * 1024:(b % 2) * 1024 + 1024]


def rstd_from(C, src, src_keys, n, eps, eng="act"):
    P = C.P
    i = small_slot(C)
    ss = C.small[:, i:i + 1]
    key = ("small", i)
    if eng == "dve":
        P.op("dve", lambda e: e.scalar_tensor_tensor(out=C.junk2[:, 0:n], in0=src, scalar=1.0, in1=src,
                                                     op0=ALU.mult, op1=ALU.mult, accum_out=ss),
             reads=list(src_keys), writes=["junk2", key])
    else:
        P.op("act", lambda e: e.activation(out=C.junk[:, 0:n], in_=src, func=AF.Square, accum_out=ss),
             reads=list(src_keys), writes=["junk", key])
    P.op("dve", lambda e: e.tensor_scalar(out=ss, in0=ss, scalar1=1.0 / n, scalar2=float(eps), op0=ALU.mult, op1=ALU.add),
         reads=[key], writes=[key])
    P.op("pool", lambda e: e.tensor_tensor(out=ss, in0=ss, in1=C.negh, op=ALU.pow), reads=[key, "epsb"], writes=[key])
    return ss, key


def transpose_to(C, src_bf, src_keys, dst, dst_keys, bank, nk=8, evac="dve"):
    P = C.P
    pk = ("ps", bank)
    pv = psbf(C, bank)
    for k in range(nk):
        P.op("pe", lambda e, k=k: e.transpose(out=pv[:, k * 128:(k + 1) * 128], in_=src_bf[:, k * 128:(k + 1) * 128],
                                              identity=C.identb[:]),
             reads=list(src_keys) + ["identb"], writes=[pk])
    srcv = pv[:, 0:nk * 128].rearrange("p (k t) -> p k t", k=nk)
    if evac == "act":
        P.op("act", lambda e: e.copy(out=dst, in_=srcv), reads=[pk], writes=list(dst_keys))
    else:
        P.op("dve", lambda e: e.tensor_copy(out=dst, in_=srcv), reads=[pk], writes=list(dst_keys))


def load_w_bf16(C, dst, dram2d, nk, key, nsplit=1):
    P = C.P
    src = dram2d.rearrange("(k p) n -> p k n", p=128)
    step = nk // nsplit
    for i in range(nsplit):
        P.dma("pool", dst[:, i * step:(i + 1) * step, :], src[:, i * step:(i + 1) * step, :], writes=[(key, i)])
    return [(key, i) for i in range(nsplit)]


def emit_norm_residual(C, m_src, m_keys, gain, gain_key, res, res_keys, out, out_keys):
    P = C.P
    rs, rk = rstd_from(C, m_src, m_keys, D, 1e-6)
    P.op("dve", lambda e: e.scalar_tensor_tensor(out=out, in0=m_src, scalar=rs, in1=gain, op0=ALU.mult, op1=ALU.mult),
         reads=list(m_keys) + [rk, gain_key], writes=list(out_keys))
    P.op("pool", lambda e: e.tensor_tensor(out=out, in0=out, in1=res, op=ALU.add),
         reads=list(out_keys) + list(res_keys), writes=list(out_keys))


def emit_norm_bf16(C, src, src_keys, gain, gain_key, out_bf, out_keys):
    P = C.P
    rs, rk = rstd_from(C, src, src_keys, D, 1e-6)
    P.op("dve", lambda e: e.scalar_tensor_tensor(out=out_bf, in0=src, scalar=rs, in1=gain, op0=ALU.mult, op1=ALU.mult),
         reads=list(src_keys) + [rk, gain_key], writes=list(out_keys))


def load_gains(C, es, vec_d, rows, name):
    nc, P = C.nc, C.P
    g = es.enter_context(nc.sbuf_tensor(name, [128, len(rows), D], F32))
    for i, r in enumerate(rows):
        P.dma("sp", g[:, i, :], vec_d[r, :].partition_broadcast(128), writes=[(name, i)])
    return g


def setup_eps(C, es):
    nc, P = C.nc, C.P
    t = es.enter_context(nc.sbuf_tensor("epsb", [128, 3], F32))
    P.op("pool", lambda e: e.memset(t[:, 2:3], -0.5), writes=["epsb"])
    C.negh = t[:, 2:3]
    P.op("pool", lambda e: e.memset(t[:, 0:1], 1e-6), writes=["epsb"])
    P.op("pool", lambda e: e.memset(t[:, 1:2], 1e-5), writes=["epsb"])
    C.epsb = {1e-6: t[:, 0:1], 1e-5: t[:, 1:2]}


def emit_mlp(C, H, w_up_d, w_dn_d, vec_d, row_pre, row_post, tag, ntok=NOWN):
    nc, P = C.nc, C.P
    MT = 256
    with scope(P) as es:
        wup = es.enter_context(nc.sbuf_tensor("wup" + tag, [128, 8, DFF], BF16))
        wdn = es.enter_context(nc.sbuf_tensor("wdn" + tag, [128, 32, D], BF16))
        wup_src = w_up_d.rearrange("(k p) n -> p k n", p=128)
        kup = []
        for i in range(8):
            P.dma("pool", wup[:, :, i * 512:(i + 1) * 512], wup_src[:, :, i * 512:(i + 1) * 512], writes=[("wup" + tag, i)])
            kup.append(("wup" + tag, i))
        kdn = load_w_bf16(C, wdn, w_dn_d, 32, "wdn" + tag, nsplit=8)
        g = load_gains(C, es, vec_d, [row_pre, row_post], "gm" + tag)
        ha = es.enter_context(nc.sbuf_tensor("ha" + tag, [128, 2, 2, D], F32))
        ubf = es.enter_context(nc.sbuf_tensor("ubf" + tag, [128, 2, D], BF16))
        uT = es.enter_context(nc.sbuf_tensor("uT" + tag, [128, 2, 8, MT], BF16))
        hid = es.enter_context(nc.sbuf_tensor("hid" + tag, [128, 32, MT], BF16))
        rl = es.enter_context(nc.sbuf_tensor("rl" + tag, [128, 2, MT], F32))
        hb = es.enter_context(nc.sbuf_tensor("hb" + tag, [128, 2, D], F32))
        nmt = ntok // MT

        def prologue(t):
            sl = t % 2
            for e in range(2):
                lt = 2 * t + e
                P.dma("sp", ha[:, sl, e, :], H[lt * 128:(lt + 1) * 128, :], reads=[("H", lt)], writes=[("ha", sl, e)])
                emit_norm_bf16(C, ha[:, sl, e, :], [("ha", sl, e)], g[:, 0, :], ("gm" + tag, 0), ubf[:, e, :], [("ubf", e)])
                transpose_to(C, ubf[:, e, :], [("ubf", e)], uT[:, sl, :, e * 128:(e + 1) * 128], [("uT", sl, e)],
                             bank=e, evac="act")

        def up(t):
            sl = t % 2
            for c in range(32):
                bank = c % 4
                pk = ("ps", bank)
                pv = psf(C, bank)[:, 0:MT]
                for k in range(8):
                    P.op("pe", lambda e: e.matmul(pv, lhsT=wup[:, k, c * 128:(c + 1) * 128], rhs=uT[:, sl, k, :],
                                                  start=(k == 0), stop=(k == 7)),
                         reads=[kup[c // 4], ("uT", sl, 0), ("uT", sl, 1)], writes=[pk])
                rs_ = c % 2
                P.op("act", lambda e: e.activation(out=rl[:, rs_, :], in_=pv, func=AF.Relu),
                     reads=[pk], writes=[("rl", rs_)])
                P.op("pool", lambda e: e.tensor_tensor(out=hid[:, c, :], in0=rl[:, rs_, :], in1=rl[:, rs_, :], op=ALU.mult),
                     reads=[("rl", rs_)], writes=[("hid", c)])

        def down(t):
            sl = t % 2
            for e in range(2):
                lt = 2 * t + e
                for hf in range(2):
                    pk = ("ps", 4 + 2 * e + hf)
                    pv = psf(C, 4 + 2 * e + hf)
                    for c in range(32):
                        P.op("pe", lambda e_: e_.matmul(pv, lhsT=hid[:, c, e * 128:(e + 1) * 128],
                                                        rhs=wdn[:, c, hf * 512:(hf + 1) * 512],
                                                        start=(c == 0), stop=(c == 31)),
                             reads=[("hid", c), kdn[c // 4]], writes=[pk])
                fv = C.ps[2 + e][:, :]
                emit_norm_residual(C, fv, [("ps", 4 + 2 * e), ("ps", 5 + 2 * e)], g[:, 1, :], ("gm" + tag, 1),
                                   ha[:, sl, e, :], [("ha", sl, e)], hb[:, e, :], [("hb", e)])
                P.dma("sp", H[lt * 128:(lt + 1) * 128, :], hb[:, e, :], reads=[("hb", e)], writes=[("H", lt)])

        prologue(0)
        for t in range(nmt):
            up(t)
            if t + 1 < nmt:
                prologue(t + 1)
            down(t)


def emit_ple(C, H, pT_d, w_pp_d, w_pg_d, vec_d, row_ple, OUT, tag, hn_row=None, HNT=None, ntiles=16):
    nc, P = C.nc, C.P
    with scope(P) as es:
        wpg = es.enter_context(nc.sbuf_tensor("wpg" + tag, [128, 8, D], BF16))
        wpp = es.enter_context(nc.sbuf_tensor("wpp" + tag, [128, 2, D], BF16))
        kpg = load_w_bf16(C, wpg, w_pg_d, 8, "wpg" + tag, nsplit=2)
        kpp = load_w_bf16(C, wpp, w_pp_d, 2, "wpp" + tag, nsplit=1)
        rows = [row_ple] + ([hn_row] if hn_row is not None else [])
        g = load_gains(C, es, vec_d, rows, "gp" + tag)
        hb = es.enter_context(nc.sbuf_tensor("phb" + tag, [128, 3, D], F32))
        hbb = es.enter_context(nc.sbuf_tensor("phbb" + tag, [128, 2, D], BF16))
        hbT = es.enter_context(nc.sbuf_tensor("phbT" + tag, [128, 2, 8, 128], BF16))
        pT = es.enter_context(nc.sbuf_tensor("ppT" + tag, [128, 3, 2, 128], BF16))
        sg = es.enter_context(nc.sbuf_tensor("psg" + tag, [128, 2, D], F32))
        ee = es.enter_context(nc.sbuf_tensor("pee" + tag, [128, 2, D], F32))
        hnb = es.enter_context(nc.sbuf_tensor("phnb" + tag, [128, 2, D], BF16))
        hnT = es.enter_context(nc.sbuf_tensor("phnT" + tag, [128, 2, 8, 128], BF16))
        pTv = pT_d.rearrange("(k f) t -> f k t", f=128)
        def loads(lt):
            s3 = lt % 3
            P.dma("sp", hb[:, s3, :], H[lt * 128:(lt + 1) * 128, :], reads=[("H", lt)], writes=[("phb", s3)])
            P.dma("pool", pT[:, s3, :, :], pTv[:, :, lt * 128:(lt + 1) * 128], writes=[("ppT", s3)])

        def front(lt):
            sl = lt % 2
            s3 = lt % 3
            if lt + 1 < ntiles:
                loads(lt + 1)
            P.op("dve", lambda e: e.tensor_copy(out=hbb[:, sl, :], in_=hb[:, s3, :]), reads=[("phb", s3)], writes=[("phbb", sl)])
            transpose_to(C, hbb[:, sl, :], [("phbb", sl)], hbT[:, sl, :, :], [("phbT", sl)], bank=6, evac="act")
            for hf in range(2):
                pk = ("ps", hf)
                pv = psf(C, hf)
                for k in range(8):
                    P.op("pe", lambda e: e.matmul(pv, lhsT=hbT[:, sl, k, :], rhs=wpg[:, k, hf * 512:(hf + 1) * 512],
                                                  start=(k == 0), stop=(k == 7)),
                         reads=[("phbT", sl), kpg[k // 4]], writes=[pk])
            for hf in range(2):
                pk = ("ps", 2 + 2 * sl + hf)
                pv = psf(C, 2 + 2 * sl + hf)
                for k in range(2):
                    P.op("pe", lambda e: e.matmul(pv, lhsT=pT[:, s3, k, :], rhs=wpp[:, k, hf * 512:(hf + 1) * 512],
                                                  start=(k == 0), stop=(k == 1)),
                         reads=[("ppT", s3), kpp[0]], writes=[pk])

        def front_b(lt):
            sl = lt % 2
            P.op("act", lambda e: e.activation(out=sg[:, sl, :], in_=C.ps[0][:, :], func=AF.Sigmoid),
                 reads=[("ps", 0), ("ps", 1)], writes=[("psg", sl)])

        def back(lt):
            sl = lt % 2
            ev = C.ps[1 + sl][:, :]
            pks = [("ps", 2 + 2 * sl), ("ps", 3 + 2 * sl)]
            rs, rk = rstd_from(C, ev, pks, D, 1e-6)
            P.op("dve", lambda e: e.scalar_tensor_tensor(out=ee[:, sl, :], in0=ev, scalar=rs, in1=g[:, 0, :], op0=ALU.mult, op1=ALU.mult),
                 reads=pks + [rk, ("gp" + tag, 0)], writes=[("pee", sl)])
            P.op("pool", lambda e: e.tensor_tensor(out=ee[:, sl, :], in0=ee[:, sl, :], in1=sg[:, sl, :], op=ALU.mult),
                 reads=[("pee", sl), ("psg", sl)], writes=[("pee", sl)])
            P.op("dve", lambda e: e.tensor_tensor(out=ee[:, sl, :], in0=ee[:, sl, :], in1=hb[:, lt % 3, :], op=ALU.add),
                 reads=[("pee", sl), ("phb", lt % 3)], writes=[("pee", sl)])
            P.dma("sp", OUT[lt * 128:(lt + 1) * 128, :], ee[:, sl, :], reads=[("pee", sl)], writes=[("OUT", lt)])

        def hnpart(lt):
            sl = lt % 2
            emit_norm_bf16(C, ee[:, sl, :], [("pee", sl)], g[:, 1, :], ("gp" + tag, 1), hnb[:, sl, :], [("phnb", sl)])
            transpose_to(C, hnb[:, sl, :], [("phnb", sl)], hnT[:, sl, :, :], [("phnT", sl)], bank=7, evac="dve")
            P.dma("sp", HNT.rearrange("(k f) t -> f k t", f=128)[:, :, lt * 128:(lt + 1) * 128], hnT[:, sl, :, :],
                  reads=[("phnT", sl)], writes=[("HNT", lt)])

        loads(0)
        front(0)
        front_b(0)
        for lt in range(ntiles):
            if lt + 1 < ntiles:
                front(lt + 1)
            back(lt)
            if hn_row is not None and lt >= 1:
                hnpart(lt - 1)
            if lt + 1 < ntiles:
                front_b(lt + 1)
        if hn_row is not None:
            hnpart(ntiles - 1)


def emit_attention(C, T, H, stage=9, full=False):
    nc, P = C.nc, C.P
    with scope(P) as es:
        g = load_gains(C, es, T["vec"], [0, 1], "ga")
        hnTf = es.enter_context(nc.sbuf_tensor("hnTf", [128, 8, S], BF16))
        NT = 32 if full else 16
        NCH = NT // 2
        KPC = 2 if full else 4
        NM = 2 if full else 4
        if full:
            hnTo = hnTf
        else:
            hnTo = es.enter_context(nc.sbuf_tensor("hnTo", [128, 8, NOWN], BF16))
        obuf = es.enter_context(nc.sbuf_tensor("obuf", [128, NT, D], BF16))
        masks = es.enter_context(nc.sbuf_tensor("masks_sb", [128, NM, 256], BF16))
        alibi = es.enter_context(nc.sbuf_tensor("alibi_sb", [128, 4, 32], F32))
        fb = es.enter_context(nc.sbuf_tensor("fb", [128, 2, 2, 32], F32))
        Stm = es.enter_context(nc.sbuf_tensor("Stm", [128, 32, 8], F32))
        Xb = es.enter_context(nc.sbuf_tensor("Xb", [128, NCH * 8], F32))
        neglam = es.enter_context(nc.sbuf_tensor("neglam", [128, 4], F32))
        subg = es.enter_context(nc.sbuf_tensor("subg", [128, 128], F32))
        P.dma("pool", masks[:], T["masks"].rearrange("m p q -> p m q"), writes=["masks"])
        P.dma("sp", alibi[:], T["alibi"].rearrange("p (h r) -> p h r", h=4), writes=["alibi"])
        P.dma("sp", subg[:], T["vec"][6, 0:128].partition_broadcast(128), writes=["subg"])
        P.op("dve", lambda e: e.tensor_scalar(out=subg[:], in0=subg[:], scalar1=1.0 - LAM_INIT0, scalar2=None, op0=ALU.mult),
             reads=["subg"], writes=["subg"])

        with scope(P) as es2:
            lv = es2.enter_context(nc.sbuf_tensor("lv", [1, 256], F32))
            pr = es2.enter_context(nc.sbuf_tensor("pr", [1, 128], F32))
            dots = es2.enter_context(nc.sbuf_tensor("dots", [1, 2], F32))
            P.dma("sp", lv[:], T["lamv"], writes=["lv"])
            P.op("dve", lambda e: e.tensor_tensor(out=pr[:, 0:64], in0=lv[:, 0:64], in1=lv[:, 64:128], op=ALU.mult),
                 reads=["lv"], writes=["pr"])
            P.op("dve", lambda e: e.tensor_tensor(out=pr[:, 64:128], in0=lv[:, 128:192], in1=lv[:, 192:256], op=ALU.mult),
                 reads=["lv", "pr"], writes=["pr"])
            P.op("dve", lambda e: e.tensor_reduce(out=dots[:, 0:2], in_=pr[:, :].rearrange("p (a d) -> p a d", a=2),
                                                  axis=AX.X, op=ALU.add), reads=["pr"], writes=["dots"])
            pv = psf(C, 0)[:, 0:2]
            P.op("pe", lambda e: e.matmul(pv, lhsT=C.onesf[0:1, 0:128], rhs=dots[0:1, 0:2], start=True, stop=True),
                 reads=["onesf", "dots"], writes=[("ps", 0)])
            P.op("act", lambda e: e.activation(out=neglam[:, 0:2], in_=pv, func=AF.Exp), reads=[("ps", 0)], writes=["neglam"])
            P.op("dve", lambda e: e.tensor_tensor(out=neglam[:, 2:3], in0=neglam[:, 1:2], in1=neglam[:, 0:1], op=ALU.subtract),
                 reads=["neglam"], writes=["neglam"])
            P.op("dve", lambda e: e.tensor_scalar(out=neglam[:, 2:3], in0=neglam[:, 2:3], scalar1=-LAM_INIT0, scalar2=None, op0=ALU.add),
                 reads=["neglam"], writes=["neglam"])

        with scope(P) as es2:
            xt = es2.enter_context(nc.sbuf_tensor("xt", [128, 2, D], F32))
            xb = es2.enter_context(nc.sbuf_tensor("xb", [128, 2, D], BF16))
            def a1_info(i):
                if i < 32:
                    src = T["xf"][i * 128:(i + 1) * 128, :]
                    dst = hnTf[:, :, i * 128:(i + 1) * 128]
                    dk = ("hnTf", i // 4)
                else:
                    j = i - 32
                    src = T["xo"][j * 128:(j + 1) * 128, :]
                    dst = hnTo[:, :, j * 128:(j + 1) * 128]
                    dk = ("hnTo", j // 4)
                if full:
                    dk = ("hnTf", i // 4)
                return src, dst, dk

            def a1_front(i):
                sl = i % 2
                src, dst, dk = a1_info(i)
                P.dma("sp", xt[:, sl, :], src, writes=[("xt", sl)])
                emit_norm_bf16(C, xt[:, sl, :], [("xt", sl)], g[:, 0, :], ("ga", 0), xb[:, sl, :], [("xb", sl)])

            def a1_back(i):
                sl = i % 2
                src, dst, dk = a1_info(i)
                transpose_to(C, xb[:, sl, :], [("xb", sl)], dst, [dk], bank=6 + sl, evac=("act" if sl else "dve"))

            n_a1 = 32 if full else 48
            a1_front(0)
            for i in range(n_a1):
                if i + 1 < n_a1:
                    a1_front(i + 1)
                a1_back(i)

        if stage < 1:
            return
        with scope(P) as es2:
            wfz = es2.enter_context(nc.sbuf_tensor("wfz", [128, 8, 8], BF16))
            negb = es2.enter_context(nc.sbuf_tensor("negb", [8, 1], F32))
            Lf = es2.enter_context(nc.sbuf_tensor("Lf", [8, S], F32))
            Sc = es2.enter_context(nc.sbuf_tensor("Sc", [8, S], F32))
            Dm = es2.enter_context(nc.sbuf_tensor("Dm", [8, NCH, 8], F32))
            P.dma("pool", wfz[:], T["w_in"].rearrange("(k p) n -> p k n", p=128)[:, :, 3072:3080], writes=["wfz"])
            P.dma("sp", negb[:], T["bf"], writes=["negb"])
            P.op("dve", lambda e: e.tensor_scalar(out=negb[:], in0=negb[:], scalar1=-1.0, scalar2=None, op0=ALU.mult),
                 reads=["negb"], writes=["negb"])
            for n in range(8):
                bank = n % 2
                pv = psf(C, bank)[0:8, :]
                for k in range(8):
                    P.op("pe", lambda e, k=k, n=n, pv=pv: e.matmul(pv, lhsT=wfz[:, k, :], rhs=hnTf[:, k, n * 512:(n + 1) * 512],
                                                                   start=(k == 0), stop=(k == 7)),
                         reads=["wfz", ("hnTf", n)], writes=[("ps", bank)])
                P.op("act", lambda e, n=n, pv=pv: e.activation(out=Lf[:, n * 512:(n + 1) * 512], in_=pv, func=AF.Exp, scale=-1.0, bias=negb[:, 0:1]),
                     reads=[("ps", bank), "negb"], writes=[("Lf", n)])
            for n in range(8):
                P.op("act", lambda e, n=n: e.activation(out=Lf[:, n * 512:(n + 1) * 512], in_=Lf[:, n * 512:(n + 1) * 512], func=AF.Ln, bias=1.0),
                     reads=[("Lf", n)], writes=[("Lf", n)])
            for n in range(8):
                init = 0.0 if n == 0 else Sc[:, n * 512 - 1:n * 512]
                rd = [("Lf", n), "onesf"] + ([("Sc", n - 1)] if n else [])
                P.op("dve", lambda e, n=n, init=init: e.tensor_tensor_scan(out=Sc[:, n * 512:(n + 1) * 512], data0=C.onesf[0:8, :],
                                                                           data1=Lf[:, n * 512:(n + 1) * 512], initial=init,
                                                                           op0=ALU.mult, op1=ALU.add),
                     reads=rd, writes=[("Sc", n)])
            pvt = psf(C, 2)[:, 0:256]
            for kb in range(32):
                P.op("pe", lambda e, kb=kb: e.transpose(out=pvt[:, kb * 8:(kb + 1) * 8], in_=Sc[0:8, kb * 128:(kb + 1) * 128],
                                                        identity=C.identf[0:8, 0:8]),
                     reads=[("Sc", kb // 4), "identf"], writes=[("ps", 2)])
            P.op("dve", lambda e: e.tensor_copy(out=Stm[:, :, :], in_=pvt.rearrange("p (k h) -> p k h", h=8)),
                 reads=[("ps", 2)], writes=["Stm"])
            CW = 128 * KPC
            ssel = Sc[:, :].rearrange("h (p t) -> h p t", t=CW)[:, :, CW - 1:CW]
            P.op("dve", lambda e: e.tensor_tensor(out=Dm[:, :, :], in0=ssel.to_broadcast([8, NCH, 8]),
                                                  in1=C.identf[0:8, 0:8].unsqueeze(1).to_broadcast([8, NCH, 8]), op=ALU.mult),
                 reads=[("Sc", n) for n in range(8)] + ["identf"], writes=["Dm"])
            pvx = psf(C, 3)[:, 0:NCH * 8]
            P.op("pe", lambda e: e.matmul(pvx, lhsT=C.onesf[0:8, 0:128], rhs=Dm[:, :, :].rearrange("h p g -> h (p g)"), start=True, stop=True),
                 reads=["onesf", "Dm"], writes=[("ps", 3)])
            P.op("dve", lambda e: e.tensor_copy(out=Xb[:, :], in_=pvx), reads=[("ps", 3)], writes=["Xb"])
        if stage < 2:
            return
        with scope(P) as es2:
            wq = es2.enter_context(nc.sbuf_tensor("wq", [128, 2, 8, 128], BF16))
            wk = es2.enter_context(nc.sbuf_tensor("wk", [128, 2, 8, 128], BF16))
            wv = es2.enter_context(nc.sbuf_tensor("wv", [128, 2, 8, 128], BF16))
            KT = es2.enter_context(nc.sbuf_tensor("KT", [128, S], BF16))
            QT = es2.enter_context(nc.sbuf_tensor("QTz", [128, 2, NT * 128], BF16))
            Vb = es2.enter_context(nc.sbuf_tensor("Vb", [128, 32, 130], BF16))
            Vd = Vb[:, :, 0:129]
            Vf = Vb[:, :, :].rearrange("p k (h d) -> p k h d", h=2)
            PT = es2.enter_context(nc.sbuf_tensor("PT", [128, 4, 2, 256], BF16))
            t1 = es2.enter_context(nc.sbuf_tensor("t1", [128, 2, 128], F32))
            dd = es2.enter_context(nc.sbuf_tensor("dd", [128, 2, 128], F32))
            rr = es2.enter_context(nc.sbuf_tensor("rr", [128, 2, 8], F32))
            P.op("pool", lambda e: e.memset(QT[:, :, :], 0.0), writes=[("QT", n) for n in range(NT // 4)])
            w_in_v = T["w_in"].rearrange("(k p) n -> p k n", p=128)
            rr_i = [0]
            for gidx in range(8):
                ws = gidx % 2
                diff = gidx < 4
                if gidx in (0, 4):
                    vk = [("V", kq) for kq in range(8)]
                    P.op("pool", lambda e: e.memset(Vb[:, :, :], 1.0), writes=vk)
                if diff:
                    qc, kc, vc = gidx * 128, 512 + gidx * 128, 1024 + gidx * 128
                else:
                    qc, kc, vc = 1536 + (gidx - 4) * 128, 2048 + (gidx - 4) * 128, 2560 + (gidx - 4) * 128
                P.dma("pool", wq[:, ws, :, :], w_in_v[:, :, qc:qc + 128], writes=[("wq", ws)])
                P.dma("pool", wk[:, ws, :, :], w_in_v[:, :, kc:kc + 128], writes=[("wk", ws)])
                P.dma("pool", wv[:, ws, :, :], w_in_v[:, :, vc:vc + 128], writes=[("wv", ws)])
                for n in range(8):
                    bank = 2 * (n % 2)
                    pv = psf(C, bank)
                    for k in range(8):
                        P.op("pe", lambda e, k=k, n=n, pv=pv: e.matmul(pv, lhsT=wk[:, ws, k, :], rhs=hnTf[:, k, n * 512:(n + 1) * 512],
                                                                       start=(k == 0), stop=(k == 7)),
                             reads=[("wk", ws), ("hnTf", n)], writes=[("ps", bank)])
                    P.op("dve", lambda e, n=n, pv=pv: e.tensor_copy(out=KT[:, n * 512:(n + 1) * 512], in_=pv),
                         reads=[("ps", bank)], writes=[("KT", n)])
                for n in range(NT // 4):
                    bank = 2 * (n % 2)
                    pv = psf(C, bank)
                    for k in range(8):
                        P.op("pe", lambda e, k=k, n=n, pv=pv: e.matmul(pv, lhsT=wq[:, ws, k, :], rhs=hnTo[:, k, n * 512:(n + 1) * 512],
                                                                       start=(k == 0), stop=(k == 7)),
                             reads=[("wq", ws), (("hnTf" if full else "hnTo"), n)], writes=[("ps", bank)])
                    P.op("dve", lambda e: e.tensor_copy(out=QT[0:64, 0, n * 512:(n + 1) * 512], in_=pv[0:64, :]),
                         reads=[("ps", bank)], writes=[("QT", n)])
                    P.op("dve", lambda e: e.tensor_copy(out=QT[64:128, 1, n * 512:(n + 1) * 512], in_=pv[64:128, :]),
                         reads=[("ps", bank), ("QT", n)], writes=[("QT", n)])
                for kq in range(8):
                    bank = 2 * (kq % 2)
                    pv = psf(C, bank)
                    for j in range(4):
                        kb = kq * 4 + j
                        for k in range(8):
                            P.op("pe", lambda e, k=k, kb=kb, j=j, pv=pv: e.matmul(pv[:, j * 128:(j + 1) * 128], lhsT=hnTf[:, k, kb * 128:(kb + 1) * 128],
                                                                                  rhs=wv[:, ws, k, :], start=(k == 0), stop=(k == 7)),
                                 reads=[("wv", ws), ("hnTf", kb // 4)], writes=[("ps", bank)])
                    if diff:
                        dst = Vd[:, kq * 4:(kq + 1) * 4, 0:128]
                        srcv = pv.rearrange("p (j d) -> p j d", j=4)
                        vkey = ("V", kq)
                    else:
                        dst = Vf[:, kq * 4:(kq + 1) * 4, :, 0:64]
                        srcv = pv.rearrange("p (j h d) -> p j h d", j=4, h=2)
                        vkey = ("V", kq)
                    P.op("dve", lambda e, dst=dst, srcv=srcv: e.tensor_copy(out=dst, in_=srcv), reads=[("ps", bank)], writes=[vkey])

                for p in range(NCH):
                    nkb = KPC * p + KPC
                    kb0 = KPC * p
                    oset = p % 2
                    if not diff:
                        fsl = p % 2
                        for sh in range(2):
                            hh_ = 2 * (gidx - 4) + sh
                            P.op("dve", lambda e: e.tensor_scalar(out=fb[:, fsl, sh, 0:nkb], in0=Stm[:, 0:nkb, hh_],
                                                                  scalar1=Xb[:, p * 8 + hh_:p * 8 + hh_ + 1], scalar2=None, op0=ALU.subtract),
                                 reads=["Stm", "Xb"], writes=[("fb", fsl)])
                    def emit_S(kb, p=p):
                        slot = kb % 4
                        for sh in range(2):
                            pvS = psf(C, slot)[:, sh * 256:(sh + 1) * 256]
                            P.op("pe", lambda e: e.matmul(pvS, lhsT=KT[:, kb * 128:(kb + 1) * 128],
                                                          rhs=QT[:, sh, p * 256:(p + 1) * 256], start=True, stop=True),
                                 reads=[("KT", kb // 4), ("QT", p // 2)], writes=[("ps", slot)])

                    def emit_PV(kb, p=p, nkb=nkb, oset=oset):
                        slot = kb % 4
                        if diff:
                            ai = kb - kb0 + (30 if full else 28)
                            P.op("act", lambda e: e.activation(out=PT[:, slot, :, :], in_=psf(C, slot).rearrange("p (s q) -> p s q", s=2),
                                                               func=AF.Exp, scale=0.125, bias=alibi[:, gidx, ai:ai + 1]),
                                 reads=[("ps", slot), "alibi"], writes=[("PT", slot, 0), ("PT", slot, 1)])
                        else:
                            for sh in range(2):
                                bias = fb[:, p % 2, sh, kb:kb + 1]
                                P.op("act", lambda e: e.activation(out=PT[:, slot, sh, :], in_=psf(C, slot)[:, sh * 256:(sh + 1) * 256],
                                                                   func=AF.Exp, scale=0.125, bias=bias),
                                     reads=[("ps", slot), ("fb", p % 2)], writes=[("PT", slot, sh)])
                        if kb >= kb0:
                            mk = masks[:, kb - kb0, :].unsqueeze(1).to_broadcast([128, 2, 256])
                            P.op("pool", lambda e: e.tensor_tensor(out=PT[:, slot, :, :], in0=PT[:, slot, :, :], in1=mk, op=ALU.mult),
                                 reads=[("PT", slot, 0), ("PT", slot, 1), "masks"], writes=[("PT", slot, 0), ("PT", slot, 1)])
                        for sh in range(2):
                            obank = 4 + oset * 2 + sh
                            for e_ in range(2):
                                if diff:
                                    ov = psf(C, obank)[:, e_ * 129:(e_ + 1) * 129]
                                    rhs = Vd[:, kb, :]
                                else:
                                    ov = psf(C, obank)[:, e_ * 65:(e_ + 1) * 65]
                                    rhs = Vf[:, kb, sh, :]
                                first = (kb == 0 and e_ == 0)
                                P.op("pe", lambda e: e.matmul(ov, lhsT=PT[:, slot, sh, e_ * 128:(e_ + 1) * 128], rhs=rhs,
                                                              start=first, stop=(kb == nkb - 1), skip_group_check=True),
                                     reads=[("PT", slot, sh), ("V", kb // 4)], writes=[("ps", obank)])

                    LOOK = 2
                    for kb in range(min(LOOK, nkb)):
                        emit_S(kb)
                    for kb in range(nkb):
                        if kb + LOOK < nkb:
                            emit_S(kb + LOOK)
                        emit_PV(kb)

                    b0 = 4 + oset * 2
                    i0 = rr_i[0] % 2
                    rr_i[0] += 1
                    rk = ("rr", i0)
                    if diff:
                        O1 = psf(C, b0)[:, 0:258].rearrange("p (e d) -> p e d", e=2)
                        O2 = psf(C, b0 + 1)[:, 0:258].rearrange("p (e d) -> p e d", e=2)
                        P.op("dve", lambda e: e.reciprocal(out=rr[:, i0, 0:2], in_=O1[:, :, 128]), reads=[("ps", b0)], writes=[rk])
                        P.op("dve", lambda e: e.reciprocal(out=rr[:, i0, 2:4], in_=O2[:, :, 128]), reads=[("ps", b0 + 1)], writes=[rk])
                        P.op("dve", lambda e: e.tensor_scalar(out=rr[:, i0, 2:4], in0=rr[:, i0, 2:4], scalar1=neglam[:, 2:3], scalar2=None, op0=ALU.mult),
                             reads=[rk, "neglam"], writes=[rk])
                        for e_ in range(2):
                            lt = 2 * p + e_
                            P.op("dve", lambda e, e_=e_: e.tensor_scalar(out=t1[:, e_, :], in0=O1[:, e_, 0:128], scalar1=rr[:, i0, e_:e_ + 1],
                                                                         scalar2=None, op0=ALU.mult),
                                 reads=[("ps", b0), rk], writes=[("t1", e_)])
                            P.op("dve", lambda e, e_=e_: e.scalar_tensor_tensor(out=dd[:, e_, :], in0=O2[:, e_, 0:128], scalar=rr[:, i0, 2 + e_:3 + e_],
                                                                                in1=t1[:, e_, :], op0=ALU.mult, op1=ALU.add),
                                 reads=[("ps", b0 + 1), rk, ("t1", e_)], writes=[("dd", e_)])
                            rs, rsk = rstd_from(C, dd[:, e_, :], [("dd", e_)], 128, 1e-5, eng="dve")
                            P.op("dve", lambda e, e_=e_, lt=lt, rs=rs: e.scalar_tensor_tensor(out=obuf[:, lt, gidx * 128:(gidx + 1) * 128], in0=dd[:, e_, :],
                                                                                             scalar=rs, in1=subg[:, :], op0=ALU.mult, op1=ALU.mult),
                                 reads=[("dd", e_), rsk, "subg"], writes=[("obuf", lt)])
                    else:
                        for sh in range(2):
                            Ov = psf(C, b0 + sh)[:, 0:130].rearrange("p (e d) -> p e d", e=2)
                            P.op("dve", lambda e, sh=sh, Ov=Ov: e.reciprocal(out=rr[:, i0, 4 + 2 * sh:6 + 2 * sh], in_=Ov[:, :, 64]),
                                 reads=[("ps", b0 + sh)], writes=[rk])
                            for e_ in range(2):
                                lt = 2 * p + e_
                                c0 = 512 + (2 * (gidx - 4) + sh) * 64
                                P.op("dve", lambda e, sh=sh, e_=e_, lt=lt, c0=c0, Ov=Ov: e.tensor_scalar(out=obuf[:, lt, c0:c0 + 64], in0=Ov[:, e_, 0:64],
                                                                                                       scalar1=rr[:, i0, 4 + 2 * sh + e_:5 + 2 * sh + e_],
                                                                                                       scalar2=None, op0=ALU.mult),
                                     reads=[("ps", b0 + sh), rk], writes=[("obuf", lt)])

        if stage < 3:
            return
        with scope(P) as es2:
            wo = es2.enter_context(nc.sbuf_tensor("wo", [128, 8, D], BF16))
            kwo = load_w_bf16(C, wo, T["w_out"], 8, "wo", nsplit=2)
            oT = es2.enter_context(nc.sbuf_tensor("oT", [128, 2, 8, 128], BF16))
            xr = es2.enter_context(nc.sbuf_tensor("xres", [128, 2, D], F32))
            hh = es2.enter_context(nc.sbuf_tensor("hh", [128, 2, D], F32))
            xsrc = T["xf" if full else "xo"]

            def a4_front(lt):
                sl = lt % 2
                P.dma("sp", xr[:, sl, :], xsrc[lt * 128:(lt + 1) * 128, :], writes=[("xres", sl)])
                transpose_to(C, obuf[:, lt, :], [("obuf", lt)], oT[:, sl, :, :], [("oT", sl)], bank=6 + sl, evac="act")
                for hf in range(2):
                    pk = ("ps", 2 * sl + hf)
                    pv = psf(C, 2 * sl + hf)
                    for k in range(8):
                        P.op("pe", lambda e: e.matmul(pv, lhsT=oT[:, sl, k, :], rhs=wo[:, k, hf * 512:(hf + 1) * 512],
                                                      start=(k == 0), stop=(k == 7)),
                             reads=[("oT", sl), kwo[k // 4]], writes=[pk])

            def a4_back(lt):
                sl = lt % 2
                emit_norm_residual(C, C.ps[sl][:, :], [("ps", 2 * sl), ("ps", 2 * sl + 1)], g[:, 1, :], ("ga", 1),
                                   xr[:, sl, :], [("xres", sl)], hh[:, sl, :], [("hh", sl)])
                P.dma("sp", H[lt * 128:(lt + 1) * 128, :], hh[:, sl, :], reads=[("hh", sl)], writes=[("H", lt)])

            a4_front(0)
            for lt in range(NT):
                if lt + 1 < NT:
                    a4_front(lt + 1)
                a4_back(lt)


def build_phase_A(upto=99, stage=9):
    nc = bass.Bass("TRN2", target_bir_lowering=False)
    T = {}

    def din(name, shape, dt=F32):
        T[name] = nc.dram_tensor(name, shape, dt, kind="ExternalInput").ap()

    din("xf", [S, D]); din("xo", [NOWN, D]); din("pT", [256, NOWN])
    din("w_in", [D, 3080]); din("w_out", [D, D]); din("w_up", [D, DFF]); din("w_dn", [DFF, D])
    din("w_pp", [256, D]); din("w_pg", [D, D]); din("vec", [8, D]); din("lamv", [1, 256]); din("bf", [8, 1])
    din("ident", [128, 128]); din("masks", [4, 128, 256]); din("alibi", [128, 128])
    H = nc.dram_tensor("H", [NOWN, D], F32, kind="ExternalOutput").ap()
    H1 = nc.dram_tensor("H1", [NOWN, D], F32, kind="ExternalOutput").ap()
    HNT = nc.dram_tensor("HNT", [D, NOWN], BF16, kind="ExternalOutput").ap()
    with contextlib.ExitStack() as es:
        P = Prog(nc, es)
        C = Ctx()
        setup_common(nc, P, es, C, T["ident"])
        setup_eps(C, es)
        emit_attention(C, T, H, stage)
        if upto >= 2:
            emit_mlp(C, H, T["w_up"], T["w_dn"], T["vec"], 2, 3, "0")
        if upto >= 3:
            emit_ple(C, H, T["pT"], T["w_pp"], T["w_pg"], T["vec"], 4, H1, "0", hn_row=5, HNT=HNT)
        P.finish()
    return nc


def emit_rec(C, T, G, ncb=5):
    import itertools
    nc, P = C.nc, C.P
    TH = 1024
    NTH = S // TH
    assert ncb % 2 == 0 or ncb == 5
    with scope(P) as es:
        hnT = es.enter_context(nc.sbuf_tensor("r_hnT", [128, 8, S], BF16))
        wg = es.enter_context(nc.sbuf_tensor("r_wg", [128, 8, ncb * 128], BF16))
        wxr = es.enter_context(nc.sbuf_tensor("r_wxr", [128, 8, ncb * 128], BF16))
        wxs = es.enter_context(nc.sbuf_tensor("r_wxs", [128, ncb, 128], BF16))
        was = es.enter_context(nc.sbuf_tensor("r_was", [128, ncb, 128], BF16))
        rv = es.enter_context(nc.sbuf_tensor("r_rv", [128, ncb, 8], F32))
        sc = es.enter_context(nc.sbuf_tensor("r_sc", [128, ncb], F32))
        hlast = es.enter_context(nc.sbuf_tensor("r_hlast", [128, 2], F32))
        ybf = es.enter_context(nc.sbuf_tensor("r_y", [128, 2, 2, TH], BF16))
        xr = es.enter_context(nc.sbuf_tensor("r_xr", [128, 2, 2, TH + 3], F32))
        xc = es.enter_context(nc.sbuf_tensor("r_xc", [128, 2, TH], F32))
        xcb = es.enter_context(nc.sbuf_tensor("r_xcb", [128, 2, TH], BF16))
        gx = es.enter_context(nc.sbuf_tensor("r_gx", [128, 2, TH], F32))
        ga = es.enter_context(nc.sbuf_tensor("r_ga", [128, 2, TH], F32))
        tt = es.enter_context(nc.sbuf_tensor("r_tt", [128, 2, TH], F32))
        hs = es.enter_context(nc.sbuf_tensor("r_hs", [128, 2, TH], F32))
        gout = es.enter_context(nc.sbuf_tensor("r_gout", [128, 2, TH], BF16))
        hv = T["hnT"].rearrange("(k f) t -> f k t", f=128)
        for k in range(8):
            P.dma("sp", hnT[:, k, :], hv[:, k, :], writes=[("r_hnT", k)])
        kg = load_w_bf16(C, wg, T["w_g"], 8, "r_wg", nsplit=2)
        kx = load_w_bf16(C, wxr, T["w_x"], 8, "r_wxr", nsplit=2)
        P.dma("pool", wxs[:], T["wx"].rearrange("n i j -> i n j"), writes=["r_wxs"])
        P.dma("pool", was[:], T["wa"].rearrange("n i j -> i n j"), writes=["r_was"])
        P.dma("sp", rv[:], T["rvec"], writes=["r_rv"])
        P.op("act", lambda e: e.activation(out=sc[:, :], in_=rv[:, :, 7], func=AF.Exp, scale=-1.0), reads=["r_rv"], writes=["r_sc"])
        P.op("act", lambda e: e.activation(out=sc[:, :], in_=sc[:, :], func=AF.Ln, bias=1.0), reads=["r_sc"], writes=["r_sc"])
        P.op("dve", lambda e: e.tensor_scalar(out=sc[:, :], in0=sc[:, :], scalar1=-8.0, scalar2=None, op0=ALU.mult), reads=["r_sc"], writes=["r_sc"])

        def stage1(cb, th, st, bs):
            if th == 0:
                P.op("dve", lambda e: e.memset(xr[:, st, bs, 0:3], 0.0), writes=[("r_xr", st, bs)])
            else:
                P.op("dve", lambda e: e.tensor_copy(out=xr[:, st, bs, 0:3], in_=xr[:, st, 1 - bs, TH:TH + 3]),
                     reads=[("r_xr", st, 1 - bs)], writes=[("r_xr", st, bs)])
            yield
            for n in range(TH // 512):
                N = (TH // 512) * th + n
                bg = 2 * st
                pv = psf(C, bg)
                for k in range(8):
                    P.op("pe", lambda e: e.matmul(pv, lhsT=wg[:, k, cb * 128:(cb + 1) * 128], rhs=hnT[:, k, N * 512:(N + 1) * 512],
                                                  start=(k == 0), stop=(k == 7)),
                         reads=[kg[k // 4], ("r_hnT", k)], writes=[("ps", bg)])
                P.op("act", lambda e: e.activation(out=ybf[:, st, bs, n * 512:(n + 1) * 512], in_=pv, func=AF.Gelu_apprx_tanh),
                     reads=[("ps", bg)], writes=[("r_y", st, bs)])
                yield
                bx_ = 2 * st + 1
                pv2 = psf(C, bx_)
                for k in range(8):
                    P.op("pe", lambda e: e.matmul(pv2, lhsT=wxr[:, k, cb * 128:(cb + 1) * 128], rhs=hnT[:, k, N * 512:(N + 1) * 512],
                                                  start=(k == 0), stop=(k == 7)),
                         reads=[kx[k // 4], ("r_hnT", k)], writes=[("ps", bx_)])
                P.op("dve", lambda e: e.tensor_copy(out=xr[:, st, bs, 3 + n * 512:3 + (n + 1) * 512], in_=pv2),
                     reads=[("ps", bx_)], writes=[("r_xr", st, bs)])
                yield

        def stage2(cb, th, st, bs):
            xk = ("r_xr", st, bs)
            K = lambda name: (name, st)
            P.op("act", lambda e: e.activation(out=xc[:, st, :], in_=xr[:, st, bs, 3:3 + TH], func=AF.Identity, scale=rv[:, cb, 3:4], bias=rv[:, cb, 4:5]),
                 reads=[xk, "r_rv"], writes=[K("r_xc")])
            yield
            for w in range(3):
                P.op("dve", lambda e: e.scalar_tensor_tensor(out=xc[:, st, :], in0=xr[:, st, bs, w:w + TH], scalar=rv[:, cb, w:w + 1], in1=xc[:, st, :],
                                                             op0=ALU.mult, op1=ALU.add),
                     reads=[xk, "r_rv", K("r_xc")], writes=[K("r_xc")])
                yield
            P.op("act", lambda e: e.copy(out=xcb[:, st, :], in_=xc[:, st, :]), reads=[K("r_xc")], writes=[K("r_xcb")])
            yield
            for n in range(TH // 512):
                b1 = 4 + 2 * st
                pv = psf(C, b1)
                P.op("pe", lambda e: e.matmul(pv, lhsT=wxs[:, cb, :], rhs=xcb[:, st, n * 512:(n + 1) * 512], start=True, stop=True),
                     reads=["r_wxs", K("r_xcb")], writes=[("ps", b1)])
                P.op("act", lambda e: e.activation(out=gx[:, st, n * 512:(n + 1) * 512], in_=pv, func=AF.Sigmoid, bias=rv[:, cb, 5:6]),
                     reads=[("ps", b1), "r_rv"], writes=[K("r_gx")])
                yield
                b2 = 5 + 2 * st
                pv2 = psf(C, b2)
                P.op("pe", lambda e: e.matmul(pv2, lhsT=was[:, cb, :], rhs=xcb[:, st, n * 512:(n + 1) * 512], start=True, stop=True),
                     reads=["r_was", K("r_xcb")], writes=[("ps", b2)])
                P.op("act", lambda e: e.activation(out=ga[:, st, n * 512:(n + 1) * 512], in_=pv2, func=AF.Sigmoid, bias=rv[:, cb, 6:7]),
                     reads=[("ps", b2), "r_rv"], writes=[K("r_ga")])
                yield
            P.op("act", lambda e: e.activation(out=ga[:, st, :], in_=ga[:, st, :], func=AF.Exp, scale=sc[:, cb:cb + 1]),
                 reads=[K("r_ga"), "r_sc"], writes=[K("r_ga")])
            yield
            P.op("dve", lambda e: e.tensor_tensor(out=tt[:, st, :], in0=ga[:, st, :], in1=ga[:, st, :], op=ALU.mult), reads=[K("r_ga")], writes=[K("r_tt")])
            yield
            P.op("pool", lambda e: e.tensor_tensor(out=gx[:, st, :], in0=gx[:, st, :], in1=xc[:, st, :], op=ALU.mult),
                 reads=[K("r_gx"), K("r_xc")], writes=[K("r_gx")])
            yield
            P.op("act", lambda e: e.activation(out=tt[:, st, :], in_=tt[:, st, :], func=AF.Sqrt, scale=-1.0, bias=1.0), reads=[K("r_tt")], writes=[K("r_tt")])
            yield
            P.op("dve", lambda e: e.tensor_tensor(out=tt[:, st, :], in0=tt[:, st, :], in1=gx[:, st, :], op=ALU.mult),
                 reads=[K("r_tt"), K("r_gx")], writes=[K("r_tt")])
            if th == 0:
                P.op("dve", lambda e: e.tensor_copy(out=tt[:, st, 0:1], in_=gx[:, st, 0:1]), reads=[K("r_gx"), K("r_tt")], writes=[K("r_tt")])
            yield
            init = 0.0 if th == 0 else hlast[:, st:st + 1]
            P.op("dve", lambda e: e.tensor_tensor_scan(out=hs[:, st, :], data0=ga[:, st, :], data1=tt[:, st, :], initial=init, op0=ALU.mult, op1=ALU.add),
                 reads=[K("r_ga"), K("r_tt"), K("r_hlast")], writes=[K("r_hs")])
            P.op("dve", lambda e: e.tensor_copy(out=hlast[:, st:st + 1], in_=hs[:, st, TH - 1:TH]), reads=[K("r_hs")], writes=[K("r_hlast")])
            yield
            P.op("pool", lambda e: e.tensor_tensor(out=gout[:, st, :], in0=hs[:, st, :], in1=ybf[:, st, bs, :], op=ALU.mult),
                 reads=[K("r_hs"), ("r_y", st, bs)], writes=[K("r_gout")])
            P.dma("sp", G[cb * 128:(cb + 1) * 128, th * TH:(th + 1) * TH], gout[:, st, :], reads=[K("r_gout")], writes=[("G", cb, th)])
            yield

        def interleave(gens):
            for _ in itertools.zip_longest(*gens):
                pass

        supers = []
        for j in range((ncb + 1) // 2):
            cbs = [c for c in (2 * j, 2 * j + 1) if c < ncb]
            for th in range(NTH):
                supers.append((cbs, th))
        interleave([stage1(cb, supers[0][1], st, 0) for st, cb in enumerate(supers[0][0])])
        for i, (cbs, th) in enumerate(supers):
            bs = i % 2
            if i + 1 < len(supers):
                ncbs, nth = supers[i + 1]
                interleave([stage1(cb, nth, st, 1 - bs) for st, cb in enumerate(ncbs)])
            interleave([stage2(cb, th, st, bs) for st, cb in enumerate(cbs)])


def build_phase_B():
    nc = bass.Bass("TRN2", target_bir_lowering=False)
    T = {}

    def din(name, shape, dt=F32):
        T[name] = nc.dram_tensor(name, shape, dt, kind="ExternalInput").ap()

    din("hnT", [D, S], BF16); din("w_g", [D, 640]); din("w_x", [D, 640]); din("wx", [5, 128, 128]); din("wa", [5, 128, 128])
    din("rvec", [128, 5, 8]); din("ident", [128, 128])
    G = nc.dram_tensor("G", [640, S], BF16, kind="ExternalOutput").ap()
    with contextlib.ExitStack() as es:
        P = Prog(nc, es)
        C = Ctx()
        setup_common(nc, P, es, C, T["ident"])
        setup_eps(C, es)
        emit_rec(C, T, G)
        P.finish()
    return nc


def emit_recout(C, T, H, row=0, blend=None):
    nc, P = C.nc, C.P
    with scope(P) as es:
        g = load_gains(C, es, T["vec"], [row], "gro")
        gT = es.enter_context(nc.sbuf_tensor("gTs", [128, 10, NOWN], BF16))
        wro = es.enter_context(nc.sbuf_tensor("wro", [128, 10, D], BF16))
        if blend is None:
            gv = T["gT"].rearrange("(c p) t -> p c t", p=128)
            for c in range(10):
                P.dma("sp", gT[:, c, :], gv[:, c, :], writes=[("gTs", c)])
        else:
            Gd, H1d, sel_d = blend
            sel = es.enter_context(nc.sbuf_tensor("sel_sb", [128, 2], F32))
            P.dma("sp", sel[:], sel_d, writes=["sel"])
            gT2 = es.enter_context(nc.sbuf_tensor("gTs2", [128, 2, NOWN], BF16))
            hres2 = es.enter_context(nc.sbuf_tensor("hres2", [128, 2, D], F32))
            gv = Gd.rearrange("(c p) t -> p c t", p=128)
            for c in range(10):
                s2 = c % 2
                P.dma("sp", gT[:, c, :], gv[:, c, 0:NOWN], writes=[("gTs", c)])
                P.dma("sp", gT2[:, s2, :], gv[:, c, NOWN:2 * NOWN], writes=[("gTs2", s2)])
                P.op("act", lambda e: e.activation(out=gT[:, c, :], in_=gT[:, c, :], func=AF.Copy, scale=sel[:, 0:1]),
                     reads=[("gTs", c), "sel"], writes=[("gTs", c)])
                P.op("dve", lambda e: e.scalar_tensor_tensor(out=gT[:, c, :], in0=gT2[:, s2, :], scalar=sel[:, 1:2], in1=gT[:, c, :],
                                                             op0=ALU.mult, op1=ALU.add),
                     reads=[("gTs", c), ("gTs2", s2), "sel"], writes=[("gTs", c)])
        kw = load_w_bf16(C, wro, T["w_ro"], 10, "wro", nsplit=2)
        hres = es.enter_context(nc.sbuf_tensor("hres", [128, 2, D], F32))
        hh = es.enter_context(nc.sbuf_tensor("hh1", [128, 2, D], F32))
        for lt in range(16):
            sl = lt % 2
            if blend is None:
                P.dma("sp", hres[:, sl, :], T["h1"][lt * 128:(lt + 1) * 128, :], writes=[("hres", sl)])
            else:
                P.dma("sp", hres[:, sl, :], H1d[lt * 128:(lt + 1) * 128, :], writes=[("hres", sl)])
                P.dma("sp", hres2[:, sl, :], H1d[NOWN + lt * 128:NOWN + (lt + 1) * 128, :], writes=[("hres2", sl)])
                P.op("act", lambda e: e.activation(out=hres[:, sl, :], in_=hres[:, sl, :], func=AF.Copy, scale=sel[:, 0:1]),
                     reads=[("hres", sl), "sel"], writes=[("hres", sl)])
                P.op("dve", lambda e: e.scalar_tensor_tensor(out=hres[:, sl, :], in0=hres2[:, sl, :], scalar=sel[:, 1:2], in1=hres[:, sl, :],
                                                             op0=ALU.mult, op1=ALU.add),
                     reads=[("hres", sl), ("hres2", sl), "sel"], writes=[("hres", sl)])
            for hf in range(2):
                pk = ("ps", 2 * sl + hf)
                pv = psf(C, 2 * sl + hf)
                for c in range(10):
                    P.op("pe", lambda e: e.matmul(pv, lhsT=gT[:, c, lt * 128:(lt + 1) * 128], rhs=wro[:, c, hf * 512:(hf + 1) * 512],
                                                  start=(c == 0), stop=(c == 9)),
                         reads=[("gTs", c), kw[c // 5]], writes=[pk])
            emit_norm_residual(C, C.ps[sl][:, :], [("ps", 2 * sl), ("ps", 2 * sl + 1)], g[:, 0, :], ("gro", 0),
                               hres[:, sl, :], [("hres", sl)], hh[:, sl, :], [("hh1", sl)])
            P.dma("sp", H[lt * 128:(lt + 1) * 128, :], hh[:, sl, :], reads=[("hh1", sl)], writes=[("H", lt)])


def build_phase_C():
    nc = bass.Bass("TRN2", target_bir_lowering=False)
    T = {}

    def din(name, shape, dt=F32):
        T[name] = nc.dram_tensor(name, shape, dt, kind="ExternalInput").ap()

    din("gT", [RW, NOWN], BF16); din("h1", [NOWN, D]); din("pT", [256, NOWN]); din("w_ro", [RW, D])
    din("w_up", [D, DFF]); din("w_dn", [DFF, D]); din("w_pp", [256, D]); din("w_pg", [D, D]); din("vec", [8, D]); din("ident", [128, 128])
    H = nc.dram_tensor("H", [NOWN, D], F32, kind="ExternalOutput").ap()
    OUT = nc.dram_tensor("OUT", [NOWN, D], F32, kind="ExternalOutput").ap()
    with contextlib.ExitStack() as es:
        P = Prog(nc, es)
        C = Ctx()
        setup_common(nc, P, es, C, T["ident"])
        setup_eps(C, es)
        emit_recout(C, T, H)
        emit_mlp(C, H, T["w_up"], T["w_dn"], T["vec"], 1, 2, "1")
        emit_ple(C, H, T["pT"], T["w_pp"], T["w_pg"], T["vec"], 3, OUT, "1")
        P.finish()
    return nc


def build_fused():
    nc = bass.Bass("TRN2", target_bir_lowering=False)
    T = {}

    def din(name, shape, dt=F32):
        T[name] = nc.dram_tensor(name, shape, dt, kind="ExternalInput").ap()

    din("xf", [S, D]); din("pT", [256, S]); din("pT1", [256, NOWN])
    din("w_in", [D, 3080]); din("w_out", [D, D]); din("w_up", [D, DFF]); din("w_dn", [DFF, D])
    din("w_pp", [256, D]); din("w_pg", [D, D]); din("vec", [16, D]); din("lamv", [1, 256]); din("bf", [8, 1])
    din("ident", [128, 128]); din("masks", [2, 128, 256]); din("alibi", [128, 128]); din("sel", [128, 2])
    din("w_g", [D, RW]); din("w_x", [D, RW]); din("wx", [10, 128, 128]); din("wa", [10, 128, 128]); din("rvec", [128, 10, 8])
    din("w_ro", [RW, D]); din("w_up1", [D, DFF]); din("w_dn1", [DFF, D]); din("w_pp1", [256, D]); din("w_pg1", [D, D])
    HA = nc.dram_tensor("HA", [S, D], F32, kind="Internal").ap()
    H1 = nc.dram_tensor("H1", [S, D], F32, kind="Internal").ap()
    HNT = nc.dram_tensor("HNT", [D, S], BF16, kind="Internal").ap()
    G = nc.dram_tensor("G", [RW, S], BF16, kind="Internal").ap()
    HC = nc.dram_tensor("HC", [NOWN, D], F32, kind="Internal").ap()
    OUT = nc.dram_tensor("OUT", [NOWN, D], F32, kind="ExternalOutput").ap()
    with contextlib.ExitStack() as es:
        P = Prog(nc, es)
        C = Ctx()
        setup_common(nc, P, es, C, T["ident"])
        setup_eps(C, es)
        emit_attention(C, T, HA, full=True)
        emit_mlp(C, HA, T["w_up"], T["w_dn"], T["vec"], 2, 3, "0", ntok=S)
        emit_ple(C, HA, T["pT"], T["w_pp"], T["w_pg"], T["vec"], 4, H1, "0", hn_row=5, HNT=HNT, ntiles=32)
        T1 = {"hnT": HNT, "w_g": T["w_g"], "w_x": T["w_x"], "wx": T["wx"], "wa": T["wa"], "rvec": T["rvec"]}
        emit_rec(C, T1, G, ncb=10)
        T2 = {"vec": T["vec"], "w_ro": T["w_ro"]}
        emit_recout(C, T2, HC, row=8, blend=(G, H1, T["sel"]))
        emit_mlp(C, HC, T["w_up1"], T["w_dn1"], T["vec"], 9, 10, "1", ntok=NOWN)
        emit_ple(C, HC, T["pT1"], T["w_pp1"], T["w_pg1"], T["vec"], 11, OUT, "1", ntiles=16)
        P.finish()
    return nc


def own_tiles(r):
    return [4 * p + 2 * r + e for p in range(8) for e in range(2)]


def own_index(r):
    return np.concatenate([np.arange(g * 128, (g + 1) * 128) for g in own_tiles(r)])


def role_consts(r):
    ident = np.eye(128, dtype=np.float32)
    masks = np.zeros((4, 128, 256), np.float32)
    jj = np.arange(128)[:, None]
    ii = np.arange(128)[None, :]
    tri = (jj <= ii).astype(np.float32)
    for m in range(4):
        for e in range(2):
            qt = 2 * r + e
            if m < qt:
                masks[m, :, e * 128:(e + 1) * 128] = 1.0
            elif m == qt:
                masks[m, :, e * 128:(e + 1) * 128] = tri
    alibi = np.zeros((128, 4, 32), np.float32)
    for h in range(4):
        for idx in range(32):
            d = (idx - 28 - 2 * r - 2) * 128 + np.arange(128)
            alibi[:, h, idx] = np.minimum(SLOPES[h] * d, 0.0)
    return ident, masks, alibi.reshape(128, 128)


def f32c(a):
    return np.ascontiguousarray(a, dtype=np.float32)


def prep_A(inp, c):
    b, r = c // 2, c % 2
    oi = own_index(r)
    ident, masks, alibi = role_consts(r)
    vec = np.zeros((8, D), np.float32)
    vec[0] = inp["ln_mix_pre"][0]
    vec[1] = inp["ln_mix_post"][0]
    vec[2] = inp["ln_mlp_pre"][0]
    vec[3] = inp["ln_mlp_post"][0]
    vec[4] = inp["ple_norm"][0]
    vec[5] = inp["ln_mix_pre"][1]
    vec[6, :128] = inp["diff_subln"][0]
    lamv = np.concatenate([inp["diff_lambda_q1"][0], inp["diff_lambda_k1"][0],
                           inp["diff_lambda_q2"][0], inp["diff_lambda_k2"][0]])[None, :]
    return {
        "xf": f32c(inp["x"][b]), "xo": f32c(inp["x"][b][oi]), "pT": f32c(inp["p"][0, b][oi].T),
        "w_in": f32c(inp["attn_w_in"][0]), "w_out": f32c(inp["attn_w_out"][0]),
        "w_up": f32c(inp["mlp_w_up"][0]), "w_dn": f32c(inp["mlp_w_down"][0]),
        "w_pp": f32c(inp["ple_w_proj"][0]), "w_pg": f32c(inp["ple_w_gate"][0]),
        "vec": vec, "lamv": f32c(lamv), "bf": f32c(inp["attn_b_forget"][0][:, None]),
        "ident": ident, "masks": masks, "alibi": alibi,
    }


def prep_B(inp, c, hnT_full):
    r = c % 2
    cols_g = np.arange(5 * r * 128, (5 * r + 5) * 128)
    cols_x = RW + cols_g
    w_in = inp["rec_w_in"][0]
    rvec = np.zeros((128, 5, 8), np.float32)
    for j in range(5):
        ch = np.arange((5 * r + j) * 128, (5 * r + j + 1) * 128)
        rvec[:, j, 0:4] = inp["rec_conv_w"][0][:, ch].T
        rvec[:, j, 4] = inp["rec_conv_b"][0][ch]
        rvec[:, j, 5] = inp["rec_bx"][0][ch]
        rvec[:, j, 6] = inp["rec_ba"][0][ch]
        rvec[:, j, 7] = inp["rec_a_param"][0][ch]
    return {
        "hnT": hnT_full, "w_g": f32c(w_in[:, cols_g]), "w_x": f32c(w_in[:, cols_x]),
        "wx": f32c(inp["rec_wx"][0][5 * r:5 * r + 5]), "wa": f32c(inp["rec_wa"][0][5 * r:5 * r + 5]),
        "rvec": rvec, "ident": np.eye(128, dtype=np.float32),
    }


def prep_C(inp, c, gT_own, h1_own):
    b, r = c // 2, c % 2
    oi = own_index(r)
    vec = np.zeros((8, D), np.float32)
    vec[0] = inp["ln_mix_post"][1]
    vec[1] = inp["ln_mlp_pre"][1]
    vec[2] = inp["ln_mlp_post"][1]
    vec[3] = inp["ple_norm"][1]
    return {
        "gT": gT_own, "h1": h1_own, "pT": f32c(inp["p"][1, b][oi].T), "w_ro": f32c(inp["rec_w_out"][0]),
        "w_up": f32c(inp["mlp_w_up"][1]), "w_dn": f32c(inp["mlp_w_down"][1]),
        "w_pp": f32c(inp["ple_w_proj"][1]), "w_pg": f32c(inp["ple_w_gate"][1]),
        "vec": vec, "ident": np.eye(128, dtype=np.float32),
    }


def full_consts():
    ident = np.eye(128, dtype=np.float32)
    jj = np.arange(128)[:, None]
    ii = np.arange(128)[None, :]
    tri = (jj <= ii).astype(np.float32)
    masks = np.zeros((2, 128, 256), np.float32)
    masks[0, :, 0:128] = tri
    masks[0, :, 128:256] = 1.0
    masks[1, :, 128:256] = tri
    alibi = np.zeros((128, 4, 32), np.float32)
    for h in range(4):
        for idx in range(32):
            d = (idx - 30 - 2) * 128 + np.arange(128)
            alibi[:, h, idx] = np.minimum(SLOPES[h] * d, 0.0)
    return ident, masks, alibi.reshape(128, 128)


def prep_fused(inp, c):
    b, r = c // 2, c % 2
    ident, masks, alibi = full_consts()
    vec = np.zeros((16, D), np.float32)
    vec[0] = inp["ln_mix_pre"][0]
    vec[1] = inp["ln_mix_post"][0]
    vec[2] = inp["ln_mlp_pre"][0]
    vec[3] = inp["ln_mlp_post"][0]
    vec[4] = inp["ple_norm"][0]
    vec[5] = inp["ln_mix_pre"][1]
    vec[6, :128] = inp["diff_subln"][0]
    vec[8] = inp["ln_mix_post"][1]
    vec[9] = inp["ln_mlp_pre"][1]
    vec[10] = inp["ln_mlp_post"][1]
    vec[11] = inp["ple_norm"][1]
    lamv = np.concatenate([inp["diff_lambda_q1"][0], inp["diff_lambda_k1"][0],
                           inp["diff_lambda_q2"][0], inp["diff_lambda_k2"][0]])[None, :]
    rvec = np.zeros((128, 10, 8), np.float32)
    for j in range(10):
        ch = np.arange(j * 128, (j + 1) * 128)
        rvec[:, j, 0:4] = inp["rec_conv_w"][0][:, ch].T
        rvec[:, j, 4] = inp["rec_conv_b"][0][ch]
        rvec[:, j, 5] = inp["rec_bx"][0][ch]
        rvec[:, j, 6] = inp["rec_ba"][0][ch]
        rvec[:, j, 7] = inp["rec_a_param"][0][ch]
    sel = np.zeros((128, 2), np.float32)
    sel[:, r] = 1.0
    w_in1 = inp["rec_w_in"][0]
    return {
        "xf": f32c(inp["x"][b]), "pT": f32c(inp["p"][0, b].T), "pT1": f32c(inp["p"][1, b][r * NOWN:(r + 1) * NOWN].T),
        "w_in": f32c(inp["attn_w_in"][0]), "w_out": f32c(inp["attn_w_out"][0]),
        "w_up": f32c(inp["mlp_w_up"][0]), "w_dn": f32c(inp["mlp_w_down"][0]),
        "w_pp": f32c(inp["ple_w_proj"][0]), "w_pg": f32c(inp["ple_w_gate"][0]),
        "vec": vec, "lamv": f32c(lamv), "bf": f32c(inp["attn_b_forget"][0][:, None]),
        "ident": ident, "masks": masks, "alibi": alibi, "sel": sel,
        "w_g": f32c(w_in1[:, :RW]), "w_x": f32c(w_in1[:, RW:]), "wx": f32c(inp["rec_wx"][0]), "wa": f32c(inp["rec_wa"][0]),
        "rvec": rvec, "w_ro": f32c(inp["rec_w_out"][0]),
        "w_up1": f32c(inp["mlp_w_up"][1]), "w_dn1": f32c(inp["mlp_w_down"][1]),
        "w_pp1": f32c(inp["ple_w_proj"][1]), "w_pg1": f32c(inp["ple_w_gate"][1]),
    }


_NC_CACHE = {}


def _get(name, fn):
    if name not in _NC_CACHE:
        _NC_CACHE[name] = fn()
    return _NC_CACHE[name]


def kernel_unfused(**inputs):
    inp = {k: np.asarray(v) for k, v in inputs.items()}
    cores = list(range(NCORES))
    ncA = _get("A", build_phase_A)
    resA = run_bass_kernel_spmd(ncA, [prep_A(inp, c) for c in cores], core_ids=cores).results
    hnT_full = []
    for b in range(4):
        full = np.zeros((D, S), dtype=resA[0]["HNT"].dtype)
        for r in range(2):
            full[:, own_index(r)] = resA[2 * b + r]["HNT"]
        hnT_full.append(full)
    ncB = _get("B", build_phase_B)
    resB = run_bass_kernel_spmd(ncB, [prep_B(inp, c, hnT_full[c // 2]) for c in cores], core_ids=cores).results
    ncC = _get("C", build_phase_C)
    mapsC = []
    for c in cores:
        b, r = c // 2, c % 2
        oi = own_index(r)
        gfull = np.concatenate([resB[2 * b]["G"], resB[2 * b + 1]["G"]], axis=0)
        mapsC.append(prep_C(inp, c, np.ascontiguousarray(gfull[:, oi]), resA[c]["H1"]))
    resC = run_bass_kernel_spmd(ncC, mapsC, core_ids=cores).results
    out = np.zeros((4, S, D), np.float32)
    for c in cores:
        b, r = c // 2, c % 2
        out[b, own_index(r)] = resC[c]["OUT"]
    return out


def kernel(**inputs):
    inp = {k: np.asarray(v) for k, v in inputs.items()}
    cores = list(range(NCORES))
    nc = _get("F", build_fused)
    res = run_bass_kernel_spmd(nc, [prep_fused(inp, c) for c in cores], core_ids=cores).results
    out = np.zeros((4, S, D), np.float32)
    for c in cores:
        b, r = c // 2, c % 2
        out[b, r * NOWN:(r + 1) * NOWN] = res[c]["OUT"]
    return out
```

```python
import contextlib
import numpy as np
import concourse.bass as bass
import concourse.mybir as mybir
from concourse.bass_utils import run_bass_kernel_spmd

F32 = mybir.dt.float32
BF16 = mybir.dt.bfloat16
AF = mybir.ActivationFunctionType
ALU = mybir.AluOpType
AX = mybir.AxisListType

NCORES = 8
D = 1024
S = 4096
NOWN = 2048
DFF = 4096
RW = 1280
LAM_INIT0 = 0.8 - 0.6 * 1.0
SLOPES = [2.0 ** (-8.0 * (h + 1) / 4) for h in range(4)]


class Prog:
    K_DMA = 8

    def __init__(self, nc, es):
        self.nc = nc
        self.engs = {"pe": nc.tensor, "act": nc.scalar, "dve": nc.vector,
                     "pool": nc.gpsimd, "sp": nc.sync}
        self.semobj = {}
        for k in self.engs:
            self.semobj[k] = es.enter_context(nc.semaphore("sem_" + k))
        self.cnt = {k: 0 for k in self.engs}
        self.seen = {k: {} for k in self.engs}
        self.lastw = {}
        self.readers = {}
        self.dcnt = {}
        for q in ("sp", "act", "pool"):
            self.dcnt[q] = 0
            for i in range(self.K_DMA):
                self.semobj[(q, i)] = es.enter_context(nc.semaphore("d_%s_%d" % (q, i)))

    def _wait(self, eng, ev):
        sk, val = ev
        if self.seen[eng].get(sk, 0) >= val:
            return
        self.engs[eng].wait_ge(self.semobj[sk], val)
        self.seen[eng][sk] = val

    def _deps(self, eng, reads, writes):
        deps = {}
        for k in reads:
            w = self.lastw.get(k)
            if w is not None:
                deps[w[0]] = max(deps.get(w[0], 0), w[1])
        for k in writes:
            w = self.lastw.get(k)
            if w is not None:
                deps[w[0]] = max(deps.get(w[0], 0), w[1])
            for sk, v in self.readers.get(k, {}).items():
                deps[sk] = max(deps.get(sk, 0), v)
        for sk, v in deps.items():
            if eng == "pe" and sk == "pe":
                continue
            self._wait(eng, (sk, v))

    def _record(self, ev, reads, writes):
        for k in writes:
            self.lastw[k] = ev
            self.readers[k] = {}
        for k in reads:
            if k in writes:
                continue
            d = self.readers.setdefault(k, {})
            d[ev[0]] = max(d.get(ev[0], 0), ev[1])

    def op(self, eng, fn, reads=(), writes=()):
        self._deps(eng, reads, writes)
        inst = fn(self.engs[eng])
        self.cnt[eng] += 1
        inst.then_inc(self.semobj[eng], 1)
        self._record((eng, self.cnt[eng]), reads, writes)

    def dma(self, q, out, in_, reads=(), writes=()):
        self._deps(q, reads, writes)
        i = self.dcnt[q]
        s = i % self.K_DMA
        rnd = i // self.K_DMA
        if rnd > 0:
            self._wait(q, ((q, s), 16 * rnd))
        inst = self.engs[q].dma_start(out=out, in_=in_)
        inst.then_inc(self.semobj[(q, s)], 16)
        self.dcnt[q] = i + 1
        self._record(((q, s), 16 * (rnd + 1)), reads, writes)

    def barrier(self):
        evs = []
        for q in ("sp", "act", "pool"):
            n = self.dcnt[q]
            for s in range(self.K_DMA):
                m = (n - s + self.K_DMA - 1) // self.K_DMA if n > s else 0
                if m > 0:
                    evs.append(((q, s), 16 * m))
        for k in ("pe", "act", "dve", "pool", "sp"):
            if self.cnt[k] > 0:
                evs.append((k, self.cnt[k]))
        for eng in ("pe", "act", "dve", "pool", "sp"):
            for ev in evs:
                if ev[0] == eng:
                    continue
                self._wait(eng, ev)

    def barrier_keys(self, eng, keys):
        self._deps(eng, [], list(keys))

    def finish(self):
        for q in ("sp", "act", "pool"):
            n = self.dcnt[q]
            for s in range(self.K_DMA):
                m = (n - s + self.K_DMA - 1) // self.K_DMA if n > s else 0
                if m > 0:
                    self._wait("sp", ((q, s), 16 * m))
        for k in ("pe", "act", "dve", "pool"):
            if self.cnt[k] > 0:
                self._wait("sp", (k, self.cnt[k]))


class Ctx:
    pass


@contextlib.contextmanager
def scope(P):
    with contextlib.ExitStack() as es:
        yield es
        P.barrier()


def setup_common(nc, P, es, C, ident_d):
    C.nc = nc
    C.P = P
    C.identf = es.enter_context(nc.sbuf_tensor("identf", [128, 128], F32))
    C.identb = es.enter_context(nc.sbuf_tensor("identb", [128, 128], BF16))
    C.onesf = es.enter_context(nc.sbuf_tensor("onesf", [128, 512], F32))
    P.dma("sp", C.identf[:], ident_d, writes=["identf"])
    P.dma("pool", C.identb[:], ident_d, writes=["identb"])
    P.op("pool", lambda e: e.memset(C.onesf[:], 1.0), writes=["onesf"])
    C.ps = [es.enter_context(nc.psum_tensor("ps%d" % i, [128, 1024], F32)) for i in range(4)]
    C.psb = [t.bitcast(BF16) for t in C.ps]
    C.junk = es.enter_context(nc.sbuf_tensor("junk", [128, 1024], BF16))
    C.junk2 = es.enter_context(nc.sbuf_tensor("junk2", [128, 128], BF16))
    C.small = es.enter_context(nc.sbuf_tensor("small", [128, 64], F32))
    C.small_i = 0


def small_slot(C, n=1):
    i = C.small_i
    if i + n > 64:
        i = 0
    C.small_i = i + n
    return i


def psf(C, b):
    return C.ps[b // 2][:, (b % 2) * 512:(b % 2) * 512 + 512]


def psbf(C, b):
    return C.psb[b // 2][:, (b % 2) * 1024:(b % 2) * 1024 + 1024]


def rstd_from(C, src, src_keys, n, eps, eng="act"):
    P = C.P
    i = small_slot(C)
    ss = C.small[:, i:i + 1]
    key = ("small", i)
    if eng == "dve":
        P.op("dve", lambda e: e.scalar_tensor_tensor(out=C.junk2[:, 0:n], in0=src, scalar=1.0, in1=src,
                                                     op0=ALU.mult, op1=ALU.mult, accum_out=ss),
             reads=list(src_keys), writes=["junk2", key])
    else:
        P.op("act", lambda e: e.activation(out=C.junk[:, 0:n], in_=src, func=AF.Square, accum_out=ss),
             reads=list(src_keys), writes=["junk", key])
    P.op("dve", lambda e: e.tensor_scalar(out=ss, in0=ss, scalar1=1.0 / n, scalar2=float(eps), op0=ALU.mult, op1=ALU.add),
         reads=[key], writes=[key])
    P.op("pool", lambda e: e.tensor_tensor(out=ss, in0=ss, in1=C.negh, op=ALU.pow), reads=[key, "epsb"], writes=[key])
    return ss, key


def transpose_to(C, src_bf, src_keys, dst, dst_keys, bank, nk=8, evac="dve"):
    P = C.P
    pk = ("ps", bank)
    pv = psbf(C, bank)
    for k in range(nk):
        P.op("pe", lambda e, k=k: e.transpose(out=pv[:, k * 128:(k + 1) * 128], in_=src_bf[:, k * 128:(k + 1) * 128],
                                              identity=C.identb[:]),
             reads=list(src_keys) + ["identb"], writes=[pk])
    srcv = pv[:, 0:nk * 128].rearrange("p (k t) -> p k t", k=nk)
    if evac == "act":
        P.op("act", lambda e: e.copy(out=dst, in_=srcv), reads=[pk], writes=list(dst_keys))
    else:
        P.op("dve", lambda e: e.tensor_copy(out=dst, in_=srcv), reads=[pk], writes=list(dst_keys))


def load_w_bf16(C, dst, dram2d, nk, key, nsplit=1):
    P = C.P
    src = dram2d.rearrange("(k p) n -> p k n", p=128)
    step = nk // nsplit
    for i in range(nsplit):
        P.dma("pool", dst[:, i * step:(i + 1) * step, :], src[:, i * step:(i + 1) * step, :], writes=[(key, i)])
    return [(key, i) for i in range(nsplit)]


def emit_norm_residual(C, m_src, m_keys, gain, gain_key, res, res_keys, out, out_keys):
    P = C.P
    rs, rk = rstd_from(C, m_src, m_keys, D, 1e-6)
    P.op("dve", lambda e: e.scalar_tensor_tensor(out=out, in0=m_src, scalar=rs, in1=gain, op0=ALU.mult, op1=ALU.mult),
         reads=list(m_keys) + [rk, gain_key], writes=list(out_keys))
    P.op("pool", lambda e: e.tensor_tensor(out=out, in0=out, in1=res, op=ALU.add),
         reads=list(out_keys) + list(res_keys), writes=list(out_keys))


def emit_norm_bf16(C, src, src_keys, gain, gain_key, out_bf, out_keys):
    P = C.P
    rs, rk = rstd_from(C, src, src_keys, D, 1e-6)
    P.op("dve", lambda e: e.scalar_tensor_tensor(out=out_bf, in0=src, scalar=rs, in1=gain, op0=ALU.mult, op1=ALU.mult),
         reads=list(src_keys) + [rk, gain_key], writes=list(out_keys))


def load_gains(C, es, vec_d, rows, name):
    nc, P = C.nc, C.P
    g = es.enter_context(nc.sbuf_tensor(name, [128, len(rows), D], F32))
    for i, r in enumerate(rows):
        P.dma("sp", g[:, i, :], vec_d[r, :].partition_broadcast(128), writes=[(name, i)])
    return g


def setup_eps(C, es):
    nc, P = C.nc, C.P
    t = es.enter_context(nc.sbuf_tensor("epsb", [128, 3], F32))
    P.op("pool", lambda e: e.memset(t[:, 2:3], -0.5), writes=["epsb"])
    C.negh = t[:, 2:3]
    P.op("pool", lambda e: e.memset(t[:, 0:1], 1e-6), writes=["epsb"])
    P.op("pool", lambda e: e.memset(t[:, 1:2], 1e-5), writes=["epsb"])
    C.epsb = {1e-6: t[:, 0:1], 1e-5: t[:, 1:2]}


def emit_mlp(C, H, w_up_d, w_dn_d, vec_d, row_pre, row_post, tag, ntok=NOWN):
    nc, P = C.nc, C.P
    MT = 256
    with scope(P) as es:
        wup = es.enter_context(nc.sbuf_tensor("wup" + tag, [128, 8, DFF], BF16))
        wdn = es.enter_context(nc.sbuf_tensor("wdn" + tag, [128, 32, D], BF16))
        wup_src = w_up_d.rearrange("(k p) n -> p k n", p=128)
        kup = []
        for i in range(8):
            P.dma("pool", wup[:, :, i * 512:(i + 1) * 512], wup_src[:, :, i * 512:(i + 1) * 512], writes=[("wup" + tag, i)])
            kup.append(("wup" + tag, i))
        kdn = load_w_bf16(C, wdn, w_dn_d, 32, "wdn" + tag, nsplit=8)
        g = load_gains(C, es, vec_d, [row_pre, row_post], "gm" + tag)
        ha = es.enter_context(nc.sbuf_tensor("ha" + tag, [128, 2, 2, D], F32))
        ubf = es.enter_context(nc.sbuf_tensor("ubf" + tag, [128, 2, D], BF16))
        uT = es.enter_context(nc.sbuf_tensor("uT" + tag, [128, 2, 8, MT], BF16))
        hid = es.enter_context(nc.sbuf_tensor("hid" + tag, [128, 32, MT], BF16))
        rl = es.enter_context(nc.sbuf_tensor("rl" + tag, [128, 2, MT], F32))
        hb = es.enter_context(nc.sbuf_tensor("hb" + tag, [128, 2, D], F32))
        nmt = ntok // MT

        def prologue(t):
            sl = t % 2
            for e in range(2):
                lt = 2 * t + e
                P.dma("sp", ha[:, sl, e, :], H[lt * 128:(lt + 1) * 128, :], reads=[("H", lt)], writes=[("ha", sl, e)])
                emit_norm_bf16(C, ha[:, sl, e, :], [("ha", sl, e)], g[:, 0, :], ("gm" + tag, 0), ubf[:, e, :], [("ubf", e)])
                transpose_to(C, ubf[:, e, :], [("ubf", e)], uT[:, sl, :, e * 128:(e + 1) * 128], [("uT", sl, e)],
                             bank=e, evac="act")

        def up(t):
            sl = t % 2
            for c in range(32):
                bank = c % 4
                pk = ("ps", bank)
                pv = psf(C, bank)[:, 0:MT]
                for k in range(8):
                    P.op("pe", lambda e: e.matmul(pv, lhsT=wup[:, k, c * 128:(c + 1) * 128], rhs=uT[:, sl, k, :],
                                                  start=(k == 0), stop=(k == 7)),
                         reads=[kup[c // 4], ("uT", sl, 0), ("uT", sl, 1)], writes=[pk])
                rs_ = c % 2
                P.op("act", lambda e: e.activation(out=rl[:, rs_, :], in_=pv, func=AF.Relu),
                     reads=[pk], writes=[("rl", rs_)])
                P.op("pool", lambda e: e.tensor_tensor(out=hid[:, c, :], in0=rl[:, rs_, :], in1=rl[:, rs_, :], op=ALU.mult),
                     reads=[("rl", rs_)], writes=[("hid", c)])

        def down(t):
            sl = t % 2
            for e in range(2):
                lt = 2 * t + e
                for hf in range(2):
                    pk = ("ps", 4 + 2 * e + hf)
                    pv = psf(C, 4 + 2 * e + hf)
                    for c in range(32):
                        P.op("pe", lambda e_: e_.matmul(pv, lhsT=hid[:, c, e * 128:(e + 1) * 128],
                                                        rhs=wdn[:, c, hf * 512:(hf + 1) * 512],
                                                        start=(c == 0), stop=(c == 31)),
                             reads=[("hid", c), kdn[c // 4]], writes=[pk])
                fv = C.ps[2 + e][:, :]
                emit_norm_residual(C, fv, [("ps", 4 + 2 * e), ("ps", 5 + 2 * e)], g[:, 1, :], ("gm" + tag, 1),
                                   ha[:, sl, e, :], [("ha", sl, e)], hb[:, e, :], [("hb", e)])
                P.dma("sp", H[lt * 128:(lt + 1) * 128, :], hb[:, e, :], reads=[("hb", e)], writes=[("H", lt)])

        prologue(0)
        for t in range(nmt):
            up(t)
            if t + 1 < nmt:
                prologue(t + 1)
            down(t)


def emit_ple(C, H, pT_d, w_pp_d, w_pg_d, vec_d, row_ple, OUT, tag, hn_row=None, HNT=None, ntiles=16):
    nc, P = C.nc, C.P
    with scope(P) as es:
        wpg = es.enter_context(nc.sbuf_tensor("wpg" + tag, [128, 8, D], BF16))
        wpp = es.enter_context(nc.sbuf_tensor("wpp" + tag, [128, 2, D], BF16))
        kpg = load_w_bf16(C, wpg, w_pg_d, 8, "wpg" + tag, nsplit=2)
        kpp = load_w_bf16(C, wpp, w_pp_d, 2, "wpp" + tag, nsplit=1)
        rows = [row_ple] + ([hn_row] if hn_row is not None else [])
        g = load_gains(C, es, vec_d, rows, "gp" + tag)
        hb = es.enter_context(nc.sbuf_tensor("phb" + tag, [128, 3, D], F32))
        hbb = es.enter_context(nc.sbuf_tensor("phbb" + tag, [128, 2, D], BF16))
        hbT = es.enter_context(nc.sbuf_tensor("phbT" + tag, [128, 2, 8, 128], BF16))
        pT = es.enter_context(nc.sbuf_tensor("ppT" + tag, [128, 3, 2, 128], BF16))
        sg = es.enter_context(nc.sbuf_tensor("psg" + tag, [128, 2, D], F32))
        ee = es.enter_context(nc.sbuf_tensor("pee" + tag, [128, 2, D], F32))
        hnb = es.enter_context(nc.sbuf_tensor("phnb" + tag, [128, 2, D], BF16))
        hnT = es.enter_context(nc.sbuf_tensor("phnT" + tag, [128, 2, 8, 128], BF16))
        pTv = pT_d.rearrange("(k f) t -> f k t", f=128)
        def loads(lt):
            s3 = lt % 3
            P.dma("sp", hb[:, s3, :], H[lt * 128:(lt + 1) * 128, :], reads=[("H", lt)], writes=[("phb", s3)])
            P.dma("pool", pT[:, s3, :, :], pTv[:, :, lt * 128:(lt + 1) * 128], writes=[("ppT", s3)])

        def front(lt):
            sl = lt % 2
            s3 = lt % 3
            if lt + 1 < ntiles:
                loads(lt + 1)
            P.op("dve", lambda e: e.tensor_copy(out=hbb[:, sl, :], in_=hb[:, s3, :]), reads=[("phb", s3)], writes=[("phbb", sl)])
            transpose_to(C, hbb[:, sl, :], [("phbb", sl)], hbT[:, sl, :, :], [("phbT", sl)], bank=6, evac="act")
            for hf in range(2):
                pk = ("ps", hf)
                pv = psf(C, hf)
                for k in range(8):
                    P.op("pe", lambda e: e.matmul(pv, lhsT=hbT[:, sl, k, :], rhs=wpg[:, k, hf * 512:(hf + 1) * 512],
                                                  start=(k == 0), stop=(k == 7)),
                         reads=[("phbT", sl), kpg[k // 4]], writes=[pk])
            for hf in range(2):
                pk = ("ps", 2 + 2 * sl + hf)
                pv = psf(C, 2 + 2 * sl + hf)
                for k in range(2):
                    P.op("pe", lambda e: e.matmul(pv, lhsT=pT[:, s3, k, :], rhs=wpp[:, k, hf * 512:(hf + 1) * 512],
                                                  start=(k == 0), stop=(k == 1)),
                         reads=[("ppT", s3), kpp[0]], writes=[pk])

        def front_b(lt):
            sl = lt % 2
            P.op("act", lambda e: e.activation(out=sg[:, sl, :], in_=C.ps[0][:, :], func=AF.Sigmoid),
                 reads=[("ps", 0), ("ps", 1)], writes=[("psg", sl)])

        def back(lt):
            sl = lt % 2
            ev = C.ps[1 + sl][:, :]
            pks = [("ps", 2 + 2 * sl), ("ps", 3 + 2 * sl)]
            rs, rk = rstd_from(C, ev, pks, D, 1e-6)
            P.op("dve", lambda e: e.scalar_tensor_tensor(out=ee[:, sl, :], in0=ev, scalar=rs, in1=g[:, 0, :], op0=ALU.mult, op1=ALU.mult),
                 reads=pks + [rk, ("gp" + tag, 0)], writes=[("pee", sl)])
            P.op("pool", lambda e: e.tensor_tensor(out=ee[:, sl, :], in0=ee[:, sl, :], in1=sg[:, sl, :], op=ALU.mult),
                 reads=[("pee", sl), ("psg", sl)], writes=[("pee", sl)])
            P.op("dve", lambda e: e.tensor_tensor(out=ee[:, sl, :], in0=ee[:, sl, :], in1=hb[:, lt % 3, :], op=ALU.add),
                 reads=[("pee", sl), ("phb", lt % 3)], writes=[("pee", sl)])
            P.dma("sp", OUT[lt * 128:(lt + 1) * 128, :], ee[:, sl, :], reads=[("pee", sl)], writes=[("OUT", lt)])

        def hnpart(lt):
            sl = lt % 2
            emit_norm_bf16(C, ee[:, sl, :], [("pee", sl)], g[:, 1, :], ("gp" + tag, 1), hnb[:, sl, :], [("phnb", sl)])
            transpose_to(C, hnb[:, sl, :], [("phnb", sl)], hnT[:, sl, :, :], [("phnT", sl)], bank=7, evac="dve")
            P.dma("sp", HNT.rearrange("(k f) t -> f k t", f=128)[:, :, lt * 128:(lt + 1) * 128], hnT[:, sl, :, :],
                  reads=[("phnT", sl)], writes=[("HNT", lt)])

        loads(0)
        front(0)
        front_b(0)
        for lt in range(ntiles):
            if lt + 1 < ntiles:
                front(lt + 1)
            back(lt)
            if hn_row is not None and lt >= 1:
                hnpart(lt - 1)
            if lt + 1 < ntiles:
                front_b(lt + 1)
        if hn_row is not None:
            hnpart(ntiles - 1)


def emit_attention(C, T, H, stage=9, full=False):
    nc, P = C.nc, C.P
    with scope(P) as es:
        g = load_gains(C, es, T["vec"], [0, 1], "ga")
        hnTf = es.enter_context(nc.sbuf_tensor("hnTf", [128, 8, S], BF16))
        NT = 32 if full else 16
        NCH = NT // 2
        KPC = 2 if full else 4
        NM = 2 if full else 4
        if full:
            hnTo = hnTf
        else:
            hnTo = es.enter_context(nc.sbuf_tensor("hnTo", [128, 8, NOWN], BF16))
        obuf = es.enter_context(nc.sbuf_tensor("obuf", [128, NT, D], BF16))
        masks = es.enter_context(nc.sbuf_tensor("masks_sb", [128, NM, 256], BF16))
        alibi = es.enter_context(nc.sbuf_tensor("alibi_sb", [128, 4, 32], F32))
        fb = es.enter_context(nc.sbuf_tensor("fb", [128, 2, 2, 32], F32))
        Stm = es.enter_context(nc.sbuf_tensor("Stm", [128, 32, 8], F32))
        Xb = es.enter_context(nc.sbuf_tensor("Xb", [128, NCH * 8], F32))
        neglam = es.enter_context(nc.sbuf_tensor("neglam", [128, 4], F32))
        subg = es.enter_context(nc.sbuf_tensor("subg", [128, 128], F32))
        P.dma("pool", masks[:], T["masks"].rearrange("m p q -> p m q"), writes=["masks"])
        P.dma("sp", alibi[:], T["alibi"].rearrange("p (h r) -> p h r", h=4), writes=["alibi"])
        P.dma("sp", subg[:], T["vec"][6, 0:128].partition_broadcast(128), writes=["subg"])
        P.op("dve", lambda e: e.tensor_scalar(out=subg[:], in0=subg[:], scalar1=1.0 - LAM_INIT0, scalar2=None, op0=ALU.mult),
             reads=["subg"], writes=["subg"])

        with scope(P) as es2:
            lv = es2.enter_context(nc.sbuf_tensor("lv", [1, 256], F32))
            pr = es2.enter_context(nc.sbuf_tensor("pr", [1, 128], F32))
            dots = es2.enter_context(nc.sbuf_tensor("dots", [1, 2], F32))
            P.dma("sp", lv[:], T["lamv"], writes=["lv"])
            P.op("dve", lambda e: e.tensor_tensor(out=pr[:, 0:64], in0=lv[:, 0:64], in1=lv[:, 64:128], op=ALU.mult),
                 reads=["lv"], writes=["pr"])
            P.op("dve", lambda e: e.tensor_tensor(out=pr[:, 64:128], in0=lv[:, 128:192], in1=lv[:, 192:256], op=ALU.mult),
                 reads=["lv", "pr"], writes=["pr"])
            P.op("dve", lambda e: e.tensor_reduce(out=dots[:, 0:2], in_=pr[:, :].rearrange("p (a d) -> p a d", a=2),
                                                  axis=AX.X, op=ALU.add), reads=["pr"], writes=["dots"])
            pv = psf(C, 0)[:, 0:2]
            P.op("pe", lambda e: e.matmul(pv, lhsT=C.onesf[0:1, 0:128], rhs=dots[0:1, 0:2], start=True, stop=True),
                 reads=["onesf", "dots"], writes=[("ps", 0)])
            P.op("act", lambda e: e.activation(out=neglam[:, 0:2], in_=pv, func=AF.Exp), reads=[("ps", 0)], writes=["neglam"])
            P.op("dve", lambda e: e.tensor_tensor(out=neglam[:, 2:3], in0=neglam[:, 1:2], in1=neglam[:, 0:1], op=ALU.subtract),
                 reads=["neglam"], writes=["neglam"])
            P.op("dve", lambda e: e.tensor_scalar(out=neglam[:, 2:3], in0=neglam[:, 2:3], scalar1=-LAM_INIT0, scalar2=None, op0=ALU.add),
                 reads=["neglam"], writes=["neglam"])

        with scope(P) as es2:
            xt = es2.enter_context(nc.sbuf_tensor("xt", [128, 2, D], F32))
            xb = es2.enter_context(nc.sbuf_tensor("xb", [128, 2, D], BF16))
            def a1_info(i):
                if i < 32:
                    src = T["xf"][i * 128:(i + 1) * 128, :]
                    dst = hnTf[:, :, i * 128:(i + 1) * 128]
                    dk = ("hnTf", i // 4)
                else:
                    j = i - 32
                    src = T["xo"][j * 128:(j + 1) * 128, :]
                    dst = hnTo[:, :, j * 128:(j + 1) * 128]
                    dk = ("hnTo", j // 4)
                if full:
                    dk = ("hnTf", i // 4)
                return src, dst, dk

            def a1_front(i):
                sl = i % 2
                src, dst, dk = a1_info(i)
                P.dma("sp", xt[:, sl, :], src, writes=[("xt", sl)])
                emit_norm_bf16(C, xt[:, sl, :], [("xt", sl)], g[:, 0, :], ("ga", 0), xb[:, sl, :], [("xb", sl)])

            def a1_back(i):
                sl = i % 2
                src, dst, dk = a1_info(i)
                transpose_to(C, xb[:, sl, :], [("xb", sl)], dst, [dk], bank=6 + sl, evac=("act" if sl else "dve"))

            n_a1 = 32 if full else 48
            a1_front(0)
            for i in range(n_a1):
                if i + 1 < n_a1:
                    a1_front(i + 1)
                a1_back(i)

        if stage < 1:
            return
        with scope(P) as es2:
            wfz = es2.enter_context(nc.sbuf_tensor("wfz", [128, 8, 8], BF16))
            negb = es2.enter_context(nc.sbuf_tensor("negb", [8, 1], F32))
            Lf = es2.enter_context(nc.sbuf_tensor("Lf", [8, S], F32))
            Sc = es2.enter_context(nc.sbuf_tensor("Sc", [8, S], F32))
            Dm = es2.enter_context(nc.sbuf_tensor("Dm", [8, NCH, 8], F32))
            P.dma("pool", wfz[:], T["w_in"].rearrange("(k p) n -> p k n", p=128)[:, :, 3072:3080], writes=["wfz"])
            P.dma("sp", negb[:], T["bf"], writes=["negb"])
            P.op("dve", lambda e: e.tensor_scalar(out=negb[:], in0=negb[:], scalar1=-1.0, scalar2=None, op0=ALU.mult),
                 reads=["negb"], writes=["negb"])
            for n in range(8):
                bank = n % 2
                pv = psf(C, bank)[0:8, :]
                for k in range(8):
                    P.op("pe", lambda e, k=k, n=n, pv=pv: e.matmul(pv, lhsT=wfz[:, k, :], rhs=hnTf[:, k, n * 512:(n + 1) * 512],
                                                                   start=(k == 0), stop=(k == 7)),
                         reads=["wfz", ("hnTf", n)], writes=[("ps", bank)])
                P.op("act", lambda e, n=n, pv=pv: e.activation(out=Lf[:, n * 512:(n + 1) * 512], in_=pv, func=AF.Exp, scale=-1.0, bias=negb[:, 0:1]),
                     reads=[("ps", bank), "negb"], writes=[("Lf", n)])
            for n in range(8):
                P.op("act", lambda e, n=n: e.activation(out=Lf[:, n * 512:(n + 1) * 512], in_=Lf[:, n * 512:(n + 1) * 512], func=AF.Ln, bias=1.0),
                     reads=[("Lf", n)], writes=[("Lf", n)])
            for n in range(8):
                init = 0.0 if n == 0 else Sc[:, n * 512 - 1:n * 512]
                rd = [("Lf", n), "onesf"] + ([("Sc", n - 1)] if n else [])
                P.op("dve", lambda e, n=n, init=init: e.tensor_tensor_scan(out=Sc[:, n * 512:(n + 1) * 512], data0=C.onesf[0:8, :],
                                                                           data1=Lf[:, n * 512:(n + 1) * 512], initial=init,
                                                                           op0=ALU.mult, op1=ALU.add),
                     reads=rd, writes=[("Sc", n)])
            pvt = psf(C, 2)[:, 0:256]
            for kb in range(32):
                P.op("pe", lambda e, kb=kb: e.transpose(out=pvt[:, kb * 8:(kb + 1) * 8], in_=Sc[0:8, kb * 128:(kb + 1) * 128],
                                                        identity=C.identf[0:8, 0:8]),
                     reads=[("Sc", kb // 4), "identf"], writes=[("ps", 2)])
            P.op("dve", lambda e: e.tensor_copy(out=Stm[:, :, :], in_=pvt.rearrange("p (k h) -> p k h", h=8)),
                 reads=[("ps", 2)], writes=["Stm"])
            CW = 128 * KPC
            ssel = Sc[:, :].rearrange("h (p t) -> h p t", t=CW)[:, :, CW - 1:CW]
            P.op("dve", lambda e: e.tensor_tensor(out=Dm[:, :, :], in0=ssel.to_broadcast([8, NCH, 8]),
                                                  in1=C.identf[0:8, 0:8].unsqueeze(1).to_broadcast([8, NCH, 8]), op=ALU.mult),
                 reads=[("Sc", n) for n in range(8)] + ["identf"], writes=["Dm"])
            pvx = psf(C, 3)[:, 0:NCH * 8]
            P.op("pe", lambda e: e.matmul(pvx, lhsT=C.onesf[0:8, 0:128], rhs=Dm[:, :, :].rearrange("h p g -> h (p g)"), start=True, stop=True),
                 reads=["onesf", "Dm"], writes=[("ps", 3)])
            P.op("dve", lambda e: e.tensor_copy(out=Xb[:, :], in_=pvx), reads=[("ps", 3)], writes=["Xb"])
        if stage < 2:
            return
        with scope(P) as es2:
            wq = es2.enter_context(nc.sbuf_tensor("wq", [128, 2, 8, 128], BF16))
            wk = es2.enter_context(nc.sbuf_tensor("wk", [128, 2, 8, 128], BF16))
            wv = es2.enter_context(nc.sbuf_tensor("wv", [128, 2, 8, 128], BF16))
            KT = es2.enter_context(nc.sbuf_tensor("KT", [128, S], BF16))
            QT = es2.enter_context(nc.sbuf_tensor("QTz", [128, 2, NT * 128], BF16))
            Vb = es2.enter_context(nc.sbuf_tensor("Vb", [128, 32, 130], BF16))
            Vd = Vb[:, :, 0:129]
            Vf = Vb[:, :, :].rearrange("p k (h d) -> p k h d", h=2)
            PT = es2.enter_context(nc.sbuf_tensor("PT", [128, 4, 2, 256], BF16))
            t1 = es2.enter_context(nc.sbuf_tensor("t1", [128, 2, 128], F32))
            dd = es2.enter_context(nc.sbuf_tensor("dd", [128, 2, 128], F32))
            rr = es2.enter_context(nc.sbuf_tensor("rr", [128, 2, 8], F32))
            P.op("pool", lambda e: e.memset(QT[:, :, :], 0.0), writes=[("QT", n) for n in range(NT // 4)])
            w_in_v = T["w_in"].rearrange("(k p) n -> p k n", p=128)
            rr_i = [0]
            for gidx in range(8):
                ws = gidx % 2
                diff = gidx < 4
                if gidx in (0, 4):
                    vk = [("V", kq) for kq in range(8)]
                    P.op("pool", lambda e: e.memset(Vb[:, :, :], 1.0), writes=vk)
                if diff:
                    qc, kc, vc = gidx * 128, 512 + gidx * 128, 1024 + gidx * 128
                else:
                    qc, kc, vc = 1536 + (gidx - 4) * 128, 2048 + (gidx - 4) * 128, 2560 + (gidx - 4) * 128
                P.dma("pool", wq[:, ws, :, :], w_in_v[:, :, qc:qc + 128], writes=[("wq", ws)])
                P.dma("pool", wk[:, ws, :, :], w_in_v[:, :, kc:kc + 128], writes=[("wk", ws)])
                P.dma("pool", wv[:, ws, :, :], w_in_v[:, :, vc:vc + 128], writes=[("wv", ws)])
                for n in range(8):
                    bank = 2 * (n % 2)
                    pv = psf(C, bank)
                    for k in range(8):
                        P.op("pe", lambda e, k=k, n=n, pv=pv: e.matmul(pv, lhsT=wk[:, ws, k, :], rhs=hnTf[:, k, n * 512:(n + 1) * 512],
                                                                       start=(k == 0), stop=(k == 7)),
                             reads=[("wk", ws), ("hnTf", n)], writes=[("ps", bank)])
                    P.op("dve", lambda e, n=n, pv=pv: e.tensor_copy(out=KT[:, n * 512:(n + 1) * 512], in_=pv),
                         reads=[("ps", bank)], writes=[("KT", n)])
                for n in range(NT // 4):
                    bank = 2 * (n % 2)
                    pv = psf(C, bank)
                    for k in range(8):
                        P.op("pe", lambda e, k=k, n=n, pv=pv: e.matmul(pv, lhsT=wq[:, ws, k, :], rhs=hnTo[:, k, n * 512:(n + 1) * 512],
                                                                       start=(k == 0), stop=(k == 7)),
                             reads=[("wq", ws), (("hnTf" if full else "hnTo"), n)], writes=[("ps", bank)])
                    P.op("dve", lambda e: e.tensor_copy(out=QT[0:64, 0, n * 512:(n + 1) * 512], in_=pv[0:64, :]),
                         reads=[("ps", bank)], writes=[("QT", n)])
                    P.op("dve", lambda e: e.tensor_copy(out=QT[64:128, 1, n * 512:(n + 1) * 512], in_=pv[64:128, :]),
                         reads=[("ps", bank), ("QT", n)], writes=[("QT", n)])
                for kq in range(8):
                    bank = 2 * (kq % 2)
                    pv = psf(C, bank)
                    for j in range(4):
                        kb = kq * 4 + j
                        for k in range(8):
                            P.op("pe", lambda e, k=k, kb=kb, j=j, pv=pv: e.matmul(pv[:, j * 128:(j + 1) * 128], lhsT=hnTf[:, k, kb * 128:(kb + 1) * 128],
                                                                                  rhs=wv[:, ws, k, :], start=(k == 0), stop=(k == 7)),
                                 reads=[("wv", ws), ("hnTf", kb // 4)], writes=[("ps", bank)])
                    if diff:
                        dst = Vd[:, kq * 4:(kq + 1) * 4, 0:128]
                        srcv = pv.rearrange("p (j d) -> p j d", j=4)
                        vkey = ("V", kq)
                    else:
                        dst = Vf[:, kq * 4:(kq + 1) * 4, :, 0:64]
                        srcv = pv.rearrange("p (j h d) -> p j h d", j=4, h=2)
                        vkey = ("V", kq)
                    P.op("dve", lambda e, dst=dst, srcv=srcv: e.tensor_copy(out=dst, in_=srcv), reads=[("ps", bank)], writes=[vkey])

                for p in range(NCH):
                    nkb = KPC * p + KPC
                    kb0 = KPC * p
                    oset = p % 2
                    if not diff:
                        fsl = p % 2
                        for sh in range(2):
                            hh_ = 2 * (gidx - 4) + sh
                            P.op("dve", lambda e: e.tensor_scalar(out=fb[:, fsl, sh, 0:nkb], in0=Stm[:, 0:nkb, hh_],
                                                                  scalar1=Xb[:, p * 8 + hh_:p * 8 + hh_ + 1], scalar2=None, op0=ALU.subtract),
                                 reads=["Stm", "Xb"], writes=[("fb", fsl)])
                    def emit_S(kb, p=p):
                        slot = kb % 4
                        for sh in range(2):
                            pvS = psf(C, slot)[:, sh * 256:(sh + 1) * 256]
                            P.op("pe", lambda e: e.matmul(pvS, lhsT=KT[:, kb * 128:(kb + 1) * 128],
                                                          rhs=QT[:, sh, p * 256:(p + 1) * 256], start=True, stop=True),
                                 reads=[("KT", kb // 4), ("QT", p // 2)], writes=[("ps", slot)])

                    def emit_PV(kb, p=p, nkb=nkb, oset=oset):
                        slot = kb % 4
                        if diff:
                            ai = kb - kb0 + (30 if full else 28)
                            P.op("act", lambda e: e.activation(out=PT[:, slot, :, :], in_=psf(C, slot).rearrange("p (s q) -> p s q", s=2),
                                                               func=AF.Exp, scale=0.125, bias=alibi[:, gidx, ai:ai + 1]),
                                 reads=[("ps", slot), "alibi"], writes=[("PT", slot, 0), ("PT", slot, 1)])
                        else:
                            for sh in range(2):
                                bias = fb[:, p % 2, sh, kb:kb + 1]
                                P.op("act", lambda e: e.activation(out=PT[:, slot, sh, :], in_=psf(C, slot)[:, sh * 256:(sh + 1) * 256],
                                                                   func=AF.Exp, scale=0.125, bias=bias),
                                     reads=[("ps", slot), ("fb", p % 2)], writes=[("PT", slot, sh)])
                        if kb >= kb0:
                            mk = masks[:, kb - kb0, :].unsqueeze(1).to_broadcast([128, 2, 256])
                            P.op("pool", lambda e: e.tensor_tensor(out=PT[:, slot, :, :], in0=PT[:, slot, :, :], in1=mk, op=ALU.mult),
                                 reads=[("PT", slot, 0), ("PT", slot, 1), "masks"], writes=[("PT", slot, 0), ("PT", slot, 1)])
                        for sh in range(2):
                            obank = 4 + oset * 2 + sh
                            for e_ in range(2):
                                if diff:
                                    ov = psf(C, obank)[:, e_ * 129:(e_ + 1) * 129]
                                    rhs = Vd[:, kb, :]
                                else:
                                    ov = psf(C, obank)[:, e_ * 65:(e_ + 1) * 65]
                                    rhs = Vf[:, kb, sh, :]
                                first = (kb == 0 and e_ == 0)
                                P.op("pe", lambda e: e.matmul(ov, lhsT=PT[:, slot, sh, e_ * 128:(e_ + 1) * 128], rhs=rhs,
                                                              start=first, stop=(kb == nkb - 1), skip_group_check=True),
                                     reads=[("PT", slot, sh), ("V", kb // 4)], writes=[("ps", obank)])

                    LOOK = 2
                    for kb in range(min(LOOK, nkb)):
                        emit_S(kb)
                    for kb in range(nkb):
                        if kb + LOOK < nkb:
                            emit_S(kb + LOOK)
                        emit_PV(kb)

                    b0 = 4 + oset * 2
                    i0 = rr_i[0] % 2
                    rr_i[0] += 1
                    rk = ("rr", i0)
                    if diff:
                        O1 = psf(C, b0)[:, 0:258].rearrange("p (e d) -> p e d", e=2)
                        O2 = psf(C, b0 + 1)[:, 0:258].rearrange("p (e d) -> p e d", e=2)
                        P.op("dve", lambda e: e.reciprocal(out=rr[:, i0, 0:2], in_=O1[:, :, 128]), reads=[("ps", b0)], writes=[rk])
                        P.op("dve", lambda e: e.reciprocal(out=rr[:, i0, 2:4], in_=O2[:, :, 128]), reads=[("ps", b0 + 1)], writes=[rk])
                        P.op("dve", lambda e: e.tensor_scalar(out=rr[:, i0, 2:4], in0=rr[:, i0, 2:4], scalar1=neglam[:, 2:3], scalar2=None, op0=ALU.mult),
                             reads=[rk, "neglam"], writes=[rk])
                        for e_ in range(2):
                            lt = 2 * p + e_
                            P.op("dve", lambda e, e_=e_: e.tensor_scalar(out=t1[:, e_, :], in0=O1[:, e_, 0:128], scalar1=rr[:, i0, e_:e_ + 1],
                                                                         scalar2=None, op0=ALU.mult),
                                 reads=[("ps", b0), rk], writes=[("t1", e_)])
                            P.op("dve", lambda e, e_=e_: e.scalar_tensor_tensor(out=dd[:, e_, :], in0=O2[:, e_, 0:128], scalar=rr[:, i0, 2 + e_:3 + e_],
                                                                                in1=t1[:, e_, :], op0=ALU.mult, op1=ALU.add),
                                 reads=[("ps", b0 + 1), rk, ("t1", e_)], writes=[("dd", e_)])
                            rs, rsk = rstd_from(C, dd[:, e_, :], [("dd", e_)], 128, 1e-5, eng="dve")
                            P.op("dve", lambda e, e_=e_, lt=lt, rs=rs: e.scalar_tensor_tensor(out=obuf[:, lt, gidx * 128:(gidx + 1) * 128], in0=dd[:, e_, :],
                                                                                             scalar=rs, in1=subg[:, :], op0=ALU.mult, op1=ALU.mult),
                                 reads=[("dd", e_), rsk, "subg"], writes=[("obuf", lt)])
                    else:
                        for sh in range(2):
                            Ov = psf(C, b0 + sh)[:, 0:130].rearrange("p (e d) -> p e d", e=2)
                            P.op("dve", lambda e, sh=sh, Ov=Ov: e.reciprocal(out=rr[:, i0, 4 + 2 * sh:6 + 2 * sh], in_=Ov[:, :, 64]),
                                 reads=[("ps", b0 + sh)], writes=[rk])
                            for e_ in range(2):
                                lt = 2 * p + e_
                                c0 = 512 + (2 * (gidx - 4) + sh) * 64
                                P.op("dve", lambda e, sh=sh, e_=e_, lt=lt, c0=c0, Ov=Ov: e.tensor_scalar(out=obuf[:, lt, c0:c0 + 64], in0=Ov[:, e_, 0:64],
                                                                                                       scalar1=rr[:, i0, 4 + 2 * sh + e_:5 + 2 * sh + e_],
                                                                                                       scalar2=None, op0=ALU.mult),
                                     reads=[("ps", b0 + sh), rk], writes=[("obuf", lt)])

        if stage < 3:
            return
        with scope(P) as es2:
            wo = es2.enter_context(nc.sbuf_tensor("wo", [128, 8, D], BF16))
            kwo = load_w_bf16(C, wo, T["w_out"], 8, "wo", nsplit=2)
            oT = es2.enter_context(nc.sbuf_tensor("oT", [128, 2, 8, 128], BF16))
            xr = es2.enter_context(nc.sbuf_tensor("xres", [128, 2, D], F32))
            hh = es2.enter_context(nc.sbuf_tensor("hh", [128, 2, D], F32))
            xsrc = T["xf" if full else "xo"]

            def a4_front(lt):
                sl = lt % 2
                P.dma("sp", xr[:, sl, :], xsrc[lt * 128:(lt + 1) * 128, :], writes=[("xres", sl)])
                transpose_to(C, obuf[:, lt, :], [("obuf", lt)], oT[:, sl, :, :], [("oT", sl)], bank=6 + sl, evac="act")
                for hf in range(2):
                    pk = ("ps", 2 * sl + hf)
                    pv = psf(C, 2 * sl + hf)
                    for k in range(8):
                        P.op("pe", lambda e: e.matmul(pv, lhsT=oT[:, sl, k, :], rhs=wo[:, k, hf * 512:(hf + 1) * 512],
                                                      start=(k == 0), stop=(k == 7)),
                             reads=[("oT", sl), kwo[k // 4]], writes=[pk])

            def a4_back(lt):
                sl = lt % 2
                emit_norm_residual(C, C.ps[sl][:, :], [("ps", 2 * sl), ("ps", 2 * sl + 1)], g[:, 1, :], ("ga", 1),
                                   xr[:, sl, :], [("xres", sl)], hh[:, sl, :], [("hh", sl)])
                P.dma("sp", H[lt * 128:(lt + 1) * 128, :], hh[:, sl, :], reads=[("hh", sl)], writes=[("H", lt)])

            a4_front(0)
            for lt in range(NT):
                if lt + 1 < NT:
                    a4_front(lt + 1)
                a4_back(lt)


def build_phase_A(upto=99, stage=9):
    nc = bass.Bass("TRN2", target_bir_lowering=False)
    T = {}

    def din(name, shape, dt=F32):
        T[name] = nc.dram_tensor(name, shape, dt, kind="ExternalInput").ap()

    din("xf", [S, D]); din("xo", [NOWN, D]); din("pT", [256, NOWN])
    din("w_in", [D, 3080]); din("w_out", [D, D]); din("w_up", [D, DFF]); din("w_dn", [DFF, D])
    din("w_pp", [256, D]); din("w_pg", [D, D]); din("vec", [8, D]); din("lamv", [1, 256]); din("bf", [8, 1])
    din("ident", [128, 128]); din("masks", [4, 128, 256]); din("alibi", [128, 128])
    H = nc.dram_tensor("H", [NOWN, D], F32, kind="ExternalOutput").ap()
    H1 = nc.dram_tensor("H1", [NOWN, D], F32, kind="ExternalOutput").ap()
    HNT = nc.dram_tensor("HNT", [D, NOWN], BF16, kind="ExternalOutput").ap()
    with contextlib.ExitStack() as es:
        P = Prog(nc, es)
        C = Ctx()
        setup_common(nc, P, es, C, T["ident"])
        setup_eps(C, es)
        emit_attention(C, T, H, stage)
        if upto >= 2:
            emit_mlp(C, H, T["w_up"], T["w_dn"], T["vec"], 2, 3, "0")
        if upto >= 3:
            emit_ple(C, H, T["pT"], T["w_pp"], T["w_pg"], T["vec"], 4, H1, "0", hn_row=5, HNT=HNT)
        P.finish()
    return nc


def emit_rec(C, T, G, ncb=5):
    import itertools
    nc, P = C.nc, C.P
    TH = 1024
    NTH = S // TH
    assert ncb % 2 == 0 or ncb == 5
    with scope(P) as es:
        hnT = es.enter_context(nc.sbuf_tensor("r_hnT", [128, 8, S], BF16))
        wg = es.enter_context(nc.sbuf_tensor("r_wg", [128, 8, ncb * 128], BF16))
        wxr = es.enter_context(nc.sbuf_tensor("r_wxr", [128, 8, ncb * 128], BF16))
        wxs = es.enter_context(nc.sbuf_tensor("r_wxs", [128, ncb, 128], BF16))
        was = es.enter_context(nc.sbuf_tensor("r_was", [128, ncb, 128], BF16))
        rv = es.enter_context(nc.sbuf_tensor("r_rv", [128, ncb, 8], F32))
        sc = es.enter_context(nc.sbuf_tensor("r_sc", [128, ncb], F32))
        hlast = es.enter_context(nc.sbuf_tensor("r_hlast", [128, 2], F32))
        ybf = es.enter_context(nc.sbuf_tensor("r_y", [128, 2, 2, TH], BF16))
        xr = es.enter_context(nc.sbuf_tensor("r_xr", [128, 2, 2, TH + 3], F32))
        xc = es.enter_context(nc.sbuf_tensor("r_xc", [128, 2, TH], F32))
        xcb = es.enter_context(nc.sbuf_tensor("r_xcb", [128, 2, TH], BF16))
        gx = es.enter_context(nc.sbuf_tensor("r_gx", [128, 2, TH], F32))
        ga = es.enter_context(nc.sbuf_tensor("r_ga", [128, 2, TH], F32))
        tt = es.enter_context(nc.sbuf_tensor("r_tt", [128, 2, TH], F32))
        hs = es.enter_context(nc.sbuf_tensor("r_hs", [128, 2, TH], F32))
        gout = es.enter_context(nc.sbuf_tensor("r_gout", [128, 2, TH], BF16))
        hv = T["hnT"].rearrange("(k f) t -> f k t", f=128)
        for k in range(8):
            P.dma("sp", hnT[:, k, :], hv[:, k, :], writes=[("r_hnT", k)])
        kg = load_w_bf16(C, wg, T["w_g"], 8, "r_wg", nsplit=2)
        kx = load_w_bf16(C, wxr, T["w_x"], 8, "r_wxr", nsplit=2)
        P.dma("pool", wxs[:], T["wx"].rearrange("n i j -> i n j"), writes=["r_wxs"])
        P.dma("pool", was[:], T["wa"].rearrange("n i j -> i n j"), writes=["r_was"])
        P.dma("sp", rv[:], T["rvec"], writes=["r_rv"])
        P.op("act", lambda e: e.activation(out=sc[:, :], in_=rv[:, :, 7], func=AF.Exp, scale=-1.0), reads=["r_rv"], writes=["r_sc"])
        P.op("act", lambda e: e.activation(out=sc[:, :], in_=sc[:, :], func=AF.Ln, bias=1.0), reads=["r_sc"], writes=["r_sc"])
        P.op("dve", lambda e: e.tensor_scalar(out=sc[:, :], in0=sc[:, :], scalar1=-8.0, scalar2=None, op0=ALU.mult), reads=["r_sc"], writes=["r_sc"])

        def stage1(cb, th, st, bs):
            if th == 0:
                P.op("dve", lambda e: e.memset(xr[:, st, bs, 0:3], 0.0), writes=[("r_xr", st, bs)])
            else:
                P.op("dve", lambda e: e.tensor_copy(out=xr[:, st, bs, 0:3], in_=xr[:, st, 1 - bs, TH:TH + 3]),
                     reads=[("r_xr", st, 1 - bs)], writes=[("r_xr", st, bs)])
            yield
            for n in range(TH // 512):
                N = (TH // 512) * th + n
                bg = 2 * st
                pv = psf(C, bg)
                for k in range(8):
                    P.op("pe", lambda e: e.matmul(pv, lhsT=wg[:, k, cb * 128:(cb + 1) * 128], rhs=hnT[:, k, N * 512:(N + 1) * 512],
                                                  start=(k == 0), stop=(k == 7)),
                         reads=[kg[k // 4], ("r_hnT", k)], writes=[("ps", bg)])
                P.op("act", lambda e: e.activation(out=ybf[:, st, bs, n * 512:(n + 1) * 512], in_=pv, func=AF.Gelu_apprx_tanh),
                     reads=[("ps", bg)], writes=[("r_y", st, bs)])
                yield
                bx_ = 2 * st + 1
                pv2 = psf(C, bx_)
                for k in range(8):
                    P.op("pe", lambda e: e.matmul(pv2, lhsT=wxr[:, k, cb * 128:(cb + 1) * 128], rhs=hnT[:, k, N * 512:(N + 1) * 512],
                                                  start=(k == 0), stop=(k == 7)),
                         reads=[kx[k // 4], ("r_hnT", k)], writes=[("ps", bx_)])
                P.op("dve", lambda e: e.tensor_copy(out=xr[:, st, bs, 3 + n * 512:3 + (n + 1) * 512], in_=pv2),
                     reads=[("ps", bx_)], writes=[("r_xr", st, bs)])
                yield

        def stage2(cb, th, st, bs):
            xk = ("r_xr", st, bs)
            K = lambda name: (name, st)
            P.op("act", lambda e: e.activation(out=xc[:, st, :], in_=xr[:, st, bs, 3:3 + TH], func=AF.Identity, scale=rv[:, cb, 3:4], bias=rv[:, cb, 4:5]),
                 reads=[xk, "r_rv"], writes=[K("r_xc")])
            yield
            for w in range(3):
                P.op("dve", lambda e: e.scalar_tensor_tensor(out=xc[:, st, :], in0=xr[:, st, bs, w:w + TH], scalar=rv[:, cb, w:w + 1], in1=xc[:, st, :],
                                                             op0=ALU.mult, op1=ALU.add),
                     reads=[xk, "r_rv", K("r_xc")], writes=[K("r_xc")])
                yield
            P.op("act", lambda e: e.copy(out=xcb[:, st, :], in_=xc[:, st, :]), reads=[K("r_xc")], writes=[K("r_xcb")])
            yield
            for n in range(TH // 512):
                b1 = 4 + 2 * st
                pv = psf(C, b1)
                P.op("pe", lambda e: e.matmul(pv, lhsT=wxs[:, cb, :], rhs=xcb[:, st, n * 512:(n + 1) * 512], start=True, stop=True),
                     reads=["r_wxs", K("r_xcb")], writes=[("ps", b1)])
                P.op("act", lambda e: e.activation(out=gx[:, st, n * 512:(n + 1) * 512], in_=pv, func=AF.Sigmoid, bias=rv[:, cb, 5:6]),
                     reads=[("ps", b1), "r_rv"], writes=[K("r_gx")])
                yield
                b2 = 5 + 2 * st
                pv2 = psf(C, b2)
                P.op("pe", lambda e: e.matmul(pv2, lhsT=was[:, cb, :], rhs=xcb[:, st, n * 512:(n + 1) * 512], start=True, stop=True),
                     reads=["r_was", K("r_xcb")], writes=[("ps", b2)])
                P.op("act", lambda e: e.activation(out=ga[:, st, n * 512:(n + 1) * 512], in_=pv2, func=AF.Sigmoid, bias=rv[:, cb, 6:7]),
                     reads=[("ps", b2), "r_rv"], writes=[K("r_ga")])
                yield
            P.op("act", lambda e: e.activation(out=ga[:, st, :], in_=ga[:, st, :], func=AF.Exp, scale=sc[:, cb:cb + 1]),
                 reads=[K("r_ga"), "r_sc"], writes=[K("r_ga")])
            yield
            P.op("dve", lambda e: e.tensor_tensor(out=tt[:, st, :], in0=ga[:, st, :], in1=ga[:, st, :], op=ALU.mult), reads=[K("r_ga")], writes=[K("r_tt")])
            yield
            P.op("pool", lambda e: e.tensor_tensor(out=gx[:, st, :], in0=gx[:, st, :], in1=xc[:, st, :], op=ALU.mult),
                 reads=[K("r_gx"), K("r_xc")], writes=[K("r_gx")])
            yield
            P.op("act", lambda e: e.activation(out=tt[:, st, :], in_=tt[:, st, :], func=AF.Sqrt, scale=-1.0, bias=1.0), reads=[K("r_tt")], writes=[K("r_tt")])
            yield
            P.op("dve", lambda e: e.tensor_tensor(out=tt[:, st, :], in0=tt[:, st, :], in1=gx[:, st, :], op=ALU.mult),
                 reads=[K("r_tt"), K("r_gx")], writes=[K("r_tt")])
            if th == 0:
                P.op("dve", lambda e: e.tensor_copy(out=tt[:, st, 0:1], in_=gx[:, st, 0:1]), reads=[K("r_gx"), K("r_tt")], writes=[K("r_tt")])
            yield
            init = 0.0 if th == 0 else hlast[:, st:st + 1]
            P.op("dve", lambda e: e.tensor_tensor_scan(out=hs[:, st, :], data0=ga[:, st, :], data1=tt[:, st, :], initial=init, op0=ALU.mult, op1=ALU.add),
                 reads=[K("r_ga"), K("r_tt"), K("r_hlast")], writes=[K("r_hs")])
            P.op("dve", lambda e: e.tensor_copy(out=hlast[:, st:st + 1], in_=hs[:, st, TH - 1:TH]), reads=[K("r_hs")], writes=[K("r_hlast")])
            yield
            P.op("pool", lambda e: e.tensor_tensor(out=gout[:, st, :], in0=hs[:, st, :], in1=ybf[:, st, bs, :], op=ALU.mult),
                 reads=[K("r_hs"), ("r_y", st, bs)], writes=[K("r_gout")])
            P.dma("sp", G[cb * 128:(cb + 1) * 128, th * TH:(th + 1) * TH], gout[:, st, :], reads=[K("r_gout")], writes=[("G", cb, th)])
            yield

        def interleave(gens):
            for _ in itertools.zip_longest(*gens):
                pass

        supers = []
        for j in range((ncb + 1) // 2):
            cbs = [c for c in (2 * j, 2 * j + 1) if c < ncb]
            for th in range(NTH):
                supers.append((cbs, th))
        interleave([stage1(cb, supers[0][1], st, 0) for st, cb in enumerate(supers[0][0])])
        for i, (cbs, th) in enumerate(supers):
            bs = i % 2
            gens = [stage2(cb, th, st, bs) for st, cb in enumerate(cbs)]
            if i + 1 < len(supers):
                ncbs, nth = supers[i + 1]
                gens += [stage1(cb, nth, st, 1 - bs) for st, cb in enumerate(ncbs)]
            interleave(gens)


def build_phase_B():
    nc = bass.Bass("TRN2", target_bir_lowering=False)
    T = {}

    def din(name, shape, dt=F32):
        T[name] = nc.dram_tensor(name, shape, dt, kind="ExternalInput").ap()

    din("hnT", [D, S], BF16); din("w_g", [D, 640]); din("w_x", [D, 640]); din("wx", [5, 128, 128]); din("wa", [5, 128, 128])
    din("rvec", [128, 5, 8]); din("ident", [128, 128])
    G = nc.dram_tensor("G", [640, S], BF16, kind="ExternalOutput").ap()
    with contextlib.ExitStack() as es:
        P = Prog(nc, es)
        C = Ctx()
        setup_common(nc, P, es, C, T["ident"])
        setup_eps(C, es)
        emit_rec(C, T, G)
        P.finish()
    return nc


def emit_recout(C, T, H, row=0, blend=None):
    nc, P = C.nc, C.P
    with scope(P) as es:
        g = load_gains(C, es, T["vec"], [row], "gro")
        gT = es.enter_context(nc.sbuf_tensor("gTs", [128, 10, NOWN], BF16))
        wro = es.enter_context(nc.sbuf_tensor("wro", [128, 10, D], BF16))
        if blend is None:
            gv = T["gT"].rearrange("(c p) t -> p c t", p=128)
            for c in range(10):
                P.dma("sp", gT[:, c, :], gv[:, c, :], writes=[("gTs", c)])
        else:
            Gd, H1d, sel_d = blend
            sel = es.enter_context(nc.sbuf_tensor("sel_sb", [128, 2], F32))
            P.dma("sp", sel[:], sel_d, writes=["sel"])
            gT2 = es.enter_context(nc.sbuf_tensor("gTs2", [128, 2, NOWN], BF16))
            hres2 = es.enter_context(nc.sbuf_tensor("hres2", [128, 2, D], F32))
            gv = Gd.rearrange("(c p) t -> p c t", p=128)
            for c in range(10):
                s2 = c % 2
                P.dma("sp", gT[:, c, :], gv[:, c, 0:NOWN], writes=[("gTs", c)])
                P.dma("sp", gT2[:, s2, :], gv[:, c, NOWN:2 * NOWN], writes=[("gTs2", s2)])
                P.op("act", lambda e: e.activation(out=gT[:, c, :], in_=gT[:, c, :], func=AF.Copy, scale=sel[:, 0:1]),
                     reads=[("gTs", c), "sel"], writes=[("gTs", c)])
                P.op("dve", lambda e: e.scalar_tensor_tensor(out=gT[:, c, :], in0=gT2[:, s2, :], scalar=sel[:, 1:2], in1=gT[:, c, :],
                                                             op0=ALU.mult, op1=ALU.add),
                     reads=[("gTs", c), ("gTs2", s2), "sel"], writes=[("gTs", c)])
        kw = load_w_bf16(C, wro, T["w_ro"], 10, "wro", nsplit=2)
        hres = es.enter_context(nc.sbuf_tensor("hres", [128, 2, D], F32))
        hh = es.enter_context(nc.sbuf_tensor("hh1", [128, 2, D], F32))
        def ro_front(lt):
            sl = lt % 2
            if blend is None:
                P.dma("sp", hres[:, sl, :], T["h1"][lt * 128:(lt + 1) * 128, :], writes=[("hres", sl)])
            else:
                P.dma("sp", hres[:, sl, :], H1d[lt * 128:(lt + 1) * 128, :], writes=[("hres", sl)])
                P.dma("sp", hres2[:, sl, :], H1d[NOWN + lt * 128:NOWN + (lt + 1) * 128, :], writes=[("hres2", sl)])
                P.op("act", lambda e: e.activation(out=hres[:, sl, :], in_=hres[:, sl, :], func=AF.Copy, scale=sel[:, 0:1]),
                     reads=[("hres", sl), "sel"], writes=[("hres", sl)])
                P.op("dve", lambda e: e.scalar_tensor_tensor(out=hres[:, sl, :], in0=hres2[:, sl, :], scalar=sel[:, 1:2], in1=hres[:, sl, :],
                                                             op0=ALU.mult, op1=ALU.add),
                     reads=[("hres", sl), ("hres2", sl), "sel"], writes=[("hres", sl)])
            for hf in range(2):
                pk = ("ps", 2 * sl + hf)
                pv = psf(C, 2 * sl + hf)
                for c in range(10):
                    P.op("pe", lambda e: e.matmul(pv, lhsT=gT[:, c, lt * 128:(lt + 1) * 128], rhs=wro[:, c, hf * 512:(hf + 1) * 512],
                                                  start=(c == 0), stop=(c == 9)),
                         reads=[("gTs", c), kw[c // 5]], writes=[pk])

        def ro_back(lt):
            sl = lt % 2
            emit_norm_residual(C, C.ps[sl][:, :], [("ps", 2 * sl), ("ps", 2 * sl + 1)], g[:, 0, :], ("gro", 0),
                               hres[:, sl, :], [("hres", sl)], hh[:, sl, :], [("hh1", sl)])
            P.dma("sp", H[lt * 128:(lt + 1) * 128, :], hh[:, sl, :], reads=[("hh1", sl)], writes=[("H", lt)])

        ro_front(0)
        for lt in range(16):
            if lt + 1 < 16:
                ro_front(lt + 1)
            ro_back(lt)


def build_phase_C():
    nc = bass.Bass("TRN2", target_bir_lowering=False)
    T = {}

    def din(name, shape, dt=F32):
        T[name] = nc.dram_tensor(name, shape, dt, kind="ExternalInput").ap()

    din("gT", [RW, NOWN], BF16); din("h1", [NOWN, D]); din("pT", [256, NOWN]); din("w_ro", [RW, D])
    din("w_up", [D, DFF]); din("w_dn", [DFF, D]); din("w_pp", [256, D]); din("w_pg", [D, D]); din("vec", [8, D]); din("ident", [128, 128])
    H = nc.dram_tensor("H", [NOWN, D], F32, kind="ExternalOutput").ap()
    OUT = nc.dram_tensor("OUT", [NOWN, D], F32, kind="ExternalOutput").ap()
    with contextlib.ExitStack() as es:
        P = Prog(nc, es)
        C = Ctx()
        setup_common(nc, P, es, C, T["ident"])
        setup_eps(C, es)
        emit_recout(C, T, H)
        emit_mlp(C, H, T["w_up"], T["w_dn"], T["vec"], 1, 2, "1")
        emit_ple(C, H, T["pT"], T["w_pp"], T["w_pg"], T["vec"], 3, OUT, "1")
        P.finish()
    return nc


def build_fused():
    nc = bass.Bass("TRN2", target_bir_lowering=False)
    T = {}

    def din(name, shape, dt=F32):
        T[name] = nc.dram_tensor(name, shape, dt, kind="ExternalInput").ap()

    din("xf", [S, D]); din("pT", [256, S]); din("pT1", [256, NOWN])
    din("w_in", [D, 3080]); din("w_out", [D, D]); din("w_up", [D, DFF]); din("w_dn", [DFF, D])
    din("w_pp", [256, D]); din("w_pg", [D, D]); din("vec", [16, D]); din("lamv", [1, 256]); din("bf", [8, 1])
    din("ident", [128, 128]); din("masks", [2, 128, 256]); din("alibi", [128, 128]); din("sel", [128, 2])
    din("w_g", [D, RW]); din("w_x", [D, RW]); din("wx", [10, 128, 128]); din("wa", [10, 128, 128]); din("rvec", [128, 10, 8])
    din("w_ro", [RW, D]); din("w_up1", [D, DFF]); din("w_dn1", [DFF, D]); din("w_pp1", [256, D]); din("w_pg1", [D, D])
    HA = nc.dram_tensor("HA", [S, D], F32, kind="Internal").ap()
    H1 = nc.dram_tensor("H1", [S, D], F32, kind="Internal").ap()
    HNT = nc.dram_tensor("HNT", [D, S], BF16, kind="Internal").ap()
    G = nc.dram_tensor("G", [RW, S], BF16, kind="Internal").ap()
    HC = nc.dram_tensor("HC", [NOWN, D], F32, kind="Internal").ap()
    OUT = nc.dram_tensor("OUT", [NOWN, D], F32, kind="ExternalOutput").ap()
    with contextlib.ExitStack() as es:
        P = Prog(nc, es)
        C = Ctx()
        setup_common(nc, P, es, C, T["ident"])
        setup_eps(C, es)
        emit_attention(C, T, HA, full=True)
        emit_mlp(C, HA, T["w_up"], T["w_dn"], T["vec"], 2, 3, "0", ntok=S)
        emit_ple(C, HA, T["pT"], T["w_pp"], T["w_pg"], T["vec"], 4, H1, "0", hn_row=5, HNT=HNT, ntiles=32)
        T1 = {"hnT": HNT, "w_g": T["w_g"], "w_x": T["w_x"], "wx": T["wx"], "wa": T["wa"], "rvec": T["rvec"]}
        emit_rec(C, T1, G, ncb=10)
        T2 = {"vec": T["vec"], "w_ro": T["w_ro"]}
        emit_recout(C, T2, HC, row=8, blend=(G, H1, T["sel"]))
        emit_mlp(C, HC, T["w_up1"], T["w_dn1"], T["vec"], 9, 10, "1", ntok=NOWN)
        emit_ple(C, HC, T["pT1"], T["w_pp1"], T["w_pg1"], T["vec"], 11, OUT, "1", ntiles=16)
        P.finish()
    return nc


def own_tiles(r):
    return [4 * p + 2 * r + e for p in range(8) for e in range(2)]


def own_index(r):
    return np.concatenate([np.arange(g * 128, (g + 1) * 128) for g in own_tiles(r)])


def role_consts(r):
    ident = np.eye(128, dtype=np.float32)
    masks = np.zeros((4, 128, 256), np.float32)
    jj = np.arange(128)[:, None]
    ii = np.arange(128)[None, :]
    tri = (jj <= ii).astype(np.float32)
    for m in range(4):
        for e in range(2):
            qt = 2 * r + e
            if m < qt:
                masks[m, :, e * 128:(e + 1) * 128] = 1.0
            elif m == qt:
                masks[m, :, e * 128:(e + 1) * 128] = tri
    alibi = np.zeros((128, 4, 32), np.float32)
    for h in range(4):
        for idx in range(32):
            d = (idx - 28 - 2 * r - 2) * 128 + np.arange(128)
            alibi[:, h, idx] = np.minimum(SLOPES[h] * d, 0.0)
    return ident, masks, alibi.reshape(128, 128)


def f32c(a):
    return np.ascontiguousarray(a, dtype=np.float32)


def prep_A(inp, c):
    b, r = c // 2, c % 2
    oi = own_index(r)
    ident, masks, alibi = role_consts(r)
    vec = np.zeros((8, D), np.float32)
    vec[0] = inp["ln_mix_pre"][0]
    vec[1] = inp["ln_mix_post"][0]
    vec[2] = inp["ln_mlp_pre"][0]
    vec[3] = inp["ln_mlp_post"][0]
    vec[4] = inp["ple_norm"][0]
    vec[5] = inp["ln_mix_pre"][1]
    vec[6, :128] = inp["diff_subln"][0]
    lamv = np.concatenate([inp["diff_lambda_q1"][0], inp["diff_lambda_k1"][0],
                           inp["diff_lambda_q2"][0], inp["diff_lambda_k2"][0]])[None, :]
    return {
        "xf": f32c(inp["x"][b]), "xo": f32c(inp["x"][b][oi]), "pT": f32c(inp["p"][0, b][oi].T),
        "w_in": f32c(inp["attn_w_in"][0]), "w_out": f32c(inp["attn_w_out"][0]),
        "w_up": f32c(inp["mlp_w_up"][0]), "w_dn": f32c(inp["mlp_w_down"][0]),
        "w_pp": f32c(inp["ple_w_proj"][0]), "w_pg": f32c(inp["ple_w_gate"][0]),
        "vec": vec, "lamv": f32c(lamv), "bf": f32c(inp["attn_b_forget"][0][:, None]),
        "ident": ident, "masks": masks, "alibi": alibi,
    }


def prep_B(inp, c, hnT_full):
    r = c % 2
    cols_g = np.arange(5 * r * 128, (5 * r + 5) * 128)
    cols_x = RW + cols_g
    w_in = inp["rec_w_in"][0]
    rvec = np.zeros((128, 5, 8), np.float32)
    for j in range(5):
        ch = np.arange((5 * r + j) * 128, (5 * r + j + 1) * 128)
        rvec[:, j, 0:4] = inp["rec_conv_w"][0][:, ch].T
        rvec[:, j, 4] = inp["rec_conv_b"][0][ch]
        rvec[:, j, 5] = inp["rec_bx"][0][ch]
        rvec[:, j, 6] = inp["rec_ba"][0][ch]
        rvec[:, j, 7] = inp["rec_a_param"][0][ch]
    return {
        "hnT": hnT_full, "w_g": f32c(w_in[:, cols_g]), "w_x": f32c(w_in[:, cols_x]),
        "wx": f32c(inp["rec_wx"][0][5 * r:5 * r + 5]), "wa": f32c(inp["rec_wa"][0][5 * r:5 * r + 5]),
        "rvec": rvec, "ident": np.eye(128, dtype=np.float32),
    }


def prep_C(inp, c, gT_own, h1_own):
    b, r = c // 2, c % 2
    oi = own_index(r)
    vec = np.zeros((8, D), np.float32)
    vec[0] = inp["ln_mix_post"][1]
    vec[1] = inp["ln_mlp_pre"][1]
    vec[2] = inp["ln_mlp_post"][1]
    vec[3] = inp["ple_norm"][1]
    return {
        "gT": gT_own, "h1": h1_own, "pT": f32c(inp["p"][1, b][oi].T), "w_ro": f32c(inp["rec_w_out"][0]),
        "w_up": f32c(inp["mlp_w_up"][1]), "w_dn": f32c(inp["mlp_w_down"][1]),
        "w_pp": f32c(inp["ple_w_proj"][1]), "w_pg": f32c(inp["ple_w_gate"][1]),
        "vec": vec, "ident": np.eye(128, dtype=np.float32),
    }


def full_consts():
    ident = np.eye(128, dtype=np.float32)
    jj = np.arange(128)[:, None]
    ii = np.arange(128)[None, :]
    tri = (jj <= ii).astype(np.float32)
    masks = np.zeros((2, 128, 256), np.float32)
    masks[0, :, 0:128] = tri
    masks[0, :, 128:256] = 1.0
    masks[1, :, 128:256] = tri
    alibi = np.zeros((128, 4, 32), np.float32)
    for h in range(4):
        for idx in range(32):
            d = (idx - 30 - 2) * 128 + np.arange(128)
            alibi[:, h, idx] = np.minimum(SLOPES[h] * d, 0.0)
    return ident, masks, alibi.reshape(128, 128)


def prep_fused(inp, c):
    b, r = c // 2, c % 2
    ident, masks, alibi = full_consts()
    vec = np.zeros((16, D), np.float32)
    vec[0] = inp["ln_mix_pre"][0]
    vec[1] = inp["ln_mix_post"][0]
    vec[2] = inp["ln_mlp_pre"][0]
    vec[3] = inp["ln_mlp_post"][0]
    vec[4] = inp["ple_norm"][0]
    vec[5] = inp["ln_mix_pre"][1]
    vec[6, :128] = inp["diff_subln"][0]
    vec[8] = inp["ln_mix_post"][1]
    vec[9] = inp["ln_mlp_pre"][1]
    vec[10] = inp["ln_mlp_post"][1]
    vec[11] = inp["ple_norm"][1]
    lamv = np.concatenate([inp["diff_lambda_q1"][0], inp["diff_lambda_k1"][0],
                           inp["diff_lambda_q2"][0], inp["diff_lambda_k2"][0]])[None, :]
    rvec = np.zeros((128, 10, 8), np.float32)
    for j in range(10):
        ch = np.arange(j * 128, (j + 1) * 128)
        rvec[:, j, 0:4] = inp["rec_conv_w"][0][:, ch].T
        rvec[:, j, 4] = inp["rec_conv_b"][0][ch]
        rvec[:, j, 5] = inp["rec_bx"][0][ch]
        rvec[:, j, 6] = inp["rec_ba"][0][ch]
        rvec[:, j, 7] = inp["rec_a_param"][0][ch]
    sel = np.zeros((128, 2), np.float32)
    sel[:, r] = 1.0
    w_in1 = inp["rec_w_in"][0]
    return {
        "xf": f32c(inp["x"][b]), "pT": f32c(inp["p"][0, b].T), "pT1": f32c(inp["p"][1, b][r * NOWN:(r + 1) * NOWN].T),
        "w_in": f32c(inp["attn_w_in"][0]), "w_out": f32c(inp["attn_w_out"][0]),
        "w_up": f32c(inp["mlp_w_up"][0]), "w_dn": f32c(inp["mlp_w_down"][0]),
        "w_pp": f32c(inp["ple_w_proj"][0]), "w_pg": f32c(inp["ple_w_gate"][0]),
        "vec": vec, "lamv": f32c(lamv), "bf": f32c(inp["attn_b_forget"][0][:, None]),
        "ident": ident, "masks": masks, "alibi": alibi, "sel": sel,
        "w_g": f32c(w_in1[:, :RW]), "w_x": f32c(w_in1[:, RW:]), "wx": f32c(inp["rec_wx"][0]), "wa": f32c(inp["rec_wa"][0]),
        "rvec": rvec, "w_ro": f32c(inp["rec_w_out"][0]),
        "w_up1": f32c(inp["mlp_w_up"][1]), "w_dn1": f32c(inp["mlp_w_down"][1]),
        "w_pp1": f32c(inp["ple_w_proj"][1]), "w_pg1": f32c(inp["ple_w_gate"][1]),
    }


_NC_CACHE = {}


def _get(name, fn):
    if name not in _NC_CACHE:
        _NC_CACHE[name] = fn()
    return _NC_CACHE[name]


def kernel_unfused(**inputs):
    inp = {k: np.asarray(v) for k, v in inputs.items()}
    cores = list(range(NCORES))
    ncA = _get("A", build_phase_A)
    resA = run_bass_kernel_spmd(ncA, [prep_A(inp, c) for c in cores], core_ids=cores).results
    hnT_full = []
    for b in range(4):
        full = np.zeros((D, S), dtype=resA[0]["HNT"].dtype)
        for r in range(2):
            full[:, own_index(r)] = resA[2 * b + r]["HNT"]
        hnT_full.append(full)
    ncB = _get("B", build_phase_B)
    resB = run_bass_kernel_spmd(ncB, [prep_B(inp, c, hnT_full[c // 2]) for c in cores], core_ids=cores).results
    ncC = _get("C", build_phase_C)
    mapsC = []
    for c in cores:
        b, r = c // 2, c % 2
        oi = own_index(r)
        gfull = np.concatenate([resB[2 * b]["G"], resB[2 * b + 1]["G"]], axis=0)
        mapsC.append(prep_C(inp, c, np.ascontiguousarray(gfull[:, oi]), resA[c]["H1"]))
    resC = run_bass_kernel_spmd(ncC, mapsC, core_ids=cores).results
    out = np.zeros((4, S, D), np.float32)
    for c in cores:
        b, r = c // 2, c % 2
        out[b, own_index(r)] = resC[c]["OUT"]
    return out


def kernel(**inputs):
    inp = {k: np.asarray(v) for k, v in inputs.items()}
    cores = list(range(NCORES))
    nc = _get("F", build_fused)
    res = run_bass_kernel_spmd(nc, [prep_fused(inp, c) for c in cores], core_ids=cores).results
    out = np.zeros((4, S, D), np.float32)
    for c in cores:
        b, r = c // 2, c % 2
        out[b, r * NOWN:(r + 1) * NOWN] = res[c]["OUT"]
    return out
```

```python
import contextlib
import numpy as np
import concourse.bass as bass
import concourse.mybir as mybir
from concourse.bass_utils import run_bass_kernel_spmd

F32 = mybir.dt.float32
BF16 = mybir.dt.bfloat16
AF = mybir.ActivationFunctionType
ALU = mybir.AluOpType
AX = mybir.AxisListType

NCORES = 8
D = 1024
S = 4096
NOWN = 2048
DFF = 4096
RW = 1280
LAM_INIT0 = 0.8 - 0.6 * 1.0
SLOPES = [2.0 ** (-8.0 * (h + 1) / 4) for h in range(4)]


class Prog:
    K_DMA = 8

    def __init__(self, nc, es):
        self.nc = nc
        self.engs = {"pe": nc.tensor, "act": nc.scalar, "dve": nc.vector,
                     "pool": nc.gpsimd, "sp": nc.sync}
        self.semobj = {}
        for k in self.engs:
            self.semobj[k] = es.enter_context(nc.semaphore("sem_" + k))
        self.cnt = {k: 0 for k in self.engs}
        self.seen = {k: {} for k in self.engs}
        self.lastw = {}
        self.readers = {}
        self.dcnt = {}
        for q in ("sp", "act", "pool"):
            self.dcnt[q] = 0
            for i in range(self.K_DMA):
                self.semobj[(q, i)] = es.enter_context(nc.semaphore("d_%s_%d" % (q, i)))

    def _wait(self, eng, ev):
        sk, val = ev
        if self.seen[eng].get(sk, 0) >= val:
            return
        self.engs[eng].wait_ge(self.semobj[sk], val)
        self.seen[eng][sk] = val

    def _deps(self, eng, reads, writes):
        deps = {}
        for k in reads:
            w = self.lastw.get(k)
            if w is not None:
                deps[w[0]] = max(deps.get(w[0], 0), w[1])
        for k in writes:
            w = self.lastw.get(k)
            if w is not None:
                deps[w[0]] = max(deps.get(w[0], 0), w[1])
            for sk, v in self.readers.get(k, {}).items():
                deps[sk] = max(deps.get(sk, 0), v)
        for sk, v in deps.items():
            if eng == "pe" and sk == "pe":
                continue
            self._wait(eng, (sk, v))

    def _record(self, ev, reads, writes):
        for k in writes:
            self.lastw[k] = ev
            self.readers[k] = {}
        for k in reads:
            if k in writes:
                continue
            d = self.readers.setdefault(k, {})
            d[ev[0]] = max(d.get(ev[0], 0), ev[1])

    def op(self, eng, fn, reads=(), writes=()):
        self._deps(eng, reads, writes)
        inst = fn(self.engs[eng])
        self.cnt[eng] += 1
        inst.then_inc(self.semobj[eng], 1)
        self._record((eng, self.cnt[eng]), reads, writes)

    def dma(self, q, out, in_, reads=(), writes=()):
        self._deps(q, reads, writes)
        i = self.dcnt[q]
        s = i % self.K_DMA
        rnd = i // self.K_DMA
        if rnd > 0:
            self._wait(q, ((q, s), 16 * rnd))
        inst = self.engs[q].dma_start(out=out, in_=in_)
        inst.then_inc(self.semobj[(q, s)], 16)
        self.dcnt[q] = i + 1
        self._record(((q, s), 16 * (rnd + 1)), reads, writes)

    def barrier(self):
        evs = []
        for q in ("sp", "act", "pool"):
            n = self.dcnt[q]
            for s in range(self.K_DMA):
                m = (n - s + self.K_DMA - 1) // self.K_DMA if n > s else 0
                if m > 0:
                    evs.append(((q, s), 16 * m))
        for k in ("pe", "act", "dve", "pool", "sp"):
            if self.cnt[k] > 0:
                evs.append((k, self.cnt[k]))
        for eng in ("pe", "act", "dve", "pool", "sp"):
            for ev in evs:
                if ev[0] == eng:
                    continue
                self._wait(eng, ev)

    def barrier_keys(self, eng, keys):
        self._deps(eng, [], list(keys))

    def finish(self):
        for q in ("sp", "act", "pool"):
            n = self.dcnt[q]
            for s in range(self.K_DMA):
                m = (n - s + self.K_DMA - 1) // self.K_DMA if n > s else 0
                if m > 0:
                    self._wait("sp", ((q, s), 16 * m))
        for k in ("pe", "act", "dve", "pool"):
            if self.cnt[k] > 0:
                self._wait("sp", (k, self.cnt[k]))


class Ctx:
    pass


@contextlib.contextmanager
def scope(P):
    with contextlib.ExitStack() as es:
        yield es
        P.barrier()


def setup_common(nc, P, es, C, ident_d):
    C.nc = nc
    C.P = P
    C.identf = es.enter_context(nc.sbuf_tensor("identf", [128, 128], F32))
    C.identb = es.enter_context(nc.sbuf_tensor("identb", [128, 128], BF16))
    C.onesf = es.enter_context(nc.sbuf_tensor("onesf", [128, 512], F32))
    P.dma("sp", C.identf[:], ident_d, writes=["identf"])
    P.dma("pool", C.identb[:], ident_d, writes=["identb"])
    P.op("pool", lambda e: e.memset(C.onesf[:], 1.0), writes=["onesf"])
    C.ps = [es.enter_context(nc.psum_tensor("ps%d" % i, [128, 1024], F32)) for i in range(4)]
    C.psb = [t.bitcast(BF16) for t in C.ps]
    C.junk = es.enter_context(nc.sbuf_tensor("junk", [128, 1024], BF16))
    C.junk2 = es.enter_context(nc.sbuf_tensor("junk2", [128, 128], BF16))
    C.small = es.enter_context(nc.sbuf_tensor("small", [128, 64], F32))
    C.small_i = 0


def small_slot(C, n=1):
    i = C.small_i
    if i + n > 64:
        i = 0
    C.small_i = i + n
    return i


def psf(C, b):
    return C.ps[b // 2][:, (b % 2) * 512:(b % 2) * 512 + 512]


def psbf(C, b):
    return C.psb[b // 2][:, (b % 2) * 1024:(b % 2) * 1024 + 1024]


def rstd_from(C, src, src_keys, n, eps, eng="act"):
    P = C.P
    i = small_slot(C)
    ss = C.small[:, i:i + 1]
    key = ("small", i)
    if eng == "dve":
        P.op("dve", lambda e: e.scalar_tensor_tensor(out=C.junk2[:, 0:n], in0=src, scalar=1.0, in1=src,
                                                     op0=ALU.mult, op1=ALU.mult, accum_out=ss),
             reads=list(src_keys), writes=["junk2", key])
    else:
        P.op("act", lambda e: e.activation(out=C.junk[:, 0:n], in_=src, func=AF.Square, accum_out=ss),
             reads=list(src_keys), writes=["junk", key])
    P.op("dve", lambda e: e.tensor_scalar(out=ss, in0=ss, scalar1=1.0 / n, scalar2=float(eps), op0=ALU.mult, op1=ALU.add),
         reads=[key], writes=[key])
    P.op("pool", lambda e: e.tensor_tensor(out=ss, in0=ss, in1=C.negh, op=ALU.pow), reads=[key, "epsb"], writes=[key])
    return ss, key


def transpose_to(C, src_bf, src_keys, dst, dst_keys, bank, nk=8, evac="dve"):
    P = C.P
    pk = ("ps", bank)
    pv = psbf(C, bank)
    for k in range(nk):
        P.op("pe", lambda e, k=k: e.transpose(out=pv[:, k * 128:(k + 1) * 128], in_=src_bf[:, k * 128:(k + 1) * 128],
                                              identity=C.identb[:]),
             reads=list(src_keys) + ["identb"], writes=[pk])
    srcv = pv[:, 0:nk * 128].rearrange("p (k t) -> p k t", k=nk)
    if evac == "act":
        P.op("act", lambda e: e.copy(out=dst, in_=srcv), reads=[pk], writes=list(dst_keys))
    else:
        P.op("dve", lambda e: e.tensor_copy(out=dst, in_=srcv), reads=[pk], writes=list(dst_keys))


def load_w_bf16(C, dst, dram2d, nk, key, nsplit=1):
    P = C.P
    src = dram2d.rearrange("(k p) n -> p k n", p=128)
    step = nk // nsplit
    for i in range(nsplit):
        P.dma("pool", dst[:, i * step:(i + 1) * step, :], src[:, i * step:(i + 1) * step, :], writes=[(key, i)])
    return [(key, i) for i in range(nsplit)]


def emit_norm_residual(C, m_src, m_keys, gain, gain_key, res, res_keys, out, out_keys):
    P = C.P
    rs, rk = rstd_from(C, m_src, m_keys, D, 1e-6)
    P.op("dve", lambda e: e.scalar_tensor_tensor(out=out, in0=m_src, scalar=rs, in1=gain, op0=ALU.mult, op1=ALU.mult),
         reads=list(m_keys) + [rk, gain_key], writes=list(out_keys))
    P.op("pool", lambda e: e.tensor_tensor(out=out, in0=out, in1=res, op=ALU.add),
         reads=list(out_keys) + list(res_keys), writes=list(out_keys))


def emit_norm_bf16(C, src, src_keys, gain, gain_key, out_bf, out_keys):
    P = C.P
    rs, rk = rstd_from(C, src, src_keys, D, 1e-6)
    P.op("dve", lambda e: e.scalar_tensor_tensor(out=out_bf, in0=src, scalar=rs, in1=gain, op0=ALU.mult, op1=ALU.mult),
         reads=list(src_keys) + [rk, gain_key], writes=list(out_keys))


def load_gains(C, es, vec_d, rows, name):
    nc, P = C.nc, C.P
    g = es.enter_context(nc.sbuf_tensor(name, [128, len(rows), D], F32))
    for i, r in enumerate(rows):
        P.dma("sp", g[:, i, :], vec_d[r, :].partition_broadcast(128), writes=[(name, i)])
    return g


def setup_eps(C, es):
    nc, P = C.nc, C.P
    t = es.enter_context(nc.sbuf_tensor("epsb", [128, 3], F32))
    P.op("pool", lambda e: e.memset(t[:, 2:3], -0.5), writes=["epsb"])
    C.negh = t[:, 2:3]
    P.op("pool", lambda e: e.memset(t[:, 0:1], 1e-6), writes=["epsb"])
    P.op("pool", lambda e: e.memset(t[:, 1:2], 1e-5), writes=["epsb"])
    C.epsb = {1e-6: t[:, 0:1], 1e-5: t[:, 1:2]}


def emit_mlp(C, H, w_up_d, w_dn_d, vec_d, row_pre, row_post, tag, ntok=NOWN):
    nc, P = C.nc, C.P
    MT = 256
    with scope(P) as es:
        wup = es.enter_context(nc.sbuf_tensor("wup" + tag, [128, 8, DFF], BF16))
        wdn = es.enter_context(nc.sbuf_tensor("wdn" + tag, [128, 32, D], BF16))
        wup_src = w_up_d.rearrange("(k p) n -> p k n", p=128)
        kup = []
        for i in range(8):
            P.dma("pool", wup[:, :, i * 512:(i + 1) * 512], wup_src[:, :, i * 512:(i + 1) * 512], writes=[("wup" + tag, i)])
            kup.append(("wup" + tag, i))
        kdn = load_w_bf16(C, wdn, w_dn_d, 32, "wdn" + tag, nsplit=8)
        g = load_gains(C, es, vec_d, [row_pre, row_post], "gm" + tag)
        ha = es.enter_context(nc.sbuf_tensor("ha" + tag, [128, 2, 2, D], F32))
        ubf = es.enter_context(nc.sbuf_tensor("ubf" + tag, [128, 2, D], BF16))
        uT = es.enter_context(nc.sbuf_tensor("uT" + tag, [128, 2, 8, MT], BF16))
        hid = es.enter_context(nc.sbuf_tensor("hid" + tag, [128, 32, MT], BF16))
        rl = es.enter_context(nc.sbuf_tensor("rl" + tag, [128, 2, MT], F32))
        hb = es.enter_context(nc.sbuf_tensor("hb" + tag, [128, 2, D], F32))
        nmt = ntok // MT

        def pro_norm(t):
            sl = t % 2
            for e in range(2):
                lt = 2 * t + e
                P.dma("sp", ha[:, sl, e, :], H[lt * 128:(lt + 1) * 128, :], reads=[("H", lt)], writes=[("ha", sl, e)])
                emit_norm_bf16(C, ha[:, sl, e, :], [("ha", sl, e)], g[:, 0, :], ("gm" + tag, 0), ubf[:, e, :], [("ubf", e)])

        def pro_T(t):
            sl = t % 2
            for e in range(2):
                transpose_to(C, ubf[:, e, :], [("ubf", e)], uT[:, sl, :, e * 128:(e + 1) * 128], [("uT", sl, e)],
                             bank=e, evac="act")

        def up(t):
            sl = t % 2
            for c in range(32):
                bank = c % 4
                pk = ("ps", bank)
                pv = psf(C, bank)[:, 0:MT]
                for k in range(8):
                    P.op("pe", lambda e: e.matmul(pv, lhsT=wup[:, k, c * 128:(c + 1) * 128], rhs=uT[:, sl, k, :],
                                                  start=(k == 0), stop=(k == 7)),
                         reads=[kup[c // 4], ("uT", sl, 0), ("uT", sl, 1)], writes=[pk])
                rs_ = c % 2
                P.op("act", lambda e: e.activation(out=rl[:, rs_, :], in_=pv, func=AF.Relu),
                     reads=[pk], writes=[("rl", rs_)])
                P.op("pool", lambda e: e.tensor_tensor(out=hid[:, c, :], in0=rl[:, rs_, :], in1=rl[:, rs_, :], op=ALU.mult),
                     reads=[("rl", rs_)], writes=[("hid", c)])

        def down(t):
            sl = t % 2
            for e in range(2):
                lt = 2 * t + e
                for hf in range(2):
                    pk = ("ps", 4 + 2 * e + hf)
                    pv = psf(C, 4 + 2 * e + hf)
                    for c in range(32):
                        P.op("pe", lambda e_: e_.matmul(pv, lhsT=hid[:, c, e * 128:(e + 1) * 128],
                                                        rhs=wdn[:, c, hf * 512:(hf + 1) * 512],
                                                        start=(c == 0), stop=(c == 31)),
                             reads=[("hid", c), kdn[c // 4]], writes=[pk])
                fv = C.ps[2 + e][:, :]
                emit_norm_residual(C, fv, [("ps", 4 + 2 * e), ("ps", 5 + 2 * e)], g[:, 1, :], ("gm" + tag, 1),
                                   ha[:, sl, e, :], [("ha", sl, e)], hb[:, e, :], [("hb", e)])
                P.dma("sp", H[lt * 128:(lt + 1) * 128, :], hb[:, e, :], reads=[("hb", e)], writes=[("H", lt)])

        pro_norm(0)
        pro_T(0)
        for t in range(nmt):
            if t + 1 < nmt:
                pro_norm(t + 1)
            up(t)
            if t + 1 < nmt:
                pro_T(t + 1)
            down(t)


def emit_ple(C, H, pT_d, w_pp_d, w_pg_d, vec_d, row_ple, OUT, tag, hn_row=None, HNT=None, ntiles=16):
    nc, P = C.nc, C.P
    with scope(P) as es:
        wpg = es.enter_context(nc.sbuf_tensor("wpg" + tag, [128, 8, D], BF16))
        wpp = es.enter_context(nc.sbuf_tensor("wpp" + tag, [128, 2, D], BF16))
        kpg = load_w_bf16(C, wpg, w_pg_d, 8, "wpg" + tag, nsplit=2)
        kpp = load_w_bf16(C, wpp, w_pp_d, 2, "wpp" + tag, nsplit=1)
        rows = [row_ple] + ([hn_row] if hn_row is not None else [])
        g = load_gains(C, es, vec_d, rows, "gp" + tag)
        hb = es.enter_context(nc.sbuf_tensor("phb" + tag, [128, 3, D], F32))
        hbb = es.enter_context(nc.sbuf_tensor("phbb" + tag, [128, 2, D], BF16))
        hbT = es.enter_context(nc.sbuf_tensor("phbT" + tag, [128, 2, 8, 128], BF16))
        pT = es.enter_context(nc.sbuf_tensor("ppT" + tag, [128, 3, 2, 128], BF16))
        sg = es.enter_context(nc.sbuf_tensor("psg" + tag, [128, 2, D], F32))
        ee = es.enter_context(nc.sbuf_tensor("pee" + tag, [128, 2, D], F32))
        hnb = es.enter_context(nc.sbuf_tensor("phnb" + tag, [128, 2, D], BF16))
        hnT = es.enter_context(nc.sbuf_tensor("phnT" + tag, [128, 2, 8, 128], BF16))
        pTv = pT_d.rearrange("(k f) t -> f k t", f=128)
        def loads(lt):
            s3 = lt % 3
            P.dma("sp", hb[:, s3, :], H[lt * 128:(lt + 1) * 128, :], reads=[("H", lt)], writes=[("phb", s3)])
            P.dma("pool", pT[:, s3, :, :], pTv[:, :, lt * 128:(lt + 1) * 128], writes=[("ppT", s3)])

        def front(lt):
            sl = lt % 2
            s3 = lt % 3
            if lt + 1 < ntiles:
                loads(lt + 1)
            P.op("dve", lambda e: e.tensor_copy(out=hbb[:, sl, :], in_=hb[:, s3, :]), reads=[("phb", s3)], writes=[("phbb", sl)])
            transpose_to(C, hbb[:, sl, :], [("phbb", sl)], hbT[:, sl, :, :], [("phbT", sl)], bank=6, evac="act")
            for hf in range(2):
                pk = ("ps", hf)
                pv = psf(C, hf)
                for k in range(8):
                    P.op("pe", lambda e: e.matmul(pv, lhsT=hbT[:, sl, k, :], rhs=wpg[:, k, hf * 512:(hf + 1) * 512],
                                                  start=(k == 0), stop=(k == 7)),
                         reads=[("phbT", sl), kpg[k // 4]], writes=[pk])
            for hf in range(2):
                pk = ("ps", 2 + 2 * sl + hf)
                pv = psf(C, 2 + 2 * sl + hf)
                for k in range(2):
                    P.op("pe", lambda e: e.matmul(pv, lhsT=pT[:, s3, k, :], rhs=wpp[:, k, hf * 512:(hf + 1) * 512],
                                                  start=(k == 0), stop=(k == 1)),
                         reads=[("ppT", s3), kpp[0]], writes=[pk])

        def front_b(lt):
            sl = lt % 2
            P.op("act", lambda e: e.activation(out=sg[:, sl, :], in_=C.ps[0][:, :], func=AF.Sigmoid),
                 reads=[("ps", 0), ("ps", 1)], writes=[("psg", sl)])

        def back(lt):
            sl = lt % 2
            ev = C.ps[1 + sl][:, :]
            pks = [("ps", 2 + 2 * sl), ("ps", 3 + 2 * sl)]
            rs, rk = rstd_from(C, ev, pks, D, 1e-6)
            P.op("dve", lambda e: e.scalar_tensor_tensor(out=ee[:, sl, :], in0=ev, scalar=rs, in1=g[:, 0, :], op0=ALU.mult, op1=ALU.mult),
                 reads=pks + [rk, ("gp" + tag, 0)], writes=[("pee", sl)])
            P.op("pool", lambda e: e.tensor_tensor(out=ee[:, sl, :], in0=ee[:, sl, :], in1=sg[:, sl, :], op=ALU.mult),
                 reads=[("pee", sl), ("psg", sl)], writes=[("pee", sl)])
            P.op("dve", lambda e: e.tensor_tensor(out=ee[:, sl, :], in0=ee[:, sl, :], in1=hb[:, lt % 3, :], op=ALU.add),
                 reads=[("pee", sl), ("phb", lt % 3)], writes=[("pee", sl)])
            P.dma("sp", OUT[lt * 128:(lt + 1) * 128, :], ee[:, sl, :], reads=[("pee", sl)], writes=[("OUT", lt)])

        def hnpart(lt):
            sl = lt % 2
            emit_norm_bf16(C, ee[:, sl, :], [("pee", sl)], g[:, 1, :], ("gp" + tag, 1), hnb[:, sl, :], [("phnb", sl)])
            transpose_to(C, hnb[:, sl, :], [("phnb", sl)], hnT[:, sl, :, :], [("phnT", sl)], bank=7, evac="dve")
            P.dma("sp", HNT.rearrange("(k f) t -> f k t", f=128)[:, :, lt * 128:(lt + 1) * 128], hnT[:, sl, :, :],
                  reads=[("phnT", sl)], writes=[("HNT", lt)])

        loads(0)
        front(0)
        front_b(0)
        for lt in range(ntiles):
            if lt + 1 < ntiles:
                front(lt + 1)
            back(lt)
            if hn_row is not None and lt >= 1:
                hnpart(lt - 1)
            if lt + 1 < ntiles:
                front_b(lt + 1)
        if hn_row is not None:
            hnpart(ntiles - 1)


def emit_attention(C, T, H, stage=9, full=False):
    nc, P = C.nc, C.P
    with scope(P) as es:
        g = load_gains(C, es, T["vec"], [0, 1], "ga")
        hnTf = es.enter_context(nc.sbuf_tensor("hnTf", [128, 8, S], BF16))
        NT = 32 if full else 16
        NCH = NT // 2
        KPC = 2 if full else 4
        NM = 2 if full else 4
        if full:
            hnTo = hnTf
        else:
            hnTo = es.enter_context(nc.sbuf_tensor("hnTo", [128, 8, NOWN], BF16))
        obuf = es.enter_context(nc.sbuf_tensor("obuf", [128, NT, D], BF16))
        masks = es.enter_context(nc.sbuf_tensor("masks_sb", [128, NM, 256], BF16))
        alibi = es.enter_context(nc.sbuf_tensor("alibi_sb", [128, 4, 32], F32))
        fb = es.enter_context(nc.sbuf_tensor("fb", [128, 2, 2, 32], F32))
        Stm = es.enter_context(nc.sbuf_tensor("Stm", [128, 32, 8], F32))
        Xb = es.enter_context(nc.sbuf_tensor("Xb", [128, NCH * 8], F32))
        neglam = es.enter_context(nc.sbuf_tensor("neglam", [128, 4], F32))
        subg = es.enter_context(nc.sbuf_tensor("subg", [128, 128], F32))
        P.dma("pool", masks[:], T["masks"].rearrange("m p q -> p m q"), writes=["masks"])
        P.dma("sp", alibi[:], T["alibi"].rearrange("p (h r) -> p h r", h=4), writes=["alibi"])
        P.dma("sp", subg[:], T["vec"][6, 0:128].partition_broadcast(128), writes=["subg"])
        P.op("dve", lambda e: e.tensor_scalar(out=subg[:], in0=subg[:], scalar1=1.0 - LAM_INIT0, scalar2=None, op0=ALU.mult),
             reads=["subg"], writes=["subg"])

        with scope(P) as es2:
            lv = es2.enter_context(nc.sbuf_tensor("lv", [1, 256], F32))
            pr = es2.enter_context(nc.sbuf_tensor("pr", [1, 128], F32))
            dots = es2.enter_context(nc.sbuf_tensor("dots", [1, 2], F32))
            P.dma("sp", lv[:], T["lamv"], writes=["lv"])
            P.op("dve", lambda e: e.tensor_tensor(out=pr[:, 0:64], in0=lv[:, 0:64], in1=lv[:, 64:128], op=ALU.mult),
                 reads=["lv"], writes=["pr"])
            P.op("dve", lambda e: e.tensor_tensor(out=pr[:, 64:128], in0=lv[:, 128:192], in1=lv[:, 192:256], op=ALU.mult),
                 reads=["lv", "pr"], writes=["pr"])
            P.op("dve", lambda e: e.tensor_reduce(out=dots[:, 0:2], in_=pr[:, :].rearrange("p (a d) -> p a d", a=2),
                                                  axis=AX.X, op=ALU.add), reads=["pr"], writes=["dots"])
            pv = psf(C, 0)[:, 0:2]
            P.op("pe", lambda e: e.matmul(pv, lhsT=C.onesf[0:1, 0:128], rhs=dots[0:1, 0:2], start=True, stop=True),
                 reads=["onesf", "dots"], writes=[("ps", 0)])
            P.op("act", lambda e: e.activation(out=neglam[:, 0:2], in_=pv, func=AF.Exp), reads=[("ps", 0)], writes=["neglam"])
            P.op("dve", lambda e: e.tensor_tensor(out=neglam[:, 2:3], in0=neglam[:, 1:2], in1=neglam[:, 0:1], op=ALU.subtract),
                 reads=["neglam"], writes=["neglam"])
            P.op("dve", lambda e: e.tensor_scalar(out=neglam[:, 2:3], in0=neglam[:, 2:3], scalar1=-LAM_INIT0, scalar2=None, op0=ALU.add),
                 reads=["neglam"], writes=["neglam"])

        with scope(P) as es2:
            xt = es2.enter_context(nc.sbuf_tensor("xt", [128, 2, D], F32))
            xb = es2.enter_context(nc.sbuf_tensor("xb", [128, 2, D], BF16))
            def a1_info(i):
                if i < 32:
                    src = T["xf"][i * 128:(i + 1) * 128, :]
                    dst = hnTf[:, :, i * 128:(i + 1) * 128]
                    dk = ("hnTf", i // 4)
                else:
                    j = i - 32
                    src = T["xo"][j * 128:(j + 1) * 128, :]
                    dst = hnTo[:, :, j * 128:(j + 1) * 128]
                    dk = ("hnTo", j // 4)
                if full:
                    dk = ("hnTf", i // 4)
                return src, dst, dk

            def a1_front(i):
                sl = i % 2
                src, dst, dk = a1_info(i)
                P.dma("sp", xt[:, sl, :], src, writes=[("xt", sl)])
                emit_norm_bf16(C, xt[:, sl, :], [("xt", sl)], g[:, 0, :], ("ga", 0), xb[:, sl, :], [("xb", sl)])

            def a1_back(i):
                sl = i % 2
                src, dst, dk = a1_info(i)
                transpose_to(C, xb[:, sl, :], [("xb", sl)], dst, [dk], bank=6 + sl, evac=("act" if sl else "dve"))

            n_a1 = 32 if full else 48
            a1_front(0)
            for i in range(n_a1):
                if i + 1 < n_a1:
                    a1_front(i + 1)
                a1_back(i)

        if stage < 1:
            return
        with scope(P) as es2:
            wfz = es2.enter_context(nc.sbuf_tensor("wfz", [128, 8, 8], BF16))
            negb = es2.enter_context(nc.sbuf_tensor("negb", [8, 1], F32))
            Lf = es2.enter_context(nc.sbuf_tensor("Lf", [8, S], F32))
            Sc = es2.enter_context(nc.sbuf_tensor("Sc", [8, S], F32))
            Dm = es2.enter_context(nc.sbuf_tensor("Dm", [8, NCH, 8], F32))
            P.dma("pool", wfz[:], T["w_in"].rearrange("(k p) n -> p k n", p=128)[:, :, 3072:3080], writes=["wfz"])
            P.dma("sp", negb[:], T["bf"], writes=["negb"])
            P.op("dve", lambda e: e.tensor_scalar(out=negb[:], in0=negb[:], scalar1=-1.0, scalar2=None, op0=ALU.mult),
                 reads=["negb"], writes=["negb"])
            for n in range(8):
                bank = n % 2
                pv = psf(C, bank)[0:8, :]
                for k in range(8):
                    P.op("pe", lambda e, k=k, n=n, pv=pv: e.matmul(pv, lhsT=wfz[:, k, :], rhs=hnTf[:, k, n * 512:(n + 1) * 512],
                                                                   start=(k == 0), stop=(k == 7)),
                         reads=["wfz", ("hnTf", n)], writes=[("ps", bank)])
                P.op("act", lambda e, n=n, pv=pv: e.activation(out=Lf[:, n * 512:(n + 1) * 512], in_=pv, func=AF.Exp, scale=-1.0, bias=negb[:, 0:1]),
                     reads=[("ps", bank), "negb"], writes=[("Lf", n)])
            for n in range(8):
                P.op("act", lambda e, n=n: e.activation(out=Lf[:, n * 512:(n + 1) * 512], in_=Lf[:, n * 512:(n + 1) * 512], func=AF.Ln, bias=1.0),
                     reads=[("Lf", n)], writes=[("Lf", n)])
            for n in range(8):
                init = 0.0 if n == 0 else Sc[:, n * 512 - 1:n * 512]
                rd = [("Lf", n), "onesf"] + ([("Sc", n - 1)] if n else [])
                P.op("dve", lambda e, n=n, init=init: e.tensor_tensor_scan(out=Sc[:, n * 512:(n + 1) * 512], data0=C.onesf[0:8, :],
                                                                           data1=Lf[:, n * 512:(n + 1) * 512], initial=init,
                                                                           op0=ALU.mult, op1=ALU.add),
                     reads=rd, writes=[("Sc", n)])
            pvt = psf(C, 2)[:, 0:256]
            for kb in range(32):
                P.op("pe", lambda e, kb=kb: e.transpose(out=pvt[:, kb * 8:(kb + 1) * 8], in_=Sc[0:8, kb * 128:(kb + 1) * 128],
                                                        identity=C.identf[0:8, 0:8]),
                     reads=[("Sc", kb // 4), "identf"], writes=[("ps", 2)])
            P.op("dve", lambda e: e.tensor_copy(out=Stm[:, :, :], in_=pvt.rearrange("p (k h) -> p k h", h=8)),
                 reads=[("ps", 2)], writes=["Stm"])
            CW = 128 * KPC
            ssel = Sc[:, :].rearrange("h (p t) -> h p t", t=CW)[:, :, CW - 1:CW]
            P.op("dve", lambda e: e.tensor_tensor(out=Dm[:, :, :], in0=ssel.to_broadcast([8, NCH, 8]),
                                                  in1=C.identf[0:8, 0:8].unsqueeze(1).to_broadcast([8, NCH, 8]), op=ALU.mult),
                 reads=[("Sc", n) for n in range(8)] + ["identf"], writes=["Dm"])
            pvx = psf(C, 3)[:, 0:NCH * 8]
            P.op("pe", lambda e: e.matmul(pvx, lhsT=C.onesf[0:8, 0:128], rhs=Dm[:, :, :].rearrange("h p g -> h (p g)"), start=True, stop=True),
                 reads=["onesf", "Dm"], writes=[("ps", 3)])
            P.op("dve", lambda e: e.tensor_copy(out=Xb[:, :], in_=pvx), reads=[("ps", 3)], writes=["Xb"])
        if stage < 2:
            return
        with scope(P) as es2:
            wq = es2.enter_context(nc.sbuf_tensor("wq", [128, 2, 8, 128], BF16))
            wk = es2.enter_context(nc.sbuf_tensor("wk", [128, 2, 8, 128], BF16))
            wv = es2.enter_context(nc.sbuf_tensor("wv", [128, 2, 8, 128], BF16))
            KT = es2.enter_context(nc.sbuf_tensor("KT", [128, S], BF16))
            QT = es2.enter_context(nc.sbuf_tensor("QTz", [128, 2, NT * 128], BF16))
            Vb = es2.enter_context(nc.sbuf_tensor("Vb", [128, 32, 130], BF16))
            Vd = Vb[:, :, 0:129]
            Vf = Vb[:, :, :].rearrange("p k (h d) -> p k h d", h=2)
            PT = es2.enter_context(nc.sbuf_tensor("PT", [128, 4, 2, 256], BF16))
            t1 = es2.enter_context(nc.sbuf_tensor("t1", [128, 2, 128], F32))
            dd = es2.enter_context(nc.sbuf_tensor("dd", [128, 2, 128], F32))
            rr = es2.enter_context(nc.sbuf_tensor("rr", [128, 2, 8], F32))
            P.op("pool", lambda e: e.memset(QT[:, :, :], 0.0), writes=[("QT", n) for n in range(NT // 4)])
            w_in_v = T["w_in"].rearrange("(k p) n -> p k n", p=128)
            rr_i = [0]
            for gidx in range(8):
                ws = gidx % 2
                diff = gidx < 4
                if gidx in (0, 4):
                    vk = [("V", kq) for kq in range(8)]
                    P.op("pool", lambda e: e.memset(Vb[:, :, :], 1.0), writes=vk)
                if diff:
                    qc, kc, vc = gidx * 128, 512 + gidx * 128, 1024 + gidx * 128
                else:
                    qc, kc, vc = 1536 + (gidx - 4) * 128, 2048 + (gidx - 4) * 128, 2560 + (gidx - 4) * 128
                P.dma("pool", wq[:, ws, :, :], w_in_v[:, :, qc:qc + 128], writes=[("wq", ws)])
                P.dma("pool", wk[:, ws, :, :], w_in_v[:, :, kc:kc + 128], writes=[("wk", ws)])
                P.dma("pool", wv[:, ws, :, :], w_in_v[:, :, vc:vc + 128], writes=[("wv", ws)])
                for n in range(8):
                    bank = 2 * (n % 2)
                    pv = psf(C, bank)
                    for k in range(8):
                        P.op("pe", lambda e, k=k, n=n, pv=pv: e.matmul(pv, lhsT=wk[:, ws, k, :], rhs=hnTf[:, k, n * 512:(n + 1) * 512],
                                                                       start=(k == 0), stop=(k == 7)),
                             reads=[("wk", ws), ("hnTf", n)], writes=[("ps", bank)])
                    P.op("dve", lambda e, n=n, pv=pv: e.tensor_copy(out=KT[:, n * 512:(n + 1) * 512], in_=pv),
                         reads=[("ps", bank)], writes=[("KT", n)])
                for n in range(NT // 4):
                    bank = 2 * (n % 2)
                    pv = psf(C, bank)
                    for k in range(8):
                        P.op("pe", lambda e, k=k, n=n, pv=pv: e.matmul(pv, lhsT=wq[:, ws, k, :], rhs=hnTo[:, k, n * 512:(n + 1) * 512],
                                                                       start=(k == 0), stop=(k == 7)),
                             reads=[("wq", ws), (("hnTf" if full else "hnTo"), n)], writes=[("ps", bank)])
                    P.op("dve", lambda e: e.tensor_copy(out=QT[0:64, 0, n * 512:(n + 1) * 512], in_=pv[0:64, :]),
                         reads=[("ps", bank)], writes=[("QT", n)])
                    P.op("dve", lambda e: e.tensor_copy(out=QT[64:128, 1, n * 512:(n + 1) * 512], in_=pv[64:128, :]),
                         reads=[("ps", bank), ("QT", n)], writes=[("QT", n)])
                for kq in range(8):
                    bank = 2 * (kq % 2)
                    pv = psf(C, bank)
                    for j in range(4):
                        kb = kq * 4 + j
                        for k in range(8):
                            P.op("pe", lambda e, k=k, kb=kb, j=j, pv=pv: e.matmul(pv[:, j * 128:(j + 1) * 128], lhsT=hnTf[:, k, kb * 128:(kb + 1) * 128],
                                                                                  rhs=wv[:, ws, k, :], start=(k == 0), stop=(k == 7)),
                                 reads=[("wv", ws), ("hnTf", kb // 4)], writes=[("ps", bank)])
                    if diff:
                        dst = Vd[:, kq * 4:(kq + 1) * 4, 0:128]
                        srcv = pv.rearrange("p (j d) -> p j d", j=4)
                        vkey = ("V", kq)
                    else:
                        dst = Vf[:, kq * 4:(kq + 1) * 4, :, 0:64]
                        srcv = pv.rearrange("p (j h d) -> p j h d", j=4, h=2)
                        vkey = ("V", kq)
                    P.op("dve", lambda e, dst=dst, srcv=srcv: e.tensor_copy(out=dst, in_=srcv), reads=[("ps", bank)], writes=[vkey])

                for p in range(NCH):
                    nkb = KPC * p + KPC
                    kb0 = KPC * p
                    oset = p % 2
                    if not diff:
                        fsl = p % 2
                        for sh in range(2):
                            hh_ = 2 * (gidx - 4) + sh
                            P.op("dve", lambda e: e.tensor_scalar(out=fb[:, fsl, sh, 0:nkb], in0=Stm[:, 0:nkb, hh_],
                                                                  scalar1=Xb[:, p * 8 + hh_:p * 8 + hh_ + 1], scalar2=None, op0=ALU.subtract),
                                 reads=["Stm", "Xb"], writes=[("fb", fsl)])
                    def emit_S(kb, p=p):
                        slot = kb % 4
                        for sh in range(2):
                            pvS = psf(C, slot)[:, sh * 256:(sh + 1) * 256]
                            P.op("pe", lambda e: e.matmul(pvS, lhsT=KT[:, kb * 128:(kb + 1) * 128],
                                                          rhs=QT[:, sh, p * 256:(p + 1) * 256], start=True, stop=True),
                                 reads=[("KT", kb // 4), ("QT", p // 2)], writes=[("ps", slot)])

                    def emit_PV(kb, p=p, nkb=nkb, oset=oset):
                        slot = kb % 4
                        if diff:
                            ai = kb - kb0 + (30 if full else 28)
                            P.op("act", lambda e: e.activation(out=PT[:, slot, :, :], in_=psf(C, slot).rearrange("p (s q) -> p s q", s=2),
                                                               func=AF.Exp, scale=0.125, bias=alibi[:, gidx, ai:ai + 1]),
                                 reads=[("ps", slot), "alibi"], writes=[("PT", slot, 0), ("PT", slot, 1)])
                        else:
                            for sh in range(2):
                                bias = fb[:, p % 2, sh, kb:kb + 1]
                                P.op("act", lambda e: e.activation(out=PT[:, slot, sh, :], in_=psf(C, slot)[:, sh * 256:(sh + 1) * 256],
                                                                   func=AF.Exp, scale=0.125, bias=bias),
                                     reads=[("ps", slot), ("fb", p % 2)], writes=[("PT", slot, sh)])
                        if kb >= kb0:
                            mk = masks[:, kb - kb0, :].unsqueeze(1).to_broadcast([128, 2, 256])
                            P.op("pool", lambda e: e.tensor_tensor(out=PT[:, slot, :, :], in0=PT[:, slot, :, :], in1=mk, op=ALU.mult),
                                 reads=[("PT", slot, 0), ("PT", slot, 1), "masks"], writes=[("PT", slot, 0), ("PT", slot, 1)])
                        for sh in range(2):
                            obank = 4 + oset * 2 + sh
                            for e_ in range(2):
                                if diff:
                                    ov = psf(C, obank)[:, e_ * 129:(e_ + 1) * 129]
                                    rhs = Vd[:, kb, :]
                                else:
                                    ov = psf(C, obank)[:, e_ * 65:(e_ + 1) * 65]
                                    rhs = Vf[:, kb, sh, :]
                                first = (kb == 0 and e_ == 0)
                                P.op("pe", lambda e: e.matmul(ov, lhsT=PT[:, slot, sh, e_ * 128:(e_ + 1) * 128], rhs=rhs,
                                                              start=first, stop=(kb == nkb - 1), skip_group_check=True),
                                     reads=[("PT", slot, sh), ("V", kb // 4)], writes=[("ps", obank)])

                    LOOK = 2
                    for kb in range(min(LOOK, nkb)):
                        emit_S(kb)
                    for kb in range(nkb):
                        if kb + LOOK < nkb:
                            emit_S(kb + LOOK)
                        emit_PV(kb)

                    b0 = 4 + oset * 2
                    i0 = rr_i[0] % 2
                    rr_i[0] += 1
                    rk = ("rr", i0)
                    if diff:
                        O1 = psf(C, b0)[:, 0:258].rearrange("p (e d) -> p e d", e=2)
                        O2 = psf(C, b0 + 1)[:, 0:258].rearrange("p (e d) -> p e d", e=2)
                        P.op("dve", lambda e: e.reciprocal(out=rr[:, i0, 0:2], in_=O1[:, :, 128]), reads=[("ps", b0)], writes=[rk])
                        P.op("dve", lambda e: e.reciprocal(out=rr[:, i0, 2:4], in_=O2[:, :, 128]), reads=[("ps", b0 + 1)], writes=[rk])
                        P.op("dve", lambda e: e.tensor_scalar(out=rr[:, i0, 2:4], in0=rr[:, i0, 2:4], scalar1=neglam[:, 2:3], scalar2=None, op0=ALU.mult),
                             reads=[rk, "neglam"], writes=[rk])
                        for e_ in range(2):
                            lt = 2 * p + e_
                            P.op("dve", lambda e, e_=e_: e.tensor_scalar(out=t1[:, e_, :], in0=O1[:, e_, 0:128], scalar1=rr[:, i0, e_:e_ + 1],
                                                                         scalar2=None, op0=ALU.mult),
                                 reads=[("ps", b0), rk], writes=[("t1", e_)])
                            P.op("dve", lambda e, e_=e_: e.scalar_tensor_tensor(out=dd[:, e_, :], in0=O2[:, e_, 0:128], scalar=rr[:, i0, 2 + e_:3 + e_],
                                                                                in1=t1[:, e_, :], op0=ALU.mult, op1=ALU.add),
                                 reads=[("ps", b0 + 1), rk, ("t1", e_)], writes=[("dd", e_)])
                            rs, rsk = rstd_from(C, dd[:, e_, :], [("dd", e_)], 128, 1e-5, eng="dve")
                            P.op("dve", lambda e, e_=e_, lt=lt, rs=rs: e.scalar_tensor_tensor(out=obuf[:, lt, gidx * 128:(gidx + 1) * 128], in0=dd[:, e_, :],
                                                                                             scalar=rs, in1=subg[:, :], op0=ALU.mult, op1=ALU.mult),
                                 reads=[("dd", e_), rsk, "subg"], writes=[("obuf", lt)])
                    else:
                        for sh in range(2):
                            Ov = psf(C, b0 + sh)[:, 0:130].rearrange("p (e d) -> p e d", e=2)
                            P.op("dve", lambda e, sh=sh, Ov=Ov: e.reciprocal(out=rr[:, i0, 4 + 2 * sh:6 + 2 * sh], in_=Ov[:, :, 64]),
                                 reads=[("ps", b0 + sh)], writes=[rk])
                            for e_ in range(2):
                                lt = 2 * p + e_
                                c0 = 512 + (2 * (gidx - 4) + sh) * 64
                                P.op("dve", lambda e, sh=sh, e_=e_, lt=lt, c0=c0, Ov=Ov: e.tensor_scalar(out=obuf[:, lt, c0:c0 + 64], in0=Ov[:, e_, 0:64],
                                                                                                       scalar1=rr[:, i0, 4 + 2 * sh + e_:5 + 2 * sh + e_],
                                                                                                       scalar2=None, op0=ALU.mult),
                                     reads=[("ps", b0 + sh), rk], writes=[("obuf", lt)])

        if stage < 3:
            return
        with scope(P) as es2:
            wo = es2.enter_context(nc.sbuf_tensor("wo", [128, 8, D], BF16))
            kwo = load_w_bf16(C, wo, T["w_out"], 8, "wo", nsplit=2)
            oT = es2.enter_context(nc.sbuf_tensor("oT", [128, 2, 8, 128], BF16))
            xr = es2.enter_context(nc.sbuf_tensor("xres", [128, 2, D], F32))
            hh = es2.enter_context(nc.sbuf_tensor("hh", [128, 2, D], F32))
            xsrc = T["xf" if full else "xo"]

            def a4_front(lt):
                sl = lt % 2
                P.dma("sp", xr[:, sl, :], xsrc[lt * 128:(lt + 1) * 128, :], writes=[("xres", sl)])
                transpose_to(C, obuf[:, lt, :], [("obuf", lt)], oT[:, sl, :, :], [("oT", sl)], bank=6 + sl, evac="act")
                for hf in range(2):
                    pk = ("ps", 2 * sl + hf)
                    pv = psf(C, 2 * sl + hf)
                    for k in range(8):
                        P.op("pe", lambda e: e.matmul(pv, lhsT=oT[:, sl, k, :], rhs=wo[:, k, hf * 512:(hf + 1) * 512],
                                                      start=(k == 0), stop=(k == 7)),
                             reads=[("oT", sl), kwo[k // 4]], writes=[pk])

            def a4_back(lt):
                sl = lt % 2
                emit_norm_residual(C, C.ps[sl][:, :], [("ps", 2 * sl), ("ps", 2 * sl + 1)], g[:, 1, :], ("ga", 1),
                                   xr[:, sl, :], [("xres", sl)], hh[:, sl, :], [("hh", sl)])
                P.dma("sp", H[lt * 128:(lt + 1) * 128, :], hh[:, sl, :], reads=[("hh", sl)], writes=[("H", lt)])

            a4_front(0)
            for lt in range(NT):
                if lt + 1 < NT:
                    a4_front(lt + 1)
                a4_back(lt)


def build_phase_A(upto=99, stage=9):
    nc = bass.Bass("TRN2", target_bir_lowering=False)
    T = {}

    def din(name, shape, dt=F32):
        T[name] = nc.dram_tensor(name, shape, dt, kind="ExternalInput").ap()

    din("xf", [S, D]); din("xo", [NOWN, D]); din("pT", [256, NOWN])
    din("w_in", [D, 3080]); din("w_out", [D, D]); din("w_up", [D, DFF]); din("w_dn", [DFF, D])
    din("w_pp", [256, D]); din("w_pg", [D, D]); din("vec", [8, D]); din("lamv", [1, 256]); din("bf", [8, 1])
    din("ident", [128, 128]); din("masks", [4, 128, 256]); din("alibi", [128, 128])
    H = nc.dram_tensor("H", [NOWN, D], F32, kind="ExternalOutput").ap()
    H1 = nc.dram_tensor("H1", [NOWN, D], F32, kind="ExternalOutput").ap()
    HNT = nc.dram_tensor("HNT", [D, NOWN], BF16, kind="ExternalOutput").ap()
    with contextlib.ExitStack() as es:
        P = Prog(nc, es)
        C = Ctx()
        setup_common(nc, P, es, C, T["ident"])
        setup_eps(C, es)
        emit_attention(C, T, H, stage)
        if upto >= 2:
            emit_mlp(C, H, T["w_up"], T["w_dn"], T["vec"], 2, 3, "0")
        if upto >= 3:
            emit_ple(C, H, T["pT"], T["w_pp"], T["w_pg"], T["vec"], 4, H1, "0", hn_row=5, HNT=HNT)
        P.finish()
    return nc


def emit_rec(C, T, G, ncb=5):
    import itertools
    nc, P = C.nc, C.P
    TH = 1024
    NTH = S // TH
    assert ncb % 2 == 0 or ncb == 5
    with scope(P) as es:
        hnT = es.enter_context(nc.sbuf_tensor("r_hnT", [128, 8, S], BF16))
        wg = es.enter_context(nc.sbuf_tensor("r_wg", [128, 8, ncb * 128], BF16))
        wxr = es.enter_context(nc.sbuf_tensor("r_wxr", [128, 8, ncb * 128], BF16))
        wxs = es.enter_context(nc.sbuf_tensor("r_wxs", [128, ncb, 128], BF16))
        was = es.enter_context(nc.sbuf_tensor("r_was", [128, ncb, 128], BF16))
        rv = es.enter_context(nc.sbuf_tensor("r_rv", [128, ncb, 8], F32))
        sc = es.enter_context(nc.sbuf_tensor("r_sc", [128, ncb], F32))
        hlast = es.enter_context(nc.sbuf_tensor("r_hlast", [128, 2], F32))
        ybf = es.enter_context(nc.sbuf_tensor("r_y", [128, 2, 2, TH], BF16))
        xr = es.enter_context(nc.sbuf_tensor("r_xr", [128, 2, 2, TH + 3], F32))
        xc = es.enter_context(nc.sbuf_tensor("r_xc", [128, 2, TH], F32))
        xcb = es.enter_context(nc.sbuf_tensor("r_xcb", [128, 2, TH], BF16))
        gx = es.enter_context(nc.sbuf_tensor("r_gx", [128, 2, TH], F32))
        ga = es.enter_context(nc.sbuf_tensor("r_ga", [128, 2, TH], F32))
        tt = es.enter_context(nc.sbuf_tensor("r_tt", [128, 2, TH], F32))
        hs = es.enter_context(nc.sbuf_tensor("r_hs", [128, 2, TH], F32))
        gout = es.enter_context(nc.sbuf_tensor("r_gout", [128, 2, TH], BF16))
        hv = T["hnT"].rearrange("(k f) t -> f k t", f=128)
        for k in range(8):
            P.dma("sp", hnT[:, k, :], hv[:, k, :], writes=[("r_hnT", k)])
        kg = load_w_bf16(C, wg, T["w_g"], 8, "r_wg", nsplit=2)
        kx = load_w_bf16(C, wxr, T["w_x"], 8, "r_wxr", nsplit=2)
        P.dma("pool", wxs[:], T["wx"].rearrange("n i j -> i n j"), writes=["r_wxs"])
        P.dma("pool", was[:], T["wa"].rearrange("n i j -> i n j"), writes=["r_was"])
        P.dma("sp", rv[:], T["rvec"], writes=["r_rv"])
        P.op("act", lambda e: e.activation(out=sc[:, :], in_=rv[:, :, 7], func=AF.Exp, scale=-1.0), reads=["r_rv"], writes=["r_sc"])
        P.op("act", lambda e: e.activation(out=sc[:, :], in_=sc[:, :], func=AF.Ln, bias=1.0), reads=["r_sc"], writes=["r_sc"])
        P.op("dve", lambda e: e.tensor_scalar(out=sc[:, :], in0=sc[:, :], scalar1=-8.0, scalar2=None, op0=ALU.mult), reads=["r_sc"], writes=["r_sc"])

        def stage1(cb, th, st, bs):
            if th == 0:
                P.op("dve", lambda e: e.memset(xr[:, st, bs, 0:3], 0.0), writes=[("r_xr", st, bs)])
            else:
                P.op("dve", lambda e: e.tensor_copy(out=xr[:, st, bs, 0:3], in_=xr[:, st, 1 - bs, TH:TH + 3]),
                     reads=[("r_xr", st, 1 - bs)], writes=[("r_xr", st, bs)])
            yield
            for n in range(TH // 512):
                N = (TH // 512) * th + n
                bg = 2 * st
                pv = psf(C, bg)
                for k in range(8):
                    P.op("pe", lambda e: e.matmul(pv, lhsT=wg[:, k, cb * 128:(cb + 1) * 128], rhs=hnT[:, k, N * 512:(N + 1) * 512],
                                                  start=(k == 0), stop=(k == 7)),
                         reads=[kg[k // 4], ("r_hnT", k)], writes=[("ps", bg)])
                P.op("act", lambda e: e.activation(out=ybf[:, st, bs, n * 512:(n + 1) * 512], in_=pv, func=AF.Gelu_apprx_tanh),
                     reads=[("ps", bg)], writes=[("r_y", st, bs)])
                yield
                bx_ = 2 * st + 1
                pv2 = psf(C, bx_)
                for k in range(8):
                    P.op("pe", lambda e: e.matmul(pv2, lhsT=wxr[:, k, cb * 128:(cb + 1) * 128], rhs=hnT[:, k, N * 512:(N + 1) * 512],
                                                  start=(k == 0), stop=(k == 7)),
                         reads=[kx[k // 4], ("r_hnT", k)], writes=[("ps", bx_)])
                P.op("dve", lambda e: e.tensor_copy(out=xr[:, st, bs, 3 + n * 512:3 + (n + 1) * 512], in_=pv2),
                     reads=[("ps", bx_)], writes=[("r_xr", st, bs)])
                yield

        def stage2(cb, th, st, bs):
            xk = ("r_xr", st, bs)
            K = lambda name: (name, st)
            P.op("act", lambda e: e.activation(out=xc[:, st, :], in_=xr[:, st, bs, 3:3 + TH], func=AF.Identity, scale=rv[:, cb, 3:4], bias=rv[:, cb, 4:5]),
                 reads=[xk, "r_rv"], writes=[K("r_xc")])
            yield
            for w in range(3):
                P.op("dve", lambda e: e.scalar_tensor_tensor(out=xc[:, st, :], in0=xr[:, st, bs, w:w + TH], scalar=rv[:, cb, w:w + 1], in1=xc[:, st, :],
                                                             op0=ALU.mult, op1=ALU.add),
                     reads=[xk, "r_rv", K("r_xc")], writes=[K("r_xc")])
                yield
            P.op("act", lambda e: e.copy(out=xcb[:, st, :], in_=xc[:, st, :]), reads=[K("r_xc")], writes=[K("r_xcb")])
            yield
            for n in range(TH // 512):
                b1 = 4 + 2 * st
                pv = psf(C, b1)
                P.op("pe", lambda e: e.matmul(pv, lhsT=wxs[:, cb, :], rhs=xcb[:, st, n * 512:(n + 1) * 512], start=True, stop=True),
                     reads=["r_wxs", K("r_xcb")], writes=[("ps", b1)])
                P.op("act", lambda e: e.activation(out=gx[:, st, n * 512:(n + 1) * 512], in_=pv, func=AF.Sigmoid, bias=rv[:, cb, 5:6]),
                     reads=[("ps", b1), "r_rv"], writes=[K("r_gx")])
                yield
                b2 = 5 + 2 * st
                pv2 = psf(C, b2)
                P.op("pe", lambda e: e.matmul(pv2, lhsT=was[:, cb, :], rhs=xcb[:, st, n * 512:(n + 1) * 512], start=True, stop=True),
                     reads=["r_was", K("r_xcb")], writes=[("ps", b2)])
                P.op("act", lambda e: e.activation(out=ga[:, st, n * 512:(n + 1) * 512], in_=pv2, func=AF.Sigmoid, bias=rv[:, cb, 6:7]),
                     reads=[("ps", b2), "r_rv"], writes=[K("r_ga")])
                yield
            P.op("act", lambda e: e.activation(out=ga[:, st, :], in_=ga[:, st, :], func=AF.Exp, scale=sc[:, cb:cb + 1]),
                 reads=[K("r_ga"), "r_sc"], writes=[K("r_ga")])
            yield
            P.op("dve", lambda e: e.tensor_tensor(out=tt[:, st, :], in0=ga[:, st, :], in1=ga[:, st, :], op=ALU.mult), reads=[K("r_ga")], writes=[K("r_tt")])
            yield
            P.op("pool", lambda e: e.tensor_tensor(out=gx[:, st, :], in0=gx[:, st, :], in1=xc[:, st, :], op=ALU.mult),
                 reads=[K("r_gx"), K("r_xc")], writes=[K("r_gx")])
            yield
            P.op("act", lambda e: e.activation(out=tt[:, st, :], in_=tt[:, st, :], func=AF.Sqrt, scale=-1.0, bias=1.0), reads=[K("r_tt")], writes=[K("r_tt")])
            yield
            P.op("dve", lambda e: e.tensor_tensor(out=tt[:, st, :], in0=tt[:, st, :], in1=gx[:, st, :], op=ALU.mult),
                 reads=[K("r_tt"), K("r_gx")], writes=[K("r_tt")])
            if th == 0:
                P.op("dve", lambda e: e.tensor_copy(out=tt[:, st, 0:1], in_=gx[:, st, 0:1]), reads=[K("r_gx"), K("r_tt")], writes=[K("r_tt")])
            yield
            init = 0.0 if th == 0 else hlast[:, st:st + 1]
            P.op("dve", lambda e: e.tensor_tensor_scan(out=hs[:, st, :], data0=ga[:, st, :], data1=tt[:, st, :], initial=init, op0=ALU.mult, op1=ALU.add),
                 reads=[K("r_ga"), K("r_tt"), K("r_hlast")], writes=[K("r_hs")])
            P.op("dve", lambda e: e.tensor_copy(out=hlast[:, st:st + 1], in_=hs[:, st, TH - 1:TH]), reads=[K("r_hs")], writes=[K("r_hlast")])
            yield
            P.op("pool", lambda e: e.tensor_tensor(out=gout[:, st, :], in0=hs[:, st, :], in1=ybf[:, st, bs, :], op=ALU.mult),
                 reads=[K("r_hs"), ("r_y", st, bs)], writes=[K("r_gout")])
            P.dma("sp", G[cb * 128:(cb + 1) * 128, th * TH:(th + 1) * TH], gout[:, st, :], reads=[K("r_gout")], writes=[("G", cb, th)])
            yield

        def interleave(gens):
            for _ in itertools.zip_longest(*gens):
                pass

        supers = []
        for j in range((ncb + 1) // 2):
            cbs = [c for c in (2 * j, 2 * j + 1) if c < ncb]
            for th in range(NTH):
                supers.append((cbs, th))
        interleave([stage1(cb, supers[0][1], st, 0) for st, cb in enumerate(supers[0][0])])
        for i, (cbs, th) in enumerate(supers):
            bs = i % 2
            gens = [stage2(cb, th, st, bs) for st, cb in enumerate(cbs)]
            if i + 1 < len(supers):
                ncbs, nth = supers[i + 1]
                gens += [stage1(cb, nth, st, 1 - bs) for st, cb in enumerate(ncbs)]
            interleave(gens)


def build_phase_B():
    nc = bass.Bass("TRN2", target_bir_lowering=False)
    T = {}

    def din(name, shape, dt=F32):
        T[name] = nc.dram_tensor(name, shape, dt, kind="ExternalInput").ap()

    din("hnT", [D, S], BF16); din("w_g", [D, 640]); din("w_x", [D, 640]); din("wx", [5, 128, 128]); din("wa", [5, 128, 128])
    din("rvec", [128, 5, 8]); din("ident", [128, 128])
    G = nc.dram_tensor("G", [640, S], BF16, kind="ExternalOutput").ap()
    with contextlib.ExitStack() as es:
        P = Prog(nc, es)
        C = Ctx()
        setup_common(nc, P, es, C, T["ident"])
        setup_eps(C, es)
        emit_rec(C, T, G)
        P.finish()
    return nc


def emit_recout(C, T, H, row=0, blend=None):
    nc, P = C.nc, C.P
    with scope(P) as es:
        g = load_gains(C, es, T["vec"], [row], "gro")
        gT = es.enter_context(nc.sbuf_tensor("gTs", [128, 10, NOWN], BF16))
        wro = es.enter_context(nc.sbuf_tensor("wro", [128, 10, D], BF16))
        if blend is None:
            gv = T["gT"].rearrange("(c p) t -> p c t", p=128)
            for c in range(10):
                P.dma("sp", gT[:, c, :], gv[:, c, :], writes=[("gTs", c)])
        else:
            Gd, H1d, sel_d = blend
            sel = es.enter_context(nc.sbuf_tensor("sel_sb", [128, 2], F32))
            P.dma("sp", sel[:], sel_d, writes=["sel"])
            gT2 = es.enter_context(nc.sbuf_tensor("gTs2", [128, 2, NOWN], BF16))
            hres2 = es.enter_context(nc.sbuf_tensor("hres2", [128, 2, D], F32))
            gv = Gd.rearrange("(c p) t -> p c t", p=128)
            for c in range(10):
                s2 = c % 2
                P.dma("sp", gT[:, c, :], gv[:, c, 0:NOWN], writes=[("gTs", c)])
                P.dma("sp", gT2[:, s2, :], gv[:, c, NOWN:2 * NOWN], writes=[("gTs2", s2)])
                P.op("act", lambda e: e.activation(out=gT[:, c, :], in_=gT[:, c, :], func=AF.Copy, scale=sel[:, 0:1]),
                     reads=[("gTs", c), "sel"], writes=[("gTs", c)])
                P.op("dve", lambda e: e.scalar_tensor_tensor(out=gT[:, c, :], in0=gT2[:, s2, :], scalar=sel[:, 1:2], in1=gT[:, c, :],
                                                             op0=ALU.mult, op1=ALU.add),
                     reads=[("gTs", c), ("gTs2", s2), "sel"], writes=[("gTs", c)])
        kw = load_w_bf16(C, wro, T["w_ro"], 10, "wro", nsplit=2)
        hres = es.enter_context(nc.sbuf_tensor("hres", [128, 2, D], F32))
        hh = es.enter_context(nc.sbuf_tensor("hh1", [128, 2, D], F32))
        def ro_front(lt):
            sl = lt % 2
            if blend is None:
                P.dma("sp", hres[:, sl, :], T["h1"][lt * 128:(lt + 1) * 128, :], writes=[("hres", sl)])
            else:
                P.dma("sp", hres[:, sl, :], H1d[lt * 128:(lt + 1) * 128, :], writes=[("hres", sl)])
                P.dma("sp", hres2[:, sl, :], H1d[NOWN + lt * 128:NOWN + (lt + 1) * 128, :], writes=[("hres2", sl)])
                P.op("act", lambda e: e.activation(out=hres[:, sl, :], in_=hres[:, sl, :], func=AF.Copy, scale=sel[:, 0:1]),
                     reads=[("hres", sl), "sel"], writes=[("hres", sl)])
                P.op("dve", lambda e: e.scalar_tensor_tensor(out=hres[:, sl, :], in0=hres2[:, sl, :], scalar=sel[:, 1:2], in1=hres[:, sl, :],
                                                             op0=ALU.mult, op1=ALU.add),
                     reads=[("hres", sl), ("hres2", sl), "sel"], writes=[("hres", sl)])
            for hf in range(2):
                pk = ("ps", 2 * sl + hf)
                pv = psf(C, 2 * sl + hf)
                for c in range(10):
                    P.op("pe", lambda e: e.matmul(pv, lhsT=gT[:, c, lt * 128:(lt + 1) * 128], rhs=wro[:, c, hf * 512:(hf + 1) * 512],
                                                  start=(c == 0), stop=(c == 9)),
                         reads=[("gTs", c), kw[c // 5]], writes=[pk])

        def ro_back(lt):
            sl = lt % 2
            emit_norm_residual(C, C.ps[sl][:, :], [("ps", 2 * sl), ("ps", 2 * sl + 1)], g[:, 0, :], ("gro", 0),
                               hres[:, sl, :], [("hres", sl)], hh[:, sl, :], [("hh1", sl)])
            P.dma("sp", H[lt * 128:(lt + 1) * 128, :], hh[:, sl, :], reads=[("hh1", sl)], writes=[("H", lt)])

        ro_front(0)
        for lt in range(16):
            if lt + 1 < 16:
                ro_front(lt + 1)
            ro_back(lt)


def build_phase_C():
    nc = bass.Bass("TRN2", target_bir_lowering=False)
    T = {}

    def din(name, shape, dt=F32):
        T[name] = nc.dram_tensor(name, shape, dt, kind="ExternalInput").ap()

    din("gT", [RW, NOWN], BF16); din("h1", [NOWN, D]); din("pT", [256, NOWN]); din("w_ro", [RW, D])
    din("w_up", [D, DFF]); din("w_dn", [DFF, D]); din("w_pp", [256, D]); din("w_pg", [D, D]); din("vec", [8, D]); din("ident", [128, 128])
    H = nc.dram_tensor("H", [NOWN, D], F32, kind="ExternalOutput").ap()
    OUT = nc.dram_tensor("OUT", [NOWN, D], F32, kind="ExternalOutput").ap()
    with contextlib.ExitStack() as es:
        P = Prog(nc, es)
        C = Ctx()
        setup_common(nc, P, es, C, T["ident"])
        setup_eps(C, es)
        emit_recout(C, T, H)
        emit_mlp(C, H, T["w_up"], T["w_dn"], T["vec"], 1, 2, "1")
        emit_ple(C, H, T["pT"], T["w_pp"], T["w_pg"], T["vec"], 3, OUT, "1")
        P.finish()
    return nc


def build_fused():
    nc = bass.Bass("TRN2", target_bir_lowering=False)
    T = {}

    def din(name, shape, dt=F32):
        T[name] = nc.dram_tensor(name, shape, dt, kind="ExternalInput").ap()

    din("xf", [S, D]); din("pT", [256, S]); din("pT1", [256, NOWN])
    din("w_in", [D, 3080]); din("w_out", [D, D]); din("w_up", [D, DFF]); din("w_dn", [DFF, D])
    din("w_pp", [256, D]); din("w_pg", [D, D]); din("vec", [16, D]); din("lamv", [1, 256]); din("bf", [8, 1])
    din("ident", [128, 128]); din("masks", [2, 128, 256]); din("alibi", [128, 128]); din("sel", [128, 2])
    din("w_g", [D, RW]); din("w_x", [D, RW]); din("wx", [10, 128, 128]); din("wa", [10, 128, 128]); din("rvec", [128, 10, 8])
    din("w_ro", [RW, D]); din("w_up1", [D, DFF]); din("w_dn1", [DFF, D]); din("w_pp1", [256, D]); din("w_pg1", [D, D])
    HA = nc.dram_tensor("HA", [S, D], F32, kind="Internal").ap()
    H1 = nc.dram_tensor("H1", [S, D], F32, kind="Internal").ap()
    HNT = nc.dram_tensor("HNT", [D, S], BF16, kind="Internal").ap()
    G = nc.dram_tensor("G", [RW, S], BF16, kind="Internal").ap()
    HC = nc.dram_tensor("HC", [NOWN, D], F32, kind="Internal").ap()
    OUT = nc.dram_tensor("OUT", [NOWN, D], F32, kind="ExternalOutput").ap()
    with contextlib.ExitStack() as es:
        P = Prog(nc, es)
        C = Ctx()
        setup_common(nc, P, es, C, T["ident"])
        setup_eps(C, es)
        emit_attention(C, T, HA, full=True)
        emit_mlp(C, HA, T["w_up"], T["w_dn"], T["vec"], 2, 3, "0", ntok=S)
        emit_ple(C, HA, T["pT"], T["w_pp"], T["w_pg"], T["vec"], 4, H1, "0", hn_row=5, HNT=HNT, ntiles=32)
        T1 = {"hnT": HNT, "w_g": T["w_g"], "w_x": T["w_x"], "wx": T["wx"], "wa": T["wa"], "rvec": T["rvec"]}
        emit_rec(C, T1, G, ncb=10)
        T2 = {"vec": T["vec"], "w_ro": T["w_ro"]}
        emit_recout(C, T2, HC, row=8, blend=(G, H1, T["sel"]))
        emit_mlp(C, HC, T["w_up1"], T["w_dn1"], T["vec"], 9, 10, "1", ntok=NOWN)
        emit_ple(C, HC, T["pT1"], T["w_pp1"], T["w_pg1"], T["vec"], 11, OUT, "1", ntiles=16)
        P.finish()
    return nc


def own_tiles(r):
    return [4 * p + 2 * r + e for p in range(8) for e in range(2)]


def own_index(r):
    return np.concatenate([np.arange(g * 128, (g + 1) * 128) for g in own_tiles(r)])


def role_consts(r):
    ident = np.eye(128, dtype=np.float32)
    masks = np.zeros((4, 128, 256), np.float32)
    jj = np.arange(128)[:, None]
    ii = np.arange(128)[None, :]
    tri = (jj <= ii).astype(np.float32)
    for m in range(4):
        for e in range(2):
            qt = 2 * r + e
            if m < qt:
                masks[m, :, e * 128:(e + 1) * 128] = 1.0
            elif m == qt:
                masks[m, :, e * 128:(e + 1) * 128] = tri
    alibi = np.zeros((128, 4, 32), np.float32)
    for h in range(4):
        for idx in range(32):
            d = (idx - 28 - 2 * r - 2) * 128 + np.arange(128)
            alibi[:, h, idx] = np.minimum(SLOPES[h] * d, 0.0)
    return ident, masks, alibi.reshape(128, 128)


def f32c(a):
    return np.ascontiguousarray(a, dtype=np.float32)


def prep_A(inp, c):
    b, r = c // 2, c % 2
    oi = own_index(r)
    ident, masks, alibi = role_consts(r)
    vec = np.zeros((8, D), np.float32)
    vec[0] = inp["ln_mix_pre"][0]
    vec[1] = inp["ln_mix_post"][0]
    vec[2] = inp["ln_mlp_pre"][0]
    vec[3] = inp["ln_mlp_post"][0]
    vec[4] = inp["ple_norm"][0]
    vec[5] = inp["ln_mix_pre"][1]
    vec[6, :128] = inp["diff_subln"][0]
    lamv = np.concatenate([inp["diff_lambda_q1"][0], inp["diff_lambda_k1"][0],
                           inp["diff_lambda_q2"][0], inp["diff_lambda_k2"][0]])[None, :]
    return {
        "xf": f32c(inp["x"][b]), "xo": f32c(inp["x"][b][oi]), "pT": f32c(inp["p"][0, b][oi].T),
        "w_in": f32c(inp["attn_w_in"][0]), "w_out": f32c(inp["attn_w_out"][0]),
        "w_up": f32c(inp["mlp_w_up"][0]), "w_dn": f32c(inp["mlp_w_down"][0]),
        "w_pp": f32c(inp["ple_w_proj"][0]), "w_pg": f32c(inp["ple_w_gate"][0]),
        "vec": vec, "lamv": f32c(lamv), "bf": f32c(inp["attn_b_forget"][0][:, None]),
        "ident": ident, "masks": masks, "alibi": alibi,
    }


def prep_B(inp, c, hnT_full):
    r = c % 2
    cols_g = np.arange(5 * r * 128, (5 * r + 5) * 128)
    cols_x = RW + cols_g
    w_in = inp["rec_w_in"][0]
    rvec = np.zeros((128, 5, 8), np.float32)
    for j in range(5):
        ch = np.arange((5 * r + j) * 128, (5 * r + j + 1) * 128)
        rvec[:, j, 0:4] = inp["rec_conv_w"][0][:, ch].T
        rvec[:, j, 4] = inp["rec_conv_b"][0][ch]
        rvec[:, j, 5] = inp["rec_bx"][0][ch]
        rvec[:, j, 6] = inp["rec_ba"][0][ch]
        rvec[:, j, 7] = inp["rec_a_param"][0][ch]
    return {
        "hnT": hnT_full, "w_g": f32c(w_in[:, cols_g]), "w_x": f32c(w_in[:, cols_x]),
        "wx": f32c(inp["rec_wx"][0][5 * r:5 * r + 5]), "wa": f32c(inp["rec_wa"][0][5 * r:5 * r + 5]),
        "rvec": rvec, "ident": np.eye(128, dtype=np.float32),
    }


def prep_C(inp, c, gT_own, h1_own):
    b, r = c // 2, c % 2
    oi = own_index(r)
    vec = np.zeros((8, D), np.float32)
    vec[0] = inp["ln_mix_post"][1]
    vec[1] = inp["ln_mlp_pre"][1]
    vec[2] = inp["ln_mlp_post"][1]
    vec[3] = inp["ple_norm"][1]
    return {
        "gT": gT_own, "h1": h1_own, "pT": f32c(inp["p"][1, b][oi].T), "w_ro": f32c(inp["rec_w_out"][0]),
        "w_up": f32c(inp["mlp_w_up"][1]), "w_dn": f32c(inp["mlp_w_down"][1]),
        "w_pp": f32c(inp["ple_w_proj"][1]), "w_pg": f32c(inp["ple_w_gate"][1]),
        "vec": vec, "ident": np.eye(128, dtype=np.float32),
    }


def full_consts():
    ident = np.eye(128, dtype=np.float32)
    jj = np.arange(128)[:, None]
    ii = np.arange(128)[None, :]
    tri = (jj <= ii).astype(np.float32)
    masks = np.zeros((2, 128, 256), np.float32)
    masks[0, :, 0:128] = tri
    masks[0, :, 128:256] = 1.0
    masks[1, :, 128:256] = tri
    alibi = np.zeros((128, 4, 32), np.float32)
    for h in range(4):
        for idx in range(32):
            d = (idx - 30 - 2) * 128 + np.arange(128)
            alibi[:, h, idx] = np.minimum(SLOPES[h] * d, 0.0)
    return ident, masks, alibi.reshape(128, 128)


def prep_fused(inp, c):
    b, r = c // 2, c % 2
    ident, masks, alibi = full_consts()
    vec = np.zeros((16, D), np.float32)
    vec[0] = inp["ln_mix_pre"][0]
    vec[1] = inp["ln_mix_post"][0]
    vec[2] = inp["ln_mlp_pre"][0]
    vec[3] = inp["ln_mlp_post"][0]
    vec[4] = inp["ple_norm"][0]
    vec[5] = inp["ln_mix_pre"][1]
    vec[6, :128] = inp["diff_subln"][0]
    vec[8] = inp["ln_mix_post"][1]
    vec[9] = inp["ln_mlp_pre"][1]
    vec[10] = inp["ln_mlp_post"][1]
    vec[11] = inp["ple_norm"][1]
    lamv = np.concatenate([inp["diff_lambda_q1"][0], inp["diff_lambda_k1"][0],
                           inp["diff_lambda_q2"][0], inp["diff_lambda_k2"][0]])[None, :]
    rvec = np.zeros((128, 10, 8), np.float32)
    for j in range(10):
        ch = np.arange(j * 128, (j + 1) * 128)
        rvec[:, j, 0:4] = inp["rec_conv_w"][0][:, ch].T
        rvec[:, j, 4] = inp["rec_conv_b"][0][ch]
        rvec[:, j, 5] = inp["rec_bx"][0][ch]
        rvec[:, j, 6] = inp["rec_ba"][0][ch]
        rvec[:, j, 7] = inp["rec_a_param"][0][ch]
    sel = np.zeros((128, 2), np.float32)
    sel[:, r] = 1.0
    w_in1 = inp["rec_w_in"][0]
    return {
        "xf": f32c(inp["x"][b]), "pT": f32c(inp["p"][0, b].T), "pT1": f32c(inp["p"][1, b][r * NOWN:(r + 1) * NOWN].T),
        "w_in": f32c(inp["attn_w_in"][0]), "w_out": f32c(inp["attn_w_out"][0]),
        "w_up": f32c(inp["mlp_w_up"][0]), "w_dn": f32c(inp["mlp_w_down"][0]),
        "w_pp": f32c(inp["ple_w_proj"][0]), "w_pg": f32c(inp["ple_w_gate"][0]),
        "vec": vec, "lamv": f32c(lamv), "bf": f32c(inp["attn_b_forget"][0][:, None]),
        "ident": ident, "masks": masks, "alibi": alibi, "sel": sel,
        "w_g": f32c(w_in1[:, :RW]), "w_x": f32c(w_in1[:, RW:]), "wx": f32c(inp["rec_wx"][0]), "wa": f32c(inp["rec_wa"][0]),
        "rvec": rvec, "w_ro": f32c(inp["rec_w_out"][0]),
        "w_up1": f32c(inp["mlp_w_up"][1]), "w_dn1": f32c(inp["mlp_w_down"][1]),
        "w_pp1": f32c(inp["ple_w_proj"][1]), "w_pg1": f32c(inp["ple_w_gate"][1]),
    }


_NC_CACHE = {}


def _get(name, fn):
    if name not in _NC_CACHE:
        _NC_CACHE[name] = fn()
    return _NC_CACHE[name]


def kernel_unfused(**inputs):
    inp = {k: np.asarray(v) for k, v in inputs.items()}
    cores = list(range(NCORES))
    ncA = _get("A", build_phase_A)
    resA = run_bass_kernel_spmd(ncA, [prep_A(inp, c) for c in cores], core_ids=cores).results
    hnT_full = []
    for b in range(4):
        full = np.zeros((D, S), dtype=resA[0]["HNT"].dtype)
        for r in range(2):
            full[:, own_index(r)] = resA[2 * b + r]["HNT"]
        hnT_full.append(full)
    ncB = _get("B", build_phase_B)
    resB = run_bass_kernel_spmd(ncB, [prep_B(inp, c, hnT_full[c // 2]) for c in cores], core_ids=cores).results
    ncC = _get("C", build_phase_C)
    mapsC = []
    for c in cores:
        b, r = c // 2, c % 2
        oi = own_index(r)
        gfull = np.concatenate([resB[2 * b]["G"], resB[2 * b + 1]["G"]], axis=0)
        mapsC.append(prep_C(inp, c, np.ascontiguousarray(gfull[:, oi]), resA[c]["H1"]))
    resC = run_bass_kernel_spmd(ncC, mapsC, core_ids=cores).results
    out = np.zeros((4, S, D), np.float32)
    for c in cores:
        b, r = c // 2, c % 2
        out[b, own_index(r)] = resC[c]["OUT"]
    return out


def kernel(**inputs):
    inp = {k: np.asarray(v) for k, v in inputs.items()}
    cores = list(range(NCORES))
    nc = _get("F", build_fused)
    res = run_bass_kernel_spmd(nc, [prep_fused(inp, c) for c in cores], core_ids=cores).results
    out = np.zeros((4, S, D), np.float32)
    for c in cores:
        b, r = c // 2, c % 2
        out[b, r * NOWN:(r + 1) * NOWN] = res[c]["OUT"]
    return out
```
